# Optimizing a Trainium2 kernel written in Bass

```python
import jax, jax.numpy as jnp
from jax import lax
import numpy as np

D_MODEL = 1024
BATCH = 16
SEQ = 2048
DEPTH = 4

NUM_MIXERS = 4
RMS_EPS = 1e-6
ROPE_THETA = 500000.0
ATTN_BLOCK = 128

SSM_D_INNER = 2 * D_MODEL
SSM_HEAD_DIM = 64
SSM_N_HEADS = SSM_D_INNER // SSM_HEAD_DIM
SSM_D_STATE = 128
SSM_N_GROUPS = 8
SSM_CONV = 4
SSM_CHUNK = 128
SSM_BC_DIM = SSM_N_GROUPS * SSM_D_STATE
SSM_CONV_DIM = SSM_D_INNER + 2 * SSM_BC_DIM
SSM_IN_COLS = SSM_D_INNER + SSM_CONV_DIM + SSM_N_HEADS
SSM_DT_MIN = 0.001
SSM_DT_MAX = 0.1

MLA_N_HEADS = 16
MLA_Q_LORA = 3 * D_MODEL // 8
MLA_KV_LORA = D_MODEL // 4
MLA_NOPE_DIM = 128
MLA_ROPE_DIM = 64
MLA_V_DIM = 128
MLA_QK_DIM = MLA_NOPE_DIM + MLA_ROPE_DIM
MLA_WIDTH = MLA_N_HEADS * MLA_V_DIM
MLA_IN_COLS = MLA_Q_LORA + MLA_KV_LORA + MLA_ROPE_DIM + MLA_WIDTH

FOX_N_HEADS = 16
FOX_HEAD_DIM = 128
FOX_WIDTH = FOX_N_HEADS * FOX_HEAD_DIM
FOX_IN_COLS = 3 * FOX_WIDTH + FOX_N_HEADS + FOX_WIDTH
FOX_F_BIAS_MEAN = 3.0

DIL_CONFIGS = ((128, 1), (512, 4), (2048, 16))
DIL_N_HEADS = 8
DIL_HEAD_DIM = 128
DIL_WIDTH = DIL_N_HEADS * DIL_HEAD_DIM
DIL_ROPE_DIM = DIL_HEAD_DIM // 4
DIL_IN_COLS = 3 * len(DIL_CONFIGS) * DIL_WIDTH + DIL_WIDTH

N_SSM_LAYERS = (DEPTH + 3) // NUM_MIXERS
N_MLA_LAYERS = (DEPTH + 2) // NUM_MIXERS
N_FOX_LAYERS = (DEPTH + 1) // NUM_MIXERS
N_DIL_LAYERS = DEPTH // NUM_MIXERS

kernel_name = 'hybrid_interleaved_ssd_mla_fox_dilated'


def rms_norm(x, w):
    xf = x.astype(jnp.float32)
    return (xf * lax.rsqrt(jnp.mean(xf * xf, axis=-1, keepdims=True) + RMS_EPS)).astype(x.dtype) * w


def rope(x, positions):
    half = x.shape[-1] // 2
    inv_freq = ROPE_THETA ** (-jnp.arange(half, dtype=jnp.float32) / half)
    ang = positions.astype(jnp.float32)[:, None] * inv_freq[None, :]
    cos = jnp.cos(ang)[None, :, None, :]
    sin = jnp.sin(ang)[None, :, None, :]
    x1 = x[..., :half].astype(jnp.float32)
    x2 = x[..., half:].astype(jnp.float32)
    return jnp.concatenate([x1 * cos - x2 * sin, x2 * cos + x1 * sin], axis=-1).astype(x.dtype)


def partial_rope(x, positions):
    return jnp.concatenate([rope(x[..., :DIL_ROPE_DIM], positions), x[..., DIL_ROPE_DIM:]], axis=-1)


def causal_depthwise_conv(x, w, b):
    k = w.shape[0]
    y = lax.conv_general_dilated(x, w[:, None, :], window_strides=(1,), padding=[(k - 1, 0)],
                                 dimension_numbers=('NWC', 'WIO', 'NWC'), feature_group_count=x.shape[-1])
    return y + b


def gated_group_rms_norm(y, z, w, groups):
    g = y * jax.nn.silu(z.astype(jnp.float32))
    shp = g.shape
    gg = g.reshape(shp[:-1] + (groups, shp[-1] // groups))
    gg = gg * lax.rsqrt(jnp.mean(gg * gg, axis=-1, keepdims=True) + RMS_EPS)
    return gg.reshape(shp) * w


def ssd_chunked(xdt, a, b_in, c_out):
    bsz, seq, heads, p = xdt.shape
    g, n = b_in.shape[2], b_in.shape[3]
    r = heads // g
    t = SSM_CHUNK
    nc = seq // t
    xc = xdt.reshape(bsz, nc, t, g, r, p)
    bc = b_in.reshape(bsz, nc, t, g, n)
    cc = c_out.reshape(bsz, nc, t, g, n)
    a_cum = jnp.cumsum(a.reshape(bsz, nc, t, g, r).transpose(0, 3, 4, 1, 2), axis=-1)
    causal = jnp.tril(jnp.ones((t, t), dtype=bool))
    seg = a_cum[..., :, None] - a_cum[..., None, :]
    decay_in = jnp.exp(jnp.where(causal, seg, -jnp.inf))
    cb = jnp.einsum('bctgn,bcsgn->bgcts', cc, bc)
    y_diag = jnp.einsum('bgcts,bgrcts,bcsgrp->bctgrp', cb, decay_in, xc)
    decay_to_end = jnp.exp(a_cum[..., -1:] - a_cum)
    chunk_states = jnp.einsum('bctgn,bgrct,bctgrp->bcgrpn', bc, decay_to_end, xc)
    chunk_decay = jnp.exp(a_cum[..., -1])

    def carry_state(state, inp):
        st, dec = inp
        return state * dec[..., None, None] + st, state

    init = jnp.zeros_like(chunk_states[:, 0])
    _, entering = lax.scan(carry_state, init,
                           (chunk_states.swapaxes(0, 1), chunk_decay.transpose(3, 0, 1, 2)))
    entering = entering.swapaxes(0, 1)
    y_off = jnp.einsum('bctgn,bcgrpn,bgrct->bctgrp', cc, entering, jnp.exp(a_cum))
    return (y_diag + y_off).reshape(bsz, seq, heads, p)


def causal_block_attention(q, k, v, scale, log_forget_cum=None):
    bsz, seq, heads, dk = q.shape
    nb = seq // ATTN_BLOCK
    key_pos = jnp.arange(seq)
    q_blocks = q.reshape(bsz, nb, ATTN_BLOCK, heads, dk).swapaxes(0, 1)
    block_ids = jnp.arange(nb)
    use_decay = log_forget_cum is not None
    if use_decay:
        c_k = log_forget_cum.astype(jnp.float32).transpose(0, 2, 1)
        c_blocks = c_k.reshape(bsz, heads, nb, ATTN_BLOCK).transpose(2, 0, 1, 3)
        xs = (q_blocks, block_ids, c_blocks)
    else:
        xs = (q_blocks, block_ids)

    def attend(args):
        qb, bi = args[0], args[1]
        s = jnp.einsum('bqhd,bkhd->bhqk', qb, k).astype(jnp.float32) * scale
        if use_decay:
            s = s + args[2][..., :, None] - c_k[:, :, None, :]
        q_pos = bi * ATTN_BLOCK + jnp.arange(ATTN_BLOCK)
        s = jnp.where(key_pos[None, :] <= q_pos[:, None], s, -jnp.inf)
        p = jax.nn.softmax(s, axis=-1)
        return jnp.einsum('bhqk,bkhd->bqhd', p.astype(v.dtype), v)

    out = lax.map(attend, xs)
    return out.swapaxes(0, 1).reshape(bsz, seq, heads, v.shape[-1])


def dilated_band_attention(q, k, v, dilation, span):
    bsz, seq, heads, dh = q.shape
    length = seq // dilation
    nseq = bsz * dilation

    def to_residue(u):
        return u.reshape(bsz, length, dilation, heads, dh).transpose(0, 2, 1, 3, 4).reshape(nseq, length, heads, dh)

    blk = min(span, length)
    nb = -(-length // blk)
    padded = nb * blk
    padw = ((0, 0), (0, padded - length), (0, 0), (0, 0))
    qs = jnp.pad(to_residue(q), padw).reshape(nseq, nb, blk, heads, dh)
    ks = jnp.pad(to_residue(k), padw).reshape(nseq, nb, blk, heads, dh)
    vs = jnp.pad(to_residue(v), padw).reshape(nseq, nb, blk, heads, dh)

    def with_prev(u):
        prev = jnp.pad(u, ((0, 0), (1, 0), (0, 0), (0, 0), (0, 0)))[:, :nb]
        return jnp.concatenate([prev, u], axis=2)

    kw, vw = with_prev(ks), with_prev(vs)
    s = jnp.einsum('nbqhd,nbkhd->nbhqk', qs, kw).astype(jnp.float32) * (dh ** -0.5)
    q_idx = jnp.arange(blk)[:, None] + blk
    k_idx = jnp.arange(2 * blk)[None, :]
    dist = q_idx - k_idx
    key_abs = (jnp.arange(nb) * blk - blk)[:, None] + jnp.arange(2 * blk)[None, :]
    valid = ((dist >= 0) & (dist <= span))[None, :, :] & (key_abs >= 0)[:, None, :]
    s = jnp.where(valid[None, :, None, :, :], s, -jnp.inf)
    m = jnp.max(s, axis=-1, keepdims=True)
    p = jnp.exp(s - m)
    den = jnp.sum(p, axis=-1)
    o = jnp.einsum('nbhqk,nbkhd->nbqhd', p, vw.astype(jnp.float32)) / den.transpose(0, 1, 3, 2)[..., None]
    lse = (m[..., 0] + jnp.log(den)).transpose(0, 1, 3, 2)
    o = o.reshape(nseq, padded, heads, dh)[:, :length]
    lse = lse.reshape(nseq, padded, heads)[:, :length]
    o = o.reshape(bsz, dilation, length, heads, dh).transpose(0, 2, 1, 3, 4).reshape(bsz, seq, heads, dh)
    lse = lse.reshape(bsz, dilation, length, heads).transpose(0, 2, 1, 3).reshape(bsz, seq, heads)
    return o, lse


def ssd_mixer(u, in_w, conv_w, conv_b, dt_bias, a_log, d_skip, norm_w, out_w):
    bsz, seq, _ = u.shape
    proj = u @ in_w
    z = proj[..., :SSM_D_INNER]
    xbc = proj[..., SSM_D_INNER:SSM_D_INNER + SSM_CONV_DIM]
    dt_raw = proj[..., SSM_D_INNER + SSM_CONV_DIM:]
    xbc = jax.nn.silu(causal_depthwise_conv(xbc, conv_w, conv_b)).astype(jnp.float32)
    xs = xbc[..., :SSM_D_INNER].reshape(bsz, seq, SSM_N_HEADS, SSM_HEAD_DIM)
    b_in = xbc[..., SSM_D_INNER:SSM_D_INNER + SSM_BC_DIM].reshape(bsz, seq, SSM_N_GROUPS, SSM_D_STATE)
    c_out = xbc[..., SSM_D_INNER + SSM_BC_DIM:].reshape(bsz, seq, SSM_N_GROUPS, SSM_D_STATE)
    dt = jax.nn.softplus(dt_raw.astype(jnp.float32) + dt_bias.astype(jnp.float32))
    a = dt * (-jnp.exp(a_log.astype(jnp.float32)))
    y = ssd_chunked(xs * dt[..., None], a, b_in, c_out)
    y = y + d_skip.astype(jnp.float32)[:, None] * xs
    y = gated_group_rms_norm(y.reshape(bsz, seq, SSM_D_INNER), z, norm_w, SSM_N_GROUPS)
    return y.astype(u.dtype) @ out_w


def mla_mixer(u, positions, in_w, q_norm_w, kv_norm_w, uq_w, ukv_w, out_w):
    bsz, seq, _ = u.shape
    proj = u @ in_w
    o1 = MLA_Q_LORA
    o2 = o1 + MLA_KV_LORA
    o3 = o2 + MLA_ROPE_DIM
    c_q, c_kv, k_r, z = proj[..., :o1], proj[..., o1:o2], proj[..., o2:o3], proj[..., o3:]
    q = (rms_norm(c_q, q_norm_w) @ uq_w).reshape(bsz, seq, MLA_N_HEADS, MLA_QK_DIM)
    kv = (rms_norm(c_kv, kv_norm_w) @ ukv_w).reshape(bsz, seq, MLA_N_HEADS, MLA_NOPE_DIM + MLA_V_DIM)
    q_nope, q_rope = q[..., :MLA_NOPE_DIM], rope(q[..., MLA_NOPE_DIM:], positions)
    k_nope, v = kv[..., :MLA_NOPE_DIM], kv[..., MLA_NOPE_DIM:]
    k_rope = rope(k_r[:, :, None, :], positions)
    q_full = jnp.concatenate([q_nope, q_rope], axis=-1)
    k_full = jnp.concatenate([k_nope, jnp.broadcast_to(k_rope, (bsz, seq, MLA_N_HEADS, MLA_ROPE_DIM))], axis=-1)
    o = causal_block_attention(q_full, k_full, v, MLA_QK_DIM ** -0.5)
    o = o.reshape(bsz, seq, MLA_WIDTH) * jax.nn.silu(z)
    return o @ out_w


def fox_mixer(u, in_w, f_bias, out_w):
    bsz, seq, _ = u.shape
    proj = u @ in_w
    w = FOX_WIDTH
    q = proj[..., :w].reshape(bsz, seq, FOX_N_HEADS, FOX_HEAD_DIM)
    k = proj[..., w:2 * w].reshape(bsz, seq, FOX_N_HEADS, FOX_HEAD_DIM)
    v = proj[..., 2 * w:3 * w].reshape(bsz, seq, FOX_N_HEADS, FOX_HEAD_DIM)
    f_logit = proj[..., 3 * w:3 * w + FOX_N_HEADS]
    z = proj[..., 3 * w + FOX_N_HEADS:]
    log_f = jax.nn.log_sigmoid(f_logit.astype(jnp.float32) + f_bias.astype(jnp.float32))
    c = jnp.cumsum(log_f, axis=1)
    o = causal_block_attention(q, k, v, FOX_HEAD_DIM ** -0.5, log_forget_cum=c)
    o = o.reshape(bsz, seq, FOX_WIDTH) * jax.nn.silu(z)
    return o @ out_w


def dilated_mixer(u, positions, in_w, out_w):
    bsz, seq, _ = u.shape
    proj = u @ in_w
    w = DIL_WIDTH
    outs, lses = [], []
    for gi, (window, dilation) in enumerate(DIL_CONFIGS):
        base = 3 * w * gi
        q = proj[..., base:base + w].reshape(bsz, seq, DIL_N_HEADS, DIL_HEAD_DIM)
        k = proj[..., base + w:base + 2 * w].reshape(bsz, seq, DIL_N_HEADS, DIL_HEAD_DIM)
        v = proj[..., base + 2 * w:base + 3 * w].reshape(bsz, seq, DIL_N_HEADS, DIL_HEAD_DIM)
        o, lse = dilated_band_attention(partial_rope(q, positions), partial_rope(k, positions), v,
                                        dilation, window // dilation)
        outs.append(o)
        lses.append(lse)
    mix = jax.nn.softmax(jnp.stack(lses, axis=0), axis=0)
    o = jnp.sum(mix[..., None] * jnp.stack(outs, axis=0), axis=0)
    z = proj[..., 3 * w * len(DIL_CONFIGS):]
    o = o.reshape(bsz, seq, w).astype(u.dtype) * jax.nn.silu(z)
    return o @ out_w


def setup_inputs(seed: int = 0) -> dict:
    key = jax.random.key(seed)
    ks = iter(jax.random.split(key, 32))
    f32 = jnp.float32

    def dense(shape, fan_in):
        return jax.random.normal(next(ks), shape, f32) * (fan_in ** -0.5)

    def gain(shape):
        return 1.0 + 0.02 * jax.random.normal(next(ks), shape, f32)

    x = jax.random.normal(next(ks), (BATCH, SEQ, D_MODEL), f32)
    norm_w = gain((DEPTH, D_MODEL))
    final_norm_w = gain((D_MODEL,))
    ssm_in_w = dense((N_SSM_LAYERS, D_MODEL, SSM_IN_COLS), D_MODEL)
    ssm_conv_w = dense((N_SSM_LAYERS, SSM_CONV, SSM_CONV_DIM), SSM_CONV)
    ssm_conv_b = 0.02 * jax.random.normal(next(ks), (N_SSM_LAYERS, SSM_CONV_DIM), f32)
    dt0 = jnp.exp(jax.random.uniform(next(ks), (N_SSM_LAYERS, SSM_N_HEADS), f32,
                                     minval=np.log(SSM_DT_MIN), maxval=np.log(SSM_DT_MAX)))
    ssm_dt_bias = dt0 + jnp.log(-jnp.expm1(-dt0))
    ssm_A_log = jnp.log(jax.random.uniform(next(ks), (N_SSM_LAYERS, SSM_N_HEADS), f32, minval=1.0, maxval=16.0))
    ssm_D = gain((N_SSM_LAYERS, SSM_N_HEADS))
    ssm_norm_w = gain((N_SSM_LAYERS, SSM_D_INNER))
    ssm_out_w = dense((N_SSM_LAYERS, SSM_D_INNER, D_MODEL), SSM_D_INNER)
    mla_in_w = dense((N_MLA_LAYERS, D_MODEL, MLA_IN_COLS), D_MODEL)
    mla_q_norm_w = gain((N_MLA_LAYERS, MLA_Q_LORA))
    mla_kv_norm_w = gain((N_MLA_LAYERS, MLA_KV_LORA))
    mla_uq_w = dense((N_MLA_LAYERS, MLA_Q_LORA, MLA_N_HEADS * MLA_QK_DIM), MLA_Q_LORA)
    mla_ukv_w = dense((N_MLA_LAYERS, MLA_KV_LORA, MLA_N_HEADS * (MLA_NOPE_DIM + MLA_V_DIM)), MLA_KV_LORA)
    mla_out_w = dense((N_MLA_LAYERS, MLA_WIDTH, D_MODEL), MLA_WIDTH)
    fox_in_w = dense((N_FOX_LAYERS, D_MODEL, FOX_IN_COLS), D_MODEL)
    fox_f_bias = FOX_F_BIAS_MEAN + 0.5 * jax.random.normal(next(ks), (N_FOX_LAYERS, FOX_N_HEADS), f32)
    fox_out_w = dense((N_FOX_LAYERS, FOX_WIDTH, D_MODEL), FOX_WIDTH)
    dil_in_w = dense((N_DIL_LAYERS, D_MODEL, DIL_IN_COLS), D_MODEL)
    dil_out_w = dense((N_DIL_LAYERS, DIL_WIDTH, D_MODEL), DIL_WIDTH)
    return {'x': x, 'norm_w': norm_w, 'final_norm_w': final_norm_w,
            'ssm_in_w': ssm_in_w, 'ssm_conv_w': ssm_conv_w, 'ssm_conv_b': ssm_conv_b,
            'ssm_dt_bias': ssm_dt_bias, 'ssm_A_log': ssm_A_log, 'ssm_D': ssm_D,
            'ssm_norm_w': ssm_norm_w, 'ssm_out_w': ssm_out_w,
            'mla_in_w': mla_in_w, 'mla_q_norm_w': mla_q_norm_w, 'mla_kv_norm_w': mla_kv_norm_w,
            'mla_uq_w': mla_uq_w, 'mla_ukv_w': mla_ukv_w, 'mla_out_w': mla_out_w,
            'fox_in_w': fox_in_w, 'fox_f_bias': fox_f_bias, 'fox_out_w': fox_out_w,
            'dil_in_w': dil_in_w, 'dil_out_w': dil_out_w}


def reference(x, norm_w, final_norm_w, ssm_in_w, ssm_conv_w, ssm_conv_b, ssm_dt_bias, ssm_A_log, ssm_D,
              ssm_norm_w, ssm_out_w, mla_in_w, mla_q_norm_w, mla_kv_norm_w, mla_uq_w, mla_ukv_w, mla_out_w,
              fox_in_w, fox_f_bias, fox_out_w, dil_in_w, dil_out_w):
    positions = jnp.arange(x.shape[1])
    h = x
    for i in range(DEPTH):
        kind, j = i % NUM_MIXERS, i // NUM_MIXERS
        u = rms_norm(h, norm_w[i])
        if kind == 0:
            y = ssd_mixer(u, ssm_in_w[j], ssm_conv_w[j], ssm_conv_b[j], ssm_dt_bias[j], ssm_A_log[j],
                          ssm_D[j], ssm_norm_w[j], ssm_out_w[j])
        elif kind == 1:
            y = mla_mixer(u, positions, mla_in_w[j], mla_q_norm_w[j], mla_kv_norm_w[j], mla_uq_w[j],
                          mla_ukv_w[j], mla_out_w[j])
        elif kind == 2:
            y = fox_mixer(u, fox_in_w[j], fox_f_bias[j], fox_out_w[j])
        else:
            y = dilated_mixer(u, positions, dil_in_w[j], dil_out_w[j])
        h = h + y.astype(h.dtype)
    return rms_norm(h, final_norm_w)
```

```python
import numpy as np
import concourse.bass as bass
import concourse.mybir as mybir
from concourse.bass_utils import run_bass_kernel_spmd

F32 = mybir.dt.float32
BF16 = mybir.dt.bfloat16
AF = mybir.ActivationFunctionType
ALU = mybir.AluOpType

SAME_ENGINE_SYNC = True
SEM_ROT = 30000
N_DMA_SEMS = 12
S = 2048
D = 1024
NSEQ = 2
EPS = 1e-6
ROPE_THETA = 500000.0


class Buf:
    __slots__ = ("name", "writer", "readers")

    def __init__(self, name):
        self.name = name
        self.writer = None
        self.readers = []


class Op:
    __slots__ = ("eng", "fn", "deps", "sig", "idx", "is_dma", "sem", "semval", "sigcount")

    def __init__(self, eng, fn, is_dma):
        self.eng = eng
        self.fn = fn
        self.deps = []
        self.sig = False
        self.is_dma = is_dma
        self.sem = None
        self.semval = 0
        self.sigcount = 0


class Prog:
    ENGS = ("pe", "act", "dve", "pool", "sp")

    def __init__(self, nc):
        self.nc = nc
        self.ops = {e: [] for e in self.ENGS}
        self.dma_sems = {}
        self.dma_rr = {}
        self.dma_last = {}
        for q in ("sp", "pool"):
            self.dma_sems[q] = [nc.alloc_semaphore(name=f"dq_{q}_{i}") for i in range(N_DMA_SEMS)]
            self.dma_rr[q] = 0
            self.dma_last[q] = [None] * N_DMA_SEMS
        self.dma_cnt = {}
        self.eng_sems = {}
        self.nbuf = 0

    def buf(self, name=None):
        self.nbuf += 1
        return Buf(name or f"b{self.nbuf}")

    def bufs(self, n, name="b"):
        return [self.buf(f"{name}{i}") for i in range(n)]

    def _add(self, eng, fn, reads, writes, is_dma=False):
        op = Op(eng, fn, is_dma)
        deps = []
        for b in reads:
            if b.writer is not None:
                deps.append(b.writer)
        for b in writes:
            if b.writer is not None:
                deps.append(b.writer)
            deps.extend(b.readers)
        if is_dma:
            q = eng
            i = self.dma_rr[q]
            self.dma_rr[q] = (i + 1) % N_DMA_SEMS
            prev = self.dma_last[q][i]
            if prev is not None:
                deps.append(prev)
            op.sem = self.dma_sems[q][i]
            key = (q, i)
            self.dma_cnt[key] = self.dma_cnt.get(key, 0) + 1
            op.semval = 16 * self.dma_cnt[key]
            self.dma_last[q][i] = op
        seen = set()
        for d in deps:
            if d is op or id(d) in seen:
                continue
            seen.add(id(d))
            if (not d.is_dma) and (not is_dma) and d.eng == eng:
                if eng == "pe" or not SAME_ENGINE_SYNC:
                    continue
            op.deps.append(d)
        op.idx = len(self.ops[eng])
        self.ops[eng].append(op)
        for b in reads:
            if not is_dma:
                b.readers = [r for r in b.readers if r.is_dma or r.eng != eng]
            b.readers.append(op)
        for b in writes:
            b.writer = op
            b.readers = []
        return op

    def mm(self, out, lhsT, rhs, start, stop, reads, writes):
        return self._add("pe", lambda e: e.matmul(out, lhsT, rhs, start=start, stop=stop), reads, writes)

    def act(self, out, in_, func, reads, writes, **kw):
        return self._add("act", lambda e: e.activation(out, in_, func, **kw), reads, writes)

    def tt(self, eng, out, in0, in1, op, reads, writes):
        return self._add(eng, lambda e: e.tensor_tensor(out, in0, in1, op), reads, writes)

    def ts(self, eng, out, in0, s1, s2, op0, op1, reads, writes):
        if op1 is None:
            return self._add(eng, lambda e: e.tensor_scalar(out, in0, s1, s2, op0), reads, writes)
        return self._add(eng, lambda e: e.tensor_scalar(out, in0, s1, s2, op0, op1), reads, writes)

    def stt(self, eng, out, in0, scalar, in1, op0, op1, reads, writes):
        return self._add(eng, lambda e: e.scalar_tensor_tensor(out, in0, scalar, in1, op0, op1), reads, writes)

    def copy(self, eng, out, in_, reads, writes):
        if eng == "act":
            return self._add(eng, lambda e: e.copy(out, in_), reads, writes)
        return self._add(eng, lambda e: e.tensor_copy(out, in_), reads, writes)

    def memset(self, eng, ap, val, writes):
        return self._add(eng, lambda e: e.memset(ap, val), [], writes)

    def recip(self, out, in_, reads, writes):
        return self._add("dve", lambda e: e.reciprocal(out, in_), reads, writes)

    def dma(self, q, out, in_, reads, writes):
        return self._add(q, lambda e: e.dma_start(out, in_), reads, writes, is_dma=True)

    def barrier(self):
        last = []
        for e in self.ENGS:
            for op in reversed(self.ops[e]):
                if not op.is_dma:
                    last.append(op)
                    break
        for q in self.dma_last:
            for op in self.dma_last[q]:
                if op is not None:
                    last.append(op)
        for e in self.ENGS:
            op = Op(e, None, False)
            for d in last:
                if d.eng == e and not d.is_dma and e == "pe":
                    continue
                op.deps.append(d)
            op.idx = len(self.ops[e])
            self.ops[e].append(op)

    def emit(self):
        nc = self.nc
        for e in self.ENGS:
            for op in self.ops[e]:
                for d in op.deps:
                    if not d.is_dma:
                        d.sig = True
        for e in self.ENGS:
            c = 0
            for op in self.ops[e]:
                if op.is_dma:
                    continue
                if op.sig:
                    c += 1
                    op.sigcount = c
            nsem = (c + SEM_ROT - 1) // SEM_ROT
            self.eng_sems[e] = [nc.alloc_semaphore(name=f"es_{e}_{i}") for i in range(max(nsem, 1))]
        engobj = {"pe": "tensor", "act": "scalar", "dve": "vector", "pool": "gpsimd", "sp": "sync"}
        stats = {}
        with nc.Block() as block:
            for e in self.ENGS:
                ops = self.ops[e]
                if not ops:
                    continue

                def body(eng, ops=ops, e=e):
                    waited = {}
                    nw = 0
                    for op in ops:
                        for d in op.deps:
                            if d.is_dma:
                                sem, val = d.sem, d.semval
                            else:
                                k = (d.sigcount - 1) // SEM_ROT
                                sem = self.eng_sems[d.eng][k]
                                val = (d.sigcount - 1) % SEM_ROT + 1
                            key = sem.num
                            if waited.get(key, 0) >= val:
                                continue
                            waited[key] = val
                            eng.wait_ge(sem, val)
                            nw += 1
                        if op.fn is None:
                            if op.sig:
                                k = (op.sigcount - 1) // SEM_ROT
                                eng.nop().then_inc(self.eng_sems[e][k], 1)
                            continue
                        ins = op.fn(eng)
                        if op.is_dma:
                            ins.then_inc(op.sem, 16)
                        elif op.sig:
                            k = (op.sigcount - 1) // SEM_ROT
                            ins.then_inc(self.eng_sems[e][k], 1)
                    stats[e] = (len(ops), nw)

                getattr(block, engobj[e])(body)
        return stats


class Ring:
    def __init__(self, P, tiles, name):
        self.tiles = tiles
        self.bufs = P.bufs(len(tiles), name)
        self.i = 0

    def next(self):
        i = self.i
        self.i = (i + 1) % len(self.tiles)
        return self.tiles[i], self.bufs[i]


class Ctx:
    pass


def setup_common(nc, P, dram):
    c = Ctx()
    c.nc, c.P, c.dram = nc, P, dram
    c.hT = nc.alloc_sbuf_tensor("hT", [128, 8, S], F32)
    c.BhT = [[P.buf(f"hT{k}_{t}") for t in range(4)] for k in range(8)]
    c.uT = nc.alloc_sbuf_tensor("uT", [128, 8, S], BF16)
    c.BuT = [P.buf(f"uT{t}") for t in range(4)]
    pst = [nc.alloc_psum_tensor(f"ps{i}", [128, 512], F32) for i in range(8)]
    c.G = Ring(P, pst[0:4], "psG")
    c.A = Ring(P, pst[4:6], "psA")
    c.Dn = Ring(P, pst[6:8], "psD")
    c.cst = nc.alloc_sbuf_tensor("cst", [128, 5 * 128], F32)
    c.Bcst = P.buf("cst")
    P.dma("sp", c.cst[:, :], dram["consts"], [], [c.Bcst])
    c.cstb = nc.alloc_sbuf_tensor("cstb", [128, 5 * 128], BF16)
    c.Bcstb = P.buf("cstb")
    P.copy("dve", c.cstb[:, :], c.cst[:, :], [c.Bcst], [c.Bcstb])
    c.ident_f = c.cst[:, 0:128]
    c.ones_f = c.cst[:, 256:384]
    c.ident_b = c.cstb[:, 0:128]
    c.tri_b = c.cstb[:, 128:256]
    c.ones_b = c.cstb[:, 256:384]
    c.normw = nc.alloc_sbuf_tensor("sb_normw", [128, 5, 8], F32)
    c.Bnormw = P.buf("normw")
    P.dma("sp", c.normw[:, :, :], dram["normw"], [], [c.Bnormw])
    c.sq = Ring(P, [nc.alloc_sbuf_tensor(f"sq{i}", [128, 512], F32) for i in range(2)], "sq")
    c.rstd = Ring(P, [nc.alloc_sbuf_tensor(f"rstd{i}", [128, 512], F32) for i in range(2)], "rstd")
    c.wn = Ring(P, [nc.alloc_sbuf_tensor(f"wn{i}", [128, 8, 128], BF16) for i in range(4)], "wn")
    c.ww = Ring(P, [nc.alloc_sbuf_tensor(f"ww{i}", [128, 8, 256], BF16) for i in range(2)], "ww")
    c.wo = Ring(P, [nc.alloc_sbuf_tensor(f"wo{i}", [128, 2, 1024], BF16) for i in range(2)], "wo")
    c.evac_rr = 0
    return c


def rms_stats(c, src, Bsrc_fn, nk, tt, scale_inv_n):
    P = c.P
    ps, Bps = c.G.next()
    for k in range(nk):
        sq, Bsq = c.sq.next()
        P.act(sq[:, :], src[:, k, tt * 512:(tt + 1) * 512], AF.Square, [Bsrc_fn(k, tt)], [Bsq])
        P.mm(ps[:, :], c.ones_f, sq[:, :], k == 0, k == nk - 1, [Bsq, c.Bcst], [Bps])
    r, Br = c.rstd.next()
    P.act(r[:, :], ps[:, :], AF.Sqrt, [Bps], [Br], scale=scale_inv_n, bias=EPS)
    P.recip(r[:, :], r[:, :], [Br], [Br])
    return r, Br


def emit_rmsnorm_u(c, layer):
    P = c.P
    for tt in range(4):
        r, Br = rms_stats(c, c.hT, lambda k, t: c.BhT[k][t], 8, tt, 1.0 / D)
        for k in range(8):
            eng = "dve"
            P.stt(eng, c.uT[:, k, tt * 512:(tt + 1) * 512], c.hT[:, k, tt * 512:(tt + 1) * 512],
                  c.normw[:, layer, k:k + 1], r[:, :], ALU.mult, ALU.mult,
                  [c.BhT[k][tt], Br, c.Bnormw], [c.BuT[tt]])


def load_wn(c, src):
    w, Bw = c.wn.next()
    kc = src.shape[1]
    c.P.dma("pool", w[:, 0:kc, :], src, [], [Bw])
    return w, Bw


def proj_multi(c, wsrcs, evac_multi, rhs=None, Brhs=None, kc=8):
    P = c.P
    ws = []
    for src, m in wsrcs:
        w, Bw = c.wn.next()
        P.dma("pool", w[:, 0:kc, 0:m], src, [], [Bw])
        ws.append((w, Bw, m))
    if rhs is None:
        rhs, Brhs = c.uT, c.BuT
    for tt in range(4):
        pss = []
        for (w, Bw, m) in ws:
            ps, Bps = c.G.next()
            for k in range(kc):
                P.mm(ps[0:m, :], w[:, k, 0:m], rhs[:, k, tt * 512:(tt + 1) * 512], k == 0, k == kc - 1,
                     [Bw, Brhs[tt]], [Bps])
            pss.append((ps, Bps))
        evac_multi(pss, tt)


def proj_fm(c, wsrc, evac, rhs=None, Brhs=None, kc=8):
    proj_multi(c, [(wsrc, 128)], lambda pss, tt: evac(pss[0][0], pss[0][1], tt), rhs, Brhs, kc)


def evac_copy(c, dst, Bdst):
    def f(ps, Bps, tt):
        c.evac_rr += 1
        eng = "act" if c.evac_rr % 2 == 0 else "dve"
        c.P.copy(eng, dst[:, tt * 512:(tt + 1) * 512], ps[:, :], [Bps], [Bdst])
    return f


def evac_silu(c, dst, Bdst):
    def f(ps, Bps, tt):
        c.P.act(dst[:, tt * 512:(tt + 1) * 512], ps[:, :], AF.Silu, [Bps], [Bdst])
    return f


def proj_tm(c, wsrc, ncols, evac, lhs=None, Blhs=None, kc=8):
    P = c.P
    w, Bw = c.ww.next()
    P.dma("pool", w[:, 0:kc, 0:ncols], wsrc, [], [Bw])
    if lhs is None:
        lhs, Blhs = c.uT, c.BuT
    for tb in range(16):
        ps, Bps = c.G.next()
        for k in range(kc):
            P.mm(ps[:, 0:ncols], lhs[:, k, tb * 128:(tb + 1) * 128], w[:, k, 0:ncols], k == 0, k == kc - 1,
                 [Bw, Blhs[tb // 4]], [Bps])
        evac(ps, Bps, tb)


def outproj_acc(c, wsrc, gT, BgT, nk):
    P = c.P
    w, Bw = c.wo.next()
    P.dma("pool", w[:, 0:nk, :], wsrc, [], [Bw])
    for oc in range(8):
        for tt in range(4):
            ps, Bps = c.G.next()
            for j in range(nk):
                P.mm(ps[:, :], w[:, j, oc * 128:(oc + 1) * 128], gT[:, j, tt * 512:(tt + 1) * 512],
                     j == 0, j == nk - 1, [Bw, BgT], [Bps])
            P.tt("dve", c.hT[:, oc, tt * 512:(tt + 1) * 512], ps[:, :], c.hT[:, oc, tt * 512:(tt + 1) * 512],
                 ALU.add, [Bps, c.BhT[oc][tt]], [c.BhT[oc][tt]])


def attn_head(c, parts, Bparts, v_fn, Bv, scale, bias_fn, Bbias, gz, Bgz, gT, BgT, L):
    P = c.P
    for qt in range(4):
        oacc, Bo = c.A.next()
        dacc, Bd = c.Dn.next()
        nkb = 4 * qt + 4
        for kb in range(nkb):
            d = kb - 4 * qt
            c0 = max(d, 0) * 128
            ps, Bps = c.G.next()
            for i, (kT, qT) in enumerate(parts):
                P.mm(ps[:, c0:512], kT[:, kb * 128:(kb + 1) * 128], qT[:, qt * 512 + c0:(qt + 1) * 512],
                     i == 0, i == len(parts) - 1, Bparts, [Bps])
            pt, Bpt = L.pt.next()
            if bias_fn is None:
                P.act(pt[:, c0:512], ps[:, c0:512], AF.Exp, [Bps], [Bpt], scale=scale)
            else:
                for jj in range(c0 // 128, 4):
                    P.act(pt[:, jj * 128:(jj + 1) * 128], ps[:, jj * 128:(jj + 1) * 128], AF.Exp,
                          [Bps, Bbias], [Bpt], scale=scale, bias=bias_fn(kb, 4 * qt + jj))
            if d >= 0:
                P.tt("pool", pt[:, c0:c0 + 128], pt[:, c0:c0 + 128], c.tri_b, ALU.mult, [Bpt, c.Bcstb], [Bpt])
            P.mm(oacc[:, c0:512], v_fn(kb), pt[:, c0:512], kb == 0, kb == nkb - 1, [Bv, Bpt], [Bo])
            P.mm(dacc[:, c0:512], c.ones_b, pt[:, c0:512], kb == 0, kb == nkb - 1, [c.Bcstb, Bpt], [Bd])
        rd, Brd = L.rden.next()
        P.recip(rd[:, :], dacc[:, :], [Bd], [Brd])
        P.tt("dve", rd[:, :], oacc[:, :], rd[:, :], ALU.mult, [Bo, Brd], [Brd])
        P.tt("pool", gT[:, qt * 512:(qt + 1) * 512], rd[:, :], gz[:, qt * 512:(qt + 1) * 512], ALU.mult,
             [Brd, Bgz], [BgT])


def emit_fox(c, layer):
    nc, P, dram = c.nc, c.P, c.dram
    L = Ctx()
    L.pt = Ring(P, [nc.alloc_sbuf_tensor(f"fx_pt{i}", [128, 512], BF16) for i in range(4)], "pt")
    L.rden = Ring(P, [nc.alloc_sbuf_tensor(f"fx_rd{i}", [128, 512], F32) for i in range(2)], "rden")
    qT = nc.alloc_sbuf_tensor("fx_qT", [128, 2, S], BF16); BqT = P.buf("qT")
    kT = nc.alloc_sbuf_tensor("fx_kT", [128, 2, S], BF16); BkT = P.buf("kT")
    gz = nc.alloc_sbuf_tensor("fx_gz", [128, 2, S], BF16); Bgz = P.buf("gz")
    gT = nc.alloc_sbuf_tensor("fx_gT", [128, 2, S], BF16); BgT = P.buf("gT")
    vt = nc.alloc_sbuf_tensor("fx_vt", [128, 16, 256], BF16); Bvt = P.buf("vt")
    lsp = nc.alloc_sbuf_tensor("fx_lsp", [128, 16, 16], F32); Blsp = P.buf("lsp")
    cT = nc.alloc_sbuf_tensor("fx_cT", [128, 16, 16], F32); BcT = P.buf("cT")
    cref = nc.alloc_sbuf_tensor("fx_cref", [128, 16, 16], F32); Bcref = P.buf("cref")
    fb = nc.alloc_sbuf_tensor("fx_fb", [128, 16], F32); Bfb = P.buf("fb")
    tmp = nc.alloc_sbuf_tensor("fx_tmp", [128, 16], F32); Btmp = P.buf("tmp")
    btab = nc.alloc_sbuf_tensor("fx_btab", [128, 2, 16, 16], F32); Bbt = P.buf("btab")
    negtri_f = c.cst[:, 384:512]
    P.dma("sp", fb[:, :], dram["fox_fb"], [], [Bfb])
    wf_src = dram["fox_wf"]
    scale = 128 ** -0.5

    def evac_f(ps, Bps, tb):
        P.tt("dve", tmp[:, :], ps[:, 0:16], fb[:, :], ALU.add, [Bps, Bfb], [Btmp])
        P.act(tmp[:, :], tmp[:, :], AF.Exp, [Btmp], [Btmp], scale=-1.0)
        P.act(lsp[:, tb, :], tmp[:, :], AF.Ln, [Btmp], [Blsp], bias=1.0)
    proj_tm(c, wf_src, 16, evac_f)
    negones_f = c.cst[:, 512:640]
    for tb in range(16):
        ps, Bps = c.G.next()
        for t2 in range(tb + 1):
            lhs = negtri_f if t2 == tb else negones_f
            P.mm(ps[:, 0:16], lhs, lsp[:, t2, :], t2 == 0, t2 == tb, [c.Bcst, Blsp], [Bps])
        P.copy("dve", cT[:, tb, :], ps[:, 0:16], [Bps], [BcT])
    ps, Bps = c.G.next()
    for tb in range(16):
        rhs = lsp[:, tb:tb + 1, :].to_broadcast([128, 16 - tb, 16])
        P.mm(ps[:, tb * 16:256].rearrange("p (j h) -> p j h", h=16), negones_f, rhs, tb == 0, tb == 15,
             [c.Bcst, Blsp], [Bps])
    P.copy("dve", cref[:, :, :].rearrange("p j h -> p (j h)"), ps[:, 0:256], [Bps], [Bcref])

    for hp in range(8):
        for j in range(2):
            h = 2 * hp + j
            proj_fm(c, dram["fox_wq"][h], evac_copy(c, qT[:, j, :], BqT))
            proj_fm(c, dram["fox_wk"][h], evac_copy(c, kT[:, j, :], BkT))
            proj_fm(c, dram["fox_wz"][h], evac_silu(c, gz[:, j, :], Bgz))
            P.tt("dve", btab[:, j, :, :],
                 cref[:, :, h:h + 1].rearrange("p j o -> p o j").to_broadcast([128, 16, 16]),
                 cT[:, :, h:h + 1].to_broadcast([128, 16, 16]),
                 ALU.subtract, [Bcref, BcT], [Bbt])

        def evac_v(ps, Bps, tb):
            c.evac_rr += 1
            eng = "act" if c.evac_rr % 2 == 0 else "dve"
            P.copy(eng, vt[:, tb, :], ps[:, 0:256], [Bps], [Bvt])
        proj_tm(c, dram["fox_wv"][hp], 256, evac_v)
        for j in range(2):
            attn_head(c, [(kT[:, j, :], qT[:, j, :])], [BkT, BqT],
                      lambda kb, j=j: vt[:, kb, j * 128:(j + 1) * 128], Bvt, scale,
                      lambda kb, jq, j=j: btab[:, j, kb, jq:jq + 1], Bbt,
                      gz[:, j, :], Bgz, gT[:, j, :], BgT, L)
        outproj_acc(c, dram["fox_wo"][hp], gT, BgT, 2)


def rope_combine(c, L, psX, BpsX, psXr, BpsXr, rows, cos_ap, sin_ap, Btab, out_ap, Bout, d=1):
    P = c.P
    t1, Bt1 = L.rt.next()
    t2, Bt2 = L.rt.next()
    P.tt("dve", t1[0:rows, :], psX[0:rows, :], cos_ap, ALU.mult, [BpsX, Btab], [Bt1])
    P.tt("dve", t2[0:rows, :], psXr[0:rows, :], sin_ap, ALU.mult, [BpsXr, Btab], [Bt2])
    a1, a2 = t1[0:rows, :], t2[0:rows, :]
    if d > 1:
        a1 = a1.rearrange("p (n r) -> p n r", r=d)
        a2 = a2.rearrange("p (n r) -> p n r", r=d)
    P.tt("pool", out_ap, a1, a2, ALU.add, [Bt1, Bt2], [Bout])


def normed_proj(c, L, wsrcs, nw_ap, Bnw, dst, Bdst, inv_n):
    P = c.P
    nk = len(wsrcs)

    def ev(pss, tt):
        psS, BpS = c.A.next()
        for k, (ps, Bps) in enumerate(pss):
            sq, Bsq = c.sq.next()
            P.act(sq[:, :], ps[:, :], AF.Square, [Bps], [Bsq])
            P.mm(psS[:, :], c.ones_f, sq[:, :], k == 0, k == nk - 1, [Bsq, c.Bcst], [BpS])
        r, Br = c.rstd.next()
        P.act(r[:, :], psS[:, :], AF.Sqrt, [BpS], [Br], scale=inv_n, bias=EPS)
        P.recip(r[:, :], r[:, :], [Br], [Br])
        for k, (ps, Bps) in enumerate(pss):
            P.stt("dve", dst[:, k, tt * 512:(tt + 1) * 512], ps[:, :], nw_ap[:, k:k + 1], r[:, :], ALU.mult, ALU.mult,
                  [Bps, Br, Bnw], [Bdst[tt]])
    proj_multi(c, [(w, 128) for w in wsrcs], ev)


def emit_mla(c, layer):
    nc, P, dram = c.nc, c.P, c.dram
    L = Ctx()
    L.pt = Ring(P, [nc.alloc_sbuf_tensor(f"ml_pt{i}", [128, 512], BF16) for i in range(4)], "pt")
    L.rt = Ring(P, [nc.alloc_sbuf_tensor(f"ml_rt{i}", [128, 512], F32) for i in range(3)], "rt")
    L.rden = L.rt
    cqn = nc.alloc_sbuf_tensor("ml_cqn", [128, 3, S], BF16); Bcqn = P.bufs(4, "cqn")
    ckvn = nc.alloc_sbuf_tensor("ml_ckvn", [128, 2, S], BF16); Bckvn = P.bufs(4, "ckvn")
    kr = nc.alloc_sbuf_tensor("ml_kr", [128, S], BF16); Bkr = P.buf("kr")
    qn = nc.alloc_sbuf_tensor("ml_qn", [128, S], BF16); Bqn = P.buf("qn")
    qr = nc.alloc_sbuf_tensor("ml_qr", [128, S], BF16); Bqr = P.buf("qr")
    kn = nc.alloc_sbuf_tensor("ml_kn", [128, S], BF16); Bkn = P.buf("kn")
    gz = nc.alloc_sbuf_tensor("ml_gz", [128, 1, S], BF16); Bgz = P.buf("gz")
    gT = nc.alloc_sbuf_tensor("ml_gT", [128, 1, S], BF16); BgT = P.buf("gT")
    vt = nc.alloc_sbuf_tensor("ml_vt", [128, 16, 128], BF16); Bvt = P.buf("vt")
    tab = nc.alloc_sbuf_tensor("ml_tab", [128, 2, S], F32); Btab = P.buf("tab")
    nws = nc.alloc_sbuf_tensor("ml_nws", [128, 5], F32); Bnws = P.buf("nws")
    P.dma("sp", tab[:, 0, :], dram["mla_rope"][0], [], [Btab])
    P.dma("sp", tab[:, 1, :], dram["mla_rope"][1], [], [Btab])
    P.dma("sp", nws[:, :], dram["mla_nw"], [], [Bnws])
    scale = 192 ** -0.5

    normed_proj(c, L, [dram["mla_wcq"][i] for i in range(3)], nws[:, 0:3], Bnws, cqn, Bcqn, 1.0 / 384)
    normed_proj(c, L, [dram["mla_wckv"][i] for i in range(2)], nws[:, 3:5], Bnws, ckvn, Bckvn, 1.0 / 256)

    def ev_kr(pss, tt):
        (pX, BX), (pXr, BXr) = pss
        rope_combine(c, L, pX, BX, pXr, BXr, 64, tab[0:64, 0, tt * 512:(tt + 1) * 512],
                     tab[0:64, 1, tt * 512:(tt + 1) * 512], Btab, kr[0:64, tt * 512:(tt + 1) * 512], Bkr)
    proj_multi(c, [(dram["mla_wkr"], 64), (dram["mla_wkrr"], 64)], ev_kr)

    for h in range(16):
        proj_fm(c, dram["mla_wqn"][h], evac_copy(c, qn, Bqn), cqn, Bcqn, 3)

        def ev_qr(pss, tt):
            (pX, BX), (pXr, BXr) = pss
            rope_combine(c, L, pX, BX, pXr, BXr, 64, tab[0:64, 0, tt * 512:(tt + 1) * 512],
                         tab[0:64, 1, tt * 512:(tt + 1) * 512], Btab, qr[0:64, tt * 512:(tt + 1) * 512], Bqr)
        proj_multi(c, [(dram["mla_wqr"][h], 64), (dram["mla_wqrr"][h], 64)], ev_qr, cqn, Bcqn, 3)
        proj_fm(c, dram["mla_wkn"][h], evac_copy(c, kn, Bkn), ckvn, Bckvn, 2)
        proj_fm(c, dram["mla_wz"][h], evac_silu(c, gz[:, 0, :], Bgz))

        def evac_v(ps, Bps, tb):
            c.evac_rr += 1
            eng = "act" if c.evac_rr % 2 == 0 else "dve"
            P.copy(eng, vt[:, tb, :], ps[:, 0:128], [Bps], [Bvt])
        proj_tm(c, dram["mla_wv"][h], 128, evac_v, ckvn, Bckvn, 2)
        attn_head(c, [(kn, qn), (kr[0:64, :], qr[0:64, :])], [Bkn, Bqn, Bkr, Bqr],
                  lambda kb: vt[:, kb, :], Bvt, scale, None, None,
                  gz[:, 0, :], Bgz, gT[:, 0, :], BgT, L)
        outproj_acc(c, dram["mla_wo"][h], gT, BgT, 1)


DIL_CFG = ((128, 1), (512, 4), (2048, 16))
import os as _os
DIL_GROUPS = [int(x) for x in _os.environ.get('DIL_GROUPS', '0,1,2').split(',')]


def emit_dil(c, layer):
    nc, P, dram = c.nc, c.P, c.dram
    L = Ctx()
    L.pt = Ring(P, [nc.alloc_sbuf_tensor(f"dl_pt{i}", [128, 256], BF16) for i in range(4)], "pt")
    L.rt = Ring(P, [nc.alloc_sbuf_tensor(f"dl_rt{i}", [128, 512], F32) for i in range(4)], "rt")
    qT = nc.alloc_sbuf_tensor("dl_qT", [128, S], BF16); BqT = P.buf("qT")
    kT = nc.alloc_sbuf_tensor("dl_kT", [128, S], BF16); BkT = P.buf("kT")
    vt = nc.alloc_sbuf_tensor("dl_vt", [128, 16, 128], BF16); Bvt = P.buf("vt")
    gz = nc.alloc_sbuf_tensor("dl_gz", [128, 1, S], BF16); Bgz = P.buf("gz")
    gT = nc.alloc_sbuf_tensor("dl_gT", [128, 1, S], BF16); BgT = P.buf("gT")
    oN = nc.alloc_sbuf_tensor("dl_oN", [128, S], F32); BoN = P.buf("oN")
    dN = nc.alloc_sbuf_tensor("dl_dN", [128, S], F32); BdN = P.buf("dN")
    tab = nc.alloc_sbuf_tensor("dl_tab", [128, 2, S], F32); Btab = P.buf("tab")
    msk = nc.alloc_sbuf_tensor("dl_msk", [128, 256], BF16); Bmsk = P.buf("msk")
    P.dma("sp", tab[:, 0, :], dram["dil_rope"][0], [], [Btab])
    P.dma("sp", tab[:, 1, :], dram["dil_rope"][1], [], [Btab])
    P.dma("pool", msk[:, :], dram["dil_mask"], [], [Bmsk])
    scale = 128 ** -0.5

    for h in range(8):
        proj_fm(c, dram["dil_wz"][h], evac_silu(c, gz[:, 0, :], Bgz))
        for g, (window, d) in enumerate(DIL_CFG):
            if g not in DIL_GROUPS:
                continue
            nsub = S // d
            nb = nsub // 128

            def ev_rope(dst, Bdst):
                def f(pss, tt):
                    (pX, BX), (pXr, BXr) = pss
                    npt = 512 // d
                    if d == 1:
                        out_ap = dst[:, tt * 512:(tt + 1) * 512]
                        rope_combine(c, L, pX, BX, pXr, BXr, 128, tab[:, 0, tt * 512:(tt + 1) * 512],
                                     tab[:, 1, tt * 512:(tt + 1) * 512], Btab, out_ap, Bdst)
                    else:
                        out_ap = dst[:, :].rearrange("p (r n) -> p n r", r=d)[:, tt * npt:(tt + 1) * npt, :]
                        rope_combine(c, L, pX, BX, pXr, BXr, 128, tab[:, 0, tt * 512:(tt + 1) * 512],
                                     tab[:, 1, tt * 512:(tt + 1) * 512], Btab, out_ap, Bdst, d)
                return f
            proj_multi(c, [(dram["dil_wq"][g * 8 + h], 128), (dram["dil_wqr"][g * 8 + h], 128)], ev_rope(qT, BqT))
            proj_multi(c, [(dram["dil_wk"][g * 8 + h], 128), (dram["dil_wkr"][g * 8 + h], 128)], ev_rope(kT, BkT))
            wv, Bwv = c.wn.next()
            P.dma("pool", wv[:, :, :], dram["dil_wv"][g * 8 + h], [], [Bwv])
            for r in range(d):
                for kb in range(nb):
                    blk = r * nb + kb
                    t0 = kb * 128 * d + r
                    ps, Bps = c.G.next()
                    for k in range(8):
                        lhs = c.uT[:, k, t0:t0 + 127 * d + 1:d]
                        P.mm(ps[:, 0:128], lhs, wv[:, k, :], k == 0, k == 7, [Bwv] + c.BuT, [Bps])
                    c.evac_rr += 1
                    P.copy("act" if c.evac_rr % 2 == 0 else "dve", vt[:, blk, :], ps[:, 0:128], [Bps], [Bvt])
            for bank in range(4):
                oacc, Bo = c.A.next()
                dacc, Bd = c.Dn.next()
                for qi in range(4):
                    blk = bank * 4 + qi
                    b = blk % nb
                    qs = blk * 128
                    ps, Bps = c.G.next()
                    nk = 2 if b > 0 else 1
                    P.mm(ps[:, 0:128], kT[:, qs:qs + 128], qT[:, qs:qs + 128], True, True, [BkT, BqT], [Bps])
                    if b > 0:
                        P.mm(ps[:, 128:256], kT[:, qs - 128:qs], qT[:, qs:qs + 128], True, True, [BkT, BqT], [Bps])
                    pt, Bpt = L.pt.next()
                    P.act(pt[:, 0:128 * nk], ps[:, 0:128 * nk], AF.Exp, [Bps], [Bpt], scale=scale)
                    P.tt("pool", pt[:, 0:128 * nk], pt[:, 0:128 * nk], msk[:, 0:128 * nk], ALU.mult, [Bpt, Bmsk], [Bpt])
                    oc = oacc[:, qi * 128:(qi + 1) * 128]
                    dc = dacc[:, qi * 128:(qi + 1) * 128]
                    P.mm(oc, vt[:, blk, :], pt[:, 0:128], True, b == 0, [Bvt, Bpt], [Bo])
                    if b > 0:
                        P.mm(oc, vt[:, blk - 1, :], pt[:, 128:256], False, True, [Bvt, Bpt], [Bo])
                    P.mm(dc, c.ones_b, pt[:, 0:128], True, b == 0, [c.Bcstb, Bpt], [Bd])
                    if b > 0:
                        P.mm(dc, c.ones_b, pt[:, 128:256], False, True, [c.Bcstb, Bpt], [Bd])
                pieces = []
                if d == 1:
                    pieces.append((oN[:, bank * 512:(bank + 1) * 512], dN[:, bank * 512:(bank + 1) * 512], oacc[:, :], dacc[:, :]))
                elif d == 4:
                    pieces.append((oN[:, bank:S:4], dN[:, bank:S:4], oacc[:, :], dacc[:, :]))
                else:
                    for q4 in range(4):
                        r = bank * 4 + q4
                        pieces.append((oN[:, r:S:16], dN[:, r:S:16], oacc[:, q4 * 128:(q4 + 1) * 128], dacc[:, q4 * 128:(q4 + 1) * 128]))
                for (on, dn, oa, da) in pieces:
                    if g == DIL_GROUPS[0]:
                        P.copy("act", on, oa, [Bo], [BoN])
                        P.copy("dve", dn, da, [Bd], [BdN])
                    else:
                        P.tt("dve", on, oa, on, ALU.add, [Bo, BoN], [BoN])
                        P.tt("dve", dn, da, dn, ALU.add, [Bd, BdN], [BdN])
        for tt in range(4):
            sl = slice(tt * 512, (tt + 1) * 512)
            P.recip(dN[:, sl], dN[:, sl], [BdN], [BdN])
            P.tt("dve", oN[:, sl], oN[:, sl], dN[:, sl], ALU.mult, [BoN, BdN], [BoN])
            P.tt("pool", gT[:, 0, sl], oN[:, sl], gz[:, 0, sl], ALU.mult, [BoN, Bgz], [BgT])
        outproj_acc(c, dram["dil_wo"][h], gT, BgT, 1)


def emit_ssd(c, layer):
    nc, P, dram = c.nc, c.P, c.dram
    tri_f = c.cst[:, 128:256]
    negtri_f = c.cst[:, 384:512]
    negones_f = c.cst[:, 512:640]
    dt = nc.alloc_sbuf_tensor("sd_dt", [128, 16, 32], F32); Bdt = P.buf("dt")
    absa = nc.alloc_sbuf_tensor("sd_absa", [128, 16, 32], F32); Babsa = P.buf("absa")
    acum = nc.alloc_sbuf_tensor("sd_acum", [128, 16, 32], F32); Bacum = P.buf("acum")
    wts = nc.alloc_sbuf_tensor("sd_w", [128, 16, 32], F32); Bw = P.buf("w")
    dlast = nc.alloc_sbuf_tensor("sd_dlast", [128, 16, 32], F32); Bdl = P.buf("dlast")
    vecs = nc.alloc_sbuf_tensor("sd_vecs", [128, 3, 32], F32); Bvecs = P.buf("vecs")
    cw = nc.alloc_sbuf_tensor("sd_cw", [128, 32, 5], F32); Bcw = P.buf("cw")
    dsk = nc.alloc_sbuf_tensor("sd_dsk", [128, 16], F32); Bdsk = P.buf("dsk")
    nrm = nc.alloc_sbuf_tensor("sd_nrm", [128, 16], F32); Bnrm = P.buf("nrm")
    tmp32 = nc.alloc_sbuf_tensor("sd_tmp32", [128, 32], F32); Bt32 = P.buf("t32")
    pre = nc.alloc_sbuf_tensor("sd_pre", [128, S + 3], F32); Bpre = P.buf("pre")
    cacc = Ring(P, [nc.alloc_sbuf_tensor(f"sd_cacc{i}", [128, 512], F32) for i in range(2)], "cacc")
    xTf = nc.alloc_sbuf_tensor("sd_xTf", [128, 2, S], F32); BxTf = P.bufs(2, "xTf")
    xTb = nc.alloc_sbuf_tensor("sd_xTb", [128, 2, S], BF16); BxTb = P.bufs(2, "xTb")
    BT = nc.alloc_sbuf_tensor("sd_BT", [128, S], BF16); BBT = P.buf("BT")
    CT = nc.alloc_sbuf_tensor("sd_CT", [128, S], BF16); BCT = P.buf("CT")
    gz = nc.alloc_sbuf_tensor("sd_gz", [128, 2, S], BF16); Bgz = P.buf("gz")
    f128 = Ring(P, [nc.alloc_sbuf_tensor(f"sd_f{i}", [128, 128], F32) for i in range(6)], "f128")
    b128 = Ring(P, [nc.alloc_sbuf_tensor(f"sd_b{i}", [128, 128], BF16) for i in range(8)], "b128")
    b256 = Ring(P, [nc.alloc_sbuf_tensor(f"sd_c{i}", [128, 256], BF16) for i in range(4)], "b256")
    cbmR = Ring(P, [nc.alloc_sbuf_tensor(f"sd_cbm{i}", [128, 128], F32) for i in range(2)], "cbm")
    btokR = Ring(P, [nc.alloc_sbuf_tensor(f"sd_btok{i}", [128, 128], BF16) for i in range(2)], "btok")
    Sf = nc.alloc_sbuf_tensor("sd_Sf", [128, 256], F32); BSf = P.buf("Sf")
    Sb = nc.alloc_sbuf_tensor("sd_Sb", [128, 256], BF16); BSb = P.buf("Sb")
    P.dma("sp", vecs[:, :, :], dram["ssm_vecs"], [], [Bvecs])
    P.dma("sp", cw[:, :, :], dram["ssm_cw"], [], [Bcw])
    P.dma("sp", dsk[:, :], dram["ssm_dsk"], [], [Bdsk])
    P.dma("sp", nrm[:, :], dram["ssm_nrm"], [], [Bnrm])
    P.memset("pool", pre[:, 0:3], 0.0, [Bpre])
    P.act(vecs[:, 1, :], vecs[:, 1, :], AF.Exp, [Bvecs], [Bvecs])

    def evac_dt(ps, Bps, tb):
        P.tt("dve", tmp32[:, :], ps[:, 0:32], vecs[:, 0, :], ALU.add, [Bps, Bvecs], [Bt32])
        P.act(tmp32[:, :], tmp32[:, :], AF.Exp, [Bt32], [Bt32])
        P.act(dt[:, tb, :], tmp32[:, :], AF.Ln, [Bt32], [Bdt], bias=1.0)
        P.tt("dve", absa[:, tb, :], dt[:, tb, :], vecs[:, 1, :], ALU.mult, [Bdt, Bvecs], [Babsa])
    proj_tm(c, dram["ssm_wdt"], 32, evac_dt)
    for tb in range(16):
        ps, Bps = c.G.next()
        P.mm(ps[:, 0:32], negtri_f, absa[:, tb, :], True, True, [c.Bcst, Babsa], [Bps])
        P.mm(ps[:, 32:64], negones_f, absa[:, tb, :], True, True, [c.Bcst, Babsa], [Bps])
        P.copy("dve", acum[:, tb, :], ps[:, 0:32], [Bps], [Bacum])
        P.copy("dve", dlast[:, tb, :], ps[:, 32:64], [Bps], [Bdl])
        P.tt("dve", wts[:, tb, :], dlast[:, tb, :], acum[:, tb, :], ALU.subtract, [Bdl, Bacum], [Bw])
        P.act(wts[:, tb, :], wts[:, tb, :], AF.Exp, [Bw], [Bw])
        P.tt("dve", wts[:, tb, :], wts[:, tb, :], dt[:, tb, :], ALU.mult, [Bw, Bdt], [Bw])
        P.act(dlast[:, tb, :], dlast[:, tb, :], AF.Exp, [Bdl], [Bdl])

    def conv_silu(ch, outs):
        for tt in range(4):
            a, Ba = cacc.next()
            o = tt * 512
            P.ts("dve", a[:, :], pre[:, o:o + 512], cw[:, ch, 0:1], cw[:, ch, 4:5], ALU.mult, ALU.add, [Bpre, Bcw], [Ba])
            for k in range(1, 4):
                P.stt("dve", a[:, :], pre[:, o + k:o + k + 512], cw[:, ch, k:k + 1], a[:, :], ALU.mult, ALU.add,
                      [Bpre, Bcw, Ba], [Ba])
            for (dst, Bdst) in outs:
                P.act(dst[:, o:o + 512], a[:, :], AF.Silu, [Ba], [Bdst])

    def evac_pre(ps, Bps, tt):
        P.copy("act", pre[:, 3 + tt * 512:3 + (tt + 1) * 512], ps[:, :], [Bps], [Bpre])

    for g in range(8):
        for i in range(2):
            proj_fm(c, dram["ssm_wx"][2 * g + i], evac_pre)
            conv_silu(2 * g + i, [(xTf[:, i, :], BxTf[i]), (xTb[:, i, :], BxTb[i])])
            proj_fm(c, dram["ssm_wz"][2 * g + i], evac_silu(c, gz[:, i, :], Bgz))
        proj_fm(c, dram["ssm_wB"][g], evac_pre)
        conv_silu(16 + g, [(BT, BBT)])
        proj_fm(c, dram["ssm_wC"][g], evac_pre)
        conv_silu(24 + g, [(CT, BCT)])
        for ck in range(16):
            cs = slice(ck * 128, (ck + 1) * 128)
            ps, Bps = c.G.next()
            for i in range(2):
                P.mm(ps[:, i * 128:(i + 1) * 128], xTb[:, i, cs], c.ident_b, True, True, [BxTb[i], c.Bcstb], [Bps])
            P.mm(ps[:, 256:384], BT[:, cs], c.ident_b, True, True, [BBT, c.Bcstb], [Bps])
            xtok, Bxtok = b256.next()
            P.copy("dve", xtok[:, :], ps[:, 0:256], [Bps], [Bxtok])
            btok, Bbtok = btokR.next()
            P.copy("dve", btok[:, :], ps[:, 256:384], [Bps], [Bbtok])
            ps2, Bps2 = c.G.next()
            P.mm(ps2[:, 0:128], BT[:, cs], CT[:, cs], True, True, [BBT, BCT], [Bps2])
            cbm, Bcbm = cbmR.next()
            P.tt("dve", cbm[:, :], ps2[:, 0:128], tri_f, ALU.mult, [Bps2, c.Bcst], [Bcbm])
            xw, Bxw = b256.next()
            for j in range(4):
                P.ts("dve" if j % 2 == 0 else "pool", xw[:, j * 64:(j + 1) * 64], xtok[:, j * 64:(j + 1) * 64],
                     wts[:, ck, 4 * g + j:4 * g + j + 1], None, ALU.mult, None, [Bxtok, Bw], [Bxw])
            pst, Bpst = c.A.next()
            P.mm(pst[:, 0:256], btok[:, :], xw[:, :], True, True, [Bbtok, Bxw], [Bpst])
            for i in range(2):
                yps, Byps = c.Dn.next()
                for jj in range(2):
                    j = 2 * i + jj
                    h = 4 * g + j
                    pb, Bpb = c.G.next()
                    P.mm(pb[:, 0:128], absa[:, ck, h:h + 1].to_broadcast([128, 128]), negtri_f, True, True,
                         [Babsa, c.Bcst], [Bpb])
                    dm, Bdm = f128.next()
                    P.ts("dve", dm[:, :], pb[:, 0:128], acum[:, ck, h:h + 1], 0.0, ALU.subtract, ALU.min, [Bpb, Bacum], [Bdm])
                    P.act(dm[:, :], dm[:, :], AF.Exp, [Bdm], [Bdm])
                    mp, Bmp = b128.next()
                    P.stt("dve", mp[:, :], dm[:, :], dt[:, ck, h:h + 1], cbm[:, :], ALU.mult, ALU.mult, [Bdm, Bdt, Bcbm], [Bmp])
                    P.mm(yps[64 * jj:64 * jj + 64, 0:128], xtok[:, j * 64:(j + 1) * 64], mp[:, :], True, ck == 0,
                         [Bxtok, Bmp], [Byps])
                    if ck > 0:
                        ee, Bee = f128.next()
                        P.act(ee[:, :], pb[:, 0:128], AF.Exp, [Bpb, Bdm], [Bee])
                        csd, Bcsd = b128.next()
                        P.tt("dve", csd[:, :], ee[:, :], CT[:, cs], ALU.mult, [BCT, Bee], [Bcsd])
                        P.mm(yps[64 * jj:64 * jj + 64, 0:128], Sb[:, j * 64:(j + 1) * 64], csd[:, :], False, True,
                             [BSb, Bcsd], [Byps])
                P.stt("dve", xTf[:, i, cs], xTf[:, i, cs], dsk[:, 2 * g + i:2 * g + i + 1], yps[:, 0:128], ALU.mult, ALU.add,
                      [BxTf[i], Bdsk, Byps], [BxTf[i]])
            if ck == 0:
                P.copy("dve", Sf[:, :], pst[:, 0:256], [Bpst], [BSf])
            else:
                for j in range(4):
                    P.stt("dve", Sf[:, j * 64:(j + 1) * 64], Sf[:, j * 64:(j + 1) * 64], dlast[:, ck, 4 * g + j:4 * g + j + 1],
                          pst[:, j * 64:(j + 1) * 64], ALU.mult, ALU.add, [BSf, Bdl, Bpst], [BSf])
            if ck < 15:
                P.copy("act", Sb[:, :], Sf[:, :], [BSf], [BSb])
        for tt in range(4):
            sl = slice(tt * 512, (tt + 1) * 512)
            for i in range(2):
                P.tt("pool", xTf[:, i, sl], xTf[:, i, sl], gz[:, i, sl], ALU.mult, [BxTf[i], Bgz], [BxTf[i]])
            r, Br = rms_stats(c, xTf, lambda k, t: BxTf[k], 2, tt, 1.0 / 256)
            for i in range(2):
                P.stt("dve", gz[:, i, sl], xTf[:, i, sl], nrm[:, 2 * g + i:2 * g + i + 1], r[:, :], ALU.mult, ALU.mult,
                      [BxTf[i], Br, Bnrm], [Bgz])
        outproj_acc(c, dram["ssm_wo"][g], gz, Bgz, 2)


def tile_cols(W, width=128):
    K, N = W.shape
    return np.ascontiguousarray(W.reshape(K // 128, 128, N // width, width).transpose(2, 1, 0, 3))


def tile_rows(W, nk):
    R, N = W.shape
    return np.ascontiguousarray(W.reshape(R // (128 * nk), nk, 128, N).transpose(0, 2, 1, 3))


def rope_tables(half, reps):
    inv_freq = (np.float32(ROPE_THETA) ** (-np.arange(half, dtype=np.float32) / np.float32(half))).astype(np.float32)
    ang = np.arange(S, dtype=np.float32)[None, :] * inv_freq[:, None]
    cos = np.cos(ang).astype(np.float32)
    sin = np.sin(ang).astype(np.float32)
    t = np.zeros((2, 128, S), np.float32)
    t[0] = 1.0
    for r in range(reps):
        b = r * 2 * half
        t[0, b:b + half] = cos
        t[0, b + half:b + 2 * half] = cos
        t[1, b:b + half] = -sin
        t[1, b + half:b + 2 * half] = sin
    return t


def make_consts():
    cst = np.zeros((128, 5 * 128), np.float32)
    i = np.arange(128)
    cst[:, 0:128] = np.eye(128)
    cst[:, 128:256] = (i[:, None] <= i[None, :])
    cst[:, 256:384] = 1.0
    cst[:, 384:512] = -(i[:, None] <= i[None, :]).astype(np.float32)
    cst[:, 512:640] = -1.0
    return cst


def host_prep(inputs, layers):
    shared = {}
    shared["consts"] = make_consts()
    nw = np.concatenate([inputs["norm_w"], inputs["final_norm_w"][None]], axis=0)
    shared["normw"] = np.ascontiguousarray(nw.reshape(5, 8, 128).transpose(2, 0, 1))
    if 0 in layers:
        W = inputs["ssm_in_w"][0]
        shared["ssm_wz"] = tile_cols(W[:, 0:2048])
        shared["ssm_wx"] = tile_cols(W[:, 2048:4096])
        shared["ssm_wB"] = tile_cols(W[:, 4096:5120])
        shared["ssm_wC"] = tile_cols(W[:, 5120:6144])
        shared["ssm_wdt"] = tile_cols(W[:, 6144:6176], 32)[0]
        vec = np.stack([inputs["ssm_dt_bias"][0], inputs["ssm_A_log"][0], inputs["ssm_D"][0]], 0)
        shared["ssm_vecs"] = np.ascontiguousarray(np.broadcast_to(vec[None], (128, 3, 32)))
        cwb = np.concatenate([inputs["ssm_conv_w"][0], inputs["ssm_conv_b"][0][None]], 0)
        shared["ssm_cw"] = np.ascontiguousarray(cwb.reshape(5, 32, 128).transpose(2, 1, 0))
        shared["ssm_dsk"] = np.ascontiguousarray(np.repeat(inputs["ssm_D"][0], 64).reshape(16, 128).T)
        shared["ssm_nrm"] = np.ascontiguousarray(inputs["ssm_norm_w"][0].reshape(16, 128).T)
        shared["ssm_wo"] = tile_rows(inputs["ssm_out_w"][0], 2)
    if 1 in layers:
        W = inputs["mla_in_w"][0]
        perm = np.concatenate([np.arange(32, 64), np.arange(0, 32)])
        shared["mla_wcq"] = tile_cols(W[:, 0:384])
        shared["mla_wckv"] = tile_cols(W[:, 384:640])
        shared["mla_wkr"] = tile_cols(W[:, 640:704], 64)[0]
        shared["mla_wkrr"] = tile_cols(W[:, 640:704][:, perm], 64)[0]
        shared["mla_wz"] = tile_cols(W[:, 704:2752])
        UQ = inputs["mla_uq_w"][0].reshape(384, 16, 192)
        shared["mla_wqn"] = tile_cols(np.ascontiguousarray(UQ[:, :, 0:128]).reshape(384, 2048))
        shared["mla_wqr"] = tile_cols(np.ascontiguousarray(UQ[:, :, 128:192]).reshape(384, 1024), 64)
        shared["mla_wqrr"] = tile_cols(np.ascontiguousarray(UQ[:, :, 128:192][:, :, perm]).reshape(384, 1024), 64)
        UKV = inputs["mla_ukv_w"][0].reshape(256, 16, 256)
        shared["mla_wkn"] = tile_cols(np.ascontiguousarray(UKV[:, :, 0:128]).reshape(256, 2048))
        shared["mla_wv"] = tile_cols(np.ascontiguousarray(UKV[:, :, 128:256]).reshape(256, 2048))
        shared["mla_wo"] = tile_rows(inputs["mla_out_w"][0], 1)
        nwq = inputs["mla_q_norm_w"][0].reshape(3, 128).T
        nwkv = inputs["mla_kv_norm_w"][0].reshape(2, 128).T
        shared["mla_nw"] = np.ascontiguousarray(np.concatenate([nwq, nwkv], axis=1))
        shared["mla_rope"] = rope_tables(32, 2)
    if 3 in layers:
        W = inputs["dil_in_w"][0]
        perm = np.arange(128)
        perm[0:16] = np.arange(16, 32)
        perm[16:32] = np.arange(0, 16)
        wq, wk, wv, wqr, wkr = [], [], [], [], []
        for g in range(3):
            base = 3072 * g
            Q = W[:, base:base + 1024].reshape(1024, 8, 128)
            Kw = W[:, base + 1024:base + 2048].reshape(1024, 8, 128)
            wq.append(tile_cols(Q.reshape(1024, 1024)))
            wk.append(tile_cols(Kw.reshape(1024, 1024)))
            wqr.append(tile_cols(np.ascontiguousarray(Q[:, :, perm]).reshape(1024, 1024)))
            wkr.append(tile_cols(np.ascontiguousarray(Kw[:, :, perm]).reshape(1024, 1024)))
            wv.append(tile_cols(W[:, base + 2048:base + 3072]))
        shared["dil_wq"] = np.concatenate(wq, 0)
        shared["dil_wk"] = np.concatenate(wk, 0)
        shared["dil_wqr"] = np.concatenate(wqr, 0)
        shared["dil_wkr"] = np.concatenate(wkr, 0)
        shared["dil_wv"] = np.concatenate(wv, 0)
        shared["dil_wz"] = tile_cols(W[:, 9216:10240])
        shared["dil_wo"] = tile_rows(inputs["dil_out_w"][0], 1)
        shared["dil_rope"] = rope_tables(16, 1)
        i = np.arange(128)
        m = np.zeros((128, 256), np.float32)
        m[:, 0:128] = (i[:, None] <= i[None, :])
        m[:, 128:256] = (i[:, None] >= i[None, :])
        shared["dil_mask"] = m
    if 2 in layers:
        W = inputs["fox_in_w"][0]
        shared["fox_wq"] = tile_cols(W[:, 0:2048])
        shared["fox_wk"] = tile_cols(W[:, 2048:4096])
        shared["fox_wv"] = tile_cols(W[:, 4096:6144], 256)
        shared["fox_wf"] = tile_cols(W[:, 6144:6160], 16)[0]
        shared["fox_wz"] = tile_cols(W[:, 6160:8208])
        shared["fox_fb"] = np.ascontiguousarray(np.broadcast_to(inputs["fox_f_bias"][0][None, :], (128, 16)))
        shared["fox_wo"] = tile_rows(inputs["fox_out_w"][0], 2)
    return shared


def build_program(layers, shared_shapes, final_norm):
    nc = bass.Bass("TRN2", target_bir_lowering=False)
    dram = {}
    for k, shp in shared_shapes.items():
        dram[k] = nc.dram_tensor(k, list(shp), F32, kind="ExternalInput").ap()
    hin = nc.dram_tensor("hin", [NSEQ, 8, 128, S], F32, kind="ExternalInput").ap()
    hout = nc.dram_tensor("hout", [NSEQ, 8, 128, S], F32, kind="ExternalOutput").ap()
    P = Prog(nc)
    c = setup_common(nc, P, dram)
    Bout = P.buf("hout")
    for s in range(NSEQ):
        for k in range(8):
            for t in range(4):
                P.dma("sp", c.hT[:, k, t * 512:(t + 1) * 512], hin[s, k, :, t * 512:(t + 1) * 512], [], [c.BhT[k][t]])
        for layer in layers:
            emit_rmsnorm_u(c, layer)
            emit_layer_cached(c, layer, LAYER_EMITTERS[layer])
        if final_norm:
            for tt in range(4):
                r, Br = rms_stats(c, c.hT, lambda k, t: c.BhT[k][t], 8, tt, 1.0 / D)
                for k in range(8):
                    eng = "dve"
                    P.stt(eng, c.hT[:, k, tt * 512:(tt + 1) * 512], c.hT[:, k, tt * 512:(tt + 1) * 512],
                          c.normw[:, 4, k:k + 1], r[:, :], ALU.mult, ALU.mult,
                          [c.BhT[k][tt], Br, c.Bnormw], [c.BhT[k][tt]])
        for k in range(8):
            for t in range(4):
                P.dma("sp", hout[s, k, :, t * 512:(t + 1) * 512], c.hT[:, k, t * 512:(t + 1) * 512], [c.BhT[k][t]], [Bout])
    P.barrier()
    stats = P.emit()
    return nc, stats


def emit_layer_cached(c, layer, fn):
    from contextlib import ExitStack
    nc = c.nc
    c._scope_id = getattr(c, "_scope_id", 0) + 1
    sid = c._scope_id
    with ExitStack() as st:
        class NCProxy:
            def __getattr__(self, a):
                if a == "alloc_sbuf_tensor":
                    return lambda name, shape, dtype: st.enter_context(nc.sbuf_tensor(f"{name}_s{sid}", shape, dtype))
                return getattr(nc, a)
        c.nc = NCProxy()
        try:
            fn(c, layer)
        finally:
            c.nc = nc
        c.P.barrier()


_CACHE = {}
LAYER_EMITTERS = {0: emit_ssd, 1: emit_mla, 2: emit_fox, 3: emit_dil}


def run_layers(hT_all, inputs, layers, final_norm):
    shared = host_prep(inputs, layers)
    key = (tuple(layers), final_norm)
    if key not in _CACHE:
        _CACHE[key] = build_program(layers, {k: v.shape for k, v in shared.items()}, final_norm)
    nc, stats = _CACHE[key]
    in_maps = []
    for core in range(8):
        m = dict(shared)
        m["hin"] = np.ascontiguousarray(hT_all[core * NSEQ:(core + 1) * NSEQ])
        in_maps.append(m)
    res = run_bass_kernel_spmd(nc, in_maps, core_ids=list(range(8)))
    return np.concatenate([r["hout"] for r in res.results], axis=0)


def to_fm(x):
    B = x.shape[0]
    return np.ascontiguousarray(x.transpose(0, 2, 1).reshape(B, 8, 128, S))


def from_fm(hT):
    B = hT.shape[0]
    return np.ascontiguousarray(hT.reshape(B, D, S).transpose(0, 2, 1))


def kernel(**inputs):
    inputs = {k: np.asarray(v, dtype=np.float32) for k, v in inputs.items()}
    hT = to_fm(inputs["x"])
    hT = run_layers(hT, inputs, [0, 1, 2, 3], True)
    return from_fm(hT)
```

```python
import numpy as np
import concourse.bass as bass
import concourse.mybir as mybir
from concourse.bass_utils import run_bass_kernel_spmd

F32 = mybir.dt.float32
BF16 = mybir.dt.bfloat16
AF = mybir.ActivationFunctionType
ALU = mybir.AluOpType

SAME_ENGINE_SYNC = True
SEM_ROT = 30000
N_DMA_SEMS = 12
S = 2048
D = 1024
NSEQ = 2
EPS = 1e-6
ROPE_THETA = 500000.0


class Buf:
    __slots__ = ("name", "writer", "readers")

    def __init__(self, name):
        self.name = name
        self.writer = None
        self.readers = []


class Op:
    __slots__ = ("eng", "fn", "deps", "sig", "idx", "is_dma", "sem", "semval", "sigcount")

    def __init__(self, eng, fn, is_dma):
        self.eng = eng
        self.fn = fn
        self.deps = []
        self.sig = False
        self.is_dma = is_dma
        self.sem = None
        self.semval = 0
        self.sigcount = 0


class Prog:
    ENGS = ("pe", "act", "dve", "pool", "sp")

    def __init__(self, nc):
        self.nc = nc
        self.ops = {e: [] for e in self.ENGS}
        self.dma_sems = {}
        self.dma_rr = {}
        self.dma_last = {}
        for q in ("sp", "pool"):
            self.dma_sems[q] = [nc.alloc_semaphore(name=f"dq_{q}_{i}") for i in range(N_DMA_SEMS)]
            self.dma_rr[q] = 0
            self.dma_last[q] = [None] * N_DMA_SEMS
        self.dma_cnt = {}
        self.eng_sems = {}
        self.nbuf = 0

    def buf(self, name=None):
        self.nbuf += 1
        return Buf(name or f"b{self.nbuf}")

    def bufs(self, n, name="b"):
        return [self.buf(f"{name}{i}") for i in range(n)]

    def _add(self, eng, fn, reads, writes, is_dma=False):
        op = Op(eng, fn, is_dma)
        deps = []
        for b in reads:
            if b.writer is not None:
                deps.append(b.writer)
        for b in writes:
            if b.writer is not None:
                deps.append(b.writer)
            deps.extend(b.readers)
        if is_dma:
            q = eng
            i = self.dma_rr[q]
            self.dma_rr[q] = (i + 1) % N_DMA_SEMS
            prev = self.dma_last[q][i]
            if prev is not None:
                deps.append(prev)
            op.sem = self.dma_sems[q][i]
            key = (q, i)
            self.dma_cnt[key] = self.dma_cnt.get(key, 0) + 1
            op.semval = 16 * self.dma_cnt[key]
            self.dma_last[q][i] = op
        seen = set()
        for d in deps:
            if d is op or id(d) in seen:
                continue
            seen.add(id(d))
            if (not d.is_dma) and (not is_dma) and d.eng == eng:
                if eng == "pe" or not SAME_ENGINE_SYNC:
                    continue
            op.deps.append(d)
        op.idx = len(self.ops[eng])
        self.ops[eng].append(op)
        for b in reads:
            if not is_dma:
                b.readers = [r for r in b.readers if r.is_dma or r.eng != eng]
            b.readers.append(op)
        for b in writes:
            b.writer = op
            b.readers = []
        return op

    def mm(self, out, lhsT, rhs, start, stop, reads, writes):
        return self._add("pe", lambda e: e.matmul(out, lhsT, rhs, start=start, stop=stop), reads, writes)

    def act(self, out, in_, func, reads, writes, **kw):
        return self._add("act", lambda e: e.activation(out, in_, func, **kw), reads, writes)

    def tt(self, eng, out, in0, in1, op, reads, writes):
        return self._add(eng, lambda e: e.tensor_tensor(out, in0, in1, op), reads, writes)

    def ts(self, eng, out, in0, s1, s2, op0, op1, reads, writes):
        if op1 is None:
            return self._add(eng, lambda e: e.tensor_scalar(out, in0, s1, s2, op0), reads, writes)
        return self._add(eng, lambda e: e.tensor_scalar(out, in0, s1, s2, op0, op1), reads, writes)

    def stt(self, eng, out, in0, scalar, in1, op0, op1, reads, writes):
        return self._add(eng, lambda e: e.scalar_tensor_tensor(out, in0, scalar, in1, op0, op1), reads, writes)

    def copy(self, eng, out, in_, reads, writes):
        if eng == "act":
            return self._add(eng, lambda e: e.copy(out, in_), reads, writes)
        return self._add(eng, lambda e: e.tensor_copy(out, in_), reads, writes)

    def memset(self, eng, ap, val, writes):
        return self._add(eng, lambda e: e.memset(ap, val), [], writes)

    def recip(self, out, in_, reads, writes):
        return self._add("dve", lambda e: e.reciprocal(out, in_), reads, writes)

    def dma(self, q, out, in_, reads, writes):
        return self._add(q, lambda e: e.dma_start(out, in_), reads, writes, is_dma=True)

    def barrier(self):
        last = []
        for e in self.ENGS:
            for op in reversed(self.ops[e]):
                if not op.is_dma:
                    last.append(op)
                    break
        for q in self.dma_last:
            for op in self.dma_last[q]:
                if op is not None:
                    last.append(op)
        for e in self.ENGS:
            op = Op(e, None, False)
            for d in last:
                if d.eng == e and not d.is_dma and e == "pe":
                    continue
                op.deps.append(d)
            op.idx = len(self.ops[e])
            self.ops[e].append(op)

    def emit(self):
        nc = self.nc
        for e in self.ENGS:
            for op in self.ops[e]:
                for d in op.deps:
                    if not d.is_dma:
                        d.sig = True
        for e in self.ENGS:
            c = 0
            for op in self.ops[e]:
                if op.is_dma:
                    continue
                if op.sig:
                    c += 1
                    op.sigcount = c
            nsem = (c + SEM_ROT - 1) // SEM_ROT
            self.eng_sems[e] = [nc.alloc_semaphore(name=f"es_{e}_{i}") for i in range(max(nsem, 1))]
        engobj = {"pe": "tensor", "act": "scalar", "dve": "vector", "pool": "gpsimd", "sp": "sync"}
        stats = {}
        with nc.Block() as block:
            for e in self.ENGS:
                ops = self.ops[e]
                if not ops:
                    continue

                def body(eng, ops=ops, e=e):
                    waited = {}
                    nw = 0
                    for op in ops:
                        for d in op.deps:
                            if d.is_dma:
                                sem, val = d.sem, d.semval
                            else:
                                k = (d.sigcount - 1) // SEM_ROT
                                sem = self.eng_sems[d.eng][k]
                                val = (d.sigcount - 1) % SEM_ROT + 1
                            key = sem.num
                            if waited.get(key, 0) >= val:
                                continue
                            waited[key] = val
                            eng.wait_ge(sem, val)
                            nw += 1
                        if op.fn is None:
                            if op.sig:
                                k = (op.sigcount - 1) // SEM_ROT
                                eng.nop().then_inc(self.eng_sems[e][k], 1)
                            continue
                        ins = op.fn(eng)
                        if op.is_dma:
                            ins.then_inc(op.sem, 16)
                        elif op.sig:
                            k = (op.sigcount - 1) // SEM_ROT
                            ins.then_inc(self.eng_sems[e][k], 1)
                    stats[e] = (len(ops), nw)

                getattr(block, engobj[e])(body)
        return stats


class Ring:
    def __init__(self, P, tiles, name):
        self.tiles = tiles
        self.bufs = P.bufs(len(tiles), name)
        self.i = 0

    def next(self):
        i = self.i
        self.i = (i + 1) % len(self.tiles)
        return self.tiles[i], self.bufs[i]


class Ctx:
    pass


def setup_common(nc, P, dram):
    c = Ctx()
    c.nc, c.P, c.dram = nc, P, dram
    c.hT = nc.alloc_sbuf_tensor("hT", [128, 8, S], F32)
    c.BhT = [[P.buf(f"hT{k}_{t}") for t in range(4)] for k in range(8)]
    c.uT = nc.alloc_sbuf_tensor("uT", [128, 8, S], BF16)
    c.BuT = [P.buf(f"uT{t}") for t in range(4)]
    pst = [nc.alloc_psum_tensor(f"ps{i}", [128, 512], F32) for i in range(8)]
    c.G = Ring(P, pst[0:4], "psG")
    c.A = Ring(P, pst[4:6], "psA")
    c.Dn = Ring(P, pst[6:8], "psD")
    c.cst = nc.alloc_sbuf_tensor("cst", [128, 5 * 128], F32)
    c.Bcst = P.buf("cst")
    P.dma("sp", c.cst[:, :], dram["consts"], [], [c.Bcst])
    c.cstb = nc.alloc_sbuf_tensor("cstb", [128, 5 * 128], BF16)
    c.Bcstb = P.buf("cstb")
    P.copy("dve", c.cstb[:, :], c.cst[:, :], [c.Bcst], [c.Bcstb])
    c.ident_f = c.cst[:, 0:128]
    c.ones_f = c.cst[:, 256:384]
    c.ident_b = c.cstb[:, 0:128]
    c.tri_b = c.cstb[:, 128:256]
    c.ones_b = c.cstb[:, 256:384]
    c.normw = nc.alloc_sbuf_tensor("sb_normw", [128, 5, 8], F32)
    c.Bnormw = P.buf("normw")
    P.dma("sp", c.normw[:, :, :], dram["normw"], [], [c.Bnormw])
    c.sq = Ring(P, [nc.alloc_sbuf_tensor(f"sq{i}", [128, 512], F32) for i in range(2)], "sq")
    c.rstd = Ring(P, [nc.alloc_sbuf_tensor(f"rstd{i}", [128, 512], F32) for i in range(2)], "rstd")
    c.wn = Ring(P, [nc.alloc_sbuf_tensor(f"wn{i}", [128, 8, 128], BF16) for i in range(4)], "wn")
    c.ww = Ring(P, [nc.alloc_sbuf_tensor(f"ww{i}", [128, 8, 256], BF16) for i in range(2)], "ww")
    c.wo = Ring(P, [nc.alloc_sbuf_tensor(f"wo{i}", [128, 2, 1024], BF16) for i in range(2)], "wo")
    c.evac_rr = 0
    return c


def rms_stats(c, src, Bsrc_fn, nk, tt, scale_inv_n):
    P = c.P
    ps, Bps = c.G.next()
    for k in range(nk):
        sq, Bsq = c.sq.next()
        P.act(sq[:, :], src[:, k, tt * 512:(tt + 1) * 512], AF.Square, [Bsrc_fn(k, tt)], [Bsq])
        P.mm(ps[:, :], c.ones_f, sq[:, :], k == 0, k == nk - 1, [Bsq, c.Bcst], [Bps])
    r, Br = c.rstd.next()
    P.act(r[:, :], ps[:, :], AF.Sqrt, [Bps], [Br], scale=scale_inv_n, bias=EPS)
    P.recip(r[:, :], r[:, :], [Br], [Br])
    return r, Br


def emit_rmsnorm_u(c, layer):
    P = c.P
    for tt in range(4):
        r, Br = rms_stats(c, c.hT, lambda k, t: c.BhT[k][t], 8, tt, 1.0 / D)
        for k in range(8):
            eng = "dve"
            P.stt(eng, c.uT[:, k, tt * 512:(tt + 1) * 512], c.hT[:, k, tt * 512:(tt + 1) * 512],
                  c.normw[:, layer, k:k + 1], r[:, :], ALU.mult, ALU.mult,
                  [c.BhT[k][tt], Br, c.Bnormw], [c.BuT[tt]])


def load_wn(c, src):
    w, Bw = c.wn.next()
    kc = src.shape[1]
    c.P.dma("pool", w[:, 0:kc, :], src, [], [Bw])
    return w, Bw


def proj_multi(c, wsrcs, evac_multi, rhs=None, Brhs=None, kc=8):
    P = c.P
    ws = []
    for src, m in wsrcs:
        w, Bw = c.wn.next()
        P.dma("pool", w[:, 0:kc, 0:m], src, [], [Bw])
        ws.append((w, Bw, m))
    if rhs is None:
        rhs, Brhs = c.uT, c.BuT
    for tt in range(4):
        pss = []
        for (w, Bw, m) in ws:
            ps, Bps = c.G.next()
            for k in range(kc):
                P.mm(ps[0:m, :], w[:, k, 0:m], rhs[:, k, tt * 512:(tt + 1) * 512], k == 0, k == kc - 1,
                     [Bw, Brhs[tt]], [Bps])
            pss.append((ps, Bps))
        evac_multi(pss, tt)


def proj_fm(c, wsrc, evac, rhs=None, Brhs=None, kc=8):
    proj_multi(c, [(wsrc, 128)], lambda pss, tt: evac(pss[0][0], pss[0][1], tt), rhs, Brhs, kc)


def evac_copy(c, dst, Bdst):
    def f(ps, Bps, tt):
        c.evac_rr += 1
        eng = "act" if c.evac_rr % 2 == 0 else "dve"
        c.P.copy(eng, dst[:, tt * 512:(tt + 1) * 512], ps[:, :], [Bps], [Bdst])
    return f


def evac_silu(c, dst, Bdst):
    def f(ps, Bps, tt):
        c.P.act(dst[:, tt * 512:(tt + 1) * 512], ps[:, :], AF.Silu, [Bps], [Bdst])
    return f


def proj_tm(c, wsrc, ncols, evac, lhs=None, Blhs=None, kc=8):
    P = c.P
    w, Bw = c.ww.next()
    P.dma("pool", w[:, 0:kc, 0:ncols], wsrc, [], [Bw])
    if lhs is None:
        lhs, Blhs = c.uT, c.BuT
    for tb in range(16):
        ps, Bps = c.G.next()
        for k in range(kc):
            P.mm(ps[:, 0:ncols], lhs[:, k, tb * 128:(tb + 1) * 128], w[:, k, 0:ncols], k == 0, k == kc - 1,
                 [Bw, Blhs[tb // 4]], [Bps])
        evac(ps, Bps, tb)


def outproj_acc(c, wsrc, gT, BgT, nk):
    P = c.P
    w, Bw = c.wo.next()
    P.dma("pool", w[:, 0:nk, :], wsrc, [], [Bw])
    for oc in range(8):
        for tt in range(4):
            ps, Bps = c.G.next()
            for j in range(nk):
                P.mm(ps[:, :], w[:, j, oc * 128:(oc + 1) * 128], gT[:, j, tt * 512:(tt + 1) * 512],
                     j == 0, j == nk - 1, [Bw, BgT], [Bps])
            P.tt("dve", c.hT[:, oc, tt * 512:(tt + 1) * 512], ps[:, :], c.hT[:, oc, tt * 512:(tt + 1) * 512],
                 ALU.add, [Bps, c.BhT[oc][tt]], [c.BhT[oc][tt]])


ATTN_DEPTH = 2


def attn_head(c, parts, Bparts, v_fn, Bv, scale, bias_fn, Bbias, gz, Bgz, gT, BgT, L):
    P = c.P
    dq = []

    def pump(limit):
        while dq and (dq[0][0] == "fin" or sum(1 for t in dq if t[0] == "pv") > limit):
            dq.pop(0)[1]()

    for qt in range(4):
        oacc, Bo = c.A.next()
        dacc, Bd = c.Dn.next()
        nkb = 4 * qt + 4
        for kb in range(nkb):
            d = kb - 4 * qt
            c0 = max(d, 0) * 128
            ps, Bps = c.G.next()
            for i, (kT, qT) in enumerate(parts):
                P.mm(ps[:, c0:512], kT[:, kb * 128:(kb + 1) * 128], qT[:, qt * 512 + c0:(qt + 1) * 512],
                     i == 0, i == len(parts) - 1, Bparts, [Bps])
            pt, Bpt = L.pt.next()
            if bias_fn is None:
                P.act(pt[:, c0:512], ps[:, c0:512], AF.Exp, [Bps], [Bpt], scale=scale)
            else:
                for jj in range(c0 // 128, 4):
                    P.act(pt[:, jj * 128:(jj + 1) * 128], ps[:, jj * 128:(jj + 1) * 128], AF.Exp,
                          [Bps, Bbias], [Bpt], scale=scale, bias=bias_fn(kb, 4 * qt + jj))
            if d >= 0:
                P.tt("pool", pt[:, c0:c0 + 128], pt[:, c0:c0 + 128], c.tri_b, ALU.mult, [Bpt, c.Bcstb], [Bpt])

            def pv(kb=kb, c0=c0, pt=pt, Bpt=Bpt, oacc=oacc, Bo=Bo, dacc=dacc, Bd=Bd, nkb=nkb):
                P.mm(oacc[:, c0:512], v_fn(kb), pt[:, c0:512], kb == 0, kb == nkb - 1, [Bv, Bpt], [Bo])
                P.mm(dacc[:, c0:512], c.ones_b, pt[:, c0:512], kb == 0, kb == nkb - 1, [c.Bcstb, Bpt], [Bd])
            dq.append(("pv", pv))
            pump(ATTN_DEPTH)

        def fin(qt=qt, oacc=oacc, Bo=Bo, dacc=dacc, Bd=Bd):
            rd, Brd = L.rden.next()
            P.recip(rd[:, :], dacc[:, :], [Bd], [Brd])
            P.tt("dve", rd[:, :], oacc[:, :], rd[:, :], ALU.mult, [Bo, Brd], [Brd])
            P.tt("pool", gT[:, qt * 512:(qt + 1) * 512], rd[:, :], gz[:, qt * 512:(qt + 1) * 512], ALU.mult,
                 [Brd, Bgz], [BgT])
        dq.append(("fin", fin))
    pump(-1)


def emit_fox(c, layer):
    nc, P, dram = c.nc, c.P, c.dram
    L = Ctx()
    L.pt = Ring(P, [nc.alloc_sbuf_tensor(f"fx_pt{i}", [128, 512], BF16) for i in range(6)], "pt")
    L.rden = Ring(P, [nc.alloc_sbuf_tensor(f"fx_rd{i}", [128, 512], F32) for i in range(2)], "rden")
    qT = nc.alloc_sbuf_tensor("fx_qT", [128, 2, S], BF16); BqT = P.buf("qT")
    kT = nc.alloc_sbuf_tensor("fx_kT", [128, 2, S], BF16); BkT = P.buf("kT")
    gz = nc.alloc_sbuf_tensor("fx_gz", [128, 2, S], BF16); Bgz = P.buf("gz")
    gT = nc.alloc_sbuf_tensor("fx_gT", [128, 2, S], BF16); BgT = P.buf("gT")
    vt = nc.alloc_sbuf_tensor("fx_vt", [128, 16, 256], BF16); Bvt = P.buf("vt")
    lsp = nc.alloc_sbuf_tensor("fx_lsp", [128, 16, 16], F32); Blsp = P.buf("lsp")
    cT = nc.alloc_sbuf_tensor("fx_cT", [128, 16, 16], F32); BcT = P.buf("cT")
    cref = nc.alloc_sbuf_tensor("fx_cref", [128, 16, 16], F32); Bcref = P.buf("cref")
    fb = nc.alloc_sbuf_tensor("fx_fb", [128, 16], F32); Bfb = P.buf("fb")
    tmp = nc.alloc_sbuf_tensor("fx_tmp", [128, 16], F32); Btmp = P.buf("tmp")
    btab = nc.alloc_sbuf_tensor("fx_btab", [128, 2, 16, 16], F32); Bbt = P.buf("btab")
    negtri_f = c.cst[:, 384:512]
    P.dma("sp", fb[:, :], dram["fox_fb"], [], [Bfb])
    wf_src = dram["fox_wf"]
    scale = 128 ** -0.5

    def evac_f(ps, Bps, tb):
        P.tt("dve", tmp[:, :], ps[:, 0:16], fb[:, :], ALU.add, [Bps, Bfb], [Btmp])
        P.act(tmp[:, :], tmp[:, :], AF.Exp, [Btmp], [Btmp], scale=-1.0)
        P.act(lsp[:, tb, :], tmp[:, :], AF.Ln, [Btmp], [Blsp], bias=1.0)
    proj_tm(c, wf_src, 16, evac_f)
    negones_f = c.cst[:, 512:640]
    for tb in range(16):
        ps, Bps = c.G.next()
        for t2 in range(tb + 1):
            lhs = negtri_f if t2 == tb else negones_f
            P.mm(ps[:, 0:16], lhs, lsp[:, t2, :], t2 == 0, t2 == tb, [c.Bcst, Blsp], [Bps])
        P.copy("dve", cT[:, tb, :], ps[:, 0:16], [Bps], [BcT])
    ps, Bps = c.G.next()
    for tb in range(16):
        rhs = lsp[:, tb:tb + 1, :].to_broadcast([128, 16 - tb, 16])
        P.mm(ps[:, tb * 16:256].rearrange("p (j h) -> p j h", h=16), negones_f, rhs, tb == 0, tb == 15,
             [c.Bcst, Blsp], [Bps])
    P.copy("dve", cref[:, :, :].rearrange("p j h -> p (j h)"), ps[:, 0:256], [Bps], [Bcref])

    for hp in range(8):
        for j in range(2):
            h = 2 * hp + j
            proj_fm(c, dram["fox_wq"][h], evac_copy(c, qT[:, j, :], BqT))
            proj_fm(c, dram["fox_wk"][h], evac_copy(c, kT[:, j, :], BkT))
            proj_fm(c, dram["fox_wz"][h], evac_silu(c, gz[:, j, :], Bgz))
            P.tt("dve", btab[:, j, :, :],
                 cref[:, :, h:h + 1].rearrange("p j o -> p o j").to_broadcast([128, 16, 16]),
                 cT[:, :, h:h + 1].to_broadcast([128, 16, 16]),
                 ALU.subtract, [Bcref, BcT], [Bbt])

        def evac_v(ps, Bps, tb):
            c.evac_rr += 1
            eng = "act" if c.evac_rr % 2 == 0 else "dve"
            P.copy(eng, vt[:, tb, :], ps[:, 0:256], [Bps], [Bvt])
        proj_tm(c, dram["fox_wv"][hp], 256, evac_v)
        for j in range(2):
            attn_head(c, [(kT[:, j, :], qT[:, j, :])], [BkT, BqT],
                      lambda kb, j=j: vt[:, kb, j * 128:(j + 1) * 128], Bvt, scale,
                      lambda kb, jq, j=j: btab[:, j, kb, jq:jq + 1], Bbt,
                      gz[:, j, :], Bgz, gT[:, j, :], BgT, L)
        outproj_acc(c, dram["fox_wo"][hp], gT, BgT, 2)


def rope_combine(c, L, psX, BpsX, psXr, BpsXr, rows, cos_ap, sin_ap, Btab, out_ap, Bout, d=1):
    P = c.P
    t1, Bt1 = L.rt.next()
    t2, Bt2 = L.rt.next()
    P.tt("dve", t1[0:rows, :], psX[0:rows, :], cos_ap, ALU.mult, [BpsX, Btab], [Bt1])
    P.tt("dve", t2[0:rows, :], psXr[0:rows, :], sin_ap, ALU.mult, [BpsXr, Btab], [Bt2])
    a1, a2 = t1[0:rows, :], t2[0:rows, :]
    if d > 1:
        a1 = a1.rearrange("p (n r) -> p n r", r=d)
        a2 = a2.rearrange("p (n r) -> p n r", r=d)
    P.tt("pool", out_ap, a1, a2, ALU.add, [Bt1, Bt2], [Bout])


def normed_proj(c, L, wsrcs, nw_ap, Bnw, dst, Bdst, inv_n):
    P = c.P
    nk = len(wsrcs)

    def ev(pss, tt):
        psS, BpS = c.A.next()
        for k, (ps, Bps) in enumerate(pss):
            sq, Bsq = c.sq.next()
            P.act(sq[:, :], ps[:, :], AF.Square, [Bps], [Bsq])
            P.mm(psS[:, :], c.ones_f, sq[:, :], k == 0, k == nk - 1, [Bsq, c.Bcst], [BpS])
        r, Br = c.rstd.next()
        P.act(r[:, :], psS[:, :], AF.Sqrt, [BpS], [Br], scale=inv_n, bias=EPS)
        P.recip(r[:, :], r[:, :], [Br], [Br])
        for k, (ps, Bps) in enumerate(pss):
            P.stt("dve", dst[:, k, tt * 512:(tt + 1) * 512], ps[:, :], nw_ap[:, k:k + 1], r[:, :], ALU.mult, ALU.mult,
                  [Bps, Br, Bnw], [Bdst[tt]])
    proj_multi(c, [(w, 128) for w in wsrcs], ev)


def emit_mla(c, layer):
    nc, P, dram = c.nc, c.P, c.dram
    L = Ctx()
    L.pt = Ring(P, [nc.alloc_sbuf_tensor(f"ml_pt{i}", [128, 512], BF16) for i in range(5)], "pt")
    L.rt = Ring(P, [nc.alloc_sbuf_tensor(f"ml_rt{i}", [128, 512], F32) for i in range(3)], "rt")
    L.rden = L.rt
    cqn = nc.alloc_sbuf_tensor("ml_cqn", [128, 3, S], BF16); Bcqn = P.bufs(4, "cqn")
    ckvn = nc.alloc_sbuf_tensor("ml_ckvn", [128, 2, S], BF16); Bckvn = P.bufs(4, "ckvn")
    kr = nc.alloc_sbuf_tensor("ml_kr", [128, S], BF16); Bkr = P.buf("kr")
    qn = nc.alloc_sbuf_tensor("ml_qn", [128, S], BF16); Bqn = P.buf("qn")
    qr = nc.alloc_sbuf_tensor("ml_qr", [128, S], BF16); Bqr = P.buf("qr")
    kn = nc.alloc_sbuf_tensor("ml_kn", [128, S], BF16); Bkn = P.buf("kn")
    gz = nc.alloc_sbuf_tensor("ml_gz", [128, 1, S], BF16); Bgz = P.buf("gz")
    gT = nc.alloc_sbuf_tensor("ml_gT", [128, 1, S], BF16); BgT = P.buf("gT")
    vt = nc.alloc_sbuf_tensor("ml_vt", [128, 16, 128], BF16); Bvt = P.buf("vt")
    tab = nc.alloc_sbuf_tensor("ml_tab", [128, 2, S], F32); Btab = P.buf("tab")
    nws = nc.alloc_sbuf_tensor("ml_nws", [128, 5], F32); Bnws = P.buf("nws")
    P.dma("sp", tab[:, 0, :], dram["mla_rope"][0], [], [Btab])
    P.dma("sp", tab[:, 1, :], dram["mla_rope"][1], [], [Btab])
    P.dma("sp", nws[:, :], dram["mla_nw"], [], [Bnws])
    scale = 192 ** -0.5

    normed_proj(c, L, [dram["mla_wcq"][i] for i in range(3)], nws[:, 0:3], Bnws, cqn, Bcqn, 1.0 / 384)
    normed_proj(c, L, [dram["mla_wckv"][i] for i in range(2)], nws[:, 3:5], Bnws, ckvn, Bckvn, 1.0 / 256)

    def ev_kr(pss, tt):
        (pX, BX), (pXr, BXr) = pss
        rope_combine(c, L, pX, BX, pXr, BXr, 64, tab[0:64, 0, tt * 512:(tt + 1) * 512],
                     tab[0:64, 1, tt * 512:(tt + 1) * 512], Btab, kr[0:64, tt * 512:(tt + 1) * 512], Bkr)
    proj_multi(c, [(dram["mla_wkr"], 64), (dram["mla_wkrr"], 64)], ev_kr)

    for h in range(16):
        proj_fm(c, dram["mla_wqn"][h], evac_copy(c, qn, Bqn), cqn, Bcqn, 3)

        def ev_qr(pss, tt):
            (pX, BX), (pXr, BXr) = pss
            rope_combine(c, L, pX, BX, pXr, BXr, 64, tab[0:64, 0, tt * 512:(tt + 1) * 512],
                         tab[0:64, 1, tt * 512:(tt + 1) * 512], Btab, qr[0:64, tt * 512:(tt + 1) * 512], Bqr)
        proj_multi(c, [(dram["mla_wqr"][h], 64), (dram["mla_wqrr"][h], 64)], ev_qr, cqn, Bcqn, 3)
        proj_fm(c, dram["mla_wkn"][h], evac_copy(c, kn, Bkn), ckvn, Bckvn, 2)
        proj_fm(c, dram["mla_wz"][h], evac_silu(c, gz[:, 0, :], Bgz))

        def evac_v(ps, Bps, tb):
            c.evac_rr += 1
            eng = "act" if c.evac_rr % 2 == 0 else "dve"
            P.copy(eng, vt[:, tb, :], ps[:, 0:128], [Bps], [Bvt])
        proj_tm(c, dram["mla_wv"][h], 128, evac_v, ckvn, Bckvn, 2)
        attn_head(c, [(kn, qn), (kr[0:64, :], qr[0:64, :])], [Bkn, Bqn, Bkr, Bqr],
                  lambda kb: vt[:, kb, :], Bvt, scale, None, None,
                  gz[:, 0, :], Bgz, gT[:, 0, :], BgT, L)
        outproj_acc(c, dram["mla_wo"][h], gT, BgT, 1)


DIL_CFG = ((128, 1), (512, 4), (2048, 16))
import os as _os
DIL_GROUPS = [int(x) for x in _os.environ.get('DIL_GROUPS', '0,1,2').split(',')]


def emit_dil(c, layer):
    nc, P, dram = c.nc, c.P, c.dram
    L = Ctx()
    L.pt = Ring(P, [nc.alloc_sbuf_tensor(f"dl_pt{i}", [128, 256], BF16) for i in range(6)], "pt")
    L.rt = Ring(P, [nc.alloc_sbuf_tensor(f"dl_rt{i}", [128, 512], F32) for i in range(4)], "rt")
    qT = nc.alloc_sbuf_tensor("dl_qT", [128, S], BF16); BqT = P.buf("qT")
    kT = nc.alloc_sbuf_tensor("dl_kT", [128, S], BF16); BkT = P.buf("kT")
    vt = nc.alloc_sbuf_tensor("dl_vt", [128, 16, 128], BF16); Bvt = P.buf("vt")
    gz = nc.alloc_sbuf_tensor("dl_gz", [128, 1, S], BF16); Bgz = P.buf("gz")
    gT = nc.alloc_sbuf_tensor("dl_gT", [128, 1, S], BF16); BgT = P.buf("gT")
    oN = nc.alloc_sbuf_tensor("dl_oN", [128, S], F32); BoN = P.buf("oN")
    dN = nc.alloc_sbuf_tensor("dl_dN", [128, S], F32); BdN = P.buf("dN")
    tab = nc.alloc_sbuf_tensor("dl_tab", [128, 2, S], F32); Btab = P.buf("tab")
    msk = nc.alloc_sbuf_tensor("dl_msk", [128, 256], BF16); Bmsk = P.buf("msk")
    P.dma("sp", tab[:, 0, :], dram["dil_rope"][0], [], [Btab])
    P.dma("sp", tab[:, 1, :], dram["dil_rope"][1], [], [Btab])
    P.dma("pool", msk[:, :], dram["dil_mask"], [], [Bmsk])
    scale = 128 ** -0.5

    for h in range(8):
        proj_fm(c, dram["dil_wz"][h], evac_silu(c, gz[:, 0, :], Bgz))
        for g, (window, d) in enumerate(DIL_CFG):
            if g not in DIL_GROUPS:
                continue
            nsub = S // d
            nb = nsub // 128

            def ev_rope(dst, Bdst):
                def f(pss, tt):
                    (pX, BX), (pXr, BXr) = pss
                    npt = 512 // d
                    if d == 1:
                        out_ap = dst[:, tt * 512:(tt + 1) * 512]
                        rope_combine(c, L, pX, BX, pXr, BXr, 128, tab[:, 0, tt * 512:(tt + 1) * 512],
                                     tab[:, 1, tt * 512:(tt + 1) * 512], Btab, out_ap, Bdst)
                    else:
                        out_ap = dst[:, :].rearrange("p (r n) -> p n r", r=d)[:, tt * npt:(tt + 1) * npt, :]
                        rope_combine(c, L, pX, BX, pXr, BXr, 128, tab[:, 0, tt * 512:(tt + 1) * 512],
                                     tab[:, 1, tt * 512:(tt + 1) * 512], Btab, out_ap, Bdst, d)
                return f
            proj_multi(c, [(dram["dil_wq"][g * 8 + h], 128), (dram["dil_wqr"][g * 8 + h], 128)], ev_rope(qT, BqT))
            proj_multi(c, [(dram["dil_wk"][g * 8 + h], 128), (dram["dil_wkr"][g * 8 + h], 128)], ev_rope(kT, BkT))
            wv, Bwv = c.wn.next()
            P.dma("pool", wv[:, :, :], dram["dil_wv"][g * 8 + h], [], [Bwv])
            for r in range(d):
                for kb in range(nb):
                    blk = r * nb + kb
                    t0 = kb * 128 * d + r
                    ps, Bps = c.G.next()
                    for k in range(8):
                        lhs = c.uT[:, k, t0:t0 + 127 * d + 1:d]
                        P.mm(ps[:, 0:128], lhs, wv[:, k, :], k == 0, k == 7, [Bwv] + c.BuT, [Bps])
                    c.evac_rr += 1
                    P.copy("act" if c.evac_rr % 2 == 0 else "dve", vt[:, blk, :], ps[:, 0:128], [Bps], [Bvt])
            dq = []

            def pump(limit):
                while dq and (dq[0][0] == "fin" or sum(1 for t in dq if t[0] == "pv") > limit):
                    dq.pop(0)[1]()

            for bank in range(4):
                oacc, Bo = c.A.next()
                dacc, Bd = c.Dn.next()
                for qi in range(4):
                    blk = bank * 4 + qi
                    b = blk % nb
                    qs = blk * 128
                    ps, Bps = c.G.next()
                    nk = 2 if b > 0 else 1
                    P.mm(ps[:, 0:128], kT[:, qs:qs + 128], qT[:, qs:qs + 128], True, True, [BkT, BqT], [Bps])
                    if b > 0:
                        P.mm(ps[:, 128:256], kT[:, qs - 128:qs], qT[:, qs:qs + 128], True, True, [BkT, BqT], [Bps])
                    pt, Bpt = L.pt.next()
                    P.act(pt[:, 0:128 * nk], ps[:, 0:128 * nk], AF.Exp, [Bps], [Bpt], scale=scale)
                    P.tt("pool", pt[:, 0:128 * nk], pt[:, 0:128 * nk], msk[:, 0:128 * nk], ALU.mult, [Bpt, Bmsk], [Bpt])

                    def pv(qi=qi, blk=blk, b=b, pt=pt, Bpt=Bpt, oacc=oacc, Bo=Bo, dacc=dacc, Bd=Bd):
                        oc = oacc[:, qi * 128:(qi + 1) * 128]
                        dc = dacc[:, qi * 128:(qi + 1) * 128]
                        P.mm(oc, vt[:, blk, :], pt[:, 0:128], True, b == 0, [Bvt, Bpt], [Bo])
                        if b > 0:
                            P.mm(oc, vt[:, blk - 1, :], pt[:, 128:256], False, True, [Bvt, Bpt], [Bo])
                        P.mm(dc, c.ones_b, pt[:, 0:128], True, b == 0, [c.Bcstb, Bpt], [Bd])
                        if b > 0:
                            P.mm(dc, c.ones_b, pt[:, 128:256], False, True, [c.Bcstb, Bpt], [Bd])
                    dq.append(("pv", pv))
                    pump(ATTN_DEPTH)

                def fin(bank=bank, oacc=oacc, Bo=Bo, dacc=dacc, Bd=Bd, g=g, d=d):
                    pieces = []
                    if d == 1:
                        pieces.append((oN[:, bank * 512:(bank + 1) * 512], dN[:, bank * 512:(bank + 1) * 512], oacc[:, :], dacc[:, :]))
                    elif d == 4:
                        pieces.append((oN[:, bank:S:4], dN[:, bank:S:4], oacc[:, :], dacc[:, :]))
                    else:
                        for q4 in range(4):
                            r = bank * 4 + q4
                            pieces.append((oN[:, r:S:16], dN[:, r:S:16], oacc[:, q4 * 128:(q4 + 1) * 128], dacc[:, q4 * 128:(q4 + 1) * 128]))
                    for (on, dn, oa, da) in pieces:
                        if g == DIL_GROUPS[0]:
                            P.copy("act", on, oa, [Bo], [BoN])
                            P.copy("dve", dn, da, [Bd], [BdN])
                        else:
                            P.tt("dve", on, oa, on, ALU.add, [Bo, BoN], [BoN])
                            P.tt("dve", dn, da, dn, ALU.add, [Bd, BdN], [BdN])
                dq.append(("fin", fin))
            pump(-1)
        for tt in range(4):
            sl = slice(tt * 512, (tt + 1) * 512)
            P.recip(dN[:, sl], dN[:, sl], [BdN], [BdN])
            P.tt("dve", oN[:, sl], oN[:, sl], dN[:, sl], ALU.mult, [BoN, BdN], [BoN])
            P.tt("pool", gT[:, 0, sl], oN[:, sl], gz[:, 0, sl], ALU.mult, [BoN, Bgz], [BgT])
        outproj_acc(c, dram["dil_wo"][h], gT, BgT, 1)


def emit_ssd(c, layer):
    nc, P, dram = c.nc, c.P, c.dram
    tri_f = c.cst[:, 128:256]
    negtri_f = c.cst[:, 384:512]
    negones_f = c.cst[:, 512:640]
    dt = nc.alloc_sbuf_tensor("sd_dt", [128, 16, 32], F32); Bdt = P.buf("dt")
    absa = nc.alloc_sbuf_tensor("sd_absa", [128, 16, 32], F32); Babsa = P.buf("absa")
    acum = nc.alloc_sbuf_tensor("sd_acum", [128, 16, 32], F32); Bacum = P.buf("acum")
    wts = nc.alloc_sbuf_tensor("sd_w", [128, 16, 32], F32); Bw = P.buf("w")
    dlast = nc.alloc_sbuf_tensor("sd_dlast", [128, 16, 32], F32); Bdl = P.buf("dlast")
    vecs = nc.alloc_sbuf_tensor("sd_vecs", [128, 3, 32], F32); Bvecs = P.buf("vecs")
    cw = nc.alloc_sbuf_tensor("sd_cw", [128, 32, 5], F32); Bcw = P.buf("cw")
    dsk = nc.alloc_sbuf_tensor("sd_dsk", [128, 16], F32); Bdsk = P.buf("dsk")
    nrm = nc.alloc_sbuf_tensor("sd_nrm", [128, 16], F32); Bnrm = P.buf("nrm")
    tmp32 = nc.alloc_sbuf_tensor("sd_tmp32", [128, 32], F32); Bt32 = P.buf("t32")
    pre = nc.alloc_sbuf_tensor("sd_pre", [128, S + 3], F32); Bpre = P.buf("pre")
    cacc = Ring(P, [nc.alloc_sbuf_tensor(f"sd_cacc{i}", [128, 512], F32) for i in range(2)], "cacc")
    xTf = nc.alloc_sbuf_tensor("sd_xTf", [128, 2, S], F32); BxTf = P.bufs(2, "xTf")
    xTb = nc.alloc_sbuf_tensor("sd_xTb", [128, 2, S], BF16); BxTb = P.bufs(2, "xTb")
    BT = nc.alloc_sbuf_tensor("sd_BT", [128, S], BF16); BBT = P.buf("BT")
    CT = nc.alloc_sbuf_tensor("sd_CT", [128, S], BF16); BCT = P.buf("CT")
    gz = nc.alloc_sbuf_tensor("sd_gz", [128, 2, S], BF16); Bgz = P.buf("gz")
    f128 = Ring(P, [nc.alloc_sbuf_tensor(f"sd_f{i}", [128, 128], F32) for i in range(6)], "f128")
    b128 = Ring(P, [nc.alloc_sbuf_tensor(f"sd_b{i}", [128, 128], BF16) for i in range(8)], "b128")
    b256 = Ring(P, [nc.alloc_sbuf_tensor(f"sd_c{i}", [128, 256], BF16) for i in range(4)], "b256")
    cbmR = Ring(P, [nc.alloc_sbuf_tensor(f"sd_cbm{i}", [128, 128], F32) for i in range(2)], "cbm")
    btokR = Ring(P, [nc.alloc_sbuf_tensor(f"sd_btok{i}", [128, 128], BF16) for i in range(2)], "btok")
    Sf = nc.alloc_sbuf_tensor("sd_Sf", [128, 256], F32); BSf = P.buf("Sf")
    Sb = nc.alloc_sbuf_tensor("sd_Sb", [128, 256], BF16); BSb = P.buf("Sb")
    P.dma("sp", vecs[:, :, :], dram["ssm_vecs"], [], [Bvecs])
    P.dma("sp", cw[:, :, :], dram["ssm_cw"], [], [Bcw])
    P.dma("sp", dsk[:, :], dram["ssm_dsk"], [], [Bdsk])
    P.dma("sp", nrm[:, :], dram["ssm_nrm"], [], [Bnrm])
    P.memset("pool", pre[:, 0:3], 0.0, [Bpre])
    P.act(vecs[:, 1, :], vecs[:, 1, :], AF.Exp, [Bvecs], [Bvecs])

    def evac_dt(ps, Bps, tb):
        P.tt("dve", tmp32[:, :], ps[:, 0:32], vecs[:, 0, :], ALU.add, [Bps, Bvecs], [Bt32])
        P.act(tmp32[:, :], tmp32[:, :], AF.Exp, [Bt32], [Bt32])
        P.act(dt[:, tb, :], tmp32[:, :], AF.Ln, [Bt32], [Bdt], bias=1.0)
        P.tt("dve", absa[:, tb, :], dt[:, tb, :], vecs[:, 1, :], ALU.mult, [Bdt, Bvecs], [Babsa])
    proj_tm(c, dram["ssm_wdt"], 32, evac_dt)
    for tb in range(16):
        ps, Bps = c.G.next()
        P.mm(ps[:, 0:32], negtri_f, absa[:, tb, :], True, True, [c.Bcst, Babsa], [Bps])
        P.mm(ps[:, 32:64], negones_f, absa[:, tb, :], True, True, [c.Bcst, Babsa], [Bps])
        P.copy("dve", acum[:, tb, :], ps[:, 0:32], [Bps], [Bacum])
        P.copy("dve", dlast[:, tb, :], ps[:, 32:64], [Bps], [Bdl])
        P.tt("dve", wts[:, tb, :], dlast[:, tb, :], acum[:, tb, :], ALU.subtract, [Bdl, Bacum], [Bw])
        P.act(wts[:, tb, :], wts[:, tb, :], AF.Exp, [Bw], [Bw])
        P.tt("dve", wts[:, tb, :], wts[:, tb, :], dt[:, tb, :], ALU.mult, [Bw, Bdt], [Bw])
        P.act(dlast[:, tb, :], dlast[:, tb, :], AF.Exp, [Bdl], [Bdl])

    def conv_silu(ch, outs):
        for tt in range(4):
            a, Ba = cacc.next()
            o = tt * 512
            P.ts("dve", a[:, :], pre[:, o:o + 512], cw[:, ch, 0:1], cw[:, ch, 4:5], ALU.mult, ALU.add, [Bpre, Bcw], [Ba])
            for k in range(1, 4):
                P.stt("dve", a[:, :], pre[:, o + k:o + k + 512], cw[:, ch, k:k + 1], a[:, :], ALU.mult, ALU.add,
                      [Bpre, Bcw, Ba], [Ba])
            for (dst, Bdst) in outs:
                P.act(dst[:, o:o + 512], a[:, :], AF.Silu, [Ba], [Bdst])

    def evac_pre(ps, Bps, tt):
        P.copy("act", pre[:, 3 + tt * 512:3 + (tt + 1) * 512], ps[:, :], [Bps], [Bpre])

    for g in range(8):
        for i in range(2):
            proj_fm(c, dram["ssm_wx"][2 * g + i], evac_pre)
            conv_silu(2 * g + i, [(xTf[:, i, :], BxTf[i]), (xTb[:, i, :], BxTb[i])])
            proj_fm(c, dram["ssm_wz"][2 * g + i], evac_silu(c, gz[:, i, :], Bgz))
        proj_fm(c, dram["ssm_wB"][g], evac_pre)
        conv_silu(16 + g, [(BT, BBT)])
        proj_fm(c, dram["ssm_wC"][g], evac_pre)
        conv_silu(24 + g, [(CT, BCT)])
        for ck in range(16):
            cs = slice(ck * 128, (ck + 1) * 128)
            ps, Bps = c.G.next()
            for i in range(2):
                P.mm(ps[:, i * 128:(i + 1) * 128], xTb[:, i, cs], c.ident_b, True, True, [BxTb[i], c.Bcstb], [Bps])
            P.mm(ps[:, 256:384], BT[:, cs], c.ident_b, True, True, [BBT, c.Bcstb], [Bps])
            xtok, Bxtok = b256.next()
            P.copy("dve", xtok[:, :], ps[:, 0:256], [Bps], [Bxtok])
            btok, Bbtok = btokR.next()
            P.copy("dve", btok[:, :], ps[:, 256:384], [Bps], [Bbtok])
            ps2, Bps2 = c.G.next()
            P.mm(ps2[:, 0:128], BT[:, cs], CT[:, cs], True, True, [BBT, BCT], [Bps2])
            cbm, Bcbm = cbmR.next()
            P.tt("dve", cbm[:, :], ps2[:, 0:128], tri_f, ALU.mult, [Bps2, c.Bcst], [Bcbm])
            xw, Bxw = b256.next()
            for j in range(4):
                P.ts("dve" if j % 2 == 0 else "pool", xw[:, j * 64:(j + 1) * 64], xtok[:, j * 64:(j + 1) * 64],
                     wts[:, ck, 4 * g + j:4 * g + j + 1], None, ALU.mult, None, [Bxtok, Bw], [Bxw])
            pst, Bpst = c.A.next()
            P.mm(pst[:, 0:256], btok[:, :], xw[:, :], True, True, [Bbtok, Bxw], [Bpst])
            for i in range(2):
                yps, Byps = c.Dn.next()
                for jj in range(2):
                    j = 2 * i + jj
                    h = 4 * g + j
                    pb, Bpb = c.G.next()
                    P.mm(pb[:, 0:128], absa[:, ck, h:h + 1].to_broadcast([128, 128]), negtri_f, True, True,
                         [Babsa, c.Bcst], [Bpb])
                    dm, Bdm = f128.next()
                    P.ts("dve", dm[:, :], pb[:, 0:128], acum[:, ck, h:h + 1], 0.0, ALU.subtract, ALU.min, [Bpb, Bacum], [Bdm])
                    P.act(dm[:, :], dm[:, :], AF.Exp, [Bdm], [Bdm])
                    mp, Bmp = b128.next()
                    P.stt("dve", mp[:, :], dm[:, :], dt[:, ck, h:h + 1], cbm[:, :], ALU.mult, ALU.mult, [Bdm, Bdt, Bcbm], [Bmp])
                    P.mm(yps[64 * jj:64 * jj + 64, 0:128], xtok[:, j * 64:(j + 1) * 64], mp[:, :], True, ck == 0,
                         [Bxtok, Bmp], [Byps])
                    if ck > 0:
                        ee, Bee = f128.next()
                        P.act(ee[:, :], pb[:, 0:128], AF.Exp, [Bpb, Bdm], [Bee])
                        csd, Bcsd = b128.next()
                        P.tt("dve", csd[:, :], ee[:, :], CT[:, cs], ALU.mult, [BCT, Bee], [Bcsd])
                        P.mm(yps[64 * jj:64 * jj + 64, 0:128], Sb[:, j * 64:(j + 1) * 64], csd[:, :], False, True,
                             [BSb, Bcsd], [Byps])
                P.stt("dve", xTf[:, i, cs], xTf[:, i, cs], dsk[:, 2 * g + i:2 * g + i + 1], yps[:, 0:128], ALU.mult, ALU.add,
                      [BxTf[i], Bdsk, Byps], [BxTf[i]])
            if ck == 0:
                P.copy("dve", Sf[:, :], pst[:, 0:256], [Bpst], [BSf])
            else:
                for j in range(4):
                    P.stt("dve", Sf[:, j * 64:(j + 1) * 64], Sf[:, j * 64:(j + 1) * 64], dlast[:, ck, 4 * g + j:4 * g + j + 1],
                          pst[:, j * 64:(j + 1) * 64], ALU.mult, ALU.add, [BSf, Bdl, Bpst], [BSf])
            if ck < 15:
                P.copy("act", Sb[:, :], Sf[:, :], [BSf], [BSb])
        for tt in range(4):
            sl = slice(tt * 512, (tt + 1) * 512)
            for i in range(2):
                P.tt("pool", xTf[:, i, sl], xTf[:, i, sl], gz[:, i, sl], ALU.mult, [BxTf[i], Bgz], [BxTf[i]])
            r, Br = rms_stats(c, xTf, lambda k, t: BxTf[k], 2, tt, 1.0 / 256)
            for i in range(2):
                P.stt("dve", gz[:, i, sl], xTf[:, i, sl], nrm[:, 2 * g + i:2 * g + i + 1], r[:, :], ALU.mult, ALU.mult,
                      [BxTf[i], Br, Bnrm], [Bgz])
        outproj_acc(c, dram["ssm_wo"][g], gz, Bgz, 2)


def tile_cols(W, width=128):
    K, N = W.shape
    return np.ascontiguousarray(W.reshape(K // 128, 128, N // width, width).transpose(2, 1, 0, 3))


def tile_rows(W, nk):
    R, N = W.shape
    return np.ascontiguousarray(W.reshape(R // (128 * nk), nk, 128, N).transpose(0, 2, 1, 3))


def rope_tables(half, reps):
    inv_freq = (np.float32(ROPE_THETA) ** (-np.arange(half, dtype=np.float32) / np.float32(half))).astype(np.float32)
    ang = np.arange(S, dtype=np.float32)[None, :] * inv_freq[:, None]
    cos = np.cos(ang).astype(np.float32)
    sin = np.sin(ang).astype(np.float32)
    t = np.zeros((2, 128, S), np.float32)
    t[0] = 1.0
    for r in range(reps):
        b = r * 2 * half
        t[0, b:b + half] = cos
        t[0, b + half:b + 2 * half] = cos
        t[1, b:b + half] = -sin
        t[1, b + half:b + 2 * half] = sin
    return t


def make_consts():
    cst = np.zeros((128, 5 * 128), np.float32)
    i = np.arange(128)
    cst[:, 0:128] = np.eye(128)
    cst[:, 128:256] = (i[:, None] <= i[None, :])
    cst[:, 256:384] = 1.0
    cst[:, 384:512] = -(i[:, None] <= i[None, :]).astype(np.float32)
    cst[:, 512:640] = -1.0
    return cst


def host_prep(inputs, layers):
    shared = {}
    shared["consts"] = make_consts()
    nw = np.concatenate([inputs["norm_w"], inputs["final_norm_w"][None]], axis=0)
    shared["normw"] = np.ascontiguousarray(nw.reshape(5, 8, 128).transpose(2, 0, 1))
    if 0 in layers:
        W = inputs["ssm_in_w"][0]
        shared["ssm_wz"] = tile_cols(W[:, 0:2048])
        shared["ssm_wx"] = tile_cols(W[:, 2048:4096])
        shared["ssm_wB"] = tile_cols(W[:, 4096:5120])
        shared["ssm_wC"] = tile_cols(W[:, 5120:6144])
        shared["ssm_wdt"] = tile_cols(W[:, 6144:6176], 32)[0]
        vec = np.stack([inputs["ssm_dt_bias"][0], inputs["ssm_A_log"][0], inputs["ssm_D"][0]], 0)
        shared["ssm_vecs"] = np.ascontiguousarray(np.broadcast_to(vec[None], (128, 3, 32)))
        cwb = np.concatenate([inputs["ssm_conv_w"][0], inputs["ssm_conv_b"][0][None]], 0)
        shared["ssm_cw"] = np.ascontiguousarray(cwb.reshape(5, 32, 128).transpose(2, 1, 0))
        shared["ssm_dsk"] = np.ascontiguousarray(np.repeat(inputs["ssm_D"][0], 64).reshape(16, 128).T)
        shared["ssm_nrm"] = np.ascontiguousarray(inputs["ssm_norm_w"][0].reshape(16, 128).T)
        shared["ssm_wo"] = tile_rows(inputs["ssm_out_w"][0], 2)
    if 1 in layers:
        W = inputs["mla_in_w"][0]
        perm = np.concatenate([np.arange(32, 64), np.arange(0, 32)])
        shared["mla_wcq"] = tile_cols(W[:, 0:384])
        shared["mla_wckv"] = tile_cols(W[:, 384:640])
        shared["mla_wkr"] = tile_cols(W[:, 640:704], 64)[0]
        shared["mla_wkrr"] = tile_cols(W[:, 640:704][:, perm], 64)[0]
        shared["mla_wz"] = tile_cols(W[:, 704:2752])
        UQ = inputs["mla_uq_w"][0].reshape(384, 16, 192)
        shared["mla_wqn"] = tile_cols(np.ascontiguousarray(UQ[:, :, 0:128]).reshape(384, 2048))
        shared["mla_wqr"] = tile_cols(np.ascontiguousarray(UQ[:, :, 128:192]).reshape(384, 1024), 64)
        shared["mla_wqrr"] = tile_cols(np.ascontiguousarray(UQ[:, :, 128:192][:, :, perm]).reshape(384, 1024), 64)
        UKV = inputs["mla_ukv_w"][0].reshape(256, 16, 256)
        shared["mla_wkn"] = tile_cols(np.ascontiguousarray(UKV[:, :, 0:128]).reshape(256, 2048))
        shared["mla_wv"] = tile_cols(np.ascontiguousarray(UKV[:, :, 128:256]).reshape(256, 2048))
        shared["mla_wo"] = tile_rows(inputs["mla_out_w"][0], 1)
        nwq = inputs["mla_q_norm_w"][0].reshape(3, 128).T
        nwkv = inputs["mla_kv_norm_w"][0].reshape(2, 128).T
        shared["mla_nw"] = np.ascontiguousarray(np.concatenate([nwq, nwkv], axis=1))
        shared["mla_rope"] = rope_tables(32, 2)
    if 3 in layers:
        W = inputs["dil_in_w"][0]
        perm = np.arange(128)
        perm[0:16] = np.arange(16, 32)
        perm[16:32] = np.arange(0, 16)
        wq, wk, wv, wqr, wkr = [], [], [], [], []
        for g in range(3):
            base = 3072 * g
            Q = W[:, base:base + 1024].reshape(1024, 8, 128)
            Kw = W[:, base + 1024:base + 2048].reshape(1024, 8, 128)
            wq.append(tile_cols(Q.reshape(1024, 1024)))
            wk.append(tile_cols(Kw.reshape(1024, 1024)))
            wqr.append(tile_cols(np.ascontiguousarray(Q[:, :, perm]).reshape(1024, 1024)))
            wkr.append(tile_cols(np.ascontiguousarray(Kw[:, :, perm]).reshape(1024, 1024)))
            wv.append(tile_cols(W[:, base + 2048:base + 3072]))
        shared["dil_wq"] = np.concatenate(wq, 0)
        shared["dil_wk"] = np.concatenate(wk, 0)
        shared["dil_wqr"] = np.concatenate(wqr, 0)
        shared["dil_wkr"] = np.concatenate(wkr, 0)
        shared["dil_wv"] = np.concatenate(wv, 0)
        shared["dil_wz"] = tile_cols(W[:, 9216:10240])
        shared["dil_wo"] = tile_rows(inputs["dil_out_w"][0], 1)
        shared["dil_rope"] = rope_tables(16, 1)
        i = np.arange(128)
        m = np.zeros((128, 256), np.float32)
        m[:, 0:128] = (i[:, None] <= i[None, :])
        m[:, 128:256] = (i[:, None] >= i[None, :])
        shared["dil_mask"] = m
    if 2 in layers:
        W = inputs["fox_in_w"][0]
        shared["fox_wq"] = tile_cols(W[:, 0:2048])
        shared["fox_wk"] = tile_cols(W[:, 2048:4096])
        shared["fox_wv"] = tile_cols(W[:, 4096:6144], 256)
        shared["fox_wf"] = tile_cols(W[:, 6144:6160], 16)[0]
        shared["fox_wz"] = tile_cols(W[:, 6160:8208])
        shared["fox_fb"] = np.ascontiguousarray(np.broadcast_to(inputs["fox_f_bias"][0][None, :], (128, 16)))
        shared["fox_wo"] = tile_rows(inputs["fox_out_w"][0], 2)
    return shared


def build_program(layers, shared_shapes, final_norm):
    nc = bass.Bass("TRN2", target_bir_lowering=False)
    dram = {}
    for k, shp in shared_shapes.items():
        dram[k] = nc.dram_tensor(k, list(shp), F32, kind="ExternalInput").ap()
    hin = nc.dram_tensor("hin", [NSEQ, 8, 128, S], F32, kind="ExternalInput").ap()
    hout = nc.dram_tensor("hout", [NSEQ, 8, 128, S], F32, kind="ExternalOutput").ap()
    P = Prog(nc)
    c = setup_common(nc, P, dram)
    Bout = P.buf("hout")
    for s in range(NSEQ):
        for k in range(8):
            for t in range(4):
                P.dma("sp", c.hT[:, k, t * 512:(t + 1) * 512], hin[s, k, :, t * 512:(t + 1) * 512], [], [c.BhT[k][t]])
        for layer in layers:
            emit_rmsnorm_u(c, layer)
            emit_layer_cached(c, layer, LAYER_EMITTERS[layer])
        if final_norm:
            for tt in range(4):
                r, Br = rms_stats(c, c.hT, lambda k, t: c.BhT[k][t], 8, tt, 1.0 / D)
                for k in range(8):
                    eng = "dve"
                    P.stt(eng, c.hT[:, k, tt * 512:(tt + 1) * 512], c.hT[:, k, tt * 512:(tt + 1) * 512],
                          c.normw[:, 4, k:k + 1], r[:, :], ALU.mult, ALU.mult,
                          [c.BhT[k][tt], Br, c.Bnormw], [c.BhT[k][tt]])
        for k in range(8):
            for t in range(4):
                P.dma("sp", hout[s, k, :, t * 512:(t + 1) * 512], c.hT[:, k, t * 512:(t + 1) * 512], [c.BhT[k][t]], [Bout])
    P.barrier()
    stats = P.emit()
    return nc, stats


def emit_layer_cached(c, layer, fn):
    from contextlib import ExitStack
    nc = c.nc
    c._scope_id = getattr(c, "_scope_id", 0) + 1
    sid = c._scope_id
    with ExitStack() as st:
        class NCProxy:
            def __getattr__(self, a):
                if a == "alloc_sbuf_tensor":
                    return lambda name, shape, dtype: st.enter_context(nc.sbuf_tensor(f"{name}_s{sid}", shape, dtype))
                return getattr(nc, a)
        c.nc = NCProxy()
        try:
            fn(c, layer)
        finally:
            c.nc = nc
        c.P.barrier()


_CACHE = {}
LAYER_EMITTERS = {0: emit_ssd, 1: emit_mla, 2: emit_fox, 3: emit_dil}


def run_layers(hT_all, inputs, layers, final_norm):
    shared = host_prep(inputs, layers)
    key = (tuple(layers), final_norm)
    if key not in _CACHE:
        _CACHE[key] = build_program(layers, {k: v.shape for k, v in shared.items()}, final_norm)
    nc, stats = _CACHE[key]
    in_maps = []
    for core in range(8):
        m = dict(shared)
        m["hin"] = np.ascontiguousarray(hT_all[core * NSEQ:(core + 1) * NSEQ])
        in_maps.append(m)
    res = run_bass_kernel_spmd(nc, in_maps, core_ids=list(range(8)))
    return np.concatenate([r["hout"] for r in res.results], axis=0)


def to_fm(x):
    B = x.shape[0]
    return np.ascontiguousarray(x.transpose(0, 2, 1).reshape(B, 8, 128, S))


def from_fm(hT):
    B = hT.shape[0]
    return np.ascontiguousarray(hT.reshape(B, D, S).transpose(0, 2, 1))


def kernel(**inputs):
    inputs = {k: np.asarray(v, dtype=np.float32) for k, v in inputs.items()}
    hT = to_fm(inputs["x"])
    hT = run_layers(hT, inputs, [0, 1, 2, 3], True)
    return from_fm(hT)
```

```python
import numpy as np
import concourse.bass as bass
import concourse.mybir as mybir
from concourse.bass_utils import run_bass_kernel_spmd

F32 = mybir.dt.float32
BF16 = mybir.dt.bfloat16
AF = mybir.ActivationFunctionType
ALU = mybir.AluOpType

SAME_ENGINE_SYNC = True
SEM_ROT = 30000
N_DMA_SEMS = 12
S = 2048
D = 1024
NSEQ = 2
EPS = 1e-6
ROPE_THETA = 500000.0


class Buf:
    __slots__ = ("name", "writer", "readers")

    def __init__(self, name):
        self.name = name
        self.writer = None
        self.readers = []


class Op:
    __slots__ = ("eng", "fn", "deps", "sig", "idx", "is_dma", "sem", "semval", "sigcount")

    def __init__(self, eng, fn, is_dma):
        self.eng = eng
        self.fn = fn
        self.deps = []
        self.sig = False
        self.is_dma = is_dma
        self.sem = None
        self.semval = 0
        self.sigcount = 0


class Prog:
    ENGS = ("pe", "act", "dve", "pool", "sp")

    def __init__(self, nc):
        self.nc = nc
        self.ops = {e: [] for e in self.ENGS}
        self.dma_sems = {}
        self.dma_rr = {}
        self.dma_last = {}
        for q in ("sp", "pool"):
            self.dma_sems[q] = [nc.alloc_semaphore(name=f"dq_{q}_{i}") for i in range(N_DMA_SEMS)]
            self.dma_rr[q] = 0
            self.dma_last[q] = [None] * N_DMA_SEMS
        self.dma_cnt = {}
        self.eng_sems = {}
        self.nbuf = 0

    def buf(self, name=None):
        self.nbuf += 1
        return Buf(name or f"b{self.nbuf}")

    def bufs(self, n, name="b"):
        return [self.buf(f"{name}{i}") for i in range(n)]

    def _add(self, eng, fn, reads, writes, is_dma=False):
        op = Op(eng, fn, is_dma)
        deps = []
        for b in reads:
            if b.writer is not None:
                deps.append(b.writer)
        for b in writes:
            if b.writer is not None:
                deps.append(b.writer)
            deps.extend(b.readers)
        if is_dma:
            q = eng
            i = self.dma_rr[q]
            self.dma_rr[q] = (i + 1) % N_DMA_SEMS
            prev = self.dma_last[q][i]
            if prev is not None:
                deps.append(prev)
            op.sem = self.dma_sems[q][i]
            key = (q, i)
            self.dma_cnt[key] = self.dma_cnt.get(key, 0) + 1
            op.semval = 16 * self.dma_cnt[key]
            self.dma_last[q][i] = op
        seen = set()
        for d in deps:
            if d is op or id(d) in seen:
                continue
            seen.add(id(d))
            if (not d.is_dma) and (not is_dma) and d.eng == eng:
                if eng == "pe" or not SAME_ENGINE_SYNC:
                    continue
            op.deps.append(d)
        op.idx = len(self.ops[eng])
        self.ops[eng].append(op)
        for b in reads:
            if not is_dma:
                b.readers = [r for r in b.readers if r.is_dma or r.eng != eng]
            b.readers.append(op)
        for b in writes:
            b.writer = op
            b.readers = []
        return op

    def mm(self, out, lhsT, rhs, start, stop, reads, writes):
        return self._add("pe", lambda e: e.matmul(out, lhsT, rhs, start=start, stop=stop), reads, writes)

    def act(self, out, in_, func, reads, writes, **kw):
        return self._add("act", lambda e: e.activation(out, in_, func, **kw), reads, writes)

    def tt(self, eng, out, in0, in1, op, reads, writes):
        return self._add(eng, lambda e: e.tensor_tensor(out, in0, in1, op), reads, writes)

    def ts(self, eng, out, in0, s1, s2, op0, op1, reads, writes):
        if op1 is None:
            return self._add(eng, lambda e: e.tensor_scalar(out, in0, s1, s2, op0), reads, writes)
        return self._add(eng, lambda e: e.tensor_scalar(out, in0, s1, s2, op0, op1), reads, writes)

    def stt(self, eng, out, in0, scalar, in1, op0, op1, reads, writes):
        return self._add(eng, lambda e: e.scalar_tensor_tensor(out, in0, scalar, in1, op0, op1), reads, writes)

    def copy(self, eng, out, in_, reads, writes):
        if eng == "act":
            return self._add(eng, lambda e: e.copy(out, in_), reads, writes)
        return self._add(eng, lambda e: e.tensor_copy(out, in_), reads, writes)

    def memset(self, eng, ap, val, writes):
        return self._add(eng, lambda e: e.memset(ap, val), [], writes)

    def recip(self, out, in_, reads, writes):
        return self._add("dve", lambda e: e.reciprocal(out, in_), reads, writes)

    def dma(self, q, out, in_, reads, writes):
        return self._add(q, lambda e: e.dma_start(out, in_), reads, writes, is_dma=True)

    def barrier(self):
        last = []
        for e in self.ENGS:
            for op in reversed(self.ops[e]):
                if not op.is_dma:
                    last.append(op)
                    break
        for q in self.dma_last:
            for op in self.dma_last[q]:
                if op is not None:
                    last.append(op)
        for e in self.ENGS:
            op = Op(e, None, False)
            for d in last:
                if d.eng == e and not d.is_dma and e == "pe":
                    continue
                op.deps.append(d)
            op.idx = len(self.ops[e])
            self.ops[e].append(op)

    def emit(self):
        nc = self.nc
        for e in self.ENGS:
            for op in self.ops[e]:
                for d in op.deps:
                    if not d.is_dma:
                        d.sig = True
        for e in self.ENGS:
            c = 0
            for op in self.ops[e]:
                if op.is_dma:
                    continue
                if op.sig:
                    c += 1
                    op.sigcount = c
            nsem = (c + SEM_ROT - 1) // SEM_ROT
            self.eng_sems[e] = [nc.alloc_semaphore(name=f"es_{e}_{i}") for i in range(max(nsem, 1))]
        engobj = {"pe": "tensor", "act": "scalar", "dve": "vector", "pool": "gpsimd", "sp": "sync"}
        stats = {}
        with nc.Block() as block:
            for e in self.ENGS:
                ops = self.ops[e]
                if not ops:
                    continue

                def body(eng, ops=ops, e=e):
                    waited = {}
                    nw = 0
                    for op in ops:
                        for d in op.deps:
                            if d.is_dma:
                                sem, val = d.sem, d.semval
                            else:
                                k = (d.sigcount - 1) // SEM_ROT
                                sem = self.eng_sems[d.eng][k]
                                val = (d.sigcount - 1) % SEM_ROT + 1
                            key = sem.num
                            if waited.get(key, 0) >= val:
                                continue
                            waited[key] = val
                            eng.wait_ge(sem, val)
                            nw += 1
                        if op.fn is None:
                            if op.sig:
                                k = (op.sigcount - 1) // SEM_ROT
                                eng.nop().then_inc(self.eng_sems[e][k], 1)
                            continue
                        ins = op.fn(eng)
                        if op.is_dma:
                            ins.then_inc(op.sem, 16)
                        elif op.sig:
                            k = (op.sigcount - 1) // SEM_ROT
                            ins.then_inc(self.eng_sems[e][k], 1)
                    stats[e] = (len(ops), nw)

                getattr(block, engobj[e])(body)
        return stats


class Ring:
    def __init__(self, P, tiles, name):
        self.tiles = tiles
        self.bufs = P.bufs(len(tiles), name)
        self.i = 0

    def next(self):
        i = self.i
        self.i = (i + 1) % len(self.tiles)
        return self.tiles[i], self.bufs[i]


class Ctx:
    pass


def setup_common(nc, P, dram):
    c = Ctx()
    c.nc, c.P, c.dram = nc, P, dram
    c.hT = nc.alloc_sbuf_tensor("hT", [128, 8, S], F32)
    c.BhT = [[P.buf(f"hT{k}_{t}") for t in range(4)] for k in range(8)]
    c.uT = nc.alloc_sbuf_tensor("uT", [128, 8, S], BF16)
    c.BuT = [P.buf(f"uT{t}") for t in range(4)]
    pst = [nc.alloc_psum_tensor(f"ps{i}", [128, 512], F32) for i in range(8)]
    c.G = Ring(P, pst[0:4], "psG")
    c.A = Ring(P, pst[4:6], "psA")
    c.Dn = Ring(P, pst[6:8], "psD")
    c.cst = nc.alloc_sbuf_tensor("cst", [128, 5 * 128], F32)
    c.Bcst = P.buf("cst")
    P.dma("sp", c.cst[:, :], dram["consts"], [], [c.Bcst])
    c.cstb = nc.alloc_sbuf_tensor("cstb", [128, 5 * 128], BF16)
    c.Bcstb = P.buf("cstb")
    P.copy("dve", c.cstb[:, :], c.cst[:, :], [c.Bcst], [c.Bcstb])
    c.ident_f = c.cst[:, 0:128]
    c.ones_f = c.cst[:, 256:384]
    c.ident_b = c.cstb[:, 0:128]
    c.tri_b = c.cstb[:, 128:256]
    c.ones_b = c.cstb[:, 256:384]
    c.normw = nc.alloc_sbuf_tensor("sb_normw", [128, 5, 8], F32)
    c.Bnormw = P.buf("normw")
    P.dma("sp", c.normw[:, :, :], dram["normw"], [], [c.Bnormw])
    c.sq = Ring(P, [nc.alloc_sbuf_tensor(f"sq{i}", [128, 512], F32) for i in range(2)], "sq")
    c.rstd = Ring(P, [nc.alloc_sbuf_tensor(f"rstd{i}", [128, 512], F32) for i in range(2)], "rstd")
    c.wn = Ring(P, [nc.alloc_sbuf_tensor(f"wn{i}", [128, 8, 128], BF16) for i in range(4)], "wn")
    c.ww = Ring(P, [nc.alloc_sbuf_tensor(f"ww{i}", [128, 8, 256], BF16) for i in range(2)], "ww")
    c.wo = Ring(P, [nc.alloc_sbuf_tensor(f"wo{i}", [128, 2, 1024], BF16) for i in range(2)], "wo")
    c.evac_rr = 0
    return c


def rms_stats(c, src, Bsrc_fn, nk, tt, scale_inv_n):
    P = c.P
    ps, Bps = c.G.next()
    for k in range(nk):
        sq, Bsq = c.sq.next()
        P.act(sq[:, :], src[:, k, tt * 512:(tt + 1) * 512], AF.Square, [Bsrc_fn(k, tt)], [Bsq])
        P.mm(ps[:, :], c.ones_f, sq[:, :], k == 0, k == nk - 1, [Bsq, c.Bcst], [Bps])
    r, Br = c.rstd.next()
    P.act(r[:, :], ps[:, :], AF.Sqrt, [Bps], [Br], scale=scale_inv_n, bias=EPS)
    P.recip(r[:, :], r[:, :], [Br], [Br])
    return r, Br


def emit_rmsnorm_u(c, layer):
    P = c.P
    for tt in range(4):
        r, Br = rms_stats(c, c.hT, lambda k, t: c.BhT[k][t], 8, tt, 1.0 / D)
        for k in range(8):
            eng = "dve"
            P.stt(eng, c.uT[:, k, tt * 512:(tt + 1) * 512], c.hT[:, k, tt * 512:(tt + 1) * 512],
                  c.normw[:, layer, k:k + 1], r[:, :], ALU.mult, ALU.mult,
                  [c.BhT[k][tt], Br, c.Bnormw], [c.BuT[tt]])


def load_wn(c, src):
    w, Bw = c.wn.next()
    kc = src.shape[1]
    c.P.dma("pool", w[:, 0:kc, :], src, [], [Bw])
    return w, Bw


def proj_multi(c, wsrcs, evac_multi, rhs=None, Brhs=None, kc=8):
    P = c.P
    ws = []
    for src, m in wsrcs:
        w, Bw = c.wn.next()
        P.dma("pool", w[:, 0:kc, 0:m], src, [], [Bw])
        ws.append((w, Bw, m))
    if rhs is None:
        rhs, Brhs = c.uT, c.BuT
    for tt in range(4):
        pss = []
        for (w, Bw, m) in ws:
            ps, Bps = c.G.next()
            for k in range(kc):
                P.mm(ps[0:m, :], w[:, k, 0:m], rhs[:, k, tt * 512:(tt + 1) * 512], k == 0, k == kc - 1,
                     [Bw, Brhs[tt]], [Bps])
            pss.append((ps, Bps))
        evac_multi(pss, tt)


def proj_fm(c, wsrc, evac, rhs=None, Brhs=None, kc=8):
    proj_multi(c, [(wsrc, 128)], lambda pss, tt: evac(pss[0][0], pss[0][1], tt), rhs, Brhs, kc)


def evac_copy(c, dst, Bdst):
    def f(ps, Bps, tt):
        c.evac_rr += 1
        eng = "act" if c.evac_rr % 2 == 0 else "dve"
        c.P.copy(eng, dst[:, tt * 512:(tt + 1) * 512], ps[:, :], [Bps], [Bdst])
    return f


def evac_silu(c, dst, Bdst):
    def f(ps, Bps, tt):
        c.P.act(dst[:, tt * 512:(tt + 1) * 512], ps[:, :], AF.Silu, [Bps], [Bdst])
    return f


def proj_tm(c, wsrc, ncols, evac, lhs=None, Blhs=None, kc=8):
    P = c.P
    w, Bw = c.ww.next()
    P.dma("pool", w[:, 0:kc, 0:ncols], wsrc, [], [Bw])
    if lhs is None:
        lhs, Blhs = c.uT, c.BuT
    for tb in range(16):
        ps, Bps = c.G.next()
        for k in range(kc):
            P.mm(ps[:, 0:ncols], lhs[:, k, tb * 128:(tb + 1) * 128], w[:, k, 0:ncols], k == 0, k == kc - 1,
                 [Bw, Blhs[tb // 4]], [Bps])
        evac(ps, Bps, tb)


def outproj_acc(c, wsrc, gT, BgT, nk):
    P = c.P
    w, Bw = c.wo.next()
    P.dma("pool", w[:, 0:nk, :], wsrc, [], [Bw])
    for oc in range(8):
        for tt in range(4):
            ps, Bps = c.G.next()
            for j in range(nk):
                P.mm(ps[:, :], w[:, j, oc * 128:(oc + 1) * 128], gT[:, j, tt * 512:(tt + 1) * 512],
                     j == 0, j == nk - 1, [Bw, BgT], [Bps])
            P.tt("dve", c.hT[:, oc, tt * 512:(tt + 1) * 512], ps[:, :], c.hT[:, oc, tt * 512:(tt + 1) * 512],
                 ALU.add, [Bps, c.BhT[oc][tt]], [c.BhT[oc][tt]])


ATTN_DEPTH = 2


def attn_head(c, parts, Bparts, v_fn, Bv, scale, bias_fn, Bbias, gz, Bgz, gT, BgT, L):
    P = c.P
    dq = []

    def pump(limit):
        while dq and (dq[0][0] == "fin" or sum(1 for t in dq if t[0] == "pv") > limit):
            dq.pop(0)[1]()

    for qt in range(4):
        oacc, Bo = c.A.next()
        dacc, Bd = c.Dn.next()
        nkb = 4 * qt + 4
        for kb in range(nkb):
            d = kb - 4 * qt
            c0 = max(d, 0) * 128
            ps, Bps = c.G.next()
            for i, (kT, qT) in enumerate(parts):
                P.mm(ps[:, c0:512], kT[:, kb * 128:(kb + 1) * 128], qT[:, qt * 512 + c0:(qt + 1) * 512],
                     i == 0, i == len(parts) - 1, Bparts, [Bps])
            pt, Bpt = L.pt.next()
            if bias_fn is None:
                P.act(pt[:, c0:512], ps[:, c0:512], AF.Exp, [Bps], [Bpt], scale=scale)
            else:
                for jj in range(c0 // 128, 4):
                    P.act(pt[:, jj * 128:(jj + 1) * 128], ps[:, jj * 128:(jj + 1) * 128], AF.Exp,
                          [Bps, Bbias], [Bpt], scale=scale, bias=bias_fn(kb, 4 * qt + jj))
            if d >= 0:
                P.tt("pool", pt[:, c0:c0 + 128], pt[:, c0:c0 + 128], c.tri_b, ALU.mult, [Bpt, c.Bcstb], [Bpt])

            def pv(kb=kb, c0=c0, pt=pt, Bpt=Bpt, oacc=oacc, Bo=Bo, dacc=dacc, Bd=Bd, nkb=nkb):
                P.mm(oacc[:, c0:512], v_fn(kb), pt[:, c0:512], kb == 0, kb == nkb - 1, [Bv, Bpt], [Bo])
                P.mm(dacc[:, c0:512], c.ones_b, pt[:, c0:512], kb == 0, kb == nkb - 1, [c.Bcstb, Bpt], [Bd])
            dq.append(("pv", pv))
            pump(ATTN_DEPTH)

        def fin(qt=qt, oacc=oacc, Bo=Bo, dacc=dacc, Bd=Bd):
            rd, Brd = L.rden.next()
            P.recip(rd[:, :], dacc[:, :], [Bd], [Brd])
            P.tt("dve", rd[:, :], oacc[:, :], rd[:, :], ALU.mult, [Bo, Brd], [Brd])
            P.tt("pool", gT[:, qt * 512:(qt + 1) * 512], rd[:, :], gz[:, qt * 512:(qt + 1) * 512], ALU.mult,
                 [Brd, Bgz], [BgT])
        dq.append(("fin", fin))
    pump(-1)


def emit_fox(c, layer):
    nc, P, dram = c.nc, c.P, c.dram
    L = Ctx()
    L.pt = Ring(P, [nc.alloc_sbuf_tensor(f"fx_pt{i}", [128, 512], BF16) for i in range(6)], "pt")
    L.rden = Ring(P, [nc.alloc_sbuf_tensor(f"fx_rd{i}", [128, 512], F32) for i in range(2)], "rden")
    qT = nc.alloc_sbuf_tensor("fx_qT", [128, 2, S], BF16); BqT = P.buf("qT")
    kT = nc.alloc_sbuf_tensor("fx_kT", [128, 2, S], BF16); BkT = P.buf("kT")
    gz = nc.alloc_sbuf_tensor("fx_gz", [128, 2, S], BF16); Bgz = P.buf("gz")
    gT = nc.alloc_sbuf_tensor("fx_gT", [128, 2, S], BF16); BgT = P.buf("gT")
    vt = nc.alloc_sbuf_tensor("fx_vt", [128, 16, 256], BF16); Bvt = P.buf("vt")
    lsp = nc.alloc_sbuf_tensor("fx_lsp", [128, 16, 16], F32); Blsp = P.buf("lsp")
    cT = nc.alloc_sbuf_tensor("fx_cT", [128, 16, 16], F32); BcT = P.buf("cT")
    cref = nc.alloc_sbuf_tensor("fx_cref", [128, 16, 16], F32); Bcref = P.buf("cref")
    fb = nc.alloc_sbuf_tensor("fx_fb", [128, 16], F32); Bfb = P.buf("fb")
    tmp = nc.alloc_sbuf_tensor("fx_tmp", [128, 16], F32); Btmp = P.buf("tmp")
    btab = nc.alloc_sbuf_tensor("fx_btab", [128, 2, 16, 16], F32); Bbt = P.buf("btab")
    negtri_f = c.cst[:, 384:512]
    P.dma("sp", fb[:, :], dram["fox_fb"], [], [Bfb])
    wf_src = dram["fox_wf"]
    scale = 128 ** -0.5

    def evac_f(ps, Bps, tb):
        P.tt("dve", tmp[:, :], ps[:, 0:16], fb[:, :], ALU.add, [Bps, Bfb], [Btmp])
        P.act(tmp[:, :], tmp[:, :], AF.Exp, [Btmp], [Btmp], scale=-1.0)
        P.act(lsp[:, tb, :], tmp[:, :], AF.Ln, [Btmp], [Blsp], bias=1.0)
    proj_tm(c, wf_src, 16, evac_f)
    negones_f = c.cst[:, 512:640]
    for tb in range(16):
        ps, Bps = c.G.next()
        for t2 in range(tb + 1):
            lhs = negtri_f if t2 == tb else negones_f
            P.mm(ps[:, 0:16], lhs, lsp[:, t2, :], t2 == 0, t2 == tb, [c.Bcst, Blsp], [Bps])
        P.copy("dve", cT[:, tb, :], ps[:, 0:16], [Bps], [BcT])
    ps, Bps = c.G.next()
    for tb in range(16):
        rhs = lsp[:, tb:tb + 1, :].to_broadcast([128, 16 - tb, 16])
        P.mm(ps[:, tb * 16:256].rearrange("p (j h) -> p j h", h=16), negones_f, rhs, tb == 0, tb == 15,
             [c.Bcst, Blsp], [Bps])
    P.copy("dve", cref[:, :, :].rearrange("p j h -> p (j h)"), ps[:, 0:256], [Bps], [Bcref])

    for hp in range(8):
        for j in range(2):
            h = 2 * hp + j
            proj_fm(c, dram["fox_wq"][h], evac_copy(c, qT[:, j, :], BqT))
            proj_fm(c, dram["fox_wk"][h], evac_copy(c, kT[:, j, :], BkT))
            proj_fm(c, dram["fox_wz"][h], evac_silu(c, gz[:, j, :], Bgz))
            P.tt("dve", btab[:, j, :, :],
                 cref[:, :, h:h + 1].rearrange("p j o -> p o j").to_broadcast([128, 16, 16]),
                 cT[:, :, h:h + 1].to_broadcast([128, 16, 16]),
                 ALU.subtract, [Bcref, BcT], [Bbt])

        def evac_v(ps, Bps, tb):
            c.evac_rr += 1
            eng = "act" if c.evac_rr % 2 == 0 else "dve"
            P.copy(eng, vt[:, tb, :], ps[:, 0:256], [Bps], [Bvt])
        proj_tm(c, dram["fox_wv"][hp], 256, evac_v)
        for j in range(2):
            attn_head(c, [(kT[:, j, :], qT[:, j, :])], [BkT, BqT],
                      lambda kb, j=j: vt[:, kb, j * 128:(j + 1) * 128], Bvt, scale,
                      lambda kb, jq, j=j: btab[:, j, kb, jq:jq + 1], Bbt,
                      gz[:, j, :], Bgz, gT[:, j, :], BgT, L)
        outproj_acc(c, dram["fox_wo"][hp], gT, BgT, 2)


def rope_combine(c, L, psX, BpsX, psXr, BpsXr, rows, cos_ap, sin_ap, Btab, out_ap, Bout, d=1):
    P = c.P
    t1, Bt1 = L.rt.next()
    t2, Bt2 = L.rt.next()
    P.tt("dve", t1[0:rows, :], psX[0:rows, :], cos_ap, ALU.mult, [BpsX, Btab], [Bt1])
    P.tt("dve", t2[0:rows, :], psXr[0:rows, :], sin_ap, ALU.mult, [BpsXr, Btab], [Bt2])
    a1, a2 = t1[0:rows, :], t2[0:rows, :]
    if d > 1:
        a1 = a1.rearrange("p (n r) -> p n r", r=d)
        a2 = a2.rearrange("p (n r) -> p n r", r=d)
    P.tt("pool", out_ap, a1, a2, ALU.add, [Bt1, Bt2], [Bout])


def normed_proj(c, L, wsrcs, nw_ap, Bnw, dst, Bdst, inv_n):
    P = c.P
    nk = len(wsrcs)

    def ev(pss, tt):
        psS, BpS = c.A.next()
        for k, (ps, Bps) in enumerate(pss):
            sq, Bsq = c.sq.next()
            P.act(sq[:, :], ps[:, :], AF.Square, [Bps], [Bsq])
            P.mm(psS[:, :], c.ones_f, sq[:, :], k == 0, k == nk - 1, [Bsq, c.Bcst], [BpS])
        r, Br = c.rstd.next()
        P.act(r[:, :], psS[:, :], AF.Sqrt, [BpS], [Br], scale=inv_n, bias=EPS)
        P.recip(r[:, :], r[:, :], [Br], [Br])
        for k, (ps, Bps) in enumerate(pss):
            P.stt("dve", dst[:, k, tt * 512:(tt + 1) * 512], ps[:, :], nw_ap[:, k:k + 1], r[:, :], ALU.mult, ALU.mult,
                  [Bps, Br, Bnw], [Bdst[tt]])
    proj_multi(c, [(w, 128) for w in wsrcs], ev)


def emit_mla(c, layer):
    nc, P, dram = c.nc, c.P, c.dram
    L = Ctx()
    L.pt = Ring(P, [nc.alloc_sbuf_tensor(f"ml_pt{i}", [128, 512], BF16) for i in range(5)], "pt")
    L.rt = Ring(P, [nc.alloc_sbuf_tensor(f"ml_rt{i}", [128, 512], F32) for i in range(3)], "rt")
    L.rden = L.rt
    cqn = nc.alloc_sbuf_tensor("ml_cqn", [128, 3, S], BF16); Bcqn = P.bufs(4, "cqn")
    ckvn = nc.alloc_sbuf_tensor("ml_ckvn", [128, 2, S], BF16); Bckvn = P.bufs(4, "ckvn")
    kr = nc.alloc_sbuf_tensor("ml_kr", [128, S], BF16); Bkr = P.buf("kr")
    qn = nc.alloc_sbuf_tensor("ml_qn", [128, S], BF16); Bqn = P.buf("qn")
    qr = nc.alloc_sbuf_tensor("ml_qr", [128, S], BF16); Bqr = P.buf("qr")
    kn = nc.alloc_sbuf_tensor("ml_kn", [128, S], BF16); Bkn = P.buf("kn")
    gz = nc.alloc_sbuf_tensor("ml_gz", [128, 1, S], BF16); Bgz = P.buf("gz")
    gT = nc.alloc_sbuf_tensor("ml_gT", [128, 1, S], BF16); BgT = P.buf("gT")
    vt = nc.alloc_sbuf_tensor("ml_vt", [128, 16, 128], BF16); Bvt = P.buf("vt")
    tab = nc.alloc_sbuf_tensor("ml_tab", [128, 2, S], F32); Btab = P.buf("tab")
    nws = nc.alloc_sbuf_tensor("ml_nws", [128, 5], F32); Bnws = P.buf("nws")
    P.dma("sp", tab[:, 0, :], dram["mla_rope"][0], [], [Btab])
    P.dma("sp", tab[:, 1, :], dram["mla_rope"][1], [], [Btab])
    P.dma("sp", nws[:, :], dram["mla_nw"], [], [Bnws])
    scale = 192 ** -0.5

    normed_proj(c, L, [dram["mla_wcq"][i] for i in range(3)], nws[:, 0:3], Bnws, cqn, Bcqn, 1.0 / 384)
    normed_proj(c, L, [dram["mla_wckv"][i] for i in range(2)], nws[:, 3:5], Bnws, ckvn, Bckvn, 1.0 / 256)

    def ev_kr(pss, tt):
        (pX, BX), (pXr, BXr) = pss
        rope_combine(c, L, pX, BX, pXr, BXr, 64, tab[0:64, 0, tt * 512:(tt + 1) * 512],
                     tab[0:64, 1, tt * 512:(tt + 1) * 512], Btab, kr[0:64, tt * 512:(tt + 1) * 512], Bkr)
    proj_multi(c, [(dram["mla_wkr"], 64), (dram["mla_wkrr"], 64)], ev_kr)

    for h in range(16):
        proj_fm(c, dram["mla_wqn"][h], evac_copy(c, qn, Bqn), cqn, Bcqn, 3)

        def ev_qr(pss, tt):
            (pX, BX), (pXr, BXr) = pss
            rope_combine(c, L, pX, BX, pXr, BXr, 64, tab[0:64, 0, tt * 512:(tt + 1) * 512],
                         tab[0:64, 1, tt * 512:(tt + 1) * 512], Btab, qr[0:64, tt * 512:(tt + 1) * 512], Bqr)
        proj_multi(c, [(dram["mla_wqr"][h], 64), (dram["mla_wqrr"][h], 64)], ev_qr, cqn, Bcqn, 3)
        proj_fm(c, dram["mla_wkn"][h], evac_copy(c, kn, Bkn), ckvn, Bckvn, 2)
        proj_fm(c, dram["mla_wz"][h], evac_silu(c, gz[:, 0, :], Bgz))

        def evac_v(ps, Bps, tb):
            c.evac_rr += 1
            eng = "act" if c.evac_rr % 2 == 0 else "dve"
            P.copy(eng, vt[:, tb, :], ps[:, 0:128], [Bps], [Bvt])
        proj_tm(c, dram["mla_wv"][h], 128, evac_v, ckvn, Bckvn, 2)
        attn_head(c, [(kn, qn), (kr[0:64, :], qr[0:64, :])], [Bkn, Bqn, Bkr, Bqr],
                  lambda kb: vt[:, kb, :], Bvt, scale, None, None,
                  gz[:, 0, :], Bgz, gT[:, 0, :], BgT, L)
        outproj_acc(c, dram["mla_wo"][h], gT, BgT, 1)


DIL_CFG = ((128, 1), (512, 4), (2048, 16))
import os as _os
DIL_GROUPS = [int(x) for x in _os.environ.get('DIL_GROUPS', '0,1,2').split(',')]


def emit_dil(c, layer):
    nc, P, dram = c.nc, c.P, c.dram
    L = Ctx()
    L.pt = Ring(P, [nc.alloc_sbuf_tensor(f"dl_pt{i}", [128, 256], BF16) for i in range(6)], "pt")
    L.rt = Ring(P, [nc.alloc_sbuf_tensor(f"dl_rt{i}", [128, 512], F32) for i in range(4)], "rt")
    qT = nc.alloc_sbuf_tensor("dl_qT", [128, S], BF16); BqT = P.buf("qT")
    kT = nc.alloc_sbuf_tensor("dl_kT", [128, S], BF16); BkT = P.buf("kT")
    vt = nc.alloc_sbuf_tensor("dl_vt", [128, 16, 128], BF16); Bvt = P.buf("vt")
    gz = nc.alloc_sbuf_tensor("dl_gz", [128, 1, S], BF16); Bgz = P.buf("gz")
    gT = nc.alloc_sbuf_tensor("dl_gT", [128, 1, S], BF16); BgT = P.buf("gT")
    oN = nc.alloc_sbuf_tensor("dl_oN", [128, S], F32); BoN = P.buf("oN")
    dN = nc.alloc_sbuf_tensor("dl_dN", [128, S], F32); BdN = P.buf("dN")
    tab = nc.alloc_sbuf_tensor("dl_tab", [128, 2, S], F32); Btab = P.buf("tab")
    msk = nc.alloc_sbuf_tensor("dl_msk", [128, 256], BF16); Bmsk = P.buf("msk")
    P.dma("sp", tab[:, 0, :], dram["dil_rope"][0], [], [Btab])
    P.dma("sp", tab[:, 1, :], dram["dil_rope"][1], [], [Btab])
    P.dma("pool", msk[:, :], dram["dil_mask"], [], [Bmsk])
    scale = 128 ** -0.5

    for h in range(8):
        proj_fm(c, dram["dil_wz"][h], evac_silu(c, gz[:, 0, :], Bgz))
        for g, (window, d) in enumerate(DIL_CFG):
            if g not in DIL_GROUPS:
                continue
            nsub = S // d
            nb = nsub // 128

            def ev_rope(dst, Bdst):
                def f(pss, tt):
                    (pX, BX), (pXr, BXr) = pss
                    npt = 512 // d
                    if d == 1:
                        out_ap = dst[:, tt * 512:(tt + 1) * 512]
                        rope_combine(c, L, pX, BX, pXr, BXr, 128, tab[:, 0, tt * 512:(tt + 1) * 512],
                                     tab[:, 1, tt * 512:(tt + 1) * 512], Btab, out_ap, Bdst)
                    else:
                        out_ap = dst[:, :].rearrange("p (r n) -> p n r", r=d)[:, tt * npt:(tt + 1) * npt, :]
                        rope_combine(c, L, pX, BX, pXr, BXr, 128, tab[:, 0, tt * 512:(tt + 1) * 512],
                                     tab[:, 1, tt * 512:(tt + 1) * 512], Btab, out_ap, Bdst, d)
                return f
            proj_multi(c, [(dram["dil_wq"][g * 8 + h], 128), (dram["dil_wqr"][g * 8 + h], 128)], ev_rope(qT, BqT))
            proj_multi(c, [(dram["dil_wk"][g * 8 + h], 128), (dram["dil_wkr"][g * 8 + h], 128)], ev_rope(kT, BkT))
            wv, Bwv = c.wn.next()
            P.dma("pool", wv[:, :, :], dram["dil_wv"][g * 8 + h], [], [Bwv])
            for r in range(d):
                for kb in range(nb):
                    blk = r * nb + kb
                    t0 = kb * 128 * d + r
                    ps, Bps = c.G.next()
                    for k in range(8):
                        lhs = c.uT[:, k, t0:t0 + 127 * d + 1:d]
                        P.mm(ps[:, 0:128], lhs, wv[:, k, :], k == 0, k == 7, [Bwv] + c.BuT, [Bps])
                    c.evac_rr += 1
                    P.copy("act" if c.evac_rr % 2 == 0 else "dve", vt[:, blk, :], ps[:, 0:128], [Bps], [Bvt])
            dq = []

            def pump(limit):
                while dq and (dq[0][0] == "fin" or sum(1 for t in dq if t[0] == "pv") > limit):
                    dq.pop(0)[1]()

            for bank in range(4):
                oacc, Bo = c.A.next()
                dacc, Bd = c.Dn.next()
                for qi in range(4):
                    blk = bank * 4 + qi
                    b = blk % nb
                    qs = blk * 128
                    ps, Bps = c.G.next()
                    nk = 2 if b > 0 else 1
                    P.mm(ps[:, 0:128], kT[:, qs:qs + 128], qT[:, qs:qs + 128], True, True, [BkT, BqT], [Bps])
                    if b > 0:
                        P.mm(ps[:, 128:256], kT[:, qs - 128:qs], qT[:, qs:qs + 128], True, True, [BkT, BqT], [Bps])
                    pt, Bpt = L.pt.next()
                    P.act(pt[:, 0:128 * nk], ps[:, 0:128 * nk], AF.Exp, [Bps], [Bpt], scale=scale)
                    P.tt("pool", pt[:, 0:128 * nk], pt[:, 0:128 * nk], msk[:, 0:128 * nk], ALU.mult, [Bpt, Bmsk], [Bpt])

                    def pv(qi=qi, blk=blk, b=b, pt=pt, Bpt=Bpt, oacc=oacc, Bo=Bo, dacc=dacc, Bd=Bd):
                        oc = oacc[:, qi * 128:(qi + 1) * 128]
                        dc = dacc[:, qi * 128:(qi + 1) * 128]
                        P.mm(oc, vt[:, blk, :], pt[:, 0:128], True, b == 0, [Bvt, Bpt], [Bo])
                        if b > 0:
                            P.mm(oc, vt[:, blk - 1, :], pt[:, 128:256], False, True, [Bvt, Bpt], [Bo])
                        P.mm(dc, c.ones_b, pt[:, 0:128], True, b == 0, [c.Bcstb, Bpt], [Bd])
                        if b > 0:
                            P.mm(dc, c.ones_b, pt[:, 128:256], False, True, [c.Bcstb, Bpt], [Bd])
                    dq.append(("pv", pv))
                    pump(ATTN_DEPTH)

                def fin(bank=bank, oacc=oacc, Bo=Bo, dacc=dacc, Bd=Bd, g=g, d=d):
                    pieces = []
                    if d == 1:
                        pieces.append((oN[:, bank * 512:(bank + 1) * 512], dN[:, bank * 512:(bank + 1) * 512], oacc[:, :], dacc[:, :]))
                    elif d == 4:
                        pieces.append((oN[:, bank:S:4], dN[:, bank:S:4], oacc[:, :], dacc[:, :]))
                    else:
                        for q4 in range(4):
                            r = bank * 4 + q4
                            pieces.append((oN[:, r:S:16], dN[:, r:S:16], oacc[:, q4 * 128:(q4 + 1) * 128], dacc[:, q4 * 128:(q4 + 1) * 128]))
                    for (on, dn, oa, da) in pieces:
                        if g == DIL_GROUPS[0]:
                            P.copy("act", on, oa, [Bo], [BoN])
                            P.copy("dve", dn, da, [Bd], [BdN])
                        else:
                            P.tt("dve", on, oa, on, ALU.add, [Bo, BoN], [BoN])
                            P.tt("dve", dn, da, dn, ALU.add, [Bd, BdN], [BdN])
                dq.append(("fin", fin))
            pump(-1)
        for tt in range(4):
            sl = slice(tt * 512, (tt + 1) * 512)
            P.recip(dN[:, sl], dN[:, sl], [BdN], [BdN])
            P.tt("dve", oN[:, sl], oN[:, sl], dN[:, sl], ALU.mult, [BoN, BdN], [BoN])
            P.tt("pool", gT[:, 0, sl], oN[:, sl], gz[:, 0, sl], ALU.mult, [BoN, Bgz], [BgT])
        outproj_acc(c, dram["dil_wo"][h], gT, BgT, 1)


def emit_ssd(c, layer):
    nc, P, dram = c.nc, c.P, c.dram
    tri_f = c.cst[:, 128:256]
    negtri_f = c.cst[:, 384:512]
    negones_f = c.cst[:, 512:640]
    dt = nc.alloc_sbuf_tensor("sd_dt", [128, 16, 32], F32); Bdt = P.buf("dt")
    absa = nc.alloc_sbuf_tensor("sd_absa", [128, 16, 32], F32); Babsa = P.buf("absa")
    acum = nc.alloc_sbuf_tensor("sd_acum", [128, 16, 32], F32); Bacum = P.buf("acum")
    wts = nc.alloc_sbuf_tensor("sd_w", [128, 16, 32], F32); Bw = P.buf("w")
    dlast = nc.alloc_sbuf_tensor("sd_dlast", [128, 16, 32], F32); Bdl = P.buf("dlast")
    vecs = nc.alloc_sbuf_tensor("sd_vecs", [128, 3, 32], F32); Bvecs = P.buf("vecs")
    cw = nc.alloc_sbuf_tensor("sd_cw", [128, 32, 5], F32); Bcw = P.buf("cw")
    dsk = nc.alloc_sbuf_tensor("sd_dsk", [128, 16], F32); Bdsk = P.buf("dsk")
    nrm = nc.alloc_sbuf_tensor("sd_nrm", [128, 16], F32); Bnrm = P.buf("nrm")
    tmp32 = nc.alloc_sbuf_tensor("sd_tmp32", [128, 32], F32); Bt32 = P.buf("t32")
    pre = nc.alloc_sbuf_tensor("sd_pre", [128, S + 3], F32); Bpre = P.buf("pre")
    xTf = nc.alloc_sbuf_tensor("sd_xTf", [128, 2, S], F32); BxTf = P.bufs(2, "xTf")
    xTb = nc.alloc_sbuf_tensor("sd_xTb", [128, 2, S], BF16); BxTb = P.bufs(2, "xTb")
    BT = nc.alloc_sbuf_tensor("sd_BT", [128, S], BF16); BBT = P.buf("BT")
    CT = nc.alloc_sbuf_tensor("sd_CT", [128, S], BF16); BCT = P.buf("CT")
    gz = nc.alloc_sbuf_tensor("sd_gz", [128, 2, S], BF16); Bgz = P.buf("gz")
    dmR = Ring(P, [nc.alloc_sbuf_tensor(f"sd_dm{i}", [128, 512], F32) for i in range(2)], "dm4")
    cacc = dmR
    eeR = Ring(P, [nc.alloc_sbuf_tensor(f"sd_ee{i}", [128, 512], F32) for i in range(1)], "ee4")
    mpR = Ring(P, [nc.alloc_sbuf_tensor(f"sd_mp{i}", [128, 512], BF16) for i in range(2)], "mp4")
    csdR = Ring(P, [nc.alloc_sbuf_tensor(f"sd_csd{i}", [128, 512], BF16) for i in range(2)], "csd4")
    xtokR = Ring(P, [nc.alloc_sbuf_tensor(f"sd_xtok{i}", [128, 256], BF16) for i in range(3)], "xtok")
    xwR = Ring(P, [nc.alloc_sbuf_tensor(f"sd_xw{i}", [128, 256], BF16) for i in range(3)], "xw")
    cbmR = Ring(P, [nc.alloc_sbuf_tensor(f"sd_cbm{i}", [128, 128], F32) for i in range(2)], "cbm")
    btokR = Ring(P, [nc.alloc_sbuf_tensor(f"sd_btok{i}", [128, 128], BF16) for i in range(2)], "btok")
    Sf = nc.alloc_sbuf_tensor("sd_Sf", [128, 256], F32); BSf = P.buf("Sf")
    Sb = nc.alloc_sbuf_tensor("sd_Sb", [128, 256], BF16); BSb = P.buf("Sb")
    P.dma("sp", vecs[:, :, :], dram["ssm_vecs"], [], [Bvecs])
    P.dma("sp", cw[:, :, :], dram["ssm_cw"], [], [Bcw])
    P.dma("sp", dsk[:, :], dram["ssm_dsk"], [], [Bdsk])
    P.dma("sp", nrm[:, :], dram["ssm_nrm"], [], [Bnrm])
    P.memset("pool", pre[:, 0:3], 0.0, [Bpre])
    P.act(vecs[:, 1, :], vecs[:, 1, :], AF.Exp, [Bvecs], [Bvecs])

    def evac_dt(ps, Bps, tb):
        P.tt("dve", tmp32[:, :], ps[:, 0:32], vecs[:, 0, :], ALU.add, [Bps, Bvecs], [Bt32])
        P.act(tmp32[:, :], tmp32[:, :], AF.Exp, [Bt32], [Bt32])
        P.act(dt[:, tb, :], tmp32[:, :], AF.Ln, [Bt32], [Bdt], bias=1.0)
        P.tt("dve", absa[:, tb, :], dt[:, tb, :], vecs[:, 1, :], ALU.mult, [Bdt, Bvecs], [Babsa])
    proj_tm(c, dram["ssm_wdt"], 32, evac_dt)
    for tb in range(16):
        ps, Bps = c.G.next()
        P.mm(ps[:, 0:32], negtri_f, absa[:, tb, :], True, True, [c.Bcst, Babsa], [Bps])
        P.mm(ps[:, 32:64], negones_f, absa[:, tb, :], True, True, [c.Bcst, Babsa], [Bps])
        P.copy("dve", acum[:, tb, :], ps[:, 0:32], [Bps], [Bacum])
        P.copy("dve", dlast[:, tb, :], ps[:, 32:64], [Bps], [Bdl])
        P.tt("dve", wts[:, tb, :], dlast[:, tb, :], acum[:, tb, :], ALU.subtract, [Bdl, Bacum], [Bw])
        P.act(wts[:, tb, :], wts[:, tb, :], AF.Exp, [Bw], [Bw])
        P.tt("dve", wts[:, tb, :], wts[:, tb, :], dt[:, tb, :], ALU.mult, [Bw, Bdt], [Bw])
        P.act(dlast[:, tb, :], dlast[:, tb, :], AF.Exp, [Bdl], [Bdl])

    def conv_silu(ch, outs):
        for tt in range(4):
            a, Ba = cacc.next()
            o = tt * 512
            P.ts("dve", a[:, :], pre[:, o:o + 512], cw[:, ch, 0:1], cw[:, ch, 4:5], ALU.mult, ALU.add, [Bpre, Bcw], [Ba])
            for k in range(1, 4):
                P.stt("dve", a[:, :], pre[:, o + k:o + k + 512], cw[:, ch, k:k + 1], a[:, :], ALU.mult, ALU.add,
                      [Bpre, Bcw, Ba], [Ba])
            for (dst, Bdst) in outs:
                P.act(dst[:, o:o + 512], a[:, :], AF.Silu, [Ba], [Bdst])

    def evac_pre(ps, Bps, tt):
        P.copy("act", pre[:, 3 + tt * 512:3 + (tt + 1) * 512], ps[:, :], [Bps], [Bpre])

    for g in range(8):
        for i in range(2):
            proj_fm(c, dram["ssm_wx"][2 * g + i], evac_pre)
            conv_silu(2 * g + i, [(xTf[:, i, :], BxTf[i]), (xTb[:, i, :], BxTb[i])])
            proj_fm(c, dram["ssm_wz"][2 * g + i], evac_silu(c, gz[:, i, :], Bgz))
        proj_fm(c, dram["ssm_wB"][g], evac_pre)
        conv_silu(16 + g, [(BT, BBT)])
        proj_fm(c, dram["ssm_wC"][g], evac_pre)
        conv_silu(24 + g, [(CT, BCT)])
        dq = []
        for ck in range(16):
            cs = slice(ck * 128, (ck + 1) * 128)
            psT, BpsT = c.G.next()
            for i in range(2):
                P.mm(psT[:, i * 128:(i + 1) * 128], xTb[:, i, cs], c.ident_b, True, True, [BxTb[i], c.Bcstb], [BpsT])
            P.mm(psT[:, 256:384], BT[:, cs], c.ident_b, True, True, [BBT, c.Bcstb], [BpsT])
            pst, Bpst = c.A.next()
            P.mm(pst[:, 256:384], BT[:, cs], CT[:, cs], True, True, [BBT, BCT], [Bpst])
            pb4, Bpb4 = c.G.next()
            for j in range(4):
                h = 4 * g + j
                P.mm(pb4[:, j * 128:(j + 1) * 128], absa[:, ck, h:h + 1].to_broadcast([128, 128]), negtri_f, True, True,
                     [Babsa, c.Bcst], [Bpb4])
            xtok, Bxtok = xtokR.next()
            P.copy("act", xtok[:, :], psT[:, 0:256], [BpsT], [Bxtok])
            btok, Bbtok = btokR.next()
            P.copy("act", btok[:, :], psT[:, 256:384], [BpsT], [Bbtok])
            cbm, Bcbm = cbmR.next()
            P.tt("dve", cbm[:, :], pst[:, 256:384], tri_f, ALU.mult, [Bpst, c.Bcst], [Bcbm])
            dm4, Bdm4 = dmR.next()
            extra = []
            if ck > 0:
                ee4, Bee4 = eeR.next()
                P.act(ee4[:, :], pb4[:, :], AF.Exp, [Bpb4], [Bee4])
                extra = [Bee4]
            for j in range(4):
                h = 4 * g + j
                P.ts("dve", dm4[:, j * 128:(j + 1) * 128], pb4[:, j * 128:(j + 1) * 128], acum[:, ck, h:h + 1], 0.0,
                     ALU.subtract, ALU.min, [Bpb4, Bacum] + extra, [Bdm4])
            P.act(dm4[:, :], dm4[:, :], AF.Exp, [Bdm4], [Bdm4])
            mp4, Bmp4 = mpR.next()
            for j in range(4):
                h = 4 * g + j
                P.stt("dve", mp4[:, j * 128:(j + 1) * 128], dm4[:, j * 128:(j + 1) * 128], dt[:, ck, h:h + 1], cbm[:, :],
                      ALU.mult, ALU.mult, [Bdm4, Bdt, Bcbm], [Bmp4])
            csd4, Bcsd4 = csdR.next()
            if ck > 0:
                for j in range(4):
                    P.tt("pool", csd4[:, j * 128:(j + 1) * 128], ee4[:, j * 128:(j + 1) * 128], CT[:, cs], ALU.mult,
                         [BCT, Bee4], [Bcsd4])
            xw, Bxw = xwR.next()
            for j in range(4):
                P.ts("pool", xw[:, j * 64:(j + 1) * 64], xtok[:, j * 64:(j + 1) * 64],
                     wts[:, ck, 4 * g + j:4 * g + j + 1], None, ALU.mult, None, [Bxtok, Bw], [Bxw])

            def cd(ck=ck, cs=cs, pst=pst, Bpst=Bpst, xtok=xtok, Bxtok=Bxtok, btok=btok, Bbtok=Bbtok, xw=xw, Bxw=Bxw,
                   mp4=mp4, Bmp4=Bmp4, csd4=csd4, Bcsd4=Bcsd4):
                P.mm(pst[:, 0:256], btok[:, :], xw[:, :], True, True, [Bbtok, Bxw], [Bpst])
                yps, Byps = c.Dn.next()
                for i in range(2):
                    for jj in range(2):
                        j = 2 * i + jj
                        yo = yps[64 * jj:64 * jj + 64, i * 128:(i + 1) * 128]
                        P.mm(yo, xtok[:, j * 64:(j + 1) * 64], mp4[:, j * 128:(j + 1) * 128], True, ck == 0,
                             [Bxtok, Bmp4], [Byps])
                        if ck > 0:
                            P.mm(yo, Sb[:, j * 64:(j + 1) * 64], csd4[:, j * 128:(j + 1) * 128], False, True,
                                 [BSb, Bcsd4], [Byps])
                for i in range(2):
                    P.stt("dve", xTf[:, i, cs], xTf[:, i, cs], dsk[:, 2 * g + i:2 * g + i + 1], yps[:, i * 128:(i + 1) * 128],
                          ALU.mult, ALU.add, [BxTf[i], Bdsk, Byps], [BxTf[i]])
                if ck == 0:
                    P.copy("dve", Sf[:, :], pst[:, 0:256], [Bpst], [BSf])
                else:
                    for j in range(4):
                        P.stt("dve", Sf[:, j * 64:(j + 1) * 64], Sf[:, j * 64:(j + 1) * 64],
                              dlast[:, ck, 4 * g + j:4 * g + j + 1], pst[:, j * 64:(j + 1) * 64], ALU.mult, ALU.add,
                              [BSf, Bdl, Bpst], [BSf])
                if ck < 15:
                    P.copy("act", Sb[:, :], Sf[:, :], [BSf], [BSb])
            dq.append(cd)
            while len(dq) > 1:
                dq.pop(0)()
        while dq:
            dq.pop(0)()
        for tt in range(4):
            sl = slice(tt * 512, (tt + 1) * 512)
            for i in range(2):
                P.tt("pool", xTf[:, i, sl], xTf[:, i, sl], gz[:, i, sl], ALU.mult, [BxTf[i], Bgz], [BxTf[i]])
            r, Br = rms_stats(c, xTf, lambda k, t: BxTf[k], 2, tt, 1.0 / 256)
            for i in range(2):
                P.stt("dve", gz[:, i, sl], xTf[:, i, sl], nrm[:, 2 * g + i:2 * g + i + 1], r[:, :], ALU.mult, ALU.mult,
                      [BxTf[i], Br, Bnrm], [Bgz])
        outproj_acc(c, dram["ssm_wo"][g], gz, Bgz, 2)


def tile_cols(W, width=128):
    K, N = W.shape
    return np.ascontiguousarray(W.reshape(K // 128, 128, N // width, width).transpose(2, 1, 0, 3))


def tile_rows(W, nk):
    R, N = W.shape
    return np.ascontiguousarray(W.reshape(R // (128 * nk), nk, 128, N).transpose(0, 2, 1, 3))


def rope_tables(half, reps):
    inv_freq = (np.float32(ROPE_THETA) ** (-np.arange(half, dtype=np.float32) / np.float32(half))).astype(np.float32)
    ang = np.arange(S, dtype=np.float32)[None, :] * inv_freq[:, None]
    cos = np.cos(ang).astype(np.float32)
    sin = np.sin(ang).astype(np.float32)
    t = np.zeros((2, 128, S), np.float32)
    t[0] = 1.0
    for r in range(reps):
        b = r * 2 * half
        t[0, b:b + half] = cos
        t[0, b + half:b + 2 * half] = cos
        t[1, b:b + half] = -sin
        t[1, b + half:b + 2 * half] = sin
    return t


def make_consts():
    cst = np.zeros((128, 5 * 128), np.float32)
    i = np.arange(128)
    cst[:, 0:128] = np.eye(128)
    cst[:, 128:256] = (i[:, None] <= i[None, :])
    cst[:, 256:384] = 1.0
    cst[:, 384:512] = -(i[:, None] <= i[None, :]).astype(np.float32)
    cst[:, 512:640] = -1.0
    return cst


def host_prep(inputs, layers):
    shared = {}
    shared["consts"] = make_consts()
    nw = np.concatenate([inputs["norm_w"], inputs["final_norm_w"][None]], axis=0)
    shared["normw"] = np.ascontiguousarray(nw.reshape(5, 8, 128).transpose(2, 0, 1))
    if 0 in layers:
        W = inputs["ssm_in_w"][0]
        shared["ssm_wz"] = tile_cols(W[:, 0:2048])
        shared["ssm_wx"] = tile_cols(W[:, 2048:4096])
        shared["ssm_wB"] = tile_cols(W[:, 4096:5120])
        shared["ssm_wC"] = tile_cols(W[:, 5120:6144])
        shared["ssm_wdt"] = tile_cols(W[:, 6144:6176], 32)[0]
        vec = np.stack([inputs["ssm_dt_bias"][0], inputs["ssm_A_log"][0], inputs["ssm_D"][0]], 0)
        shared["ssm_vecs"] = np.ascontiguousarray(np.broadcast_to(vec[None], (128, 3, 32)))
        cwb = np.concatenate([inputs["ssm_conv_w"][0], inputs["ssm_conv_b"][0][None]], 0)
        shared["ssm_cw"] = np.ascontiguousarray(cwb.reshape(5, 32, 128).transpose(2, 1, 0))
        shared["ssm_dsk"] = np.ascontiguousarray(np.repeat(inputs["ssm_D"][0], 64).reshape(16, 128).T)
        shared["ssm_nrm"] = np.ascontiguousarray(inputs["ssm_norm_w"][0].reshape(16, 128).T)
        shared["ssm_wo"] = tile_rows(inputs["ssm_out_w"][0], 2)
    if 1 in layers:
        W = inputs["mla_in_w"][0]
        perm = np.concatenate([np.arange(32, 64), np.arange(0, 32)])
        shared["mla_wcq"] = tile_cols(W[:, 0:384])
        shared["mla_wckv"] = tile_cols(W[:, 384:640])
        shared["mla_wkr"] = tile_cols(W[:, 640:704], 64)[0]
        shared["mla_wkrr"] = tile_cols(W[:, 640:704][:, perm], 64)[0]
        shared["mla_wz"] = tile_cols(W[:, 704:2752])
        UQ = inputs["mla_uq_w"][0].reshape(384, 16, 192)
        shared["mla_wqn"] = tile_cols(np.ascontiguousarray(UQ[:, :, 0:128]).reshape(384, 2048))
        shared["mla_wqr"] = tile_cols(np.ascontiguousarray(UQ[:, :, 128:192]).reshape(384, 1024), 64)
        shared["mla_wqrr"] = tile_cols(np.ascontiguousarray(UQ[:, :, 128:192][:, :, perm]).reshape(384, 1024), 64)
        UKV = inputs["mla_ukv_w"][0].reshape(256, 16, 256)
        shared["mla_wkn"] = tile_cols(np.ascontiguousarray(UKV[:, :, 0:128]).reshape(256, 2048))
        shared["mla_wv"] = tile_cols(np.ascontiguousarray(UKV[:, :, 128:256]).reshape(256, 2048))
        shared["mla_wo"] = tile_rows(inputs["mla_out_w"][0], 1)
        nwq = inputs["mla_q_norm_w"][0].reshape(3, 128).T
        nwkv = inputs["mla_kv_norm_w"][0].reshape(2, 128).T
        shared["mla_nw"] = np.ascontiguousarray(np.concatenate([nwq, nwkv], axis=1))
        shared["mla_rope"] = rope_tables(32, 2)
    if 3 in layers:
        W = inputs["dil_in_w"][0]
        perm = np.arange(128)
        perm[0:16] = np.arange(16, 32)
        perm[16:32] = np.arange(0, 16)
        wq, wk, wv, wqr, wkr = [], [], [], [], []
        for g in range(3):
            base = 3072 * g
            Q = W[:, base:base + 1024].reshape(1024, 8, 128)
            Kw = W[:, base + 1024:base + 2048].reshape(1024, 8, 128)
            wq.append(tile_cols(Q.reshape(1024, 1024)))
            wk.append(tile_cols(Kw.reshape(1024, 1024)))
            wqr.append(tile_cols(np.ascontiguousarray(Q[:, :, perm]).reshape(1024, 1024)))
            wkr.append(tile_cols(np.ascontiguousarray(Kw[:, :, perm]).reshape(1024, 1024)))
            wv.append(tile_cols(W[:, base + 2048:base + 3072]))
        shared["dil_wq"] = np.concatenate(wq, 0)
        shared["dil_wk"] = np.concatenate(wk, 0)
        shared["dil_wqr"] = np.concatenate(wqr, 0)
        shared["dil_wkr"] = np.concatenate(wkr, 0)
        shared["dil_wv"] = np.concatenate(wv, 0)
        shared["dil_wz"] = tile_cols(W[:, 9216:10240])
        shared["dil_wo"] = tile_rows(inputs["dil_out_w"][0], 1)
        shared["dil_rope"] = rope_tables(16, 1)
        i = np.arange(128)
        m = np.zeros((128, 256), np.float32)
        m[:, 0:128] = (i[:, None] <= i[None, :])
        m[:, 128:256] = (i[:, None] >= i[None, :])
        shared["dil_mask"] = m
    if 2 in layers:
        W = inputs["fox_in_w"][0]
        shared["fox_wq"] = tile_cols(W[:, 0:2048])
        shared["fox_wk"] = tile_cols(W[:, 2048:4096])
        shared["fox_wv"] = tile_cols(W[:, 4096:6144], 256)
        shared["fox_wf"] = tile_cols(W[:, 6144:6160], 16)[0]
        shared["fox_wz"] = tile_cols(W[:, 6160:8208])
        shared["fox_fb"] = np.ascontiguousarray(np.broadcast_to(inputs["fox_f_bias"][0][None, :], (128, 16)))
        shared["fox_wo"] = tile_rows(inputs["fox_out_w"][0], 2)
    return shared


def build_program(layers, shared_shapes, final_norm):
    nc = bass.Bass("TRN2", target_bir_lowering=False)
    dram = {}
    for k, shp in shared_shapes.items():
        dram[k] = nc.dram_tensor(k, list(shp), F32, kind="ExternalInput").ap()
    hin = nc.dram_tensor("hin", [NSEQ, 8, 128, S], F32, kind="ExternalInput").ap()
    hout = nc.dram_tensor("hout", [NSEQ, 8, 128, S], F32, kind="ExternalOutput").ap()
    P = Prog(nc)
    c = setup_common(nc, P, dram)
    Bout = P.buf("hout")
    for s in range(NSEQ):
        for k in range(8):
            for t in range(4):
                P.dma("sp", c.hT[:, k, t * 512:(t + 1) * 512], hin[s, k, :, t * 512:(t + 1) * 512], [], [c.BhT[k][t]])
        for layer in layers:
            emit_rmsnorm_u(c, layer)
            emit_layer_cached(c, layer, LAYER_EMITTERS[layer])
        if final_norm:
            for tt in range(4):
                r, Br = rms_stats(c, c.hT, lambda k, t: c.BhT[k][t], 8, tt, 1.0 / D)
                for k in range(8):
                    eng = "dve"
                    P.stt(eng, c.hT[:, k, tt * 512:(tt + 1) * 512], c.hT[:, k, tt * 512:(tt + 1) * 512],
                          c.normw[:, 4, k:k + 1], r[:, :], ALU.mult, ALU.mult,
                          [c.BhT[k][tt], Br, c.Bnormw], [c.BhT[k][tt]])
        for k in range(8):
            for t in range(4):
                P.dma("sp", hout[s, k, :, t * 512:(t + 1) * 512], c.hT[:, k, t * 512:(t + 1) * 512], [c.BhT[k][t]], [Bout])
    P.barrier()
    stats = P.emit()
    return nc, stats


def emit_layer_cached(c, layer, fn):
    from contextlib import ExitStack
    nc = c.nc
    c._scope_id = getattr(c, "_scope_id", 0) + 1
    sid = c._scope_id
    with ExitStack() as st:
        class NCProxy:
            def __getattr__(self, a):
                if a == "alloc_sbuf_tensor":
                    return lambda name, shape, dtype: st.enter_context(nc.sbuf_tensor(f"{name}_s{sid}", shape, dtype))
                return getattr(nc, a)
        c.nc = NCProxy()
        try:
            fn(c, layer)
        finally:
            c.nc = nc
        c.P.barrier()


_CACHE = {}
LAYER_EMITTERS = {0: emit_ssd, 1: emit_mla, 2: emit_fox, 3: emit_dil}


def run_layers(hT_all, inputs, layers, final_norm):
    shared = host_prep(inputs, layers)
    key = (tuple(layers), final_norm)
    if key not in _CACHE:
        _CACHE[key] = build_program(layers, {k: v.shape for k, v in shared.items()}, final_norm)
    nc, stats = _CACHE[key]
    in_maps = []
    for core in range(8):
        m = dict(shared)
        m["hin"] = np.ascontiguousarray(hT_all[core * NSEQ:(core + 1) * NSEQ])
        in_maps.append(m)
    res = run_bass_kernel_spmd(nc, in_maps, core_ids=list(range(8)))
    return np.concatenate([r["hout"] for r in res.results], axis=0)


def to_fm(x):
    B = x.shape[0]
    return np.ascontiguousarray(x.transpose(0, 2, 1).reshape(B, 8, 128, S))


def from_fm(hT):
    B = hT.shape[0]
    return np.ascontiguousarray(hT.reshape(B, D, S).transpose(0, 2, 1))


def kernel(**inputs):
    inputs = {k: np.asarray(v, dtype=np.float32) for k, v in inputs.items()}
    hT = to_fm(inputs["x"])
    hT = run_layers(hT, inputs, [0, 1, 2, 3], True)
    return from_fm(hT)
```

```python
import numpy as np
import concourse.bass as bass
import concourse.mybir as mybir
from concourse.bass_utils import run_bass_kernel_spmd

F32 = mybir.dt.float32
BF16 = mybir.dt.bfloat16
AF = mybir.ActivationFunctionType
ALU = mybir.AluOpType

SAME_ENGINE_SYNC = True
SEM_ROT = 30000
N_DMA_SEMS = 12
S = 2048
D = 1024
NSEQ = 2
EPS = 1e-6
ROPE_THETA = 500000.0


class Buf:
    __slots__ = ("name", "writer", "readers")

    def __init__(self, name):
        self.name = name
        self.writer = None
        self.readers = []


class Op:
    __slots__ = ("eng", "fn", "deps", "sig", "idx", "is_dma", "sem", "semval", "sigcount")

    def __init__(self, eng, fn, is_dma):
        self.eng = eng
        self.fn = fn
        self.deps = []
        self.sig = False
        self.is_dma = is_dma
        self.sem = None
        self.semval = 0
        self.sigcount = 0


class Prog:
    ENGS = ("pe", "act", "dve", "pool", "sp")

    def __init__(self, nc):
        self.nc = nc
        self.ops = {e: [] for e in self.ENGS}
        self.dma_sems = {}
        self.dma_rr = {}
        self.dma_last = {}
        for q in ("sp", "pool"):
            self.dma_sems[q] = [nc.alloc_semaphore(name=f"dq_{q}_{i}") for i in range(N_DMA_SEMS)]
            self.dma_rr[q] = 0
            self.dma_last[q] = [None] * N_DMA_SEMS
        self.dma_cnt = {}
        self.eng_sems = {}
        self.nbuf = 0

    def buf(self, name=None):
        self.nbuf += 1
        return Buf(name or f"b{self.nbuf}")

    def bufs(self, n, name="b"):
        return [self.buf(f"{name}{i}") for i in range(n)]

    def _add(self, eng, fn, reads, writes, is_dma=False):
        op = Op(eng, fn, is_dma)
        deps = []
        for b in reads:
            if b.writer is not None:
                deps.append(b.writer)
        for b in writes:
            if b.writer is not None:
                deps.append(b.writer)
            deps.extend(b.readers)
        if is_dma:
            q = eng
            i = self.dma_rr[q]
            self.dma_rr[q] = (i + 1) % N_DMA_SEMS
            prev = self.dma_last[q][i]
            if prev is not None:
                deps.append(prev)
            op.sem = self.dma_sems[q][i]
            key = (q, i)
            self.dma_cnt[key] = self.dma_cnt.get(key, 0) + 1
            op.semval = 16 * self.dma_cnt[key]
            self.dma_last[q][i] = op
        seen = set()
        for d in deps:
            if d is op or id(d) in seen:
                continue
            seen.add(id(d))
            if (not d.is_dma) and (not is_dma) and d.eng == eng:
                if eng == "pe" or not SAME_ENGINE_SYNC:
                    continue
            op.deps.append(d)
        op.idx = len(self.ops[eng])
        self.ops[eng].append(op)
        for b in reads:
            if not is_dma:
                b.readers = [r for r in b.readers if r.is_dma or r.eng != eng]
            b.readers.append(op)
        for b in writes:
            b.writer = op
            b.readers = []
        return op

    def mm(self, out, lhsT, rhs, start, stop, reads, writes):
        return self._add("pe", lambda e: e.matmul(out, lhsT, rhs, start=start, stop=stop), reads, writes)

    def act(self, out, in_, func, reads, writes, **kw):
        return self._add("act", lambda e: e.activation(out, in_, func, **kw), reads, writes)

    def tt(self, eng, out, in0, in1, op, reads, writes):
        return self._add(eng, lambda e: e.tensor_tensor(out, in0, in1, op), reads, writes)

    def ts(self, eng, out, in0, s1, s2, op0, op1, reads, writes):
        if op1 is None:
            return self._add(eng, lambda e: e.tensor_scalar(out, in0, s1, s2, op0), reads, writes)
        return self._add(eng, lambda e: e.tensor_scalar(out, in0, s1, s2, op0, op1), reads, writes)

    def stt(self, eng, out, in0, scalar, in1, op0, op1, reads, writes):
        return self._add(eng, lambda e: e.scalar_tensor_tensor(out, in0, scalar, in1, op0, op1), reads, writes)

    def copy(self, eng, out, in_, reads, writes):
        if eng == "act":
            return self._add(eng, lambda e: e.copy(out, in_), reads, writes)
        return self._add(eng, lambda e: e.tensor_copy(out, in_), reads, writes)

    def memset(self, eng, ap, val, writes):
        return self._add(eng, lambda e: e.memset(ap, val), [], writes)

    def recip(self, out, in_, reads, writes):
        return self._add("dve", lambda e: e.reciprocal(out, in_), reads, writes)

    def dma(self, q, out, in_, reads, writes):
        return self._add(q, lambda e: e.dma_start(out, in_), reads, writes, is_dma=True)

    def barrier(self):
        last = []
        for e in self.ENGS:
            for op in reversed(self.ops[e]):
                if not op.is_dma:
                    last.append(op)
                    break
        for q in self.dma_last:
            for op in self.dma_last[q]:
                if op is not None:
                    last.append(op)
        for e in self.ENGS:
            op = Op(e, None, False)
            for d in last:
                if d.eng == e and not d.is_dma and e == "pe":
                    continue
                op.deps.append(d)
            op.idx = len(self.ops[e])
            self.ops[e].append(op)

    def emit(self):
        nc = self.nc
        for e in self.ENGS:
            for op in self.ops[e]:
                for d in op.deps:
                    if not d.is_dma:
                        d.sig = True
        for e in self.ENGS:
            c = 0
            for op in self.ops[e]:
                if op.is_dma:
                    continue
                if op.sig:
                    c += 1
                    op.sigcount = c
            nsem = (c + SEM_ROT - 1) // SEM_ROT
            self.eng_sems[e] = [nc.alloc_semaphore(name=f"es_{e}_{i}") for i in range(max(nsem, 1))]
        engobj = {"pe": "tensor", "act": "scalar", "dve": "vector", "pool": "gpsimd", "sp": "sync"}
        stats = {}
        with nc.Block() as block:
            for e in self.ENGS:
                ops = self.ops[e]
                if not ops:
                    continue

                def body(eng, ops=ops, e=e):
                    waited = {}
                    nw = 0
                    for op in ops:
                        for d in op.deps:
                            if d.is_dma:
                                sem, val = d.sem, d.semval
                            else:
                                k = (d.sigcount - 1) // SEM_ROT
                                sem = self.eng_sems[d.eng][k]
                                val = (d.sigcount - 1) % SEM_ROT + 1
                            key = sem.num
                            if waited.get(key, 0) >= val:
                                continue
                            waited[key] = val
                            eng.wait_ge(sem, val)
                            nw += 1
                        if op.fn is None:
                            if op.sig:
                                k = (op.sigcount - 1) // SEM_ROT
                                eng.nop().then_inc(self.eng_sems[e][k], 1)
                            continue
                        ins = op.fn(eng)
                        if op.is_dma:
                            ins.then_inc(op.sem, 16)
                        elif op.sig:
                            k = (op.sigcount - 1) // SEM_ROT
                            ins.then_inc(self.eng_sems[e][k], 1)
                    stats[e] = (len(ops), nw)

                getattr(block, engobj[e])(body)
        return stats


class Ring:
    def __init__(self, P, tiles, name, nsub=1):
        self.tiles = tiles
        if nsub == 1:
            self.bufs = P.bufs(len(tiles), name)
        else:
            self.bufs = [P.bufs(nsub, f"{name}{i}_") for i in range(len(tiles))]
        self.i = 0

    def next(self):
        i = self.i
        self.i = (i + 1) % len(self.tiles)
        return self.tiles[i], self.bufs[i]


class Ctx:
    pass


def setup_common(nc, P, dram):
    c = Ctx()
    c.nc, c.P, c.dram = nc, P, dram
    c.hT = nc.alloc_sbuf_tensor("hT", [128, 8, S], F32)
    c.BhT = [[P.buf(f"hT{k}_{t}") for t in range(4)] for k in range(8)]
    c.uT = nc.alloc_sbuf_tensor("uT", [128, 8, S], BF16)
    c.BuT = [P.buf(f"uT{t}") for t in range(4)]
    pst = [nc.alloc_psum_tensor(f"ps{i}", [128, 512], F32) for i in range(8)]
    c.G = Ring(P, pst[0:4], "psG")
    c.A = Ring(P, pst[4:6], "psA")
    c.Dn = Ring(P, pst[6:8], "psD")
    c.cst = nc.alloc_sbuf_tensor("cst", [128, 5 * 128 + 32], F32)
    c.Bcst = P.buf("cst")
    P.dma("sp", c.cst[:, :], dram["consts"], [], [c.Bcst])
    c.cstb = nc.alloc_sbuf_tensor("cstb", [128, 5 * 128 + 32], BF16)
    c.Bcstb = P.buf("cstb")
    P.copy("dve", c.cstb[:, :], c.cst[:, :], [c.Bcst], [c.Bcstb])
    c.ident_f = c.cst[:, 0:128]
    c.ones_f = c.cst[:, 256:384]
    c.ident_b = c.cstb[:, 0:128]
    c.tri_b = c.cstb[:, 128:256]
    c.ones_b = c.cstb[:, 256:384]
    c.normw = nc.alloc_sbuf_tensor("sb_normw", [128, 5, 8], F32)
    c.Bnormw = P.buf("normw")
    P.dma("sp", c.normw[:, :, :], dram["normw"], [], [c.Bnormw])
    c.sq = Ring(P, [nc.alloc_sbuf_tensor(f"sq{i}", [128, 512], F32) for i in range(2)], "sq")
    c.rstd = Ring(P, [nc.alloc_sbuf_tensor(f"rstd{i}", [128, 512], F32) for i in range(2)], "rstd")
    c.wn = Ring(P, [nc.alloc_sbuf_tensor(f"wn{i}", [128, 8, 128], BF16) for i in range(4)], "wn")
    c.ww = Ring(P, [nc.alloc_sbuf_tensor(f"ww{i}", [128, 8, 256], BF16) for i in range(2)], "ww")
    c.wo = Ring(P, [nc.alloc_sbuf_tensor(f"wo{i}", [128, 2, 1024], BF16) for i in range(2)], "wo")
    c.evac_rr = 0
    return c


def rms_stats(c, src, Bsrc_fn, nk, tt, scale_inv_n):
    P = c.P
    ps, Bps = c.G.next()
    for k in range(nk):
        sq, Bsq = c.sq.next()
        P.act(sq[:, :], src[:, k, tt * 512:(tt + 1) * 512], AF.Square, [Bsrc_fn(k, tt)], [Bsq])
        P.mm(ps[:, :], c.ones_f, sq[:, :], k == 0, k == nk - 1, [Bsq, c.Bcst], [Bps])
    r, Br = c.rstd.next()
    P.act(r[:, :], ps[:, :], AF.Sqrt, [Bps], [Br], scale=scale_inv_n, bias=EPS)
    P.recip(r[:, :], r[:, :], [Br], [Br])
    return r, Br


def emit_rmsnorm_u(c, layer):
    P = c.P
    for tt in range(4):
        r, Br = rms_stats(c, c.hT, lambda k, t: c.BhT[k][t], 8, tt, 1.0 / D)
        for k in range(8):
            eng = "dve"
            P.stt(eng, c.uT[:, k, tt * 512:(tt + 1) * 512], c.hT[:, k, tt * 512:(tt + 1) * 512],
                  c.normw[:, layer, k:k + 1], r[:, :], ALU.mult, ALU.mult,
                  [c.BhT[k][tt], Br, c.Bnormw], [c.BuT[tt]])


def load_wn(c, src):
    w, Bw = c.wn.next()
    kc = src.shape[1]
    c.P.dma("pool", w[:, 0:kc, :], src, [], [Bw])
    return w, Bw


def proj_multi(c, wsrcs, evac_multi, rhs=None, Brhs=None, kc=8):
    P = c.P
    ws = []
    for src, m in wsrcs:
        w, Bw = c.wn.next()
        P.dma("pool", w[:, 0:kc, 0:m], src, [], [Bw])
        ws.append((w, Bw, m))
    if rhs is None:
        rhs, Brhs = c.uT, c.BuT
    for tt in range(4):
        pss = []
        for (w, Bw, m) in ws:
            ps, Bps = c.G.next()
            for k in range(kc):
                P.mm(ps[0:m, :], w[:, k, 0:m], rhs[:, k, tt * 512:(tt + 1) * 512], k == 0, k == kc - 1,
                     [Bw, Brhs[tt]], [Bps])
            pss.append((ps, Bps))
        evac_multi(pss, tt)


def proj_fm(c, wsrc, evac, rhs=None, Brhs=None, kc=8):
    proj_multi(c, [(wsrc, 128)], lambda pss, tt: evac(pss[0][0], pss[0][1], tt), rhs, Brhs, kc)


def evac_copy(c, dst, Bdst):
    def f(ps, Bps, tt):
        c.evac_rr += 1
        eng = "act" if c.evac_rr % 2 == 0 else "dve"
        c.P.copy(eng, dst[:, tt * 512:(tt + 1) * 512], ps[:, :], [Bps], [Bdst[tt] if isinstance(Bdst, list) else Bdst])
    return f


def evac_silu(c, dst, Bdst):
    def f(ps, Bps, tt):
        c.P.act(dst[:, tt * 512:(tt + 1) * 512], ps[:, :], AF.Silu, [Bps], [Bdst[tt] if isinstance(Bdst, list) else Bdst])
    return f


def proj_tm(c, wsrc, ncols, evac, lhs=None, Blhs=None, kc=8):
    P = c.P
    w, Bw = c.ww.next()
    P.dma("pool", w[:, 0:kc, 0:ncols], wsrc, [], [Bw])
    if lhs is None:
        lhs, Blhs = c.uT, c.BuT
    for tb in range(16):
        ps, Bps = c.G.next()
        for k in range(kc):
            P.mm(ps[:, 0:ncols], lhs[:, k, tb * 128:(tb + 1) * 128], w[:, k, 0:ncols], k == 0, k == kc - 1,
                 [Bw, Blhs[tb // 4]], [Bps])
        evac(ps, Bps, tb)


def outproj_acc(c, wsrc, gT, BgT, nk):
    P = c.P
    w, Bw = c.wo.next()
    P.dma("pool", w[:, 0:nk, :], wsrc, [], [Bw])
    for oc in range(8):
        for tt in range(4):
            ps, Bps = c.G.next()
            for j in range(nk):
                P.mm(ps[:, :], w[:, j, oc * 128:(oc + 1) * 128], gT[:, j, tt * 512:(tt + 1) * 512],
                     j == 0, j == nk - 1, [Bw, BgT[j][tt] if isinstance(BgT, list) else BgT], [Bps])
            P.tt("dve", c.hT[:, oc, tt * 512:(tt + 1) * 512], ps[:, :], c.hT[:, oc, tt * 512:(tt + 1) * 512],
                 ALU.add, [Bps, c.BhT[oc][tt]], [c.BhT[oc][tt]])


ATTN_DEPTH = 2


def attn_head(c, parts, Bparts, v_fn, Bv, scale, bias_fn, Bbias, gz, Bgz, gT, BgT, L):
    P = c.P
    dq = []

    def pump(limit):
        while dq and (dq[0][0] == "fin" or sum(1 for t in dq if t[0] == "pv") > limit):
            dq.pop(0)[1]()

    for qt in range(4):
        oacc, Bo = c.A.next()
        dacc, Bd = c.Dn.next()
        nkb = 4 * qt + 4
        for kb in range(nkb):
            d = kb - 4 * qt
            c0 = max(d, 0) * 128
            ps, Bps = c.G.next()
            for i, (kT, qT) in enumerate(parts):
                P.mm(ps[:, c0:512], kT[:, kb * 128:(kb + 1) * 128], qT[:, qt * 512 + c0:(qt + 1) * 512],
                     i == 0, i == len(parts) - 1, Bparts, [Bps])
            pt, Bpt0 = L.pt.next()
            if not hasattr(L, "_ptsub"):
                L._ptsub = {}
            if id(Bpt0) not in L._ptsub:
                L._ptsub[id(Bpt0)] = P.bufs(4, "ptsub")
            Bsub = L._ptsub[id(Bpt0)]
            j0 = c0 // 128
            if bias_fn is None:
                P.act(pt[:, c0:512], ps[:, c0:512], AF.Exp, [Bps], Bsub[j0:], scale=scale)
            else:
                for jj in range(j0, 4):
                    P.act(pt[:, jj * 128:(jj + 1) * 128], ps[:, jj * 128:(jj + 1) * 128], AF.Exp,
                          [Bps, Bbias], [Bsub[jj]], scale=scale, bias=bias_fn(kb, 4 * qt + jj))
            if d >= 0:
                P.tt("pool", pt[:, c0:c0 + 128], pt[:, c0:c0 + 128], c.tri_b, ALU.mult, [Bsub[j0], c.Bcstb], [Bsub[j0]])

            def pv(kb=kb, c0=c0, pt=pt, Bpt=Bsub[j0:], oacc=oacc, Bo=Bo, dacc=dacc, Bd=Bd, nkb=nkb):
                P.mm(oacc[:, c0:512], v_fn(kb), pt[:, c0:512], kb == 0, kb == nkb - 1,
                     [Bv[kb] if isinstance(Bv, list) else Bv] + Bpt, [Bo])
                P.mm(dacc[:, c0:512], c.ones_b, pt[:, c0:512], kb == 0, kb == nkb - 1, [c.Bcstb] + Bpt, [Bd])
            dq.append(("pv", pv))
            pump(ATTN_DEPTH)

        def fin(qt=qt, oacc=oacc, Bo=Bo, dacc=dacc, Bd=Bd):
            rd, Brd = L.rden.next()
            P.recip(rd[:, :], dacc[:, :], [Bd], [Brd])
            P.tt("dve", rd[:, :], oacc[:, :], rd[:, :], ALU.mult, [Bo, Brd], [Brd])
            P.tt("pool", gT[:, qt * 512:(qt + 1) * 512], rd[:, :], gz[:, qt * 512:(qt + 1) * 512], ALU.mult,
                 [Brd, Bgz[qt] if isinstance(Bgz, list) else Bgz], [BgT[qt] if isinstance(BgT, list) else BgT])
        dq.append(("fin", fin))
    pump(-1)


def emit_fox(c, layer):
    nc, P, dram = c.nc, c.P, c.dram
    L = Ctx()
    L.pt = Ring(P, [nc.alloc_sbuf_tensor(f"fx_pt{i}", [128, 512], BF16) for i in range(6)], "pt")
    L.rden = Ring(P, [nc.alloc_sbuf_tensor(f"fx_rd{i}", [128, 512], F32) for i in range(2)], "rden")
    qT = nc.alloc_sbuf_tensor("fx_qT", [128, 2, S], BF16); BqT = [P.bufs(4, f"qT{j}_") for j in range(2)]
    kT = nc.alloc_sbuf_tensor("fx_kT", [128, 2, S], BF16); BkT = [P.bufs(4, f"kT{j}_") for j in range(2)]
    gz = nc.alloc_sbuf_tensor("fx_gz", [128, 2, S], BF16); Bgz = [P.bufs(4, f"gz{j}_") for j in range(2)]
    gT = nc.alloc_sbuf_tensor("fx_gT", [128, 2, S], BF16); BgT = [P.bufs(4, f"gT{j}_") for j in range(2)]
    vt = nc.alloc_sbuf_tensor("fx_vt", [128, 16, 256], BF16); Bvt = P.bufs(16, "vt")
    lsp = nc.alloc_sbuf_tensor("fx_lsp", [128, 16, 16], F32); Blsp = P.buf("lsp")
    cT = nc.alloc_sbuf_tensor("fx_cT", [128, 16, 16], F32); BcT = P.buf("cT")
    cref = nc.alloc_sbuf_tensor("fx_cref", [128, 16, 16], F32); Bcref = P.buf("cref")
    fb = nc.alloc_sbuf_tensor("fx_fb", [128, 16], F32); Bfb = P.buf("fb")
    tmp = nc.alloc_sbuf_tensor("fx_tmp", [128, 16], F32); Btmp = P.buf("tmp")
    btab = nc.alloc_sbuf_tensor("fx_btab", [128, 2, 16, 16], F32); Bbt = P.buf("btab")
    negtri_f = c.cst[:, 384:512]
    P.dma("sp", fb[:, :], dram["fox_fb"], [], [Bfb])
    wf_src = dram["fox_wf"]
    scale = 128 ** -0.5

    def evac_f(ps, Bps, tb):
        P.tt("dve", tmp[:, :], ps[:, 0:16], fb[:, :], ALU.add, [Bps, Bfb], [Btmp])
        P.act(tmp[:, :], tmp[:, :], AF.Exp, [Btmp], [Btmp], scale=-1.0)
        P.act(lsp[:, tb, :], tmp[:, :], AF.Ln, [Btmp], [Blsp], bias=1.0)
    proj_tm(c, wf_src, 16, evac_f)
    negones_f = c.cst[:, 512:640]
    for tb in range(16):
        ps, Bps = c.G.next()
        for t2 in range(tb + 1):
            lhs = negtri_f if t2 == tb else negones_f
            P.mm(ps[:, 0:16], lhs, lsp[:, t2, :], t2 == 0, t2 == tb, [c.Bcst, Blsp], [Bps])
        P.copy("dve", cT[:, tb, :], ps[:, 0:16], [Bps], [BcT])
    ps, Bps = c.G.next()
    for tb in range(16):
        rhs = lsp[:, tb:tb + 1, :].to_broadcast([128, 16 - tb, 16])
        P.mm(ps[:, tb * 16:256].rearrange("p (j h) -> p j h", h=16), negones_f, rhs, tb == 0, tb == 15,
             [c.Bcst, Blsp], [Bps])
    P.copy("dve", cref[:, :, :].rearrange("p j h -> p (j h)"), ps[:, 0:256], [Bps], [Bcref])

    for hp in range(8):
        for j in range(2):
            h = 2 * hp + j
            proj_fm(c, dram["fox_wq"][h], evac_copy(c, qT[:, j, :], BqT[j]))
            proj_fm(c, dram["fox_wk"][h], evac_copy(c, kT[:, j, :], BkT[j]))
            proj_fm(c, dram["fox_wz"][h], evac_silu(c, gz[:, j, :], Bgz[j]))
            P.tt("dve", btab[:, j, :, :],
                 cref[:, :, h:h + 1].rearrange("p j o -> p o j").to_broadcast([128, 16, 16]),
                 cT[:, :, h:h + 1].to_broadcast([128, 16, 16]),
                 ALU.subtract, [Bcref, BcT], [Bbt])

        def evac_v(ps, Bps, tb):
            c.evac_rr += 1
            eng = "act" if c.evac_rr % 2 == 0 else "dve"
            P.copy(eng, vt[:, tb, :], ps[:, 0:256], [Bps], [Bvt[tb]])
        proj_tm(c, dram["fox_wv"][hp], 256, evac_v)
        for j in range(2):
            attn_head(c, [(kT[:, j, :], qT[:, j, :])], BkT[j] + BqT[j],
                      lambda kb, j=j: vt[:, kb, j * 128:(j + 1) * 128], Bvt, scale,
                      lambda kb, jq, j=j: btab[:, j, kb, jq:jq + 1], Bbt,
                      gz[:, j, :], Bgz[j], gT[:, j, :], BgT[j], L)
        outproj_acc(c, dram["fox_wo"][hp], gT, BgT, 2)


def rope_combine(c, L, psX, BpsX, psXr, BpsXr, rows, cos_ap, sin_ap, Btab, out_ap, Bout, d=1):
    P = c.P
    t1, Bt1 = L.rt.next()
    t2, Bt2 = L.rt.next()
    P.tt("dve", t1[0:rows, :], psX[0:rows, :], cos_ap, ALU.mult, [BpsX, Btab], [Bt1])
    P.tt("dve", t2[0:rows, :], psXr[0:rows, :], sin_ap, ALU.mult, [BpsXr, Btab], [Bt2])
    a1, a2 = t1[0:rows, :], t2[0:rows, :]
    if d > 1:
        a1 = a1.rearrange("p (n r) -> p n r", r=d)
        a2 = a2.rearrange("p (n r) -> p n r", r=d)
    P.tt("pool", out_ap, a1, a2, ALU.add, [Bt1, Bt2], [Bout])


def normed_proj(c, L, wsrcs, nw_ap, Bnw, dst, Bdst, inv_n):
    P = c.P
    nk = len(wsrcs)

    def ev(pss, tt):
        psS, BpS = c.A.next()
        for k, (ps, Bps) in enumerate(pss):
            sq, Bsq = c.sq.next()
            P.act(sq[:, :], ps[:, :], AF.Square, [Bps], [Bsq])
            P.mm(psS[:, :], c.ones_f, sq[:, :], k == 0, k == nk - 1, [Bsq, c.Bcst], [BpS])
        r, Br = c.rstd.next()
        P.act(r[:, :], psS[:, :], AF.Sqrt, [BpS], [Br], scale=inv_n, bias=EPS)
        P.recip(r[:, :], r[:, :], [Br], [Br])
        for k, (ps, Bps) in enumerate(pss):
            P.stt("dve", dst[:, k, tt * 512:(tt + 1) * 512], ps[:, :], nw_ap[:, k:k + 1], r[:, :], ALU.mult, ALU.mult,
                  [Bps, Br, Bnw], [Bdst[tt]])
    proj_multi(c, [(w, 128) for w in wsrcs], ev)


def emit_mla(c, layer):
    nc, P, dram = c.nc, c.P, c.dram
    L = Ctx()
    L.pt = Ring(P, [nc.alloc_sbuf_tensor(f"ml_pt{i}", [128, 512], BF16) for i in range(5)], "pt")
    L.rt = Ring(P, [nc.alloc_sbuf_tensor(f"ml_rt{i}", [128, 512], F32) for i in range(3)], "rt")
    L.rden = L.rt
    cqn = nc.alloc_sbuf_tensor("ml_cqn", [128, 3, S], BF16); Bcqn = P.bufs(4, "cqn")
    ckvn = nc.alloc_sbuf_tensor("ml_ckvn", [128, 2, S], BF16); Bckvn = P.bufs(4, "ckvn")
    kr = nc.alloc_sbuf_tensor("ml_kr", [128, S], BF16); Bkr = P.buf("kr")
    qn = nc.alloc_sbuf_tensor("ml_qn", [128, S], BF16); Bqn = P.bufs(4, "qn")
    qr = nc.alloc_sbuf_tensor("ml_qr", [128, S], BF16); Bqr = P.bufs(4, "qr")
    kn = nc.alloc_sbuf_tensor("ml_kn", [128, S], BF16); Bkn = P.bufs(4, "kn")
    gz = nc.alloc_sbuf_tensor("ml_gz", [128, 1, S], BF16); Bgz = P.bufs(4, "gz")
    gT = nc.alloc_sbuf_tensor("ml_gT", [128, 1, S], BF16); BgT = P.bufs(4, "gT")
    vt = nc.alloc_sbuf_tensor("ml_vt", [128, 16, 128], BF16); Bvt = P.bufs(16, "vt")
    tab = nc.alloc_sbuf_tensor("ml_tab", [128, 2, S], F32); Btab = P.buf("tab")
    nws = nc.alloc_sbuf_tensor("ml_nws", [128, 5], F32); Bnws = P.buf("nws")
    P.dma("sp", tab[:, 0, :], dram["mla_rope"][0], [], [Btab])
    P.dma("sp", tab[:, 1, :], dram["mla_rope"][1], [], [Btab])
    P.dma("sp", nws[:, :], dram["mla_nw"], [], [Bnws])
    scale = 192 ** -0.5

    normed_proj(c, L, [dram["mla_wcq"][i] for i in range(3)], nws[:, 0:3], Bnws, cqn, Bcqn, 1.0 / 384)
    normed_proj(c, L, [dram["mla_wckv"][i] for i in range(2)], nws[:, 3:5], Bnws, ckvn, Bckvn, 1.0 / 256)

    def ev_kr(pss, tt):
        (pX, BX), (pXr, BXr) = pss
        rope_combine(c, L, pX, BX, pXr, BXr, 64, tab[0:64, 0, tt * 512:(tt + 1) * 512],
                     tab[0:64, 1, tt * 512:(tt + 1) * 512], Btab, kr[0:64, tt * 512:(tt + 1) * 512], Bkr)
    proj_multi(c, [(dram["mla_wkr"], 64), (dram["mla_wkrr"], 64)], ev_kr)

    for h in range(16):
        proj_fm(c, dram["mla_wqn"][h], evac_copy(c, qn, Bqn), cqn, Bcqn, 3)

        def ev_qr(pss, tt):
            (pX, BX), (pXr, BXr) = pss
            rope_combine(c, L, pX, BX, pXr, BXr, 64, tab[0:64, 0, tt * 512:(tt + 1) * 512],
                         tab[0:64, 1, tt * 512:(tt + 1) * 512], Btab, qr[0:64, tt * 512:(tt + 1) * 512], Bqr[tt])
        proj_multi(c, [(dram["mla_wqr"][h], 64), (dram["mla_wqrr"][h], 64)], ev_qr, cqn, Bcqn, 3)
        proj_fm(c, dram["mla_wkn"][h], evac_copy(c, kn, Bkn), ckvn, Bckvn, 2)
        proj_fm(c, dram["mla_wz"][h], evac_silu(c, gz[:, 0, :], Bgz))

        def evac_v(ps, Bps, tb):
            c.evac_rr += 1
            eng = "act" if c.evac_rr % 2 == 0 else "dve"
            P.copy(eng, vt[:, tb, :], ps[:, 0:128], [Bps], [Bvt[tb]])
        proj_tm(c, dram["mla_wv"][h], 128, evac_v, ckvn, Bckvn, 2)
        attn_head(c, [(kn, qn), (kr[0:64, :], qr[0:64, :])], Bkn + Bqn + [Bkr] + Bqr,
                  lambda kb: vt[:, kb, :], Bvt, scale, None, None,
                  gz[:, 0, :], Bgz, gT[:, 0, :], BgT, L)
        outproj_acc(c, dram["mla_wo"][h], gT, [BgT], 1)


DIL_CFG = ((128, 1), (512, 4), (2048, 16))
import os as _os
DIL_GROUPS = [int(x) for x in _os.environ.get('DIL_GROUPS', '0,1,2').split(',')]


def emit_dil(c, layer):
    nc, P, dram = c.nc, c.P, c.dram
    L = Ctx()
    L.pt = Ring(P, [nc.alloc_sbuf_tensor(f"dl_pt{i}", [128, 256], BF16) for i in range(6)], "pt")
    L.xs = Ring(P, [nc.alloc_sbuf_tensor(f"dl_xs{i}", [32, 512], F32) for i in range(3)], "xs")
    L.xr = Ring(P, [nc.alloc_sbuf_tensor(f"dl_xr{i}", [32, 512], F32) for i in range(1)], "xr")
    qTs = [nc.alloc_sbuf_tensor(f"dl_qT{i}", [128, S], BF16) for i in range(2)]
    kTs = [nc.alloc_sbuf_tensor(f"dl_kT{i}", [128, S], BF16) for i in range(2)]
    vts = [nc.alloc_sbuf_tensor(f"dl_vt{i}", [128, 16, 128], BF16) for i in range(2)]
    BqTs = [P.bufs(4, f"qT{i}_") for i in range(2)]
    BkTs = [P.bufs(4, f"kT{i}_") for i in range(2)]
    Bvts = [P.buf(f"vt{i}") for i in range(2)]
    gz = nc.alloc_sbuf_tensor("dl_gz", [128, 1, S], BF16); Bgz = P.buf("gz")
    gT = nc.alloc_sbuf_tensor("dl_gT", [128, 1, S], BF16); BgT = P.buf("gT")
    oN = nc.alloc_sbuf_tensor("dl_oN", [128, S], F32); BoN = P.buf("oN")
    dN = nc.alloc_sbuf_tensor("dl_dN", [128, S], F32); BdN = P.buf("dN")
    tab = nc.alloc_sbuf_tensor("dl_tab", [128, 2, S], F32); Btab = P.buf("tab")
    msk = nc.alloc_sbuf_tensor("dl_msk", [128, 256], BF16); Bmsk = P.buf("msk")
    P.dma("sp", tab[:, 0, :], dram["dil_rope"][0], [], [Btab])
    P.dma("sp", tab[:, 1, :], dram["dil_rope"][1], [], [Btab])
    P.dma("pool", msk[:, :], dram["dil_mask"], [], [Bmsk])
    scale = 128 ** -0.5

    perm_f = c.cst[0:32, 640:672]

    def ev_rope(dst, Bdst):
        pend = []

        def run(task):
            tt, xs, Bxs = task
            sl = slice(tt * 512, (tt + 1) * 512)
            pp, Bpp = c.G.next()
            P.mm(pp[0:32, :], perm_f, xs[0:32, :], True, True, [Bxs, c.Bcst], [Bpp])
            xr, Bxr = L.xr.next()
            P.tt("dve", xr[0:32, :], pp[0:32, :], tab[0:32, 1, sl], ALU.mult, [Bpp, Btab], [Bxr])
            P.tt("dve", xs[0:32, :], xs[0:32, :], tab[0:32, 0, sl], ALU.mult, [Bxs, Btab], [Bxs])
            P.tt("pool", dst[0:32, sl], xs[0:32, :], xr[0:32, :], ALU.add, [Bxs, Bxr], [Bdst[tt]])

        def f(ps, Bps, tt):
            sl = slice(tt * 512, (tt + 1) * 512)
            P.copy("act", dst[:, sl], ps[:, :], [Bps], [Bdst[tt]])
            xs, Bxs = L.xs.next()
            P.copy("act", xs[0:32, :], ps[0:32, :], [Bps], [Bxs])
            pend.append((tt, xs, Bxs))
            if len(pend) > 1:
                run(pend.pop(0))

        def flush():
            while pend:
                run(pend.pop(0))
        return f, flush

    def proj_unit(h, g, bi):
        window, d = DIL_CFG[g]
        nb = (S // d) // 128
        qT, kT, vt = qTs[bi], kTs[bi], vts[bi]
        fq, flq = ev_rope(qT, BqTs[bi])
        proj_fm(c, dram["dil_wq"][g * 8 + h], fq)
        flq()
        fk, flk = ev_rope(kT, BkTs[bi])
        proj_fm(c, dram["dil_wk"][g * 8 + h], fk)
        flk()
        if g == 1:
            proj_fm(c, dram["dil_wz"][h], evac_silu(c, gz[:, 0, :], Bgz))
        wv, Bwv = c.wn.next()
        P.dma("pool", wv[:, :, :], dram["dil_wv"][g * 8 + h], [], [Bwv])
        for r in range(d):
            for kb in range(nb):
                blk = r * nb + kb
                t0 = kb * 128 * d + r
                ps, Bps = c.G.next()
                for k in range(8):
                    lhs = c.uT[:, k, t0:t0 + 127 * d + 1:d]
                    P.mm(ps[:, 0:128], lhs, wv[:, k, :], k == 0, k == 7, [Bwv] + c.BuT, [Bps])
                c.evac_rr += 1
                P.copy("act" if c.evac_rr % 2 == 0 else "dve", vt[:, blk, :], ps[:, 0:128], [Bps], [Bvts[bi]])

    def attn_unit(h, g, bi):
        window, d = DIL_CFG[g]
        nb = (S // d) // 128
        qT, kT, vt = qTs[bi], kTs[bi], vts[bi]
        BqT, BkT, Bvt = BqTs[bi], BkTs[bi], Bvts[bi]
        dq = []

        def pump(limit):
            while dq and (dq[0][0] == "fin" or sum(1 for t in dq if t[0] == "pv") > limit):
                dq.pop(0)[1]()

        for bank in range(4):
            oacc, Bo = c.A.next()
            dacc, Bd = c.Dn.next()
            for qi in range(4):
                blk = bank * 4 + qi
                b = blk % nb
                ps, Bps = c.G.next()
                nk = 2 if b > 0 else 1
                r = blk // nb
                t0 = b * 128 * d + r
                qv = qT[:, t0:t0 + 127 * d + 1:d]
                P.mm(ps[:, 0:128], kT[:, t0:t0 + 127 * d + 1:d], qv, True, True, BkT + BqT, [Bps])
                if b > 0:
                    tp = t0 - 128 * d
                    P.mm(ps[:, 128:256], kT[:, tp:tp + 127 * d + 1:d], qv, True, True, BkT + BqT, [Bps])
                pt, Bpt = L.pt.next()
                P.act(pt[:, 0:128 * nk], ps[:, 0:128 * nk], AF.Exp, [Bps], [Bpt], scale=scale)
                P.tt("pool", pt[:, 0:128 * nk], pt[:, 0:128 * nk], msk[:, 0:128 * nk], ALU.mult, [Bpt, Bmsk], [Bpt])

                def pv(qi=qi, blk=blk, b=b, pt=pt, Bpt=Bpt, oacc=oacc, Bo=Bo, dacc=dacc, Bd=Bd):
                    oc = oacc[:, qi * 128:(qi + 1) * 128]
                    dc = dacc[:, qi * 128:(qi + 1) * 128]
                    P.mm(oc, vt[:, blk, :], pt[:, 0:128], True, b == 0, [Bvt, Bpt], [Bo])
                    if b > 0:
                        P.mm(oc, vt[:, blk - 1, :], pt[:, 128:256], False, True, [Bvt, Bpt], [Bo])
                    P.mm(dc, c.ones_b, pt[:, 0:128], True, b == 0, [c.Bcstb, Bpt], [Bd])
                    if b > 0:
                        P.mm(dc, c.ones_b, pt[:, 128:256], False, True, [c.Bcstb, Bpt], [Bd])
                dq.append(("pv", pv))
                pump(ATTN_DEPTH)

            def fin(bank=bank, oacc=oacc, Bo=Bo, dacc=dacc, Bd=Bd):
                pieces = []
                if d == 1:
                    pieces.append((oN[:, bank * 512:(bank + 1) * 512], dN[:, bank * 512:(bank + 1) * 512], oacc[:, :], dacc[:, :]))
                elif d == 4:
                    pieces.append((oN[:, bank:S:4], dN[:, bank:S:4], oacc[:, :], dacc[:, :]))
                else:
                    for q4 in range(4):
                        r = bank * 4 + q4
                        pieces.append((oN[:, r:S:16], dN[:, r:S:16], oacc[:, q4 * 128:(q4 + 1) * 128], dacc[:, q4 * 128:(q4 + 1) * 128]))
                for (on, dn, oa, da) in pieces:
                    if g == 0:
                        P.copy("act", on, oa, [Bo], [BoN])
                        P.copy("dve", dn, da, [Bd], [BdN])
                    else:
                        P.tt("dve", on, oa, on, ALU.add, [Bo, BoN], [BoN])
                        P.tt("dve", dn, da, dn, ALU.add, [Bd, BdN], [BdN])
            dq.append(("fin", fin))
        pump(-1)

    def fin_head(h):
        for tt in range(4):
            sl = slice(tt * 512, (tt + 1) * 512)
            P.recip(dN[:, sl], dN[:, sl], [BdN], [BdN])
            P.tt("dve", oN[:, sl], oN[:, sl], dN[:, sl], ALU.mult, [BoN, BdN], [BoN])
            P.tt("pool", gT[:, 0, sl], oN[:, sl], gz[:, 0, sl], ALU.mult, [BoN, Bgz], [BgT])
        outproj_acc(c, dram["dil_wo"][h], gT, BgT, 1)

    units = [(h, g) for h in range(8) for g in range(3)]
    proj_unit(units[0][0], units[0][1], 0)
    for i, (h, g) in enumerate(units):
        if i + 1 < len(units):
            proj_unit(units[i + 1][0], units[i + 1][1], (i + 1) % 2)
        attn_unit(h, g, i % 2)
        if g == 2:
            fin_head(h)


def emit_ssd(c, layer):
    nc, P, dram = c.nc, c.P, c.dram
    tri_f = c.cst[:, 128:256]
    negtri_f = c.cst[:, 384:512]
    negones_f = c.cst[:, 512:640]
    dt = nc.alloc_sbuf_tensor("sd_dt", [128, 16, 32], F32); Bdt = P.buf("dt")
    absa = nc.alloc_sbuf_tensor("sd_absa", [128, 16, 32], F32); Babsa = P.buf("absa")
    acum = nc.alloc_sbuf_tensor("sd_acum", [128, 16, 32], F32); Bacum = P.buf("acum")
    wts = nc.alloc_sbuf_tensor("sd_w", [128, 16, 32], F32); Bw = P.buf("w")
    dlast = nc.alloc_sbuf_tensor("sd_dlast", [128, 16, 32], F32); Bdl = P.buf("dlast")
    vecs = nc.alloc_sbuf_tensor("sd_vecs", [128, 3, 32], F32); Bvecs = P.buf("vecs")
    cw = nc.alloc_sbuf_tensor("sd_cw", [128, 32, 5], F32); Bcw = P.buf("cw")
    dsk = nc.alloc_sbuf_tensor("sd_dsk", [128, 16], F32); Bdsk = P.buf("dsk")
    nrm = nc.alloc_sbuf_tensor("sd_nrm", [128, 16], F32); Bnrm = P.buf("nrm")
    tmp32 = nc.alloc_sbuf_tensor("sd_tmp32", [128, 32], F32); Bt32 = P.buf("t32")
    pre = nc.alloc_sbuf_tensor("sd_pre", [128, S + 3], F32); Bpre = P.buf("pre")
    xTf = nc.alloc_sbuf_tensor("sd_xTf", [128, 2, S], F32); BxTf = P.bufs(2, "xTf")
    xTb = nc.alloc_sbuf_tensor("sd_xTb", [128, 2, S], BF16); BxTb = P.bufs(2, "xTb")
    BT = nc.alloc_sbuf_tensor("sd_BT", [128, S], BF16); BBT = P.buf("BT")
    CT = nc.alloc_sbuf_tensor("sd_CT", [128, S], BF16); BCT = P.buf("CT")
    gz = nc.alloc_sbuf_tensor("sd_gz", [128, 2, S], BF16); Bgz = P.buf("gz")
    dmR = Ring(P, [nc.alloc_sbuf_tensor(f"sd_dm{i}", [128, 512], F32) for i in range(2)], "dm4", 4)
    cacc = dmR
    eeR = Ring(P, [nc.alloc_sbuf_tensor(f"sd_ee{i}", [128, 512], F32) for i in range(1)], "ee4")
    mpR = Ring(P, [nc.alloc_sbuf_tensor(f"sd_mp{i}", [128, 512], BF16) for i in range(2)], "mp4", 4)
    csdR = Ring(P, [nc.alloc_sbuf_tensor(f"sd_csd{i}", [128, 512], BF16) for i in range(2)], "csd4", 4)
    xtokR = Ring(P, [nc.alloc_sbuf_tensor(f"sd_xtok{i}", [128, 256], BF16) for i in range(3)], "xtok")
    xwR = Ring(P, [nc.alloc_sbuf_tensor(f"sd_xw{i}", [128, 256], BF16) for i in range(3)], "xw", 4)
    cbmR = Ring(P, [nc.alloc_sbuf_tensor(f"sd_cbm{i}", [128, 128], F32) for i in range(2)], "cbm")
    btokR = Ring(P, [nc.alloc_sbuf_tensor(f"sd_btok{i}", [128, 128], BF16) for i in range(2)], "btok")
    Sf = nc.alloc_sbuf_tensor("sd_Sf", [128, 256], F32); BSf = P.bufs(4, "Sf")
    Sb = nc.alloc_sbuf_tensor("sd_Sb", [128, 256], BF16); BSb = P.buf("Sb")
    P.dma("sp", vecs[:, :, :], dram["ssm_vecs"], [], [Bvecs])
    P.dma("sp", cw[:, :, :], dram["ssm_cw"], [], [Bcw])
    P.dma("sp", dsk[:, :], dram["ssm_dsk"], [], [Bdsk])
    P.dma("sp", nrm[:, :], dram["ssm_nrm"], [], [Bnrm])
    P.memset("pool", pre[:, 0:3], 0.0, [Bpre])
    P.act(vecs[:, 1, :], vecs[:, 1, :], AF.Exp, [Bvecs], [Bvecs])

    def evac_dt(ps, Bps, tb):
        P.tt("dve", tmp32[:, :], ps[:, 0:32], vecs[:, 0, :], ALU.add, [Bps, Bvecs], [Bt32])
        P.act(tmp32[:, :], tmp32[:, :], AF.Exp, [Bt32], [Bt32])
        P.act(dt[:, tb, :], tmp32[:, :], AF.Ln, [Bt32], [Bdt], bias=1.0)
        P.tt("dve", absa[:, tb, :], dt[:, tb, :], vecs[:, 1, :], ALU.mult, [Bdt, Bvecs], [Babsa])
    proj_tm(c, dram["ssm_wdt"], 32, evac_dt)
    for tb in range(16):
        ps, Bps = c.G.next()
        P.mm(ps[:, 0:32], negtri_f, absa[:, tb, :], True, True, [c.Bcst, Babsa], [Bps])
        P.mm(ps[:, 32:64], negones_f, absa[:, tb, :], True, True, [c.Bcst, Babsa], [Bps])
        P.copy("dve", acum[:, tb, :], ps[:, 0:32], [Bps], [Bacum])
        P.copy("dve", dlast[:, tb, :], ps[:, 32:64], [Bps], [Bdl])
        P.tt("dve", wts[:, tb, :], dlast[:, tb, :], acum[:, tb, :], ALU.subtract, [Bdl, Bacum], [Bw])
        P.act(wts[:, tb, :], wts[:, tb, :], AF.Exp, [Bw], [Bw])
        P.tt("dve", wts[:, tb, :], wts[:, tb, :], dt[:, tb, :], ALU.mult, [Bw, Bdt], [Bw])
        P.act(dlast[:, tb, :], dlast[:, tb, :], AF.Exp, [Bdl], [Bdl])

    def conv_silu(ch, outs):
        for tt in range(4):
            a, Ba4 = cacc.next()
            Ba = Ba4[0]
            o = tt * 512
            P.ts("dve", a[:, :], pre[:, o:o + 512], cw[:, ch, 0:1], cw[:, ch, 4:5], ALU.mult, ALU.add, [Bpre, Bcw], [Ba])
            for k in range(1, 4):
                P.stt("dve", a[:, :], pre[:, o + k:o + k + 512], cw[:, ch, k:k + 1], a[:, :], ALU.mult, ALU.add,
                      [Bpre, Bcw, Ba], [Ba])
            for (dst, Bdst) in outs:
                P.act(dst[:, o:o + 512], a[:, :], AF.Silu, [Ba], [Bdst])

    def evac_pre(ps, Bps, tt):
        P.copy("act", pre[:, 3 + tt * 512:3 + (tt + 1) * 512], ps[:, :], [Bps], [Bpre])

    for g in range(8):
        for i in range(2):
            proj_fm(c, dram["ssm_wx"][2 * g + i], evac_pre)
            conv_silu(2 * g + i, [(xTf[:, i, :], BxTf[i]), (xTb[:, i, :], BxTb[i])])
            proj_fm(c, dram["ssm_wz"][2 * g + i], evac_silu(c, gz[:, i, :], Bgz))
        proj_fm(c, dram["ssm_wB"][g], evac_pre)
        conv_silu(16 + g, [(BT, BBT)])
        proj_fm(c, dram["ssm_wC"][g], evac_pre)
        conv_silu(24 + g, [(CT, BCT)])
        dq = []
        for ck in range(16):
            cs = slice(ck * 128, (ck + 1) * 128)
            psT, BpsT = c.G.next()
            for i in range(2):
                P.mm(psT[:, i * 128:(i + 1) * 128], xTb[:, i, cs], c.ident_b, True, True, [BxTb[i], c.Bcstb], [BpsT])
            P.mm(psT[:, 256:384], BT[:, cs], c.ident_b, True, True, [BBT, c.Bcstb], [BpsT])
            pst, Bpst = c.A.next()
            P.mm(pst[:, 256:384], BT[:, cs], CT[:, cs], True, True, [BBT, BCT], [Bpst])
            pb4, Bpb4 = c.G.next()
            for j in range(4):
                h = 4 * g + j
                P.mm(pb4[:, j * 128:(j + 1) * 128], absa[:, ck, h:h + 1].to_broadcast([128, 128]), negtri_f, True, True,
                     [Babsa, c.Bcst], [Bpb4])
            xtok, Bxtok = xtokR.next()
            P.copy("act", xtok[:, :], psT[:, 0:256], [BpsT], [Bxtok])
            btok, Bbtok = btokR.next()
            P.copy("act", btok[:, :], psT[:, 256:384], [BpsT], [Bbtok])
            cbm, Bcbm = cbmR.next()
            P.tt("dve", cbm[:, :], pst[:, 256:384], tri_f, ALU.mult, [Bpst, c.Bcst], [Bcbm])
            dm4, Bdm4 = dmR.next()
            extra = []
            if ck > 0:
                ee4, Bee4 = eeR.next()
                P.act(ee4[:, :], pb4[:, :], AF.Exp, [Bpb4], [Bee4])
                extra = [Bee4]
            for j in range(4):
                h = 4 * g + j
                P.ts("dve", dm4[:, j * 128:(j + 1) * 128], pb4[:, j * 128:(j + 1) * 128], acum[:, ck, h:h + 1], 0.0,
                     ALU.subtract, ALU.min, [Bpb4, Bacum] + extra, [Bdm4[j]])
            P.act(dm4[:, :], dm4[:, :], AF.Exp, Bdm4, Bdm4)
            mp4, Bmp4 = mpR.next()
            for j in range(4):
                h = 4 * g + j
                P.stt("dve", mp4[:, j * 128:(j + 1) * 128], dm4[:, j * 128:(j + 1) * 128], dt[:, ck, h:h + 1], cbm[:, :],
                      ALU.mult, ALU.mult, [Bdm4[j], Bdt, Bcbm], [Bmp4[j]])
            csd4, Bcsd4 = csdR.next()
            if ck > 0:
                for j in range(4):
                    P.tt("pool", csd4[:, j * 128:(j + 1) * 128], ee4[:, j * 128:(j + 1) * 128], CT[:, cs], ALU.mult,
                         [BCT, Bee4], [Bcsd4[j]])
            xw, Bxw = xwR.next()
            for j in range(4):
                P.ts("pool", xw[:, j * 64:(j + 1) * 64], xtok[:, j * 64:(j + 1) * 64],
                     wts[:, ck, 4 * g + j:4 * g + j + 1], None, ALU.mult, None, [Bxtok, Bw], [Bxw[j]])

            def cd(ck=ck, cs=cs, pst=pst, Bpst=Bpst, xtok=xtok, Bxtok=Bxtok, btok=btok, Bbtok=Bbtok, xw=xw, Bxw=Bxw,
                   mp4=mp4, Bmp4=Bmp4, csd4=csd4, Bcsd4=Bcsd4):
                P.mm(pst[:, 0:256], btok[:, :], xw[:, :], True, True, [Bbtok] + Bxw, [Bpst])
                yps, Byps = c.Dn.next()
                for i in range(2):
                    for jj in range(2):
                        j = 2 * i + jj
                        yo = yps[64 * jj:64 * jj + 64, i * 128:(i + 1) * 128]
                        P.mm(yo, xtok[:, j * 64:(j + 1) * 64], mp4[:, j * 128:(j + 1) * 128], True, ck == 0,
                             [Bxtok, Bmp4[j]], [Byps])
                        if ck > 0:
                            P.mm(yo, Sb[:, j * 64:(j + 1) * 64], csd4[:, j * 128:(j + 1) * 128], False, True,
                                 [BSb, Bcsd4[j]], [Byps])
                for i in range(2):
                    P.stt("dve", xTf[:, i, cs], xTf[:, i, cs], dsk[:, 2 * g + i:2 * g + i + 1], yps[:, i * 128:(i + 1) * 128],
                          ALU.mult, ALU.add, [BxTf[i], Bdsk, Byps], [BxTf[i]])
                if ck == 0:
                    P.copy("dve", Sf[:, :], pst[:, 0:256], [Bpst], BSf)
                else:
                    for j in range(4):
                        P.stt("dve", Sf[:, j * 64:(j + 1) * 64], Sf[:, j * 64:(j + 1) * 64],
                              dlast[:, ck, 4 * g + j:4 * g + j + 1], pst[:, j * 64:(j + 1) * 64], ALU.mult, ALU.add,
                              [BSf[j], Bdl, Bpst], [BSf[j]])
                if ck < 15:
                    P.copy("act", Sb[:, :], Sf[:, :], BSf, [BSb])
            dq.append(cd)
            while len(dq) > 1:
                dq.pop(0)()
        while dq:
            dq.pop(0)()
        for tt in range(4):
            sl = slice(tt * 512, (tt + 1) * 512)
            for i in range(2):
                P.tt("pool", xTf[:, i, sl], xTf[:, i, sl], gz[:, i, sl], ALU.mult, [BxTf[i], Bgz], [BxTf[i]])
            r, Br = rms_stats(c, xTf, lambda k, t: BxTf[k], 2, tt, 1.0 / 256)
            for i in range(2):
                P.stt("dve", gz[:, i, sl], xTf[:, i, sl], nrm[:, 2 * g + i:2 * g + i + 1], r[:, :], ALU.mult, ALU.mult,
                      [BxTf[i], Br, Bnrm], [Bgz])
        outproj_acc(c, dram["ssm_wo"][g], gz, Bgz, 2)


def tile_cols(W, width=128):
    K, N = W.shape
    return np.ascontiguousarray(W.reshape(K // 128, 128, N // width, width).transpose(2, 1, 0, 3))


def tile_rows(W, nk):
    R, N = W.shape
    return np.ascontiguousarray(W.reshape(R // (128 * nk), nk, 128, N).transpose(0, 2, 1, 3))


def rope_tables(half, reps):
    inv_freq = (np.float32(ROPE_THETA) ** (-np.arange(half, dtype=np.float32) / np.float32(half))).astype(np.float32)
    ang = np.arange(S, dtype=np.float32)[None, :] * inv_freq[:, None]
    cos = np.cos(ang).astype(np.float32)
    sin = np.sin(ang).astype(np.float32)
    t = np.zeros((2, 128, S), np.float32)
    t[0] = 1.0
    for r in range(reps):
        b = r * 2 * half
        t[0, b:b + half] = cos
        t[0, b + half:b + 2 * half] = cos
        t[1, b:b + half] = -sin
        t[1, b + half:b + 2 * half] = sin
    return t


def make_consts():
    cst = np.zeros((128, 5 * 128 + 32), np.float32)
    for m in range(32):
        cst[(m + 16) % 32, 640 + m] = 1.0
    i = np.arange(128)
    cst[:, 0:128] = np.eye(128)
    cst[:, 128:256] = (i[:, None] <= i[None, :])
    cst[:, 256:384] = 1.0
    cst[:, 384:512] = -(i[:, None] <= i[None, :]).astype(np.float32)
    cst[:, 512:640] = -1.0
    return cst


def host_prep(inputs, layers):
    shared = {}
    shared["consts"] = make_consts()
    nw = np.concatenate([inputs["norm_w"], inputs["final_norm_w"][None]], axis=0)
    shared["normw"] = np.ascontiguousarray(nw.reshape(5, 8, 128).transpose(2, 0, 1))
    if 0 in layers:
        W = inputs["ssm_in_w"][0]
        shared["ssm_wz"] = tile_cols(W[:, 0:2048])
        shared["ssm_wx"] = tile_cols(W[:, 2048:4096])
        shared["ssm_wB"] = tile_cols(W[:, 4096:5120])
        shared["ssm_wC"] = tile_cols(W[:, 5120:6144])
        shared["ssm_wdt"] = tile_cols(W[:, 6144:6176], 32)[0]
        vec = np.stack([inputs["ssm_dt_bias"][0], inputs["ssm_A_log"][0], inputs["ssm_D"][0]], 0)
        shared["ssm_vecs"] = np.ascontiguousarray(np.broadcast_to(vec[None], (128, 3, 32)))
        cwb = np.concatenate([inputs["ssm_conv_w"][0], inputs["ssm_conv_b"][0][None]], 0)
        shared["ssm_cw"] = np.ascontiguousarray(cwb.reshape(5, 32, 128).transpose(2, 1, 0))
        shared["ssm_dsk"] = np.ascontiguousarray(np.repeat(inputs["ssm_D"][0], 64).reshape(16, 128).T)
        shared["ssm_nrm"] = np.ascontiguousarray(inputs["ssm_norm_w"][0].reshape(16, 128).T)
        shared["ssm_wo"] = tile_rows(inputs["ssm_out_w"][0], 2)
    if 1 in layers:
        W = inputs["mla_in_w"][0]
        perm = np.concatenate([np.arange(32, 64), np.arange(0, 32)])
        shared["mla_wcq"] = tile_cols(W[:, 0:384])
        shared["mla_wckv"] = tile_cols(W[:, 384:640])
        shared["mla_wkr"] = tile_cols(W[:, 640:704], 64)[0]
        shared["mla_wkrr"] = tile_cols(W[:, 640:704][:, perm], 64)[0]
        shared["mla_wz"] = tile_cols(W[:, 704:2752])
        UQ = inputs["mla_uq_w"][0].reshape(384, 16, 192)
        shared["mla_wqn"] = tile_cols(np.ascontiguousarray(UQ[:, :, 0:128]).reshape(384, 2048))
        shared["mla_wqr"] = tile_cols(np.ascontiguousarray(UQ[:, :, 128:192]).reshape(384, 1024), 64)
        shared["mla_wqrr"] = tile_cols(np.ascontiguousarray(UQ[:, :, 128:192][:, :, perm]).reshape(384, 1024), 64)
        UKV = inputs["mla_ukv_w"][0].reshape(256, 16, 256)
        shared["mla_wkn"] = tile_cols(np.ascontiguousarray(UKV[:, :, 0:128]).reshape(256, 2048))
        shared["mla_wv"] = tile_cols(np.ascontiguousarray(UKV[:, :, 128:256]).reshape(256, 2048))
        shared["mla_wo"] = tile_rows(inputs["mla_out_w"][0], 1)
        nwq = inputs["mla_q_norm_w"][0].reshape(3, 128).T
        nwkv = inputs["mla_kv_norm_w"][0].reshape(2, 128).T
        shared["mla_nw"] = np.ascontiguousarray(np.concatenate([nwq, nwkv], axis=1))
        shared["mla_rope"] = rope_tables(32, 2)
    if 3 in layers:
        W = inputs["dil_in_w"][0]
        perm = np.arange(128)
        perm[0:16] = np.arange(16, 32)
        perm[16:32] = np.arange(0, 16)
        wq, wk, wv, wqr, wkr = [], [], [], [], []
        for g in range(3):
            base = 3072 * g
            Q = W[:, base:base + 1024].reshape(1024, 8, 128)
            Kw = W[:, base + 1024:base + 2048].reshape(1024, 8, 128)
            wq.append(tile_cols(Q.reshape(1024, 1024)))
            wk.append(tile_cols(Kw.reshape(1024, 1024)))
            wv.append(tile_cols(W[:, base + 2048:base + 3072]))
        shared["dil_wq"] = np.concatenate(wq, 0)
        shared["dil_wk"] = np.concatenate(wk, 0)
        shared["dil_wv"] = np.concatenate(wv, 0)
        shared["dil_wz"] = tile_cols(W[:, 9216:10240])
        shared["dil_wo"] = tile_rows(inputs["dil_out_w"][0], 1)
        shared["dil_rope"] = rope_tables(16, 1)
        i = np.arange(128)
        m = np.zeros((128, 256), np.float32)
        m[:, 0:128] = (i[:, None] <= i[None, :])
        m[:, 128:256] = (i[:, None] >= i[None, :])
        shared["dil_mask"] = m
    if 2 in layers:
        W = inputs["fox_in_w"][0]
        shared["fox_wq"] = tile_cols(W[:, 0:2048])
        shared["fox_wk"] = tile_cols(W[:, 2048:4096])
        shared["fox_wv"] = tile_cols(W[:, 4096:6144], 256)
        shared["fox_wf"] = tile_cols(W[:, 6144:6160], 16)[0]
        shared["fox_wz"] = tile_cols(W[:, 6160:8208])
        shared["fox_fb"] = np.ascontiguousarray(np.broadcast_to(inputs["fox_f_bias"][0][None, :], (128, 16)))
        shared["fox_wo"] = tile_rows(inputs["fox_out_w"][0], 2)
    return shared


def build_program(layers, shared_shapes, final_norm):
    nc = bass.Bass("TRN2", target_bir_lowering=False)
    dram = {}
    for k, shp in shared_shapes.items():
        dram[k] = nc.dram_tensor(k, list(shp), F32, kind="ExternalInput").ap()
    hin = nc.dram_tensor("hin", [NSEQ, 8, 128, S], F32, kind="ExternalInput").ap()
    hout = nc.dram_tensor("hout", [NSEQ, 8, 128, S], F32, kind="ExternalOutput").ap()
    P = Prog(nc)
    c = setup_common(nc, P, dram)
    Bout = P.buf("hout")
    for s in range(NSEQ):
        for k in range(8):
            for t in range(4):
                P.dma("sp", c.hT[:, k, t * 512:(t + 1) * 512], hin[s, k, :, t * 512:(t + 1) * 512], [], [c.BhT[k][t]])
        for layer in layers:
            emit_rmsnorm_u(c, layer)
            emit_layer_cached(c, layer, LAYER_EMITTERS[layer])
        if final_norm:
            for tt in range(4):
                r, Br = rms_stats(c, c.hT, lambda k, t: c.BhT[k][t], 8, tt, 1.0 / D)
                for k in range(8):
                    eng = "dve"
                    P.stt(eng, c.hT[:, k, tt * 512:(tt + 1) * 512], c.hT[:, k, tt * 512:(tt + 1) * 512],
                          c.normw[:, 4, k:k + 1], r[:, :], ALU.mult, ALU.mult,
                          [c.BhT[k][tt], Br, c.Bnormw], [c.BhT[k][tt]])
        for k in range(8):
            for t in range(4):
                P.dma("sp", hout[s, k, :, t * 512:(t + 1) * 512], c.hT[:, k, t * 512:(t + 1) * 512], [c.BhT[k][t]], [Bout])
    P.barrier()
    stats = P.emit()
    return nc, stats


def emit_layer_cached(c, layer, fn):
    from contextlib import ExitStack
    nc = c.nc
    c._scope_id = getattr(c, "_scope_id", 0) + 1
    sid = c._scope_id
    with ExitStack() as st:
        class NCProxy:
            def __getattr__(self, a):
                if a == "alloc_sbuf_tensor":
                    return lambda name, shape, dtype: st.enter_context(nc.sbuf_tensor(f"{name}_s{sid}", shape, dtype))
                return getattr(nc, a)
        c.nc = NCProxy()
        try:
            fn(c, layer)
        finally:
            c.nc = nc
        c.P.barrier()


_CACHE = {}
LAYER_EMITTERS = {0: emit_ssd, 1: emit_mla, 2: emit_fox, 3: emit_dil}


def run_layers(hT_all, inputs, layers, final_norm):
    shared = host_prep(inputs, layers)
    key = (tuple(layers), final_norm)
    if key not in _CACHE:
        _CACHE[key] = build_program(layers, {k: v.shape for k, v in shared.items()}, final_norm)
    nc, stats = _CACHE[key]
    in_maps = []
    for core in range(8):
        m = dict(shared)
        m["hin"] = np.ascontiguousarray(hT_all[core * NSEQ:(core + 1) * NSEQ])
        in_maps.append(m)
    res = run_bass_kernel_spmd(nc, in_maps, core_ids=list(range(8)))
    return np.concatenate([r["hout"] for r in res.results], axis=0)


def to_fm(x):
    B = x.shape[0]
    return np.ascontiguousarray(x.transpose(0, 2, 1).reshape(B, 8, 128, S))


def from_fm(hT):
    B = hT.shape[0]
    return np.ascontiguousarray(hT.reshape(B, D, S).transpose(0, 2, 1))


def kernel(**inputs):
    inputs = {k: np.asarray(v, dtype=np.float32) for k, v in inputs.items()}
    hT = to_fm(inputs["x"])
    hT = run_layers(hT, inputs, [0, 1, 2, 3], True)
    return from_fm(hT)
```

```python
import numpy as np
import concourse.bass as bass
import concourse.mybir as mybir
from concourse.bass_utils import run_bass_kernel_spmd

F32 = mybir.dt.float32
BF16 = mybir.dt.bfloat16
AF = mybir.ActivationFunctionType
ALU = mybir.AluOpType

SAME_ENGINE_SYNC = True
SEM_ROT = 30000
N_DMA_SEMS = 12
S = 2048
D = 1024
NSEQ = 2
EPS = 1e-6
ROPE_THETA = 500000.0


class Buf:
    __slots__ = ("name", "writer", "readers")

    def __init__(self, name):
        self.name = name
        self.writer = None
        self.readers = []


class Op:
    __slots__ = ("eng", "fn", "deps", "sig", "idx", "is_dma", "sem", "semval", "sigcount")

    def __init__(self, eng, fn, is_dma):
        self.eng = eng
        self.fn = fn
        self.deps = []
        self.sig = False
        self.is_dma = is_dma
        self.sem = None
        self.semval = 0
        self.sigcount = 0


class Prog:
    ENGS = ("pe", "act", "dve", "pool", "sp")

    def __init__(self, nc):
        self.nc = nc
        self.ops = {e: [] for e in self.ENGS}
        self.dma_sems = {}
        self.dma_rr = {}
        self.dma_last = {}
        for q in ("sp", "pool"):
            self.dma_sems[q] = [nc.alloc_semaphore(name=f"dq_{q}_{i}") for i in range(N_DMA_SEMS)]
            self.dma_rr[q] = 0
            self.dma_last[q] = [None] * N_DMA_SEMS
        self.dma_cnt = {}
        self.eng_sems = {}
        self.nbuf = 0

    def buf(self, name=None):
        self.nbuf += 1
        return Buf(name or f"b{self.nbuf}")

    def bufs(self, n, name="b"):
        return [self.buf(f"{name}{i}") for i in range(n)]

    def _add(self, eng, fn, reads, writes, is_dma=False):
        op = Op(eng, fn, is_dma)
        deps = []
        for b in reads:
            if b.writer is not None:
                deps.append(b.writer)
        for b in writes:
            if b.writer is not None:
                deps.append(b.writer)
            deps.extend(b.readers)
        if is_dma:
            q = eng
            i = self.dma_rr[q]
            self.dma_rr[q] = (i + 1) % N_DMA_SEMS
            prev = self.dma_last[q][i]
            if prev is not None:
                deps.append(prev)
            op.sem = self.dma_sems[q][i]
            key = (q, i)
            self.dma_cnt[key] = self.dma_cnt.get(key, 0) + 1
            op.semval = 16 * self.dma_cnt[key]
            self.dma_last[q][i] = op
        seen = set()
        for d in deps:
            if d is op or id(d) in seen:
                continue
            seen.add(id(d))
            if (not d.is_dma) and (not is_dma) and d.eng == eng:
                if eng == "pe" or not SAME_ENGINE_SYNC:
                    continue
            op.deps.append(d)
        op.idx = len(self.ops[eng])
        self.ops[eng].append(op)
        for b in reads:
            if not is_dma:
                b.readers = [r for r in b.readers if r.is_dma or r.eng != eng]
            b.readers.append(op)
        for b in writes:
            b.writer = op
            b.readers = []
        return op

    def mm(self, out, lhsT, rhs, start, stop, reads, writes):
        return self._add("pe", lambda e: e.matmul(out, lhsT, rhs, start=start, stop=stop), reads, writes)

    def act(self, out, in_, func, reads, writes, **kw):
        return self._add("act", lambda e: e.activation(out, in_, func, **kw), reads, writes)

    def tt(self, eng, out, in0, in1, op, reads, writes):
        return self._add(eng, lambda e: e.tensor_tensor(out, in0, in1, op), reads, writes)

    def ts(self, eng, out, in0, s1, s2, op0, op1, reads, writes):
        if op1 is None:
            return self._add(eng, lambda e: e.tensor_scalar(out, in0, s1, s2, op0), reads, writes)
        return self._add(eng, lambda e: e.tensor_scalar(out, in0, s1, s2, op0, op1), reads, writes)

    def stt(self, eng, out, in0, scalar, in1, op0, op1, reads, writes):
        return self._add(eng, lambda e: e.scalar_tensor_tensor(out, in0, scalar, in1, op0, op1), reads, writes)

    def copy(self, eng, out, in_, reads, writes):
        if eng == "act":
            return self._add(eng, lambda e: e.copy(out, in_), reads, writes)
        return self._add(eng, lambda e: e.tensor_copy(out, in_), reads, writes)

    def memset(self, eng, ap, val, writes):
        return self._add(eng, lambda e: e.memset(ap, val), [], writes)

    def recip(self, out, in_, reads, writes):
        return self._add("dve", lambda e: e.reciprocal(out, in_), reads, writes)

    def dma(self, q, out, in_, reads, writes):
        return self._add(q, lambda e: e.dma_start(out, in_), reads, writes, is_dma=True)

    def barrier(self):
        last = []
        for e in self.ENGS:
            for op in reversed(self.ops[e]):
                if not op.is_dma:
                    last.append(op)
                    break
        for q in self.dma_last:
            for op in self.dma_last[q]:
                if op is not None:
                    last.append(op)
        for e in self.ENGS:
            op = Op(e, None, False)
            for d in last:
                if d.eng == e and not d.is_dma and e == "pe":
                    continue
                op.deps.append(d)
            op.idx = len(self.ops[e])
            self.ops[e].append(op)

    def emit(self):
        nc = self.nc
        for e in self.ENGS:
            for op in self.ops[e]:
                for d in op.deps:
                    if not d.is_dma:
                        d.sig = True
        for e in self.ENGS:
            c = 0
            for op in self.ops[e]:
                if op.is_dma:
                    continue
                if op.sig:
                    c += 1
                    op.sigcount = c
            nsem = (c + SEM_ROT - 1) // SEM_ROT
            self.eng_sems[e] = [nc.alloc_semaphore(name=f"es_{e}_{i}") for i in range(max(nsem, 1))]
        engobj = {"pe": "tensor", "act": "scalar", "dve": "vector", "pool": "gpsimd", "sp": "sync"}
        stats = {}
        with nc.Block() as block:
            for e in self.ENGS:
                ops = self.ops[e]
                if not ops:
                    continue

                def body(eng, ops=ops, e=e):
                    waited = {}
                    nw = 0
                    for op in ops:
                        for d in op.deps:
                            if d.is_dma:
                                sem, val = d.sem, d.semval
                            else:
                                k = (d.sigcount - 1) // SEM_ROT
                                sem = self.eng_sems[d.eng][k]
                                val = (d.sigcount - 1) % SEM_ROT + 1
                            key = sem.num
                            if waited.get(key, 0) >= val:
                                continue
                            waited[key] = val
                            eng.wait_ge(sem, val)
                            nw += 1
                        if op.fn is None:
                            if op.sig:
                                k = (op.sigcount - 1) // SEM_ROT
                                eng.nop().then_inc(self.eng_sems[e][k], 1)
                            continue
                        ins = op.fn(eng)
                        if op.is_dma:
                            ins.then_inc(op.sem, 16)
                        elif op.sig:
                            k = (op.sigcount - 1) // SEM_ROT
                            ins.then_inc(self.eng_sems[e][k], 1)
                    stats[e] = (len(ops), nw)

                getattr(block, engobj[e])(body)
        return stats


class Ring:
    def __init__(self, P, tiles, name, nsub=1):
        self.tiles = tiles
        if nsub == 1:
            self.bufs = P.bufs(len(tiles), name)
        else:
            self.bufs = [P.bufs(nsub, f"{name}{i}_") for i in range(len(tiles))]
        self.i = 0

    def next(self):
        i = self.i
        self.i = (i + 1) % len(self.tiles)
        return self.tiles[i], self.bufs[i]


class Ctx:
    pass


def setup_common(nc, P, dram):
    c = Ctx()
    c.nc, c.P, c.dram = nc, P, dram
    c.hT = nc.alloc_sbuf_tensor("hT", [128, 8, S], F32)
    c.BhT = [[P.buf(f"hT{k}_{t}") for t in range(4)] for k in range(8)]
    c.uT = nc.alloc_sbuf_tensor("uT", [128, 8, S], BF16)
    c.BuT = [P.buf(f"uT{t}") for t in range(4)]
    pst = [nc.alloc_psum_tensor(f"ps{i}", [128, 512], F32) for i in range(8)]
    c.G = Ring(P, pst[0:4], "psG")
    c.A = Ring(P, pst[4:6], "psA")
    c.Dn = Ring(P, pst[6:8], "psD")
    c.cst = nc.alloc_sbuf_tensor("cst", [128, 5 * 128 + 32], F32)
    c.Bcst = P.buf("cst")
    P.dma("sp", c.cst[:, :], dram["consts"], [], [c.Bcst])
    c.cstb = nc.alloc_sbuf_tensor("cstb", [128, 5 * 128 + 32], BF16)
    c.Bcstb = P.buf("cstb")
    P.copy("dve", c.cstb[:, :], c.cst[:, :], [c.Bcst], [c.Bcstb])
    c.ident_f = c.cst[:, 0:128]
    c.ones_f = c.cst[:, 256:384]
    c.ident_b = c.cstb[:, 0:128]
    c.tri_b = c.cstb[:, 128:256]
    c.ones_b = c.cstb[:, 256:384]
    c.normw = nc.alloc_sbuf_tensor("sb_normw", [128, 5, 8], F32)
    c.Bnormw = P.buf("normw")
    P.dma("sp", c.normw[:, :, :], dram["normw"], [], [c.Bnormw])
    c.sq = Ring(P, [nc.alloc_sbuf_tensor(f"sq{i}", [128, 512], F32) for i in range(2)], "sq")
    c.rstd = Ring(P, [nc.alloc_sbuf_tensor(f"rstd{i}", [128, 512], F32) for i in range(2)], "rstd")
    c.wn = Ring(P, [nc.alloc_sbuf_tensor(f"wn{i}", [128, 8, 128], BF16) for i in range(4)], "wn")
    c.ww = Ring(P, [nc.alloc_sbuf_tensor(f"ww{i}", [128, 8, 256], BF16) for i in range(2)], "ww")
    c.wo = Ring(P, [nc.alloc_sbuf_tensor(f"wo{i}", [128, 2, 1024], BF16) for i in range(2)], "wo")
    c.evac_rr = 0
    return c


def rms_stats(c, src, Bsrc_fn, nk, tt, scale_inv_n):
    P = c.P
    ps, Bps = c.G.next()
    for k in range(nk):
        sq, Bsq = c.sq.next()
        P.act(sq[:, :], src[:, k, tt * 512:(tt + 1) * 512], AF.Square, [Bsrc_fn(k, tt)], [Bsq])
        P.mm(ps[:, :], c.ones_f, sq[:, :], k == 0, k == nk - 1, [Bsq, c.Bcst], [Bps])
    r, Br = c.rstd.next()
    P.act(r[:, :], ps[:, :], AF.Sqrt, [Bps], [Br], scale=scale_inv_n, bias=EPS)
    P.recip(r[:, :], r[:, :], [Br], [Br])
    return r, Br


def emit_rmsnorm_u(c, layer):
    P = c.P
    for tt in range(4):
        r, Br = rms_stats(c, c.hT, lambda k, t: c.BhT[k][t], 8, tt, 1.0 / D)
        for k in range(8):
            eng = "dve"
            P.stt(eng, c.uT[:, k, tt * 512:(tt + 1) * 512], c.hT[:, k, tt * 512:(tt + 1) * 512],
                  c.normw[:, layer, k:k + 1], r[:, :], ALU.mult, ALU.mult,
                  [c.BhT[k][tt], Br, c.Bnormw], [c.BuT[tt]])


def load_wn(c, src):
    w, Bw = c.wn.next()
    kc = src.shape[1]
    c.P.dma("pool", w[:, 0:kc, :], src, [], [Bw])
    return w, Bw


def proj_multi(c, wsrcs, evac_multi, rhs=None, Brhs=None, kc=8):
    P = c.P
    ws = []
    for src, m in wsrcs:
        w, Bw = c.wn.next()
        P.dma("pool", w[:, 0:kc, 0:m], src, [], [Bw])
        ws.append((w, Bw, m))
    if rhs is None:
        rhs, Brhs = c.uT, c.BuT
    for tt in range(4):
        pss = []
        for (w, Bw, m) in ws:
            ps, Bps = c.G.next()
            for k in range(kc):
                P.mm(ps[0:m, :], w[:, k, 0:m], rhs[:, k, tt * 512:(tt + 1) * 512], k == 0, k == kc - 1,
                     [Bw, Brhs[tt]], [Bps])
            pss.append((ps, Bps))
        evac_multi(pss, tt)


def proj_fm(c, wsrc, evac, rhs=None, Brhs=None, kc=8):
    proj_multi(c, [(wsrc, 128)], lambda pss, tt: evac(pss[0][0], pss[0][1], tt), rhs, Brhs, kc)


def evac_copy(c, dst, Bdst):
    def f(ps, Bps, tt):
        c.evac_rr += 1
        eng = "act" if c.evac_rr % 2 == 0 else "dve"
        c.P.copy(eng, dst[:, tt * 512:(tt + 1) * 512], ps[:, :], [Bps], [Bdst[tt] if isinstance(Bdst, list) else Bdst])
    return f


def evac_silu(c, dst, Bdst):
    def f(ps, Bps, tt):
        c.P.act(dst[:, tt * 512:(tt + 1) * 512], ps[:, :], AF.Silu, [Bps], [Bdst[tt] if isinstance(Bdst, list) else Bdst])
    return f


def proj_tm(c, wsrc, ncols, evac, lhs=None, Blhs=None, kc=8):
    P = c.P
    w, Bw = c.ww.next()
    P.dma("pool", w[:, 0:kc, 0:ncols], wsrc, [], [Bw])
    if lhs is None:
        lhs, Blhs = c.uT, c.BuT
    for tb in range(16):
        ps, Bps = c.G.next()
        for k in range(kc):
            P.mm(ps[:, 0:ncols], lhs[:, k, tb * 128:(tb + 1) * 128], w[:, k, 0:ncols], k == 0, k == kc - 1,
                 [Bw, Blhs[tb // 4]], [Bps])
        evac(ps, Bps, tb)


def outproj_acc(c, wsrc, gT, BgT, nk):
    P = c.P
    w, Bw = c.wo.next()
    P.dma("pool", w[:, 0:nk, :], wsrc, [], [Bw])
    for oc in range(8):
        for tt in range(4):
            ps, Bps = c.G.next()
            for j in range(nk):
                P.mm(ps[:, :], w[:, j, oc * 128:(oc + 1) * 128], gT[:, j, tt * 512:(tt + 1) * 512],
                     j == 0, j == nk - 1, [Bw, BgT[j][tt] if isinstance(BgT, list) else BgT], [Bps])
            P.tt("dve", c.hT[:, oc, tt * 512:(tt + 1) * 512], ps[:, :], c.hT[:, oc, tt * 512:(tt + 1) * 512],
                 ALU.add, [Bps, c.BhT[oc][tt]], [c.BhT[oc][tt]])


ATTN_DEPTH = 2


def attn_head(c, parts, Bparts, v_fn, Bv, scale, bias_fn, Bbias, gz, Bgz, gT, BgT, L):
    P = c.P
    dq = []

    def pump(limit):
        while dq and (dq[0][0] == "fin" or sum(1 for t in dq if t[0] == "pv") > limit):
            dq.pop(0)[1]()

    for qt in range(4):
        oacc, Bo = c.A.next()
        dacc, Bd = c.Dn.next()
        nkb = 4 * qt + 4
        for kb in range(nkb):
            d = kb - 4 * qt
            c0 = max(d, 0) * 128
            ps, Bps = c.G.next()
            for i, (kT, qT) in enumerate(parts):
                P.mm(ps[:, c0:512], kT[:, kb * 128:(kb + 1) * 128], qT[:, qt * 512 + c0:(qt + 1) * 512],
                     i == 0, i == len(parts) - 1, Bparts, [Bps])
            pt, Bpt0 = L.pt.next()
            if not hasattr(L, "_ptsub"):
                L._ptsub = {}
            if id(Bpt0) not in L._ptsub:
                L._ptsub[id(Bpt0)] = P.bufs(4, "ptsub")
            Bsub = L._ptsub[id(Bpt0)]
            j0 = c0 // 128
            if bias_fn is None:
                P.act(pt[:, c0:512], ps[:, c0:512], AF.Exp, [Bps], Bsub[j0:], scale=scale)
            else:
                for jj in range(j0, 4):
                    P.act(pt[:, jj * 128:(jj + 1) * 128], ps[:, jj * 128:(jj + 1) * 128], AF.Exp,
                          [Bps, Bbias], [Bsub[jj]], scale=scale, bias=bias_fn(kb, 4 * qt + jj))
            if d >= 0:
                P.tt("pool", pt[:, c0:c0 + 128], pt[:, c0:c0 + 128], c.tri_b, ALU.mult, [Bsub[j0], c.Bcstb], [Bsub[j0]])

            def pv(kb=kb, c0=c0, pt=pt, Bpt=Bsub[j0:], oacc=oacc, Bo=Bo, dacc=dacc, Bd=Bd, nkb=nkb):
                P.mm(oacc[:, c0:512], v_fn(kb), pt[:, c0:512], kb == 0, kb == nkb - 1,
                     [Bv[kb] if isinstance(Bv, list) else Bv] + Bpt, [Bo])
                P.mm(dacc[:, c0:512], c.ones_b, pt[:, c0:512], kb == 0, kb == nkb - 1, [c.Bcstb] + Bpt, [Bd])
            dq.append(("pv", pv))
            pump(ATTN_DEPTH)

        def fin(qt=qt, oacc=oacc, Bo=Bo, dacc=dacc, Bd=Bd):
            rd, Brd = L.rden.next()
            P.recip(rd[:, :], dacc[:, :], [Bd], [Brd])
            P.tt("dve", rd[:, :], oacc[:, :], rd[:, :], ALU.mult, [Bo, Brd], [Brd])
            P.tt("pool", gT[:, qt * 512:(qt + 1) * 512], rd[:, :], gz[:, qt * 512:(qt + 1) * 512], ALU.mult,
                 [Brd, Bgz[qt] if isinstance(Bgz, list) else Bgz], [BgT[qt] if isinstance(BgT, list) else BgT])
        dq.append(("fin", fin))
    pump(-1)


def emit_fox(c, layer):
    nc, P, dram = c.nc, c.P, c.dram
    L = Ctx()
    L.pt = Ring(P, [nc.alloc_sbuf_tensor(f"fx_pt{i}", [128, 512], BF16) for i in range(6)], "pt")
    L.rden = Ring(P, [nc.alloc_sbuf_tensor(f"fx_rd{i}", [128, 512], F32) for i in range(2)], "rden")
    qT = nc.alloc_sbuf_tensor("fx_qT", [128, 2, S], BF16); BqT = [P.bufs(4, f"qT{j}_") for j in range(2)]
    kT = nc.alloc_sbuf_tensor("fx_kT", [128, 2, S], BF16); BkT = [P.bufs(4, f"kT{j}_") for j in range(2)]
    gz = nc.alloc_sbuf_tensor("fx_gz", [128, 2, S], BF16); Bgz = [P.bufs(4, f"gz{j}_") for j in range(2)]
    gT = nc.alloc_sbuf_tensor("fx_gT", [128, 2, S], BF16); BgT = [P.bufs(4, f"gT{j}_") for j in range(2)]
    vt = nc.alloc_sbuf_tensor("fx_vt", [128, 16, 256], BF16); Bvt = P.bufs(16, "vt")
    lsp = nc.alloc_sbuf_tensor("fx_lsp", [128, 16, 16], F32); Blsp = P.buf("lsp")
    cT = nc.alloc_sbuf_tensor("fx_cT", [128, 16, 16], F32); BcT = P.buf("cT")
    cref = nc.alloc_sbuf_tensor("fx_cref", [128, 16, 16], F32); Bcref = P.buf("cref")
    fb = nc.alloc_sbuf_tensor("fx_fb", [128, 16], F32); Bfb = P.buf("fb")
    tmp = nc.alloc_sbuf_tensor("fx_tmp", [128, 16], F32); Btmp = P.buf("tmp")
    btab = nc.alloc_sbuf_tensor("fx_btab", [128, 2, 16, 16], F32); Bbt = P.buf("btab")
    negtri_f = c.cst[:, 384:512]
    P.dma("sp", fb[:, :], dram["fox_fb"], [], [Bfb])
    wf_src = dram["fox_wf"]
    scale = 128 ** -0.5

    def evac_f(ps, Bps, tb):
        P.tt("dve", tmp[:, :], ps[:, 0:16], fb[:, :], ALU.add, [Bps, Bfb], [Btmp])
        P.act(tmp[:, :], tmp[:, :], AF.Exp, [Btmp], [Btmp], scale=-1.0)
        P.act(lsp[:, tb, :], tmp[:, :], AF.Ln, [Btmp], [Blsp], bias=1.0)
    proj_tm(c, wf_src, 16, evac_f)
    negones_f = c.cst[:, 512:640]
    for tb in range(16):
        ps, Bps = c.G.next()
        for t2 in range(tb + 1):
            lhs = negtri_f if t2 == tb else negones_f
            P.mm(ps[:, 0:16], lhs, lsp[:, t2, :], t2 == 0, t2 == tb, [c.Bcst, Blsp], [Bps])
        P.copy("dve", cT[:, tb, :], ps[:, 0:16], [Bps], [BcT])
    ps, Bps = c.G.next()
    for tb in range(16):
        rhs = lsp[:, tb:tb + 1, :].to_broadcast([128, 16 - tb, 16])
        P.mm(ps[:, tb * 16:256].rearrange("p (j h) -> p j h", h=16), negones_f, rhs, tb == 0, tb == 15,
             [c.Bcst, Blsp], [Bps])
    P.copy("dve", cref[:, :, :].rearrange("p j h -> p (j h)"), ps[:, 0:256], [Bps], [Bcref])

    for hp in range(8):
        for j in range(2):
            h = 2 * hp + j
            proj_fm(c, dram["fox_wq"][h], evac_copy(c, qT[:, j, :], BqT[j]))
            proj_fm(c, dram["fox_wk"][h], evac_copy(c, kT[:, j, :], BkT[j]))
            proj_fm(c, dram["fox_wz"][h], evac_silu(c, gz[:, j, :], Bgz[j]))
            P.tt("dve", btab[:, j, :, :],
                 cref[:, :, h:h + 1].rearrange("p j o -> p o j").to_broadcast([128, 16, 16]),
                 cT[:, :, h:h + 1].to_broadcast([128, 16, 16]),
                 ALU.subtract, [Bcref, BcT], [Bbt])

        def evac_v(ps, Bps, tb):
            c.evac_rr += 1
            eng = "act" if c.evac_rr % 2 == 0 else "dve"
            P.copy(eng, vt[:, tb, :], ps[:, 0:256], [Bps], [Bvt[tb]])
        proj_tm(c, dram["fox_wv"][hp], 256, evac_v)
        if hp > 0:
            outproj_acc(c, dram["fox_wo"][hp - 1], gT, BgT, 2)
        for j in range(2):
            attn_head(c, [(kT[:, j, :], qT[:, j, :])], BkT[j] + BqT[j],
                      lambda kb, j=j: vt[:, kb, j * 128:(j + 1) * 128], Bvt, scale,
                      lambda kb, jq, j=j: btab[:, j, kb, jq:jq + 1], Bbt,
                      gz[:, j, :], Bgz[j], gT[:, j, :], BgT[j], L)
    outproj_acc(c, dram["fox_wo"][7], gT, BgT, 2)


def rope_combine(c, L, psX, BpsX, psXr, BpsXr, rows, cos_ap, sin_ap, Btab, out_ap, Bout, d=1):
    P = c.P
    t1, Bt1 = L.rt.next()
    t2, Bt2 = L.rt.next()
    P.tt("dve", t1[0:rows, :], psX[0:rows, :], cos_ap, ALU.mult, [BpsX, Btab], [Bt1])
    P.tt("dve", t2[0:rows, :], psXr[0:rows, :], sin_ap, ALU.mult, [BpsXr, Btab], [Bt2])
    a1, a2 = t1[0:rows, :], t2[0:rows, :]
    if d > 1:
        a1 = a1.rearrange("p (n r) -> p n r", r=d)
        a2 = a2.rearrange("p (n r) -> p n r", r=d)
    P.tt("pool", out_ap, a1, a2, ALU.add, [Bt1, Bt2], [Bout])


def normed_proj(c, L, wsrcs, nw_ap, Bnw, dst, Bdst, inv_n):
    P = c.P
    nk = len(wsrcs)

    def ev(pss, tt):
        psS, BpS = c.A.next()
        for k, (ps, Bps) in enumerate(pss):
            sq, Bsq = c.sq.next()
            P.act(sq[:, :], ps[:, :], AF.Square, [Bps], [Bsq])
            P.mm(psS[:, :], c.ones_f, sq[:, :], k == 0, k == nk - 1, [Bsq, c.Bcst], [BpS])
        r, Br = c.rstd.next()
        P.act(r[:, :], psS[:, :], AF.Sqrt, [BpS], [Br], scale=inv_n, bias=EPS)
        P.recip(r[:, :], r[:, :], [Br], [Br])
        for k, (ps, Bps) in enumerate(pss):
            P.stt("dve", dst[:, k, tt * 512:(tt + 1) * 512], ps[:, :], nw_ap[:, k:k + 1], r[:, :], ALU.mult, ALU.mult,
                  [Bps, Br, Bnw], [Bdst[tt]])
    proj_multi(c, [(w, 128) for w in wsrcs], ev)


def emit_mla(c, layer):
    nc, P, dram = c.nc, c.P, c.dram
    L = Ctx()
    L.pt = Ring(P, [nc.alloc_sbuf_tensor(f"ml_pt{i}", [128, 512], BF16) for i in range(5)], "pt")
    L.rt = Ring(P, [nc.alloc_sbuf_tensor(f"ml_rt{i}", [128, 512], F32) for i in range(3)], "rt")
    L.rden = L.rt
    cqn = nc.alloc_sbuf_tensor("ml_cqn", [128, 3, S], BF16); Bcqn = P.bufs(4, "cqn")
    ckvn = nc.alloc_sbuf_tensor("ml_ckvn", [128, 2, S], BF16); Bckvn = P.bufs(4, "ckvn")
    kr = nc.alloc_sbuf_tensor("ml_kr", [128, S], BF16); Bkr = P.buf("kr")
    qn = nc.alloc_sbuf_tensor("ml_qn", [128, S], BF16); Bqn = P.bufs(4, "qn")
    qr = nc.alloc_sbuf_tensor("ml_qr", [128, S], BF16); Bqr = P.bufs(4, "qr")
    kn = nc.alloc_sbuf_tensor("ml_kn", [128, S], BF16); Bkn = P.bufs(4, "kn")
    gz = nc.alloc_sbuf_tensor("ml_gz", [128, 1, S], BF16); Bgz = P.bufs(4, "gz")
    gT = nc.alloc_sbuf_tensor("ml_gT", [128, 1, S], BF16); BgT = P.bufs(4, "gT")
    vt = nc.alloc_sbuf_tensor("ml_vt", [128, 16, 128], BF16); Bvt = P.bufs(16, "vt")
    tab = nc.alloc_sbuf_tensor("ml_tab", [128, 2, S], F32); Btab = P.buf("tab")
    nws = nc.alloc_sbuf_tensor("ml_nws", [128, 5], F32); Bnws = P.buf("nws")
    P.dma("sp", tab[:, 0, :], dram["mla_rope"][0], [], [Btab])
    P.dma("sp", tab[:, 1, :], dram["mla_rope"][1], [], [Btab])
    P.dma("sp", nws[:, :], dram["mla_nw"], [], [Bnws])
    scale = 192 ** -0.5

    normed_proj(c, L, [dram["mla_wcq"][i] for i in range(3)], nws[:, 0:3], Bnws, cqn, Bcqn, 1.0 / 384)
    normed_proj(c, L, [dram["mla_wckv"][i] for i in range(2)], nws[:, 3:5], Bnws, ckvn, Bckvn, 1.0 / 256)

    def ev_kr(pss, tt):
        (pX, BX), (pXr, BXr) = pss
        rope_combine(c, L, pX, BX, pXr, BXr, 64, tab[0:64, 0, tt * 512:(tt + 1) * 512],
                     tab[0:64, 1, tt * 512:(tt + 1) * 512], Btab, kr[0:64, tt * 512:(tt + 1) * 512], Bkr)
    proj_multi(c, [(dram["mla_wkr"], 64), (dram["mla_wkrr"], 64)], ev_kr)

    for h in range(16):
        proj_fm(c, dram["mla_wqn"][h], evac_copy(c, qn, Bqn), cqn, Bcqn, 3)

        def ev_qr(pss, tt):
            (pX, BX), (pXr, BXr) = pss
            rope_combine(c, L, pX, BX, pXr, BXr, 64, tab[0:64, 0, tt * 512:(tt + 1) * 512],
                         tab[0:64, 1, tt * 512:(tt + 1) * 512], Btab, qr[0:64, tt * 512:(tt + 1) * 512], Bqr[tt])
        proj_multi(c, [(dram["mla_wqr"][h], 64), (dram["mla_wqrr"][h], 64)], ev_qr, cqn, Bcqn, 3)
        proj_fm(c, dram["mla_wkn"][h], evac_copy(c, kn, Bkn), ckvn, Bckvn, 2)
        proj_fm(c, dram["mla_wz"][h], evac_silu(c, gz[:, 0, :], Bgz))

        def evac_v(ps, Bps, tb):
            c.evac_rr += 1
            eng = "act" if c.evac_rr % 2 == 0 else "dve"
            P.copy(eng, vt[:, tb, :], ps[:, 0:128], [Bps], [Bvt[tb]])
        proj_tm(c, dram["mla_wv"][h], 128, evac_v, ckvn, Bckvn, 2)
        if h > 0:
            outproj_acc(c, dram["mla_wo"][h - 1], gT, [BgT], 1)
        attn_head(c, [(kn, qn), (kr[0:64, :], qr[0:64, :])], Bkn + Bqn + [Bkr] + Bqr,
                  lambda kb: vt[:, kb, :], Bvt, scale, None, None,
                  gz[:, 0, :], Bgz, gT[:, 0, :], BgT, L)
    outproj_acc(c, dram["mla_wo"][15], gT, [BgT], 1)


DIL_CFG = ((128, 1), (512, 4), (2048, 16))
import os as _os
DIL_GROUPS = [int(x) for x in _os.environ.get('DIL_GROUPS', '0,1,2').split(',')]


def emit_dil(c, layer):
    nc, P, dram = c.nc, c.P, c.dram
    L = Ctx()
    L.pt = Ring(P, [nc.alloc_sbuf_tensor(f"dl_pt{i}", [128, 256], BF16) for i in range(6)], "pt")
    L.xs = Ring(P, [nc.alloc_sbuf_tensor(f"dl_xs{i}", [32, 512], F32) for i in range(3)], "xs")
    L.xr = Ring(P, [nc.alloc_sbuf_tensor(f"dl_xr{i}", [32, 512], F32) for i in range(1)], "xr")
    qTs = [nc.alloc_sbuf_tensor(f"dl_qT{i}", [128, S], BF16) for i in range(2)]
    kTs = [nc.alloc_sbuf_tensor(f"dl_kT{i}", [128, S], BF16) for i in range(2)]
    vts = [nc.alloc_sbuf_tensor(f"dl_vt{i}", [128, 16, 128], BF16) for i in range(2)]
    BqTs = [P.bufs(4, f"qT{i}_") for i in range(2)]
    BkTs = [P.bufs(4, f"kT{i}_") for i in range(2)]
    Bvts = [P.bufs(16, f"vt{i}_") for i in range(2)]
    gz = nc.alloc_sbuf_tensor("dl_gz", [128, 1, S], BF16); Bgz = P.buf("gz")
    gT = nc.alloc_sbuf_tensor("dl_gT", [128, 1, S], BF16); BgT = P.buf("gT")
    oN = nc.alloc_sbuf_tensor("dl_oN", [128, S], F32); BoNg = [P.bufs(4, f"oN{g}_") for g in range(3)]
    dN = nc.alloc_sbuf_tensor("dl_dN", [128, S], F32); BdNg = [P.bufs(4, f"dN{g}_") for g in range(3)]
    tab = nc.alloc_sbuf_tensor("dl_tab", [128, 2, S], F32); Btab = P.buf("tab")
    msk = nc.alloc_sbuf_tensor("dl_msk", [128, 256], BF16); Bmsk = P.buf("msk")
    P.dma("sp", tab[:, 0, :], dram["dil_rope"][0], [], [Btab])
    P.dma("sp", tab[:, 1, :], dram["dil_rope"][1], [], [Btab])
    P.dma("pool", msk[:, :], dram["dil_mask"], [], [Bmsk])
    scale = 128 ** -0.5

    perm_f = c.cst[0:32, 640:672]

    def ev_rope(dst, Bdst):
        pend = []

        def run(task):
            tt, xs, Bxs = task
            sl = slice(tt * 512, (tt + 1) * 512)
            pp, Bpp = c.G.next()
            P.mm(pp[0:32, :], perm_f, xs[0:32, :], True, True, [Bxs, c.Bcst], [Bpp])
            xr, Bxr = L.xr.next()
            P.tt("dve", xr[0:32, :], pp[0:32, :], tab[0:32, 1, sl], ALU.mult, [Bpp, Btab], [Bxr])
            P.tt("dve", xs[0:32, :], xs[0:32, :], tab[0:32, 0, sl], ALU.mult, [Bxs, Btab], [Bxs])
            P.tt("pool", dst[0:32, sl], xs[0:32, :], xr[0:32, :], ALU.add, [Bxs, Bxr], [Bdst[tt]])

        def f(ps, Bps, tt):
            sl = slice(tt * 512, (tt + 1) * 512)
            P.copy("act", dst[:, sl], ps[:, :], [Bps], [Bdst[tt]])
            xs, Bxs = L.xs.next()
            P.copy("act", xs[0:32, :], ps[0:32, :], [Bps], [Bxs])
            pend.append((tt, xs, Bxs))
            if len(pend) > 1:
                run(pend.pop(0))

        def flush():
            while pend:
                run(pend.pop(0))
        return f, flush

    def proj_unit(h, g, bi):
        window, d = DIL_CFG[g]
        nb = (S // d) // 128
        qT, kT, vt = qTs[bi], kTs[bi], vts[bi]
        fq, flq = ev_rope(qT, BqTs[bi])
        proj_fm(c, dram["dil_wq"][g * 8 + h], fq)
        flq()
        fk, flk = ev_rope(kT, BkTs[bi])
        proj_fm(c, dram["dil_wk"][g * 8 + h], fk)
        flk()
        if g == 1:
            proj_fm(c, dram["dil_wz"][h], evac_silu(c, gz[:, 0, :], Bgz))
        wv, Bwv = c.wn.next()
        P.dma("pool", wv[:, :, :], dram["dil_wv"][g * 8 + h], [], [Bwv])
        for r in range(d):
            for kb in range(nb):
                blk = r * nb + kb
                t0 = kb * 128 * d + r
                ps, Bps = c.G.next()
                for k in range(8):
                    lhs = c.uT[:, k, t0:t0 + 127 * d + 1:d]
                    P.mm(ps[:, 0:128], lhs, wv[:, k, :], k == 0, k == 7, [Bwv] + c.BuT, [Bps])
                c.evac_rr += 1
                P.copy("act" if c.evac_rr % 2 == 0 else "dve", vt[:, blk, :], ps[:, 0:128], [Bps], [Bvts[bi][blk]])

    def attn_unit(h, g, bi):
        window, d = DIL_CFG[g]
        nb = (S // d) // 128
        qT, kT, vt = qTs[bi], kTs[bi], vts[bi]
        BqT, BkT, Bvt = BqTs[bi], BkTs[bi], Bvts[bi]
        dq = []

        def pump(limit):
            while dq and (dq[0][0] == "fin" or sum(1 for t in dq if t[0] == "pv") > limit):
                dq.pop(0)[1]()

        for bank in range(4):
            oacc, Bo = c.A.next()
            dacc, Bd = c.Dn.next()
            for qi in range(4):
                blk = bank * 4 + qi
                b = blk % nb
                ps, Bps = c.G.next()
                nk = 2 if b > 0 else 1
                r = blk // nb
                t0 = b * 128 * d + r
                qv = qT[:, t0:t0 + 127 * d + 1:d]
                P.mm(ps[:, 0:128], kT[:, t0:t0 + 127 * d + 1:d], qv, True, True, BkT + BqT, [Bps])
                if b > 0:
                    tp = t0 - 128 * d
                    P.mm(ps[:, 128:256], kT[:, tp:tp + 127 * d + 1:d], qv, True, True, BkT + BqT, [Bps])
                pt, Bpt = L.pt.next()
                P.act(pt[:, 0:128 * nk], ps[:, 0:128 * nk], AF.Exp, [Bps], [Bpt], scale=scale)
                P.tt("dve", pt[:, 0:128 * nk], pt[:, 0:128 * nk], msk[:, 0:128 * nk], ALU.mult, [Bpt, Bmsk], [Bpt])

                def pv(qi=qi, blk=blk, b=b, pt=pt, Bpt=Bpt, oacc=oacc, Bo=Bo, dacc=dacc, Bd=Bd):
                    oc = oacc[:, qi * 128:(qi + 1) * 128]
                    dc = dacc[:, qi * 128:(qi + 1) * 128]
                    P.mm(oc, vt[:, blk, :], pt[:, 0:128], True, b == 0, [Bvt[blk], Bpt], [Bo])
                    if b > 0:
                        P.mm(oc, vt[:, blk - 1, :], pt[:, 128:256], False, True, [Bvt[blk - 1], Bpt], [Bo])
                    P.mm(dc, c.ones_b, pt[:, 0:128], True, b == 0, [c.Bcstb, Bpt], [Bd])
                    if b > 0:
                        P.mm(dc, c.ones_b, pt[:, 128:256], False, True, [c.Bcstb, Bpt], [Bd])
                dq.append(("pv", pv))
                pump(ATTN_DEPTH)

            def fin(bank=bank, oacc=oacc, Bo=Bo, dacc=dacc, Bd=Bd):
                pieces = []
                if d == 1:
                    pieces.append((oN[:, bank * 512:(bank + 1) * 512], dN[:, bank * 512:(bank + 1) * 512], oacc[:, :], dacc[:, :]))
                elif d == 4:
                    pieces.append((oN[:, bank:S:4], dN[:, bank:S:4], oacc[:, :], dacc[:, :]))
                else:
                    for q4 in range(4):
                        r = bank * 4 + q4
                        pieces.append((oN[:, r:S:16], dN[:, r:S:16], oacc[:, q4 * 128:(q4 + 1) * 128], dacc[:, q4 * 128:(q4 + 1) * 128]))
                for (on, dn, oa, da) in pieces:
                    if g == 0:
                        P.copy("act", on, oa, [Bo], [BoNg[0][bank]])
                        P.copy("dve", dn, da, [Bd], [BdNg[0][bank]])
                    else:
                        P.tt("dve", on, oa, on, ALU.add, [Bo] + BoNg[g - 1], [BoNg[g][bank]])
                        P.tt("dve", dn, da, dn, ALU.add, [Bd] + BdNg[g - 1], [BdNg[g][bank]])
            dq.append(("fin", fin))
        pump(-1)

    def fin_head(h):
        for tt in range(4):
            sl = slice(tt * 512, (tt + 1) * 512)
            allo = BoNg[0] + BoNg[1] + BoNg[2]
            alld = BdNg[0] + BdNg[1] + BdNg[2]
            P.recip(dN[:, sl], dN[:, sl], alld, alld)
            P.tt("dve", oN[:, sl], oN[:, sl], dN[:, sl], ALU.mult, allo + alld, allo)
            P.tt("pool", gT[:, 0, sl], oN[:, sl], gz[:, 0, sl], ALU.mult, allo + [Bgz], [BgT])

    units = [(h, g) for h in range(8) for g in range(3)]
    proj_unit(units[0][0], units[0][1], 0)
    pending_out = None
    for i, (h, g) in enumerate(units):
        if i + 1 < len(units):
            proj_unit(units[i + 1][0], units[i + 1][1], (i + 1) % 2)
        if pending_out is not None:
            outproj_acc(c, dram["dil_wo"][pending_out], gT, BgT, 1)
            pending_out = None
        attn_unit(h, g, i % 2)
        if g == 2:
            fin_head(h)
            pending_out = h
    outproj_acc(c, dram["dil_wo"][pending_out], gT, BgT, 1)


def emit_ssd(c, layer):
    nc, P, dram = c.nc, c.P, c.dram
    tri_f = c.cst[:, 128:256]
    negtri_f = c.cst[:, 384:512]
    negones_f = c.cst[:, 512:640]
    dt = nc.alloc_sbuf_tensor("sd_dt", [128, 16, 32], F32); Bdt = P.buf("dt")
    absa = nc.alloc_sbuf_tensor("sd_absa", [128, 16, 32], F32); Babsa = P.buf("absa")
    acum = nc.alloc_sbuf_tensor("sd_acum", [128, 16, 32], F32); Bacum = P.buf("acum")
    wts = nc.alloc_sbuf_tensor("sd_w", [128, 16, 32], F32); Bw = P.buf("w")
    dlast = nc.alloc_sbuf_tensor("sd_dlast", [128, 16, 32], F32); Bdl = P.buf("dlast")
    vecs = nc.alloc_sbuf_tensor("sd_vecs", [128, 3, 32], F32); Bvecs = P.buf("vecs")
    cw = nc.alloc_sbuf_tensor("sd_cw", [128, 32, 5], F32); Bcw = P.buf("cw")
    dsk = nc.alloc_sbuf_tensor("sd_dsk", [128, 16], F32); Bdsk = P.buf("dsk")
    nrm = nc.alloc_sbuf_tensor("sd_nrm", [128, 16], F32); Bnrm = P.buf("nrm")
    tmp32 = nc.alloc_sbuf_tensor("sd_tmp32", [128, 32], F32); Bt32 = P.buf("t32")
    pre = nc.alloc_sbuf_tensor("sd_pre", [128, S + 3], F32); Bpre = P.bufs(4, "pre"); Bpad = P.buf("prepad")
    xTf = nc.alloc_sbuf_tensor("sd_xTf", [128, 2, S], F32); BxTf = P.bufs(2, "xTf")
    xTb = nc.alloc_sbuf_tensor("sd_xTb", [128, 2, S], BF16); BxTb = P.bufs(2, "xTb")
    BT = nc.alloc_sbuf_tensor("sd_BT", [128, S], BF16); BBT = P.buf("BT")
    CT = nc.alloc_sbuf_tensor("sd_CT", [128, S], BF16); BCT = P.buf("CT")
    gz = nc.alloc_sbuf_tensor("sd_gz", [128, 2, S], BF16); Bgz = P.buf("gz")
    dmR = Ring(P, [nc.alloc_sbuf_tensor(f"sd_dm{i}", [128, 512], F32) for i in range(2)], "dm4", 4)
    cacc = dmR
    eeR = Ring(P, [nc.alloc_sbuf_tensor(f"sd_ee{i}", [128, 512], F32) for i in range(1)], "ee4")
    mpR = Ring(P, [nc.alloc_sbuf_tensor(f"sd_mp{i}", [128, 512], BF16) for i in range(2)], "mp4", 4)
    csdR = Ring(P, [nc.alloc_sbuf_tensor(f"sd_csd{i}", [128, 512], BF16) for i in range(2)], "csd4", 4)
    xtokR = Ring(P, [nc.alloc_sbuf_tensor(f"sd_xtok{i}", [128, 256], BF16) for i in range(3)], "xtok")
    xwR = Ring(P, [nc.alloc_sbuf_tensor(f"sd_xw{i}", [128, 256], BF16) for i in range(3)], "xw", 4)
    cbmR = Ring(P, [nc.alloc_sbuf_tensor(f"sd_cbm{i}", [128, 128], F32) for i in range(2)], "cbm")
    btokR = Ring(P, [nc.alloc_sbuf_tensor(f"sd_btok{i}", [128, 128], BF16) for i in range(2)], "btok")
    Sf = nc.alloc_sbuf_tensor("sd_Sf", [128, 256], F32); BSf = P.bufs(4, "Sf")
    Sb = nc.alloc_sbuf_tensor("sd_Sb", [128, 256], BF16); BSb = P.buf("Sb")
    P.dma("sp", vecs[:, :, :], dram["ssm_vecs"], [], [Bvecs])
    P.dma("sp", cw[:, :, :], dram["ssm_cw"], [], [Bcw])
    P.dma("sp", dsk[:, :], dram["ssm_dsk"], [], [Bdsk])
    P.dma("sp", nrm[:, :], dram["ssm_nrm"], [], [Bnrm])
    P.memset("pool", pre[:, 0:3], 0.0, [Bpad])
    P.act(vecs[:, 1, :], vecs[:, 1, :], AF.Exp, [Bvecs], [Bvecs])

    def evac_dt(ps, Bps, tb):
        P.tt("dve", tmp32[:, :], ps[:, 0:32], vecs[:, 0, :], ALU.add, [Bps, Bvecs], [Bt32])
        P.act(tmp32[:, :], tmp32[:, :], AF.Exp, [Bt32], [Bt32])
        P.act(dt[:, tb, :], tmp32[:, :], AF.Ln, [Bt32], [Bdt], bias=1.0)
        P.tt("dve", absa[:, tb, :], dt[:, tb, :], vecs[:, 1, :], ALU.mult, [Bdt, Bvecs], [Babsa])
    proj_tm(c, dram["ssm_wdt"], 32, evac_dt)
    for tb in range(16):
        ps, Bps = c.G.next()
        P.mm(ps[:, 0:32], negtri_f, absa[:, tb, :], True, True, [c.Bcst, Babsa], [Bps])
        P.mm(ps[:, 32:64], negones_f, absa[:, tb, :], True, True, [c.Bcst, Babsa], [Bps])
        P.copy("dve", acum[:, tb, :], ps[:, 0:32], [Bps], [Bacum])
        P.copy("dve", dlast[:, tb, :], ps[:, 32:64], [Bps], [Bdl])
        P.tt("dve", wts[:, tb, :], dlast[:, tb, :], acum[:, tb, :], ALU.subtract, [Bdl, Bacum], [Bw])
        P.act(wts[:, tb, :], wts[:, tb, :], AF.Exp, [Bw], [Bw])
        P.tt("dve", wts[:, tb, :], wts[:, tb, :], dt[:, tb, :], ALU.mult, [Bw, Bdt], [Bw])
        P.act(dlast[:, tb, :], dlast[:, tb, :], AF.Exp, [Bdl], [Bdl])

    def conv_silu(ch, outs):
        for tt in range(4):
            a, Ba4 = cacc.next()
            Ba = Ba4[0]
            o = tt * 512
            Bp = [Bpre[tt], Bpre[tt - 1] if tt > 0 else Bpad]
            P.ts("dve", a[:, :], pre[:, o:o + 512], cw[:, ch, 0:1], cw[:, ch, 4:5], ALU.mult, ALU.add, Bp + [Bcw], [Ba])
            for k in range(1, 4):
                P.stt("dve", a[:, :], pre[:, o + k:o + k + 512], cw[:, ch, k:k + 1], a[:, :], ALU.mult, ALU.add,
                      Bp + [Bcw, Ba], [Ba])
            for (dst, Bdst) in outs:
                P.act(dst[:, o:o + 512], a[:, :], AF.Silu, [Ba], [Bdst])

    def evac_pre(ps, Bps, tt):
        P.copy("act", pre[:, 3 + tt * 512:3 + (tt + 1) * 512], ps[:, :], [Bps], [Bpre[tt]])

    for g in range(8):
        for i in range(2):
            proj_fm(c, dram["ssm_wx"][2 * g + i], evac_pre)
            conv_silu(2 * g + i, [(xTf[:, i, :], BxTf[i]), (xTb[:, i, :], BxTb[i])])
            proj_fm(c, dram["ssm_wz"][2 * g + i], evac_silu(c, gz[:, i, :], Bgz))
        proj_fm(c, dram["ssm_wB"][g], evac_pre)
        conv_silu(16 + g, [(BT, BBT)])
        proj_fm(c, dram["ssm_wC"][g], evac_pre)
        conv_silu(24 + g, [(CT, BCT)])
        dq = []
        for ck in range(16):
            cs = slice(ck * 128, (ck + 1) * 128)
            psT, BpsT = c.G.next()
            for i in range(2):
                P.mm(psT[:, i * 128:(i + 1) * 128], xTb[:, i, cs], c.ident_b, True, True, [BxTb[i], c.Bcstb], [BpsT])
            P.mm(psT[:, 256:384], BT[:, cs], c.ident_b, True, True, [BBT, c.Bcstb], [BpsT])
            pst, Bpst = c.A.next()
            P.mm(pst[:, 256:384], BT[:, cs], CT[:, cs], True, True, [BBT, BCT], [Bpst])
            pb4, Bpb4 = c.G.next()
            for j in range(4):
                h = 4 * g + j
                P.mm(pb4[:, j * 128:(j + 1) * 128], absa[:, ck, h:h + 1].to_broadcast([128, 128]), negtri_f, True, True,
                     [Babsa, c.Bcst], [Bpb4])
            xtok, Bxtok = xtokR.next()
            P.copy("act", xtok[:, :], psT[:, 0:256], [BpsT], [Bxtok])
            btok, Bbtok = btokR.next()
            P.copy("act", btok[:, :], psT[:, 256:384], [BpsT], [Bbtok])
            cbm, Bcbm = cbmR.next()
            P.tt("dve", cbm[:, :], pst[:, 256:384], tri_f, ALU.mult, [Bpst, c.Bcst], [Bcbm])
            dm4, Bdm4 = dmR.next()
            extra = []
            if ck > 0:
                ee4, Bee4 = eeR.next()
                P.act(ee4[:, :], pb4[:, :], AF.Exp, [Bpb4], [Bee4])
                extra = [Bee4]
            for j in range(4):
                h = 4 * g + j
                P.ts("dve", dm4[:, j * 128:(j + 1) * 128], pb4[:, j * 128:(j + 1) * 128], acum[:, ck, h:h + 1], 0.0,
                     ALU.subtract, ALU.min, [Bpb4, Bacum] + extra, [Bdm4[j]])
            P.act(dm4[:, :], dm4[:, :], AF.Exp, Bdm4, Bdm4)
            mp4, Bmp4 = mpR.next()
            for j in range(4):
                h = 4 * g + j
                P.stt("dve", mp4[:, j * 128:(j + 1) * 128], dm4[:, j * 128:(j + 1) * 128], dt[:, ck, h:h + 1], cbm[:, :],
                      ALU.mult, ALU.mult, [Bdm4[j], Bdt, Bcbm], [Bmp4[j]])
            csd4, Bcsd4 = csdR.next()
            if ck > 0:
                for j in range(4):
                    P.tt("pool", csd4[:, j * 128:(j + 1) * 128], ee4[:, j * 128:(j + 1) * 128], CT[:, cs], ALU.mult,
                         [BCT, Bee4], [Bcsd4[j]])
            xw, Bxw = xwR.next()
            for j in range(4):
                P.ts("pool", xw[:, j * 64:(j + 1) * 64], xtok[:, j * 64:(j + 1) * 64],
                     wts[:, ck, 4 * g + j:4 * g + j + 1], None, ALU.mult, None, [Bxtok, Bw], [Bxw[j]])

            def cd(ck=ck, cs=cs, pst=pst, Bpst=Bpst, xtok=xtok, Bxtok=Bxtok, btok=btok, Bbtok=Bbtok, xw=xw, Bxw=Bxw,
                   mp4=mp4, Bmp4=Bmp4, csd4=csd4, Bcsd4=Bcsd4):
                P.mm(pst[:, 0:256], btok[:, :], xw[:, :], True, True, [Bbtok] + Bxw, [Bpst])
                yps, Byps = c.Dn.next()
                for i in range(2):
                    for jj in range(2):
                        j = 2 * i + jj
                        yo = yps[64 * jj:64 * jj + 64, i * 128:(i + 1) * 128]
                        P.mm(yo, xtok[:, j * 64:(j + 1) * 64], mp4[:, j * 128:(j + 1) * 128], True, ck == 0,
                             [Bxtok, Bmp4[j]], [Byps])
                        if ck > 0:
                            P.mm(yo, Sb[:, j * 64:(j + 1) * 64], csd4[:, j * 128:(j + 1) * 128], False, True,
                                 [BSb, Bcsd4[j]], [Byps])
                for i in range(2):
                    P.stt("dve", xTf[:, i, cs], xTf[:, i, cs], dsk[:, 2 * g + i:2 * g + i + 1], yps[:, i * 128:(i + 1) * 128],
                          ALU.mult, ALU.add, [BxTf[i], Bdsk, Byps], [BxTf[i]])
                if ck == 0:
                    P.copy("dve", Sf[:, :], pst[:, 0:256], [Bpst], BSf)
                else:
                    for j in range(4):
                        P.stt("dve", Sf[:, j * 64:(j + 1) * 64], Sf[:, j * 64:(j + 1) * 64],
                              dlast[:, ck, 4 * g + j:4 * g + j + 1], pst[:, j * 64:(j + 1) * 64], ALU.mult, ALU.add,
                              [BSf[j], Bdl, Bpst], [BSf[j]])
                if ck < 15:
                    P.copy("act", Sb[:, :], Sf[:, :], BSf, [BSb])
            dq.append(cd)
            while len(dq) > 1:
                dq.pop(0)()
        while dq:
            dq.pop(0)()
        for tt in range(4):
            sl = slice(tt * 512, (tt + 1) * 512)
            for i in range(2):
                P.tt("pool", xTf[:, i, sl], xTf[:, i, sl], gz[:, i, sl], ALU.mult, [BxTf[i], Bgz], [BxTf[i]])
            r, Br = rms_stats(c, xTf, lambda k, t: BxTf[k], 2, tt, 1.0 / 256)
            for i in range(2):
                P.stt("dve", gz[:, i, sl], xTf[:, i, sl], nrm[:, 2 * g + i:2 * g + i + 1], r[:, :], ALU.mult, ALU.mult,
                      [BxTf[i], Br, Bnrm], [Bgz])
        outproj_acc(c, dram["ssm_wo"][g], gz, Bgz, 2)


def tile_cols(W, width=128):
    K, N = W.shape
    return np.ascontiguousarray(W.reshape(K // 128, 128, N // width, width).transpose(2, 1, 0, 3))


def tile_rows(W, nk):
    R, N = W.shape
    return np.ascontiguousarray(W.reshape(R // (128 * nk), nk, 128, N).transpose(0, 2, 1, 3))


def rope_tables(half, reps):
    inv_freq = (np.float32(ROPE_THETA) ** (-np.arange(half, dtype=np.float32) / np.float32(half))).astype(np.float32)
    ang = np.arange(S, dtype=np.float32)[None, :] * inv_freq[:, None]
    cos = np.cos(ang).astype(np.float32)
    sin = np.sin(ang).astype(np.float32)
    t = np.zeros((2, 128, S), np.float32)
    t[0] = 1.0
    for r in range(reps):
        b = r * 2 * half
        t[0, b:b + half] = cos
        t[0, b + half:b + 2 * half] = cos
        t[1, b:b + half] = -sin
        t[1, b + half:b + 2 * half] = sin
    return t


def make_consts():
    cst = np.zeros((128, 5 * 128 + 32), np.float32)
    for m in range(32):
        cst[(m + 16) % 32, 640 + m] = 1.0
    i = np.arange(128)
    cst[:, 0:128] = np.eye(128)
    cst[:, 128:256] = (i[:, None] <= i[None, :])
    cst[:, 256:384] = 1.0
    cst[:, 384:512] = -(i[:, None] <= i[None, :]).astype(np.float32)
    cst[:, 512:640] = -1.0
    return cst


def host_prep(inputs, layers):
    shared = {}
    shared["consts"] = make_consts()
    nw = np.concatenate([inputs["norm_w"], inputs["final_norm_w"][None]], axis=0)
    shared["normw"] = np.ascontiguousarray(nw.reshape(5, 8, 128).transpose(2, 0, 1))
    if 0 in layers:
        W = inputs["ssm_in_w"][0]
        shared["ssm_wz"] = tile_cols(W[:, 0:2048])
        shared["ssm_wx"] = tile_cols(W[:, 2048:4096])
        shared["ssm_wB"] = tile_cols(W[:, 4096:5120])
        shared["ssm_wC"] = tile_cols(W[:, 5120:6144])
        shared["ssm_wdt"] = tile_cols(W[:, 6144:6176], 32)[0]
        vec = np.stack([inputs["ssm_dt_bias"][0], inputs["ssm_A_log"][0], inputs["ssm_D"][0]], 0)
        shared["ssm_vecs"] = np.ascontiguousarray(np.broadcast_to(vec[None], (128, 3, 32)))
        cwb = np.concatenate([inputs["ssm_conv_w"][0], inputs["ssm_conv_b"][0][None]], 0)
        shared["ssm_cw"] = np.ascontiguousarray(cwb.reshape(5, 32, 128).transpose(2, 1, 0))
        shared["ssm_dsk"] = np.ascontiguousarray(np.repeat(inputs["ssm_D"][0], 64).reshape(16, 128).T)
        shared["ssm_nrm"] = np.ascontiguousarray(inputs["ssm_norm_w"][0].reshape(16, 128).T)
        shared["ssm_wo"] = tile_rows(inputs["ssm_out_w"][0], 2)
    if 1 in layers:
        W = inputs["mla_in_w"][0]
        perm = np.concatenate([np.arange(32, 64), np.arange(0, 32)])
        shared["mla_wcq"] = tile_cols(W[:, 0:384])
        shared["mla_wckv"] = tile_cols(W[:, 384:640])
        shared["mla_wkr"] = tile_cols(W[:, 640:704], 64)[0]
        shared["mla_wkrr"] = tile_cols(W[:, 640:704][:, perm], 64)[0]
        shared["mla_wz"] = tile_cols(W[:, 704:2752])
        UQ = inputs["mla_uq_w"][0].reshape(384, 16, 192)
        shared["mla_wqn"] = tile_cols(np.ascontiguousarray(UQ[:, :, 0:128]).reshape(384, 2048))
        shared["mla_wqr"] = tile_cols(np.ascontiguousarray(UQ[:, :, 128:192]).reshape(384, 1024), 64)
        shared["mla_wqrr"] = tile_cols(np.ascontiguousarray(UQ[:, :, 128:192][:, :, perm]).reshape(384, 1024), 64)
        UKV = inputs["mla_ukv_w"][0].reshape(256, 16, 256)
        shared["mla_wkn"] = tile_cols(np.ascontiguousarray(UKV[:, :, 0:128]).reshape(256, 2048))
        shared["mla_wv"] = tile_cols(np.ascontiguousarray(UKV[:, :, 128:256]).reshape(256, 2048))
        shared["mla_wo"] = tile_rows(inputs["mla_out_w"][0], 1)
        nwq = inputs["mla_q_norm_w"][0].reshape(3, 128).T
        nwkv = inputs["mla_kv_norm_w"][0].reshape(2, 128).T
        shared["mla_nw"] = np.ascontiguousarray(np.concatenate([nwq, nwkv], axis=1))
        shared["mla_rope"] = rope_tables(32, 2)
    if 3 in layers:
        W = inputs["dil_in_w"][0]
        perm = np.arange(128)
        perm[0:16] = np.arange(16, 32)
        perm[16:32] = np.arange(0, 16)
        wq, wk, wv, wqr, wkr = [], [], [], [], []
        for g in range(3):
            base = 3072 * g
            Q = W[:, base:base + 1024].reshape(1024, 8, 128)
            Kw = W[:, base + 1024:base + 2048].reshape(1024, 8, 128)
            wq.append(tile_cols(Q.reshape(1024, 1024)))
            wk.append(tile_cols(Kw.reshape(1024, 1024)))
            wv.append(tile_cols(W[:, base + 2048:base + 3072]))
        shared["dil_wq"] = np.concatenate(wq, 0)
        shared["dil_wk"] = np.concatenate(wk, 0)
        shared["dil_wv"] = np.concatenate(wv, 0)
        shared["dil_wz"] = tile_cols(W[:, 9216:10240])
        shared["dil_wo"] = tile_rows(inputs["dil_out_w"][0], 1)
        shared["dil_rope"] = rope_tables(16, 1)
        i = np.arange(128)
        m = np.zeros((128, 256), np.float32)
        m[:, 0:128] = (i[:, None] <= i[None, :])
        m[:, 128:256] = (i[:, None] >= i[None, :])
        shared["dil_mask"] = m
    if 2 in layers:
        W = inputs["fox_in_w"][0]
        shared["fox_wq"] = tile_cols(W[:, 0:2048])
        shared["fox_wk"] = tile_cols(W[:, 2048:4096])
        shared["fox_wv"] = tile_cols(W[:, 4096:6144], 256)
        shared["fox_wf"] = tile_cols(W[:, 6144:6160], 16)[0]
        shared["fox_wz"] = tile_cols(W[:, 6160:8208])
        shared["fox_fb"] = np.ascontiguousarray(np.broadcast_to(inputs["fox_f_bias"][0][None, :], (128, 16)))
        shared["fox_wo"] = tile_rows(inputs["fox_out_w"][0], 2)
    return shared


def build_program(layers, shared_shapes, final_norm):
    nc = bass.Bass("TRN2", target_bir_lowering=False)
    dram = {}
    for k, shp in shared_shapes.items():
        dram[k] = nc.dram_tensor(k, list(shp), F32, kind="ExternalInput").ap()
    hin = nc.dram_tensor("hin", [NSEQ, 8, 128, S], F32, kind="ExternalInput").ap()
    hout = nc.dram_tensor("hout", [NSEQ, 8, 128, S], F32, kind="ExternalOutput").ap()
    P = Prog(nc)
    c = setup_common(nc, P, dram)
    Bout = P.buf("hout")
    for s in range(NSEQ):
        for k in range(8):
            for t in range(4):
                P.dma("sp", c.hT[:, k, t * 512:(t + 1) * 512], hin[s, k, :, t * 512:(t + 1) * 512], [], [c.BhT[k][t]])
        for layer in layers:
            emit_rmsnorm_u(c, layer)
            emit_layer_cached(c, layer, LAYER_EMITTERS[layer])
        if final_norm:
            for tt in range(4):
                r, Br = rms_stats(c, c.hT, lambda k, t: c.BhT[k][t], 8, tt, 1.0 / D)
                for k in range(8):
                    eng = "dve"
                    P.stt(eng, c.hT[:, k, tt * 512:(tt + 1) * 512], c.hT[:, k, tt * 512:(tt + 1) * 512],
                          c.normw[:, 4, k:k + 1], r[:, :], ALU.mult, ALU.mult,
                          [c.BhT[k][tt], Br, c.Bnormw], [c.BhT[k][tt]])
        for k in range(8):
            for t in range(4):
                P.dma("sp", hout[s, k, :, t * 512:(t + 1) * 512], c.hT[:, k, t * 512:(t + 1) * 512], [c.BhT[k][t]], [Bout])
    P.barrier()
    stats = P.emit()
    return nc, stats


def emit_layer_cached(c, layer, fn):
    from contextlib import ExitStack
    nc = c.nc
    c._scope_id = getattr(c, "_scope_id", 0) + 1
    sid = c._scope_id
    with ExitStack() as st:
        class NCProxy:
            def __getattr__(self, a):
                if a == "alloc_sbuf_tensor":
                    return lambda name, shape, dtype: st.enter_context(nc.sbuf_tensor(f"{name}_s{sid}", shape, dtype))
                return getattr(nc, a)
        c.nc = NCProxy()
        try:
            fn(c, layer)
        finally:
            c.nc = nc
        c.P.barrier()


_CACHE = {}
LAYER_EMITTERS = {0: emit_ssd, 1: emit_mla, 2: emit_fox, 3: emit_dil}


def run_layers(hT_all, inputs, layers, final_norm):
    shared = host_prep(inputs, layers)
    key = (tuple(layers), final_norm)
    if key not in _CACHE:
        _CACHE[key] = build_program(layers, {k: v.shape for k, v in shared.items()}, final_norm)
    nc, stats = _CACHE[key]
    in_maps = []
    for core in range(8):
        m = dict(shared)
        m["hin"] = np.ascontiguousarray(hT_all[core * NSEQ:(core + 1) * NSEQ])
        in_maps.append(m)
    res = run_bass_kernel_spmd(nc, in_maps, core_ids=list(range(8)))
    return np.concatenate([r["hout"] for r in res.results], axis=0)


def to_fm(x):
    B = x.shape[0]
    return np.ascontiguousarray(x.transpose(0, 2, 1).reshape(B, 8, 128, S))


def from_fm(hT):
    B = hT.shape[0]
    return np.ascontiguousarray(hT.reshape(B, D, S).transpose(0, 2, 1))


def kernel(**inputs):
    inputs = {k: np.asarray(v, dtype=np.float32) for k, v in inputs.items()}
    hT = to_fm(inputs["x"])
    hT = run_layers(hT, inputs, [0, 1, 2, 3], True)
    return from_fm(hT)
```

```python
import numpy as np
import concourse.bass as bass
import concourse.mybir as mybir
from concourse.bass_utils import run_bass_kernel_spmd

F32 = mybir.dt.float32
BF16 = mybir.dt.bfloat16
AF = mybir.ActivationFunctionType
ALU = mybir.AluOpType

SAME_ENGINE_SYNC = True
SEM_ROT = 30000
N_DMA_SEMS = 12
S = 2048
D = 1024
NSEQ = 2
EPS = 1e-6
ROPE_THETA = 500000.0


class Buf:
    __slots__ = ("name", "writer", "readers")

    def __init__(self, name):
        self.name = name
        self.writer = None
        self.readers = []


class Op:
    __slots__ = ("eng", "fn", "deps", "sig", "idx", "is_dma", "sem", "semval", "sigcount")

    def __init__(self, eng, fn, is_dma):
        self.eng = eng
        self.fn = fn
        self.deps = []
        self.sig = False
        self.is_dma = is_dma
        self.sem = None
        self.semval = 0
        self.sigcount = 0


class Prog:
    ENGS = ("pe", "act", "dve", "pool", "sp")

    def __init__(self, nc):
        self.nc = nc
        self.ops = {e: [] for e in self.ENGS}
        self.dma_sems = {}
        self.dma_rr = {}
        self.dma_last = {}
        for q in ("sp", "pool"):
            self.dma_sems[q] = [nc.alloc_semaphore(name=f"dq_{q}_{i}") for i in range(N_DMA_SEMS)]
            self.dma_rr[q] = 0
            self.dma_last[q] = [None] * N_DMA_SEMS
        self.dma_cnt = {}
        self.eng_sems = {}
        self.nbuf = 0

    def buf(self, name=None):
        self.nbuf += 1
        return Buf(name or f"b{self.nbuf}")

    def bufs(self, n, name="b"):
        return [self.buf(f"{name}{i}") for i in range(n)]

    def _add(self, eng, fn, reads, writes, is_dma=False):
        op = Op(eng, fn, is_dma)
        deps = []
        for b in reads:
            if b.writer is not None:
                deps.append(b.writer)
        for b in writes:
            if b.writer is not None:
                deps.append(b.writer)
            deps.extend(b.readers)
        if is_dma:
            q = eng
            i = self.dma_rr[q]
            self.dma_rr[q] = (i + 1) % N_DMA_SEMS
            prev = self.dma_last[q][i]
            if prev is not None:
                deps.append(prev)
            op.sem = self.dma_sems[q][i]
            key = (q, i)
            self.dma_cnt[key] = self.dma_cnt.get(key, 0) + 1
            op.semval = 16 * self.dma_cnt[key]
            self.dma_last[q][i] = op
        seen = set()
        for d in deps:
            if d is op or id(d) in seen:
                continue
            seen.add(id(d))
            if (not d.is_dma) and (not is_dma) and d.eng == eng:
                if eng == "pe" or not SAME_ENGINE_SYNC:
                    continue
            op.deps.append(d)
        op.idx = len(self.ops[eng])
        self.ops[eng].append(op)
        for b in reads:
            if not is_dma:
                b.readers = [r for r in b.readers if r.is_dma or r.eng != eng]
            b.readers.append(op)
        for b in writes:
            b.writer = op
            b.readers = []
        return op

    def mm(self, out, lhsT, rhs, start, stop, reads, writes):
        return self._add("pe", lambda e: e.matmul(out, lhsT, rhs, start=start, stop=stop), reads, writes)

    def act(self, out, in_, func, reads, writes, **kw):
        return self._add("act", lambda e: e.activation(out, in_, func, **kw), reads, writes)

    def tt(self, eng, out, in0, in1, op, reads, writes):
        return self._add(eng, lambda e: e.tensor_tensor(out, in0, in1, op), reads, writes)

    def ts(self, eng, out, in0, s1, s2, op0, op1, reads, writes):
        if op1 is None:
            return self._add(eng, lambda e: e.tensor_scalar(out, in0, s1, s2, op0), reads, writes)
        return self._add(eng, lambda e: e.tensor_scalar(out, in0, s1, s2, op0, op1), reads, writes)

    def stt(self, eng, out, in0, scalar, in1, op0, op1, reads, writes):
        return self._add(eng, lambda e: e.scalar_tensor_tensor(out, in0, scalar, in1, op0, op1), reads, writes)

    def copy(self, eng, out, in_, reads, writes):
        if eng == "act":
            return self._add(eng, lambda e: e.copy(out, in_), reads, writes)
        return self._add(eng, lambda e: e.tensor_copy(out, in_), reads, writes)

    def memset(self, eng, ap, val, writes):
        return self._add(eng, lambda e: e.memset(ap, val), [], writes)

    def recip(self, out, in_, reads, writes):
        return self._add("dve", lambda e: e.reciprocal(out, in_), reads, writes)

    def dma(self, q, out, in_, reads, writes):
        return self._add(q, lambda e: e.dma_start(out, in_), reads, writes, is_dma=True)

    def barrier(self):
        last = []
        for e in self.ENGS:
            for op in reversed(self.ops[e]):
                if not op.is_dma:
                    last.append(op)
                    break
        for q in self.dma_last:
            for op in self.dma_last[q]:
                if op is not None:
                    last.append(op)
        for e in self.ENGS:
            op = Op(e, None, False)
            for d in last:
                if d.eng == e and not d.is_dma and e == "pe":
                    continue
                op.deps.append(d)
            op.idx = len(self.ops[e])
            self.ops[e].append(op)

    def emit(self):
        nc = self.nc
        for e in self.ENGS:
            for op in self.ops[e]:
                for d in op.deps:
                    if not d.is_dma:
                        d.sig = True
        for e in self.ENGS:
            c = 0
            for op in self.ops[e]:
                if op.is_dma:
                    continue
                if op.sig:
                    c += 1
                    op.sigcount = c
            nsem = (c + SEM_ROT - 1) // SEM_ROT
            self.eng_sems[e] = [nc.alloc_semaphore(name=f"es_{e}_{i}") for i in range(max(nsem, 1))]
        engobj = {"pe": "tensor", "act": "scalar", "dve": "vector", "pool": "gpsimd", "sp": "sync"}
        stats = {}
        with nc.Block() as block:
            for e in self.ENGS:
                ops = self.ops[e]
                if not ops:
                    continue

                def body(eng, ops=ops, e=e):
                    waited = {}
                    nw = 0
                    for op in ops:
                        for d in op.deps:
                            if d.is_dma:
                                sem, val = d.sem, d.semval
                            else:
                                k = (d.sigcount - 1) // SEM_ROT
                                sem = self.eng_sems[d.eng][k]
                                val = (d.sigcount - 1) % SEM_ROT + 1
                            key = sem.num
                            if waited.get(key, 0) >= val:
                                continue
                            waited[key] = val
                            eng.wait_ge(sem, val)
                            nw += 1
                        if op.fn is None:
                            if op.sig:
                                k = (op.sigcount - 1) // SEM_ROT
                                eng.nop().then_inc(self.eng_sems[e][k], 1)
                            continue
                        ins = op.fn(eng)
                        if op.is_dma:
                            ins.then_inc(op.sem, 16)
                        elif op.sig:
                            k = (op.sigcount - 1) // SEM_ROT
                            ins.then_inc(self.eng_sems[e][k], 1)
                    stats[e] = (len(ops), nw)

                getattr(block, engobj[e])(body)
        return stats


class Ring:
    def __init__(self, P, tiles, name, nsub=1):
        self.tiles = tiles
        if nsub == 1:
            self.bufs = P.bufs(len(tiles), name)
        else:
            self.bufs = [P.bufs(nsub, f"{name}{i}_") for i in range(len(tiles))]
        self.i = 0

    def next(self):
        i = self.i
        self.i = (i + 1) % len(self.tiles)
        return self.tiles[i], self.bufs[i]


class Ctx:
    pass


def setup_common(nc, P, dram):
    c = Ctx()
    c.nc, c.P, c.dram = nc, P, dram
    c.hT = nc.alloc_sbuf_tensor("hT", [128, 8, S], F32)
    c.BhT = [[P.buf(f"hT{k}_{t}") for t in range(4)] for k in range(8)]
    c.uT = nc.alloc_sbuf_tensor("uT", [128, 8, S], BF16)
    c.BuT = [P.buf(f"uT{t}") for t in range(4)]
    pst = [nc.alloc_psum_tensor(f"ps{i}", [128, 512], F32) for i in range(8)]
    c.G = Ring(P, pst[0:4], "psG")
    c.A = Ring(P, pst[4:6], "psA")
    c.Dn = Ring(P, pst[6:8], "psD")
    c.cst = nc.alloc_sbuf_tensor("cst", [128, 5 * 128 + 32], F32)
    c.Bcst = P.buf("cst")
    P.dma("sp", c.cst[:, :], dram["consts"], [], [c.Bcst])
    c.cstb = nc.alloc_sbuf_tensor("cstb", [128, 5 * 128 + 32], BF16)
    c.Bcstb = P.buf("cstb")
    P.copy("dve", c.cstb[:, :], c.cst[:, :], [c.Bcst], [c.Bcstb])
    c.ident_f = c.cst[:, 0:128]
    c.ones_f = c.cst[:, 256:384]
    c.ident_b = c.cstb[:, 0:128]
    c.tri_b = c.cstb[:, 128:256]
    c.ones_b = c.cstb[:, 256:384]
    c.normw = nc.alloc_sbuf_tensor("sb_normw", [128, 5, 8], F32)
    c.Bnormw = P.buf("normw")
    P.dma("sp", c.normw[:, :, :], dram["normw"], [], [c.Bnormw])
    c.sq = Ring(P, [nc.alloc_sbuf_tensor(f"sq{i}", [128, 512], F32) for i in range(2)], "sq")
    c.rstd = Ring(P, [nc.alloc_sbuf_tensor(f"rstd{i}", [128, 512], F32) for i in range(2)], "rstd")
    c.wn = Ring(P, [nc.alloc_sbuf_tensor(f"wn{i}", [128, 8, 128], BF16) for i in range(4)], "wn")
    c.ww = Ring(P, [nc.alloc_sbuf_tensor(f"ww{i}", [128, 8, 256], BF16) for i in range(2)], "ww")
    c.wo = Ring(P, [nc.alloc_sbuf_tensor(f"wo{i}", [128, 2, 1024], BF16) for i in range(2)], "wo")
    c.evac_rr = 0
    return c


def rms_stats(c, src, Bsrc_fn, nk, tt, scale_inv_n):
    P = c.P
    ps, Bps = c.G.next()
    for k in range(nk):
        sq, Bsq = c.sq.next()
        P.act(sq[:, :], src[:, k, tt * 512:(tt + 1) * 512], AF.Square, [Bsrc_fn(k, tt)], [Bsq])
        P.mm(ps[:, :], c.ones_f, sq[:, :], k == 0, k == nk - 1, [Bsq, c.Bcst], [Bps])
    r, Br = c.rstd.next()
    P.act(r[:, :], ps[:, :], AF.Sqrt, [Bps], [Br], scale=scale_inv_n, bias=EPS)
    P.recip(r[:, :], r[:, :], [Br], [Br])
    return r, Br


def emit_rmsnorm_u(c, layer):
    P = c.P
    for tt in range(4):
        r, Br = rms_stats(c, c.hT, lambda k, t: c.BhT[k][t], 8, tt, 1.0 / D)
        for k in range(8):
            eng = "dve"
            P.stt(eng, c.uT[:, k, tt * 512:(tt + 1) * 512], c.hT[:, k, tt * 512:(tt + 1) * 512],
                  c.normw[:, layer, k:k + 1], r[:, :], ALU.mult, ALU.mult,
                  [c.BhT[k][tt], Br, c.Bnormw], [c.BuT[tt]])


def load_wn(c, src):
    w, Bw = c.wn.next()
    kc = src.shape[1]
    c.P.dma("pool", w[:, 0:kc, :], src, [], [Bw])
    return w, Bw


def proj_multi(c, wsrcs, evac_multi, rhs=None, Brhs=None, kc=8):
    P = c.P
    ws = []
    for src, m in wsrcs:
        w, Bw = c.wn.next()
        P.dma("pool", w[:, 0:kc, 0:m], src, [], [Bw])
        ws.append((w, Bw, m))
    if rhs is None:
        rhs, Brhs = c.uT, c.BuT
    for tt in range(4):
        pss = []
        for (w, Bw, m) in ws:
            ps, Bps = c.G.next()
            for k in range(kc):
                P.mm(ps[0:m, :], w[:, k, 0:m], rhs[:, k, tt * 512:(tt + 1) * 512], k == 0, k == kc - 1,
                     [Bw, Brhs[tt]], [Bps])
            pss.append((ps, Bps))
        evac_multi(pss, tt)


def proj_fm(c, wsrc, evac, rhs=None, Brhs=None, kc=8):
    proj_multi(c, [(wsrc, 128)], lambda pss, tt: evac(pss[0][0], pss[0][1], tt), rhs, Brhs, kc)


def evac_copy(c, dst, Bdst):
    def f(ps, Bps, tt):
        c.evac_rr += 1
        eng = "act" if c.evac_rr % 2 == 0 else "dve"
        c.P.copy(eng, dst[:, tt * 512:(tt + 1) * 512], ps[:, :], [Bps], [Bdst[tt] if isinstance(Bdst, list) else Bdst])
    return f


def evac_silu(c, dst, Bdst):
    def f(ps, Bps, tt):
        c.P.act(dst[:, tt * 512:(tt + 1) * 512], ps[:, :], AF.Silu, [Bps], [Bdst[tt] if isinstance(Bdst, list) else Bdst])
    return f


def proj_tm(c, wsrc, ncols, evac, lhs=None, Blhs=None, kc=8):
    P = c.P
    w, Bw = c.ww.next()
    P.dma("pool", w[:, 0:kc, 0:ncols], wsrc, [], [Bw])
    if lhs is None:
        lhs, Blhs = c.uT, c.BuT
    for tb in range(16):
        ps, Bps = c.G.next()
        for k in range(kc):
            P.mm(ps[:, 0:ncols], lhs[:, k, tb * 128:(tb + 1) * 128], w[:, k, 0:ncols], k == 0, k == kc - 1,
                 [Bw, Blhs[tb // 4]], [Bps])
        evac(ps, Bps, tb)


def outproj_acc(c, wsrc, gT, BgT, nk):
    P = c.P
    w, Bw = c.wo.next()
    P.dma("pool", w[:, 0:nk, :], wsrc, [], [Bw])
    for oc in range(8):
        for tt in range(4):
            ps, Bps = c.G.next()
            for j in range(nk):
                P.mm(ps[:, :], w[:, j, oc * 128:(oc + 1) * 128], gT[:, j, tt * 512:(tt + 1) * 512],
                     j == 0, j == nk - 1, [Bw, BgT[j][tt] if isinstance(BgT, list) else BgT], [Bps])
            P.tt("dve", c.hT[:, oc, tt * 512:(tt + 1) * 512], ps[:, :], c.hT[:, oc, tt * 512:(tt + 1) * 512],
                 ALU.add, [Bps, c.BhT[oc][tt]], [c.BhT[oc][tt]])


ATTN_DEPTH = 3


def attn_head(c, parts, Bparts, v_fn, Bv, scale, bias_fn, Bbias, gz, Bgz, gT, BgT, L):
    P = c.P
    dq = []

    def pump(limit):
        while dq and (dq[0][0] == "fin" or sum(1 for t in dq if t[0] == "pv") > limit):
            dq.pop(0)[1]()

    for qt in range(4):
        oacc, Bo = c.A.next()
        dacc, Bd = c.Dn.next()
        nkb = 4 * qt + 4
        for kb in range(nkb):
            d = kb - 4 * qt
            c0 = max(d, 0) * 128
            ps, Bps = c.G.next()
            for i, (kT, qT) in enumerate(parts):
                P.mm(ps[:, c0:512], kT[:, kb * 128:(kb + 1) * 128], qT[:, qt * 512 + c0:(qt + 1) * 512],
                     i == 0, i == len(parts) - 1, Bparts, [Bps])
            pt, Bpt0 = L.pt.next()
            if not hasattr(L, "_ptsub"):
                L._ptsub = {}
            if id(Bpt0) not in L._ptsub:
                L._ptsub[id(Bpt0)] = P.bufs(4, "ptsub")
            Bsub = L._ptsub[id(Bpt0)]
            j0 = c0 // 128
            if bias_fn is None:
                P.act(pt[:, c0:512], ps[:, c0:512], AF.Exp, [Bps], Bsub[j0:], scale=scale)
            else:
                for jj in range(j0, 4):
                    P.act(pt[:, jj * 128:(jj + 1) * 128], ps[:, jj * 128:(jj + 1) * 128], AF.Exp,
                          [Bps, Bbias], [Bsub[jj]], scale=scale, bias=bias_fn(kb, 4 * qt + jj))
            if d >= 0:
                P.tt("pool", pt[:, c0:c0 + 128], pt[:, c0:c0 + 128], c.tri_b, ALU.mult, [Bsub[j0], c.Bcstb], [Bsub[j0]])

            def pv(kb=kb, c0=c0, pt=pt, Bpt=Bsub[j0:], oacc=oacc, Bo=Bo, dacc=dacc, Bd=Bd, nkb=nkb):
                P.mm(oacc[:, c0:512], v_fn(kb), pt[:, c0:512], kb == 0, kb == nkb - 1,
                     [Bv[kb] if isinstance(Bv, list) else Bv] + Bpt, [Bo])
                P.mm(dacc[:, c0:512], c.ones_b, pt[:, c0:512], kb == 0, kb == nkb - 1, [c.Bcstb] + Bpt, [Bd])
            dq.append(("pv", pv))
            pump(ATTN_DEPTH)

        def fin(qt=qt, oacc=oacc, Bo=Bo, dacc=dacc, Bd=Bd):
            rd, Brd = L.rden.next()
            P.recip(rd[:, :], dacc[:, :], [Bd], [Brd])
            P.tt("dve", rd[:, :], oacc[:, :], rd[:, :], ALU.mult, [Bo, Brd], [Brd])
            P.tt("pool", gT[:, qt * 512:(qt + 1) * 512], rd[:, :], gz[:, qt * 512:(qt + 1) * 512], ALU.mult,
                 [Brd, Bgz[qt] if isinstance(Bgz, list) else Bgz], [BgT[qt] if isinstance(BgT, list) else BgT])
        dq.append(("fin", fin))
    pump(-1)


def emit_fox(c, layer):
    nc, P, dram = c.nc, c.P, c.dram
    L = Ctx()
    L.pt = Ring(P, [nc.alloc_sbuf_tensor(f"fx_pt{i}", [128, 512], BF16) for i in range(6)], "pt")
    L.rden = Ring(P, [nc.alloc_sbuf_tensor(f"fx_rd{i}", [128, 512], F32) for i in range(2)], "rden")
    qT = nc.alloc_sbuf_tensor("fx_qT", [128, 2, S], BF16); BqT = [P.bufs(4, f"qT{j}_") for j in range(2)]
    kT = nc.alloc_sbuf_tensor("fx_kT", [128, 2, S], BF16); BkT = [P.bufs(4, f"kT{j}_") for j in range(2)]
    gz = nc.alloc_sbuf_tensor("fx_gz", [128, 2, S], BF16); Bgz = [P.bufs(4, f"gz{j}_") for j in range(2)]
    gT = nc.alloc_sbuf_tensor("fx_gT", [128, 2, S], BF16); BgT = [P.bufs(4, f"gT{j}_") for j in range(2)]
    vt = nc.alloc_sbuf_tensor("fx_vt", [128, 16, 256], BF16); Bvt = P.bufs(16, "vt")
    lsp = nc.alloc_sbuf_tensor("fx_lsp", [128, 16, 16], F32); Blsp = P.buf("lsp")
    cT = nc.alloc_sbuf_tensor("fx_cT", [128, 16, 16], F32); BcT = P.buf("cT")
    cref = nc.alloc_sbuf_tensor("fx_cref", [128, 16, 16], F32); Bcref = P.buf("cref")
    fb = nc.alloc_sbuf_tensor("fx_fb", [128, 16], F32); Bfb = P.buf("fb")
    tmp = nc.alloc_sbuf_tensor("fx_tmp", [128, 16], F32); Btmp = P.buf("tmp")
    btab = nc.alloc_sbuf_tensor("fx_btab", [128, 2, 16, 16], F32); Bbt = P.buf("btab")
    negtri_f = c.cst[:, 384:512]
    P.dma("sp", fb[:, :], dram["fox_fb"], [], [Bfb])
    wf_src = dram["fox_wf"]
    scale = 128 ** -0.5

    def evac_f(ps, Bps, tb):
        P.tt("dve", tmp[:, :], ps[:, 0:16], fb[:, :], ALU.add, [Bps, Bfb], [Btmp])
        P.act(tmp[:, :], tmp[:, :], AF.Exp, [Btmp], [Btmp], scale=-1.0)
        P.act(lsp[:, tb, :], tmp[:, :], AF.Ln, [Btmp], [Blsp], bias=1.0)
    proj_tm(c, wf_src, 16, evac_f)
    negones_f = c.cst[:, 512:640]
    for tb in range(16):
        ps, Bps = c.G.next()
        for t2 in range(tb + 1):
            lhs = negtri_f if t2 == tb else negones_f
            P.mm(ps[:, 0:16], lhs, lsp[:, t2, :], t2 == 0, t2 == tb, [c.Bcst, Blsp], [Bps])
        P.copy("dve", cT[:, tb, :], ps[:, 0:16], [Bps], [BcT])
    ps, Bps = c.G.next()
    for tb in range(16):
        rhs = lsp[:, tb:tb + 1, :].to_broadcast([128, 16 - tb, 16])
        P.mm(ps[:, tb * 16:256].rearrange("p (j h) -> p j h", h=16), negones_f, rhs, tb == 0, tb == 15,
             [c.Bcst, Blsp], [Bps])
    P.copy("dve", cref[:, :, :].rearrange("p j h -> p (j h)"), ps[:, 0:256], [Bps], [Bcref])

    for hp in range(8):
        for j in range(2):
            h = 2 * hp + j
            proj_fm(c, dram["fox_wq"][h], evac_copy(c, qT[:, j, :], BqT[j]))
            proj_fm(c, dram["fox_wk"][h], evac_copy(c, kT[:, j, :], BkT[j]))
            proj_fm(c, dram["fox_wz"][h], evac_silu(c, gz[:, j, :], Bgz[j]))
            P.tt("dve", btab[:, j, :, :],
                 cref[:, :, h:h + 1].rearrange("p j o -> p o j").to_broadcast([128, 16, 16]),
                 cT[:, :, h:h + 1].to_broadcast([128, 16, 16]),
                 ALU.subtract, [Bcref, BcT], [Bbt])

        def evac_v(ps, Bps, tb):
            c.evac_rr += 1
            eng = "act" if c.evac_rr % 2 == 0 else "dve"
            P.copy(eng, vt[:, tb, :], ps[:, 0:256], [Bps], [Bvt[tb]])
        proj_tm(c, dram["fox_wv"][hp], 256, evac_v)
        if hp > 0:
            outproj_acc(c, dram["fox_wo"][hp - 1], gT, BgT, 2)
        for j in range(2):
            attn_head(c, [(kT[:, j, :], qT[:, j, :])], BkT[j] + BqT[j],
                      lambda kb, j=j: vt[:, kb, j * 128:(j + 1) * 128], Bvt, scale,
                      lambda kb, jq, j=j: btab[:, j, kb, jq:jq + 1], Bbt,
                      gz[:, j, :], Bgz[j], gT[:, j, :], BgT[j], L)
    outproj_acc(c, dram["fox_wo"][7], gT, BgT, 2)


def rope_combine(c, L, psX, BpsX, psXr, BpsXr, rows, cos_ap, sin_ap, Btab, out_ap, Bout, d=1):
    P = c.P
    t1, Bt1 = L.rt.next()
    t2, Bt2 = L.rt.next()
    P.tt("dve", t1[0:rows, :], psX[0:rows, :], cos_ap, ALU.mult, [BpsX, Btab], [Bt1])
    P.tt("dve", t2[0:rows, :], psXr[0:rows, :], sin_ap, ALU.mult, [BpsXr, Btab], [Bt2])
    a1, a2 = t1[0:rows, :], t2[0:rows, :]
    if d > 1:
        a1 = a1.rearrange("p (n r) -> p n r", r=d)
        a2 = a2.rearrange("p (n r) -> p n r", r=d)
    P.tt("pool", out_ap, a1, a2, ALU.add, [Bt1, Bt2], [Bout])


def normed_proj(c, L, wsrcs, nw_ap, Bnw, dst, Bdst, inv_n):
    P = c.P
    nk = len(wsrcs)

    def ev(pss, tt):
        psS, BpS = c.A.next()
        for k, (ps, Bps) in enumerate(pss):
            sq, Bsq = c.sq.next()
            P.act(sq[:, :], ps[:, :], AF.Square, [Bps], [Bsq])
            P.mm(psS[:, :], c.ones_f, sq[:, :], k == 0, k == nk - 1, [Bsq, c.Bcst], [BpS])
        r, Br = c.rstd.next()
        P.act(r[:, :], psS[:, :], AF.Sqrt, [BpS], [Br], scale=inv_n, bias=EPS)
        P.recip(r[:, :], r[:, :], [Br], [Br])
        for k, (ps, Bps) in enumerate(pss):
            P.stt("dve", dst[:, k, tt * 512:(tt + 1) * 512], ps[:, :], nw_ap[:, k:k + 1], r[:, :], ALU.mult, ALU.mult,
                  [Bps, Br, Bnw], [Bdst[tt]])
    proj_multi(c, [(w, 128) for w in wsrcs], ev)


def emit_mla(c, layer):
    nc, P, dram = c.nc, c.P, c.dram
    L = Ctx()
    L.pt = Ring(P, [nc.alloc_sbuf_tensor(f"ml_pt{i}", [128, 512], BF16) for i in range(5)], "pt")
    L.rt = Ring(P, [nc.alloc_sbuf_tensor(f"ml_rt{i}", [128, 512], F32) for i in range(3)], "rt")
    L.rden = L.rt
    cqn = nc.alloc_sbuf_tensor("ml_cqn", [128, 3, S], BF16); Bcqn = P.bufs(4, "cqn")
    ckvn = nc.alloc_sbuf_tensor("ml_ckvn", [128, 2, S], BF16); Bckvn = P.bufs(4, "ckvn")
    kr = nc.alloc_sbuf_tensor("ml_kr", [128, S], BF16); Bkr = P.buf("kr")
    qn = nc.alloc_sbuf_tensor("ml_qn", [128, S], BF16); Bqn = P.bufs(4, "qn")
    qr = nc.alloc_sbuf_tensor("ml_qr", [128, S], BF16); Bqr = P.bufs(4, "qr")
    kn = nc.alloc_sbuf_tensor("ml_kn", [128, S], BF16); Bkn = P.bufs(4, "kn")
    gz = nc.alloc_sbuf_tensor("ml_gz", [128, 1, S], BF16); Bgz = P.bufs(4, "gz")
    gT = nc.alloc_sbuf_tensor("ml_gT", [128, 1, S], BF16); BgT = P.bufs(4, "gT")
    vt = nc.alloc_sbuf_tensor("ml_vt", [128, 16, 128], BF16); Bvt = P.bufs(16, "vt")
    tab = nc.alloc_sbuf_tensor("ml_tab", [128, 2, S], F32); Btab = P.buf("tab")
    nws = nc.alloc_sbuf_tensor("ml_nws", [128, 5], F32); Bnws = P.buf("nws")
    P.dma("sp", tab[:, 0, :], dram["mla_rope"][0], [], [Btab])
    P.dma("sp", tab[:, 1, :], dram["mla_rope"][1], [], [Btab])
    P.dma("sp", nws[:, :], dram["mla_nw"], [], [Bnws])
    scale = 192 ** -0.5

    normed_proj(c, L, [dram["mla_wcq"][i] for i in range(3)], nws[:, 0:3], Bnws, cqn, Bcqn, 1.0 / 384)
    normed_proj(c, L, [dram["mla_wckv"][i] for i in range(2)], nws[:, 3:5], Bnws, ckvn, Bckvn, 1.0 / 256)

    def ev_kr(pss, tt):
        (pX, BX), (pXr, BXr) = pss
        rope_combine(c, L, pX, BX, pXr, BXr, 64, tab[0:64, 0, tt * 512:(tt + 1) * 512],
                     tab[0:64, 1, tt * 512:(tt + 1) * 512], Btab, kr[0:64, tt * 512:(tt + 1) * 512], Bkr)
    proj_multi(c, [(dram["mla_wkr"], 64), (dram["mla_wkrr"], 64)], ev_kr)

    for h in range(16):
        proj_fm(c, dram["mla_wqn"][h], evac_copy(c, qn, Bqn), cqn, Bcqn, 3)

        def ev_qr(pss, tt):
            (pX, BX), (pXr, BXr) = pss
            rope_combine(c, L, pX, BX, pXr, BXr, 64, tab[0:64, 0, tt * 512:(tt + 1) * 512],
                         tab[0:64, 1, tt * 512:(tt + 1) * 512], Btab, qr[0:64, tt * 512:(tt + 1) * 512], Bqr[tt])
        proj_multi(c, [(dram["mla_wqr"][h], 64), (dram["mla_wqrr"][h], 64)], ev_qr, cqn, Bcqn, 3)
        proj_fm(c, dram["mla_wkn"][h], evac_copy(c, kn, Bkn), ckvn, Bckvn, 2)
        proj_fm(c, dram["mla_wz"][h], evac_silu(c, gz[:, 0, :], Bgz))

        def evac_v(ps, Bps, tb):
            c.evac_rr += 1
            eng = "act" if c.evac_rr % 2 == 0 else "dve"
            P.copy(eng, vt[:, tb, :], ps[:, 0:128], [Bps], [Bvt[tb]])
        proj_tm(c, dram["mla_wv"][h], 128, evac_v, ckvn, Bckvn, 2)
        if h > 0:
            outproj_acc(c, dram["mla_wo"][h - 1], gT, [BgT], 1)
        attn_head(c, [(kn, qn), (kr[0:64, :], qr[0:64, :])], Bkn + Bqn + [Bkr] + Bqr,
                  lambda kb: vt[:, kb, :], Bvt, scale, None, None,
                  gz[:, 0, :], Bgz, gT[:, 0, :], BgT, L)
    outproj_acc(c, dram["mla_wo"][15], gT, [BgT], 1)


DIL_CFG = ((128, 1), (512, 4), (2048, 16))
import os as _os
DIL_GROUPS = [int(x) for x in _os.environ.get('DIL_GROUPS', '0,1,2').split(',')]


def emit_dil(c, layer):
    nc, P, dram = c.nc, c.P, c.dram
    L = Ctx()
    L.pt = Ring(P, [nc.alloc_sbuf_tensor(f"dl_pt{i}", [128, 256], BF16) for i in range(6)], "pt")
    L.xs = Ring(P, [nc.alloc_sbuf_tensor(f"dl_xs{i}", [32, 512], F32) for i in range(3)], "xs")
    L.xr = Ring(P, [nc.alloc_sbuf_tensor(f"dl_xr{i}", [32, 512], F32) for i in range(1)], "xr")
    qTs = [nc.alloc_sbuf_tensor(f"dl_qT{i}", [128, S], BF16) for i in range(2)]
    kTs = [nc.alloc_sbuf_tensor(f"dl_kT{i}", [128, S], BF16) for i in range(2)]
    vts = [nc.alloc_sbuf_tensor(f"dl_vt{i}", [128, 16, 128], BF16) for i in range(2)]
    BqTs = [P.bufs(4, f"qT{i}_") for i in range(2)]
    BkTs = [P.bufs(4, f"kT{i}_") for i in range(2)]
    Bvts = [P.bufs(16, f"vt{i}_") for i in range(2)]
    gz = nc.alloc_sbuf_tensor("dl_gz", [128, 1, S], BF16); Bgz = P.buf("gz")
    gT = nc.alloc_sbuf_tensor("dl_gT", [128, 1, S], BF16); BgT = P.buf("gT")
    oN = nc.alloc_sbuf_tensor("dl_oN", [128, S], F32); BoNg = [P.bufs(4, f"oN{g}_") for g in range(3)]
    dN = nc.alloc_sbuf_tensor("dl_dN", [128, S], F32); BdNg = [P.bufs(4, f"dN{g}_") for g in range(3)]
    tab = nc.alloc_sbuf_tensor("dl_tab", [128, 2, S], F32); Btab = P.buf("tab")
    msk = nc.alloc_sbuf_tensor("dl_msk", [128, 256], BF16); Bmsk = P.buf("msk")
    P.dma("sp", tab[:, 0, :], dram["dil_rope"][0], [], [Btab])
    P.dma("sp", tab[:, 1, :], dram["dil_rope"][1], [], [Btab])
    P.dma("pool", msk[:, :], dram["dil_mask"], [], [Bmsk])
    scale = 128 ** -0.5

    perm_f = c.cst[0:32, 640:672]

    def ev_rope(dst, Bdst):
        pend = []

        def run(task):
            tt, xs, Bxs = task
            sl = slice(tt * 512, (tt + 1) * 512)
            pp, Bpp = c.G.next()
            P.mm(pp[0:32, :], perm_f, xs[0:32, :], True, True, [Bxs, c.Bcst], [Bpp])
            xr, Bxr = L.xr.next()
            P.tt("dve", xr[0:32, :], pp[0:32, :], tab[0:32, 1, sl], ALU.mult, [Bpp, Btab], [Bxr])
            P.tt("dve", xs[0:32, :], xs[0:32, :], tab[0:32, 0, sl], ALU.mult, [Bxs, Btab], [Bxs])
            P.tt("pool", dst[0:32, sl], xs[0:32, :], xr[0:32, :], ALU.add, [Bxs, Bxr], [Bdst[tt]])

        def f(ps, Bps, tt):
            sl = slice(tt * 512, (tt + 1) * 512)
            P.copy("act", dst[:, sl], ps[:, :], [Bps], [Bdst[tt]])
            xs, Bxs = L.xs.next()
            P.copy("act", xs[0:32, :], ps[0:32, :], [Bps], [Bxs])
            pend.append((tt, xs, Bxs))
            if len(pend) > 1:
                run(pend.pop(0))

        def flush():
            while pend:
                run(pend.pop(0))
        return f, flush

    def proj_unit(h, g, bi):
        window, d = DIL_CFG[g]
        nb = (S // d) // 128
        qT, kT, vt = qTs[bi], kTs[bi], vts[bi]
        fq, flq = ev_rope(qT, BqTs[bi])
        proj_fm(c, dram["dil_wq"][g * 8 + h], fq)
        flq()
        fk, flk = ev_rope(kT, BkTs[bi])
        proj_fm(c, dram["dil_wk"][g * 8 + h], fk)
        flk()
        if g == 1:
            proj_fm(c, dram["dil_wz"][h], evac_silu(c, gz[:, 0, :], Bgz))
        wv, Bwv = c.wn.next()
        P.dma("pool", wv[:, :, :], dram["dil_wv"][g * 8 + h], [], [Bwv])
        for r in range(d):
            for kb in range(nb):
                blk = r * nb + kb
                t0 = kb * 128 * d + r
                ps, Bps = c.G.next()
                for k in range(8):
                    lhs = c.uT[:, k, t0:t0 + 127 * d + 1:d]
                    P.mm(ps[:, 0:128], lhs, wv[:, k, :], k == 0, k == 7, [Bwv] + c.BuT, [Bps])
                c.evac_rr += 1
                P.copy("act" if c.evac_rr % 2 == 0 else "dve", vt[:, blk, :], ps[:, 0:128], [Bps], [Bvts[bi][blk]])

    def attn_unit(h, g, bi):
        window, d = DIL_CFG[g]
        nb = (S // d) // 128
        qT, kT, vt = qTs[bi], kTs[bi], vts[bi]
        BqT, BkT, Bvt = BqTs[bi], BkTs[bi], Bvts[bi]
        dq = []

        def pump(limit):
            while dq and (dq[0][0] == "fin" or sum(1 for t in dq if t[0] == "pv") > limit):
                dq.pop(0)[1]()

        for bank in range(4):
            oacc, Bo = c.A.next()
            dacc, Bd = c.Dn.next()
            for qi in range(4):
                blk = bank * 4 + qi
                b = blk % nb
                ps, Bps = c.G.next()
                nk = 2 if b > 0 else 1
                r = blk // nb
                t0 = b * 128 * d + r
                qv = qT[:, t0:t0 + 127 * d + 1:d]
                P.mm(ps[:, 0:128], kT[:, t0:t0 + 127 * d + 1:d], qv, True, True, BkT + BqT, [Bps])
                if b > 0:
                    tp = t0 - 128 * d
                    P.mm(ps[:, 128:256], kT[:, tp:tp + 127 * d + 1:d], qv, True, True, BkT + BqT, [Bps])
                pt, Bpt = L.pt.next()
                P.act(pt[:, 0:128 * nk], ps[:, 0:128 * nk], AF.Exp, [Bps], [Bpt], scale=scale)
                P.tt("dve", pt[:, 0:128 * nk], pt[:, 0:128 * nk], msk[:, 0:128 * nk], ALU.mult, [Bpt, Bmsk], [Bpt])

                def pv(qi=qi, blk=blk, b=b, pt=pt, Bpt=Bpt, oacc=oacc, Bo=Bo, dacc=dacc, Bd=Bd):
                    oc = oacc[:, qi * 128:(qi + 1) * 128]
                    dc = dacc[:, qi * 128:(qi + 1) * 128]
                    P.mm(oc, vt[:, blk, :], pt[:, 0:128], True, b == 0, [Bvt[blk], Bpt], [Bo])
                    if b > 0:
                        P.mm(oc, vt[:, blk - 1, :], pt[:, 128:256], False, True, [Bvt[blk - 1], Bpt], [Bo])
                    P.mm(dc, c.ones_b, pt[:, 0:128], True, b == 0, [c.Bcstb, Bpt], [Bd])
                    if b > 0:
                        P.mm(dc, c.ones_b, pt[:, 128:256], False, True, [c.Bcstb, Bpt], [Bd])
                dq.append(("pv", pv))
                pump(ATTN_DEPTH)

            def fin(bank=bank, oacc=oacc, Bo=Bo, dacc=dacc, Bd=Bd):
                pieces = []
                if d == 1:
                    pieces.append((oN[:, bank * 512:(bank + 1) * 512], dN[:, bank * 512:(bank + 1) * 512], oacc[:, :], dacc[:, :]))
                elif d == 4:
                    pieces.append((oN[:, bank:S:4], dN[:, bank:S:4], oacc[:, :], dacc[:, :]))
                else:
                    for q4 in range(4):
                        r = bank * 4 + q4
                        pieces.append((oN[:, r:S:16], dN[:, r:S:16], oacc[:, q4 * 128:(q4 + 1) * 128], dacc[:, q4 * 128:(q4 + 1) * 128]))
                for (on, dn, oa, da) in pieces:
                    if g == 0:
                        P.copy("act", on, oa, [Bo], [BoNg[0][bank]])
                        P.copy("dve", dn, da, [Bd], [BdNg[0][bank]])
                    else:
                        P.tt("dve", on, oa, on, ALU.add, [Bo] + BoNg[g - 1], [BoNg[g][bank]])
                        P.tt("dve", dn, da, dn, ALU.add, [Bd] + BdNg[g - 1], [BdNg[g][bank]])
            dq.append(("fin", fin))
        pump(-1)

    def fin_head(h):
        for tt in range(4):
            sl = slice(tt * 512, (tt + 1) * 512)
            allo = BoNg[0] + BoNg[1] + BoNg[2]
            alld = BdNg[0] + BdNg[1] + BdNg[2]
            P.recip(dN[:, sl], dN[:, sl], alld, alld)
            P.tt("dve", oN[:, sl], oN[:, sl], dN[:, sl], ALU.mult, allo + alld, allo)
            P.tt("pool", gT[:, 0, sl], oN[:, sl], gz[:, 0, sl], ALU.mult, allo + [Bgz], [BgT])

    units = [(h, g) for h in range(8) for g in range(3)]
    proj_unit(units[0][0], units[0][1], 0)
    pending_out = None
    for i, (h, g) in enumerate(units):
        if i + 1 < len(units):
            proj_unit(units[i + 1][0], units[i + 1][1], (i + 1) % 2)
        if pending_out is not None:
            outproj_acc(c, dram["dil_wo"][pending_out], gT, BgT, 1)
            pending_out = None
        attn_unit(h, g, i % 2)
        if g == 2:
            fin_head(h)
            pending_out = h
    outproj_acc(c, dram["dil_wo"][pending_out], gT, BgT, 1)


def emit_ssd(c, layer):
    nc, P, dram = c.nc, c.P, c.dram
    tri_f = c.cst[:, 128:256]
    negtri_f = c.cst[:, 384:512]
    negones_f = c.cst[:, 512:640]
    dt = nc.alloc_sbuf_tensor("sd_dt", [128, 16, 32], F32); Bdt = P.buf("dt")
    absa = nc.alloc_sbuf_tensor("sd_absa", [128, 16, 32], F32); Babsa = P.buf("absa")
    acum = nc.alloc_sbuf_tensor("sd_acum", [128, 16, 32], F32); Bacum = P.buf("acum")
    wts = nc.alloc_sbuf_tensor("sd_w", [128, 16, 32], F32); Bw = P.buf("w")
    dlast = nc.alloc_sbuf_tensor("sd_dlast", [128, 16, 32], F32); Bdl = P.buf("dlast")
    vecs = nc.alloc_sbuf_tensor("sd_vecs", [128, 3, 32], F32); Bvecs = P.buf("vecs")
    cw = nc.alloc_sbuf_tensor("sd_cw", [128, 32, 5], F32); Bcw = P.buf("cw")
    dsk = nc.alloc_sbuf_tensor("sd_dsk", [128, 16], F32); Bdsk = P.buf("dsk")
    nrm = nc.alloc_sbuf_tensor("sd_nrm", [128, 16], F32); Bnrm = P.buf("nrm")
    tmp32 = nc.alloc_sbuf_tensor("sd_tmp32", [128, 32], F32); Bt32 = P.buf("t32")
    pre = nc.alloc_sbuf_tensor("sd_pre", [128, S + 3], F32); Bpre = P.bufs(4, "pre"); Bpad = P.buf("prepad")
    xTf = nc.alloc_sbuf_tensor("sd_xTf", [128, 2, S], F32); BxTf = P.bufs(2, "xTf")
    xTb = nc.alloc_sbuf_tensor("sd_xTb", [128, 2, S], BF16); BxTb = P.bufs(2, "xTb")
    BT = nc.alloc_sbuf_tensor("sd_BT", [128, S], BF16); BBT = P.buf("BT")
    CT = nc.alloc_sbuf_tensor("sd_CT", [128, S], BF16); BCT = P.buf("CT")
    gz = nc.alloc_sbuf_tensor("sd_gz", [128, 2, S], BF16); Bgz = P.buf("gz")
    dmR = Ring(P, [nc.alloc_sbuf_tensor(f"sd_dm{i}", [128, 512], F32) for i in range(2)], "dm4", 4)
    cacc = dmR
    eeR = Ring(P, [nc.alloc_sbuf_tensor(f"sd_ee{i}", [128, 512], F32) for i in range(1)], "ee4")
    mpR = Ring(P, [nc.alloc_sbuf_tensor(f"sd_mp{i}", [128, 512], BF16) for i in range(2)], "mp4", 4)
    csdR = Ring(P, [nc.alloc_sbuf_tensor(f"sd_csd{i}", [128, 512], BF16) for i in range(2)], "csd4", 4)
    xtokR = Ring(P, [nc.alloc_sbuf_tensor(f"sd_xtok{i}", [128, 256], BF16) for i in range(3)], "xtok")
    xwR = Ring(P, [nc.alloc_sbuf_tensor(f"sd_xw{i}", [128, 256], BF16) for i in range(3)], "xw", 4)
    cbmR = Ring(P, [nc.alloc_sbuf_tensor(f"sd_cbm{i}", [128, 128], F32) for i in range(2)], "cbm")
    btokR = Ring(P, [nc.alloc_sbuf_tensor(f"sd_btok{i}", [128, 128], BF16) for i in range(2)], "btok")
    Sf = nc.alloc_sbuf_tensor("sd_Sf", [128, 256], F32); BSf = P.bufs(4, "Sf")
    Sb = nc.alloc_sbuf_tensor("sd_Sb", [128, 256], BF16); BSb = P.buf("Sb")
    P.dma("sp", vecs[:, :, :], dram["ssm_vecs"], [], [Bvecs])
    P.dma("sp", cw[:, :, :], dram["ssm_cw"], [], [Bcw])
    P.dma("sp", dsk[:, :], dram["ssm_dsk"], [], [Bdsk])
    P.dma("sp", nrm[:, :], dram["ssm_nrm"], [], [Bnrm])
    P.memset("pool", pre[:, 0:3], 0.0, [Bpad])
    P.act(vecs[:, 1, :], vecs[:, 1, :], AF.Exp, [Bvecs], [Bvecs])

    def evac_dt(ps, Bps, tb):
        P.tt("dve", tmp32[:, :], ps[:, 0:32], vecs[:, 0, :], ALU.add, [Bps, Bvecs], [Bt32])
        P.act(tmp32[:, :], tmp32[:, :], AF.Exp, [Bt32], [Bt32])
        P.act(dt[:, tb, :], tmp32[:, :], AF.Ln, [Bt32], [Bdt], bias=1.0)
        P.tt("dve", absa[:, tb, :], dt[:, tb, :], vecs[:, 1, :], ALU.mult, [Bdt, Bvecs], [Babsa])
    proj_tm(c, dram["ssm_wdt"], 32, evac_dt)
    for tb in range(16):
        ps, Bps = c.G.next()
        P.mm(ps[:, 0:32], negtri_f, absa[:, tb, :], True, True, [c.Bcst, Babsa], [Bps])
        P.mm(ps[:, 32:64], negones_f, absa[:, tb, :], True, True, [c.Bcst, Babsa], [Bps])
        P.copy("dve", acum[:, tb, :], ps[:, 0:32], [Bps], [Bacum])
        P.copy("dve", dlast[:, tb, :], ps[:, 32:64], [Bps], [Bdl])
        P.tt("dve", wts[:, tb, :], dlast[:, tb, :], acum[:, tb, :], ALU.subtract, [Bdl, Bacum], [Bw])
        P.act(wts[:, tb, :], wts[:, tb, :], AF.Exp, [Bw], [Bw])
        P.tt("dve", wts[:, tb, :], wts[:, tb, :], dt[:, tb, :], ALU.mult, [Bw, Bdt], [Bw])
        P.act(dlast[:, tb, :], dlast[:, tb, :], AF.Exp, [Bdl], [Bdl])

    def conv_silu(ch, outs):
        for tt in range(4):
            a, Ba4 = cacc.next()
            Ba = Ba4[0]
            o = tt * 512
            Bp = [Bpre[tt], Bpre[tt - 1] if tt > 0 else Bpad]
            P.ts("dve", a[:, :], pre[:, o:o + 512], cw[:, ch, 0:1], cw[:, ch, 4:5], ALU.mult, ALU.add, Bp + [Bcw], [Ba])
            for k in range(1, 4):
                P.stt("dve", a[:, :], pre[:, o + k:o + k + 512], cw[:, ch, k:k + 1], a[:, :], ALU.mult, ALU.add,
                      Bp + [Bcw, Ba], [Ba])
            for (dst, Bdst) in outs:
                P.act(dst[:, o:o + 512], a[:, :], AF.Silu, [Ba], [Bdst])

    def evac_pre(ps, Bps, tt):
        P.copy("act", pre[:, 3 + tt * 512:3 + (tt + 1) * 512], ps[:, :], [Bps], [Bpre[tt]])

    for g in range(8):
        proj_fm(c, dram["ssm_wB"][g], evac_pre)
        if g > 0:
            outproj_acc(c, dram["ssm_wo"][g - 1], gz, Bgz, 2)
        conv_silu(16 + g, [(BT, BBT)])
        proj_fm(c, dram["ssm_wC"][g], evac_pre)
        conv_silu(24 + g, [(CT, BCT)])
        for i in range(2):
            proj_fm(c, dram["ssm_wx"][2 * g + i], evac_pre)
            proj_fm(c, dram["ssm_wz"][2 * g + i], evac_silu(c, gz[:, i, :], Bgz))
            conv_silu(2 * g + i, [(xTf[:, i, :], BxTf[i]), (xTb[:, i, :], BxTb[i])])
        dq = []
        for ck in range(16):
            cs = slice(ck * 128, (ck + 1) * 128)
            psT, BpsT = c.G.next()
            for i in range(2):
                P.mm(psT[:, i * 128:(i + 1) * 128], xTb[:, i, cs], c.ident_b, True, True, [BxTb[i], c.Bcstb], [BpsT])
            P.mm(psT[:, 256:384], BT[:, cs], c.ident_b, True, True, [BBT, c.Bcstb], [BpsT])
            pst, Bpst = c.A.next()
            P.mm(pst[:, 256:384], BT[:, cs], CT[:, cs], True, True, [BBT, BCT], [Bpst])
            pb4, Bpb4 = c.G.next()
            for j in range(4):
                h = 4 * g + j
                P.mm(pb4[:, j * 128:(j + 1) * 128], absa[:, ck, h:h + 1].to_broadcast([128, 128]), negtri_f, True, True,
                     [Babsa, c.Bcst], [Bpb4])
            xtok, Bxtok = xtokR.next()
            P.copy("act", xtok[:, :], psT[:, 0:256], [BpsT], [Bxtok])
            btok, Bbtok = btokR.next()
            P.copy("act", btok[:, :], psT[:, 256:384], [BpsT], [Bbtok])
            cbm, Bcbm = cbmR.next()
            P.tt("dve", cbm[:, :], pst[:, 256:384], tri_f, ALU.mult, [Bpst, c.Bcst], [Bcbm])
            dm4, Bdm4 = dmR.next()
            extra = []
            if ck > 0:
                ee4, Bee4 = eeR.next()
                P.act(ee4[:, :], pb4[:, :], AF.Exp, [Bpb4], [Bee4])
                extra = [Bee4]
            for j in range(4):
                h = 4 * g + j
                P.ts("dve", dm4[:, j * 128:(j + 1) * 128], pb4[:, j * 128:(j + 1) * 128], acum[:, ck, h:h + 1], 0.0,
                     ALU.subtract, ALU.min, [Bpb4, Bacum] + extra, [Bdm4[j]])
            P.act(dm4[:, :], dm4[:, :], AF.Exp, Bdm4, Bdm4)
            mp4, Bmp4 = mpR.next()
            for j in range(4):
                h = 4 * g + j
                P.stt("dve", mp4[:, j * 128:(j + 1) * 128], dm4[:, j * 128:(j + 1) * 128], dt[:, ck, h:h + 1], cbm[:, :],
                      ALU.mult, ALU.mult, [Bdm4[j], Bdt, Bcbm], [Bmp4[j]])
            csd4, Bcsd4 = csdR.next()
            if ck > 0:
                for j in range(4):
                    P.tt("pool", csd4[:, j * 128:(j + 1) * 128], ee4[:, j * 128:(j + 1) * 128], CT[:, cs], ALU.mult,
                         [BCT, Bee4], [Bcsd4[j]])
            xw, Bxw = xwR.next()
            for j in range(4):
                P.ts("pool", xw[:, j * 64:(j + 1) * 64], xtok[:, j * 64:(j + 1) * 64],
                     wts[:, ck, 4 * g + j:4 * g + j + 1], None, ALU.mult, None, [Bxtok, Bw], [Bxw[j]])

            def cd(ck=ck, cs=cs, pst=pst, Bpst=Bpst, xtok=xtok, Bxtok=Bxtok, btok=btok, Bbtok=Bbtok, xw=xw, Bxw=Bxw,
                   mp4=mp4, Bmp4=Bmp4, csd4=csd4, Bcsd4=Bcsd4):
                P.mm(pst[:, 0:256], btok[:, :], xw[:, :], True, True, [Bbtok] + Bxw, [Bpst])
                yps, Byps = c.Dn.next()
                for i in range(2):
                    for jj in range(2):
                        j = 2 * i + jj
                        yo = yps[64 * jj:64 * jj + 64, i * 128:(i + 1) * 128]
                        P.mm(yo, xtok[:, j * 64:(j + 1) * 64], mp4[:, j * 128:(j + 1) * 128], True, ck == 0,
                             [Bxtok, Bmp4[j]], [Byps])
                        if ck > 0:
                            P.mm(yo, Sb[:, j * 64:(j + 1) * 64], csd4[:, j * 128:(j + 1) * 128], False, True,
                                 [BSb, Bcsd4[j]], [Byps])
                for i in range(2):
                    P.stt("dve", xTf[:, i, cs], xTf[:, i, cs], dsk[:, 2 * g + i:2 * g + i + 1], yps[:, i * 128:(i + 1) * 128],
                          ALU.mult, ALU.add, [BxTf[i], Bdsk, Byps], [BxTf[i]])
                if ck == 0:
                    P.copy("dve", Sf[:, :], pst[:, 0:256], [Bpst], BSf)
                else:
                    for j in range(4):
                        P.stt("dve", Sf[:, j * 64:(j + 1) * 64], Sf[:, j * 64:(j + 1) * 64],
                              dlast[:, ck, 4 * g + j:4 * g + j + 1], pst[:, j * 64:(j + 1) * 64], ALU.mult, ALU.add,
                              [BSf[j], Bdl, Bpst], [BSf[j]])
                if ck < 15:
                    P.copy("act", Sb[:, :], Sf[:, :], BSf, [BSb])
            dq.append(cd)
            while len(dq) > 1:
                dq.pop(0)()
        while dq:
            dq.pop(0)()
        for tt in range(4):
            sl = slice(tt * 512, (tt + 1) * 512)
            for i in range(2):
                P.tt("pool", xTf[:, i, sl], xTf[:, i, sl], gz[:, i, sl], ALU.mult, [BxTf[i], Bgz], [BxTf[i]])
            r, Br = rms_stats(c, xTf, lambda k, t: BxTf[k], 2, tt, 1.0 / 256)
            for i in range(2):
                P.stt("dve", gz[:, i, sl], xTf[:, i, sl], nrm[:, 2 * g + i:2 * g + i + 1], r[:, :], ALU.mult, ALU.mult,
                      [BxTf[i], Br, Bnrm], [Bgz])
    outproj_acc(c, dram["ssm_wo"][7], gz, Bgz, 2)


def tile_cols(W, width=128):
    K, N = W.shape
    return np.ascontiguousarray(W.reshape(K // 128, 128, N // width, width).transpose(2, 1, 0, 3))


def tile_rows(W, nk):
    R, N = W.shape
    return np.ascontiguousarray(W.reshape(R // (128 * nk), nk, 128, N).transpose(0, 2, 1, 3))


def rope_tables(half, reps):
    inv_freq = (np.float32(ROPE_THETA) ** (-np.arange(half, dtype=np.float32) / np.float32(half))).astype(np.float32)
    ang = np.arange(S, dtype=np.float32)[None, :] * inv_freq[:, None]
    cos = np.cos(ang).astype(np.float32)
    sin = np.sin(ang).astype(np.float32)
    t = np.zeros((2, 128, S), np.float32)
    t[0] = 1.0
    for r in range(reps):
        b = r * 2 * half
        t[0, b:b + half] = cos
        t[0, b + half:b + 2 * half] = cos
        t[1, b:b + half] = -sin
        t[1, b + half:b + 2 * half] = sin
    return t


def make_consts():
    cst = np.zeros((128, 5 * 128 + 32), np.float32)
    for m in range(32):
        cst[(m + 16) % 32, 640 + m] = 1.0
    i = np.arange(128)
    cst[:, 0:128] = np.eye(128)
    cst[:, 128:256] = (i[:, None] <= i[None, :])
    cst[:, 256:384] = 1.0
    cst[:, 384:512] = -(i[:, None] <= i[None, :]).astype(np.float32)
    cst[:, 512:640] = -1.0
    return cst


def host_prep(inputs, layers):
    shared = {}
    shared["consts"] = make_consts()
    nw = np.concatenate([inputs["norm_w"], inputs["final_norm_w"][None]], axis=0)
    shared["normw"] = np.ascontiguousarray(nw.reshape(5, 8, 128).transpose(2, 0, 1))
    if 0 in layers:
        W = inputs["ssm_in_w"][0]
        shared["ssm_wz"] = tile_cols(W[:, 0:2048])
        shared["ssm_wx"] = tile_cols(W[:, 2048:4096])
        shared["ssm_wB"] = tile_cols(W[:, 4096:5120])
        shared["ssm_wC"] = tile_cols(W[:, 5120:6144])
        shared["ssm_wdt"] = tile_cols(W[:, 6144:6176], 32)[0]
        vec = np.stack([inputs["ssm_dt_bias"][0], inputs["ssm_A_log"][0], inputs["ssm_D"][0]], 0)
        shared["ssm_vecs"] = np.ascontiguousarray(np.broadcast_to(vec[None], (128, 3, 32)))
        cwb = np.concatenate([inputs["ssm_conv_w"][0], inputs["ssm_conv_b"][0][None]], 0)
        shared["ssm_cw"] = np.ascontiguousarray(cwb.reshape(5, 32, 128).transpose(2, 1, 0))
        shared["ssm_dsk"] = np.ascontiguousarray(np.repeat(inputs["ssm_D"][0], 64).reshape(16, 128).T)
        shared["ssm_nrm"] = np.ascontiguousarray(inputs["ssm_norm_w"][0].reshape(16, 128).T)
        shared["ssm_wo"] = tile_rows(inputs["ssm_out_w"][0], 2)
    if 1 in layers:
        W = inputs["mla_in_w"][0]
        perm = np.concatenate([np.arange(32, 64), np.arange(0, 32)])
        shared["mla_wcq"] = tile_cols(W[:, 0:384])
        shared["mla_wckv"] = tile_cols(W[:, 384:640])
        shared["mla_wkr"] = tile_cols(W[:, 640:704], 64)[0]
        shared["mla_wkrr"] = tile_cols(W[:, 640:704][:, perm], 64)[0]
        shared["mla_wz"] = tile_cols(W[:, 704:2752])
        UQ = inputs["mla_uq_w"][0].reshape(384, 16, 192)
        shared["mla_wqn"] = tile_cols(np.ascontiguousarray(UQ[:, :, 0:128]).reshape(384, 2048))
        shared["mla_wqr"] = tile_cols(np.ascontiguousarray(UQ[:, :, 128:192]).reshape(384, 1024), 64)
        shared["mla_wqrr"] = tile_cols(np.ascontiguousarray(UQ[:, :, 128:192][:, :, perm]).reshape(384, 1024), 64)
        UKV = inputs["mla_ukv_w"][0].reshape(256, 16, 256)
        shared["mla_wkn"] = tile_cols(np.ascontiguousarray(UKV[:, :, 0:128]).reshape(256, 2048))
        shared["mla_wv"] = tile_cols(np.ascontiguousarray(UKV[:, :, 128:256]).reshape(256, 2048))
        shared["mla_wo"] = tile_rows(inputs["mla_out_w"][0], 1)
        nwq = inputs["mla_q_norm_w"][0].reshape(3, 128).T
        nwkv = inputs["mla_kv_norm_w"][0].reshape(2, 128).T
        shared["mla_nw"] = np.ascontiguousarray(np.concatenate([nwq, nwkv], axis=1))
        shared["mla_rope"] = rope_tables(32, 2)
    if 3 in layers:
        W = inputs["dil_in_w"][0]
        perm = np.arange(128)
        perm[0:16] = np.arange(16, 32)
        perm[16:32] = np.arange(0, 16)
        wq, wk, wv, wqr, wkr = [], [], [], [], []
        for g in range(3):
            base = 3072 * g
            Q = W[:, base:base + 1024].reshape(1024, 8, 128)
            Kw = W[:, base + 1024:base + 2048].reshape(1024, 8, 128)
            wq.append(tile_cols(Q.reshape(1024, 1024)))
            wk.append(tile_cols(Kw.reshape(1024, 1024)))
            wv.append(tile_cols(W[:, base + 2048:base + 3072]))
        shared["dil_wq"] = np.concatenate(wq, 0)
        shared["dil_wk"] = np.concatenate(wk, 0)
        shared["dil_wv"] = np.concatenate(wv, 0)
        shared["dil_wz"] = tile_cols(W[:, 9216:10240])
        shared["dil_wo"] = tile_rows(inputs["dil_out_w"][0], 1)
        shared["dil_rope"] = rope_tables(16, 1)
        i = np.arange(128)
        m = np.zeros((128, 256), np.float32)
        m[:, 0:128] = (i[:, None] <= i[None, :])
        m[:, 128:256] = (i[:, None] >= i[None, :])
        shared["dil_mask"] = m
    if 2 in layers:
        W = inputs["fox_in_w"][0]
        shared["fox_wq"] = tile_cols(W[:, 0:2048])
        shared["fox_wk"] = tile_cols(W[:, 2048:4096])
        shared["fox_wv"] = tile_cols(W[:, 4096:6144], 256)
        shared["fox_wf"] = tile_cols(W[:, 6144:6160], 16)[0]
        shared["fox_wz"] = tile_cols(W[:, 6160:8208])
        shared["fox_fb"] = np.ascontiguousarray(np.broadcast_to(inputs["fox_f_bias"][0][None, :], (128, 16)))
        shared["fox_wo"] = tile_rows(inputs["fox_out_w"][0], 2)
    return shared


def build_program(layers, shared_shapes, final_norm):
    nc = bass.Bass("TRN2", target_bir_lowering=False)
    dram = {}
    for k, shp in shared_shapes.items():
        dram[k] = nc.dram_tensor(k, list(shp), F32, kind="ExternalInput").ap()
    hin = nc.dram_tensor("hin", [NSEQ, 8, 128, S], F32, kind="ExternalInput").ap()
    hout = nc.dram_tensor("hout", [NSEQ, 8, 128, S], F32, kind="ExternalOutput").ap()
    P = Prog(nc)
    c = setup_common(nc, P, dram)
    Bout = P.buf("hout")
    for s in range(NSEQ):
        for k in range(8):
            for t in range(4):
                P.dma("sp", c.hT[:, k, t * 512:(t + 1) * 512], hin[s, k, :, t * 512:(t + 1) * 512], [], [c.BhT[k][t]])
        for layer in layers:
            emit_rmsnorm_u(c, layer)
            emit_layer_cached(c, layer, LAYER_EMITTERS[layer])
        if final_norm:
            for tt in range(4):
                r, Br = rms_stats(c, c.hT, lambda k, t: c.BhT[k][t], 8, tt, 1.0 / D)
                for k in range(8):
                    eng = "dve"
                    P.stt(eng, c.hT[:, k, tt * 512:(tt + 1) * 512], c.hT[:, k, tt * 512:(tt + 1) * 512],
                          c.normw[:, 4, k:k + 1], r[:, :], ALU.mult, ALU.mult,
                          [c.BhT[k][tt], Br, c.Bnormw], [c.BhT[k][tt]])
        for k in range(8):
            for t in range(4):
                P.dma("sp", hout[s, k, :, t * 512:(t + 1) * 512], c.hT[:, k, t * 512:(t + 1) * 512], [c.BhT[k][t]], [Bout])
    P.barrier()
    stats = P.emit()
    return nc, stats


def emit_layer_cached(c, layer, fn):
    from contextlib import ExitStack
    nc = c.nc
    c._scope_id = getattr(c, "_scope_id", 0) + 1
    sid = c._scope_id
    with ExitStack() as st:
        class NCProxy:
            def __getattr__(self, a):
                if a == "alloc_sbuf_tensor":
                    return lambda name, shape, dtype: st.enter_context(nc.sbuf_tensor(f"{name}_s{sid}", shape, dtype))
                return getattr(nc, a)
        c.nc = NCProxy()
        try:
            fn(c, layer)
        finally:
            c.nc = nc
        c.P.barrier()


_CACHE = {}
LAYER_EMITTERS = {0: emit_ssd, 1: emit_mla, 2: emit_fox, 3: emit_dil}


def run_layers(hT_all, inputs, layers, final_norm):
    shared = host_prep(inputs, layers)
    key = (tuple(layers), final_norm)
    if key not in _CACHE:
        _CACHE[key] = build_program(layers, {k: v.shape for k, v in shared.items()}, final_norm)
    nc, stats = _CACHE[key]
    in_maps = []
    for core in range(8):
        m = dict(shared)
        m["hin"] = np.ascontiguousarray(hT_all[core * NSEQ:(core + 1) * NSEQ])
        in_maps.append(m)
    res = run_bass_kernel_spmd(nc, in_maps, core_ids=list(range(8)))
    return np.concatenate([r["hout"] for r in res.results], axis=0)


def to_fm(x):
    B = x.shape[0]
    return np.ascontiguousarray(x.transpose(0, 2, 1).reshape(B, 8, 128, S))


def from_fm(hT):
    B = hT.shape[0]
    return np.ascontiguousarray(hT.reshape(B, D, S).transpose(0, 2, 1))


def kernel(**inputs):
    inputs = {k: np.asarray(v, dtype=np.float32) for k, v in inputs.items()}
    hT = to_fm(inputs["x"])
    hT = run_layers(hT, inputs, [0, 1, 2, 3], True)
    return from_fm(hT)
```

```python
import numpy as np
import concourse.bass as bass
import concourse.mybir as mybir
from concourse.bass_utils import run_bass_kernel_spmd

F32 = mybir.dt.float32
BF16 = mybir.dt.bfloat16
AF = mybir.ActivationFunctionType
ALU = mybir.AluOpType

SAME_ENGINE_SYNC = True
SEM_ROT = 30000
N_DMA_SEMS = 12
S = 2048
D = 1024
NSEQ = 2
EPS = 1e-6
ROPE_THETA = 500000.0


class Buf:
    __slots__ = ("name", "writer", "readers")

    def __init__(self, name):
        self.name = name
        self.writer = None
        self.readers = []


class Op:
    __slots__ = ("eng", "fn", "deps", "sig", "idx", "is_dma", "sem", "semval", "sigcount")

    def __init__(self, eng, fn, is_dma):
        self.eng = eng
        self.fn = fn
        self.deps = []
        self.sig = False
        self.is_dma = is_dma
        self.sem = None
        self.semval = 0
        self.sigcount = 0


class Prog:
    ENGS = ("pe", "act", "dve", "pool", "sp")

    def __init__(self, nc):
        self.nc = nc
        self.ops = {e: [] for e in self.ENGS}
        self.dma_sems = {}
        self.dma_rr = {}
        self.dma_last = {}
        for q in ("sp", "pool"):
            self.dma_sems[q] = [nc.alloc_semaphore(name=f"dq_{q}_{i}") for i in range(N_DMA_SEMS)]
            self.dma_rr[q] = 0
            self.dma_last[q] = [None] * N_DMA_SEMS
        self.dma_cnt = {}
        self.eng_sems = {}
        self.nbuf = 0

    def buf(self, name=None):
        self.nbuf += 1
        return Buf(name or f"b{self.nbuf}")

    def bufs(self, n, name="b"):
        return [self.buf(f"{name}{i}") for i in range(n)]

    def _add(self, eng, fn, reads, writes, is_dma=False):
        op = Op(eng, fn, is_dma)
        deps = []
        for b in reads:
            if b.writer is not None:
                deps.append(b.writer)
        for b in writes:
            if b.writer is not None:
                deps.append(b.writer)
            deps.extend(b.readers)
        if is_dma:
            q = eng
            i = self.dma_rr[q]
            self.dma_rr[q] = (i + 1) % N_DMA_SEMS
            prev = self.dma_last[q][i]
            if prev is not None:
                deps.append(prev)
            op.sem = self.dma_sems[q][i]
            key = (q, i)
            self.dma_cnt[key] = self.dma_cnt.get(key, 0) + 1
            op.semval = 16 * self.dma_cnt[key]
            self.dma_last[q][i] = op
        seen = set()
        for d in deps:
            if d is op or id(d) in seen:
                continue
            seen.add(id(d))
            if (not d.is_dma) and (not is_dma) and d.eng == eng:
                if eng == "pe" or not SAME_ENGINE_SYNC:
                    continue
            op.deps.append(d)
        op.idx = len(self.ops[eng])
        self.ops[eng].append(op)
        for b in reads:
            if not is_dma:
                b.readers = [r for r in b.readers if r.is_dma or r.eng != eng]
            b.readers.append(op)
        for b in writes:
            b.writer = op
            b.readers = []
        return op

    def mm(self, out, lhsT, rhs, start, stop, reads, writes):
        return self._add("pe", lambda e: e.matmul(out, lhsT, rhs, start=start, stop=stop), reads, writes)

    def act(self, out, in_, func, reads, writes, **kw):
        return self._add("act", lambda e: e.activation(out, in_, func, **kw), reads, writes)

    def tt(self, eng, out, in0, in1, op, reads, writes):
        return self._add(eng, lambda e: e.tensor_tensor(out, in0, in1, op), reads, writes)

    def ts(self, eng, out, in0, s1, s2, op0, op1, reads, writes):
        if op1 is None:
            return self._add(eng, lambda e: e.tensor_scalar(out, in0, s1, s2, op0), reads, writes)
        return self._add(eng, lambda e: e.tensor_scalar(out, in0, s1, s2, op0, op1), reads, writes)

    def stt(self, eng, out, in0, scalar, in1, op0, op1, reads, writes):
        return self._add(eng, lambda e: e.scalar_tensor_tensor(out, in0, scalar, in1, op0, op1), reads, writes)

    def copy(self, eng, out, in_, reads, writes):
        if eng == "act":
            return self._add(eng, lambda e: e.copy(out, in_), reads, writes)
        return self._add(eng, lambda e: e.tensor_copy(out, in_), reads, writes)

    def memset(self, eng, ap, val, writes):
        return self._add(eng, lambda e: e.memset(ap, val), [], writes)

    def recip(self, out, in_, reads, writes):
        return self._add("dve", lambda e: e.reciprocal(out, in_), reads, writes)

    def dma(self, q, out, in_, reads, writes):
        return self._add(q, lambda e: e.dma_start(out, in_), reads, writes, is_dma=True)

    def barrier(self):
        last = []
        for e in self.ENGS:
            for op in reversed(self.ops[e]):
                if not op.is_dma:
                    last.append(op)
                    break
        for q in self.dma_last:
            for op in self.dma_last[q]:
                if op is not None:
                    last.append(op)
        for e in self.ENGS:
            op = Op(e, None, False)
            for d in last:
                if d.eng == e and not d.is_dma and e == "pe":
                    continue
                op.deps.append(d)
            op.idx = len(self.ops[e])
            self.ops[e].append(op)

    def emit(self):
        nc = self.nc
        for e in self.ENGS:
            for op in self.ops[e]:
                for d in op.deps:
                    if not d.is_dma:
                        d.sig = True
        for e in self.ENGS:
            c = 0
            for op in self.ops[e]:
                if op.is_dma:
                    continue
                if op.sig:
                    c += 1
                    op.sigcount = c
            nsem = (c + SEM_ROT - 1) // SEM_ROT
            self.eng_sems[e] = [nc.alloc_semaphore(name=f"es_{e}_{i}") for i in range(max(nsem, 1))]
        engobj = {"pe": "tensor", "act": "scalar", "dve": "vector", "pool": "gpsimd", "sp": "sync"}
        stats = {}
        with nc.Block() as block:
            for e in self.ENGS:
                ops = self.ops[e]
                if not ops:
                    continue

                def body(eng, ops=ops, e=e):
                    waited = {}
                    nw = 0
                    for op in ops:
                        for d in op.deps:
                            if d.is_dma:
                                sem, val = d.sem, d.semval
                            else:
                                k = (d.sigcount - 1) // SEM_ROT
                                sem = self.eng_sems[d.eng][k]
                                val = (d.sigcount - 1) % SEM_ROT + 1
                            key = sem.num
                            if waited.get(key, 0) >= val:
                                continue
                            waited[key] = val
                            eng.wait_ge(sem, val)
                            nw += 1
                        if op.fn is None:
                            if op.sig:
                                k = (op.sigcount - 1) // SEM_ROT
                                eng.nop().then_inc(self.eng_sems[e][k], 1)
                            continue
                        ins = op.fn(eng)
                        if op.is_dma:
                            ins.then_inc(op.sem, 16)
                        elif op.sig:
                            k = (op.sigcount - 1) // SEM_ROT
                            ins.then_inc(self.eng_sems[e][k], 1)
                    stats[e] = (len(ops), nw)

                getattr(block, engobj[e])(body)
        return stats


class Ring:
    def __init__(self, P, tiles, name, nsub=1):
        self.tiles = tiles
        if nsub == 1:
            self.bufs = P.bufs(len(tiles), name)
        else:
            self.bufs = [P.bufs(nsub, f"{name}{i}_") for i in range(len(tiles))]
        self.i = 0

    def next(self):
        i = self.i
        self.i = (i + 1) % len(self.tiles)
        return self.tiles[i], self.bufs[i]


class Ctx:
    pass


def setup_common(nc, P, dram):
    c = Ctx()
    c.nc, c.P, c.dram = nc, P, dram
    c.hT = nc.alloc_sbuf_tensor("hT", [128, 8, S], F32)
    c.BhT = [[P.buf(f"hT{k}_{t}") for t in range(4)] for k in range(8)]
    c.uT = nc.alloc_sbuf_tensor("uT", [128, 8, S], BF16)
    c.BuT = [P.buf(f"uT{t}") for t in range(4)]
    pst = [nc.alloc_psum_tensor(f"ps{i}", [128, 512], F32) for i in range(8)]
    c.G = Ring(P, pst[0:4], "psG")
    c.A = Ring(P, pst[4:6], "psA")
    c.Dn = Ring(P, pst[6:8], "psD")
    c.cst = nc.alloc_sbuf_tensor("cst", [128, 5 * 128 + 32], F32)
    c.Bcst = P.buf("cst")
    P.dma("sp", c.cst[:, :], dram["consts"], [], [c.Bcst])
    c.cstb = nc.alloc_sbuf_tensor("cstb", [128, 5 * 128 + 32], BF16)
    c.Bcstb = P.buf("cstb")
    P.copy("dve", c.cstb[:, :], c.cst[:, :], [c.Bcst], [c.Bcstb])
    c.ident_f = c.cst[:, 0:128]
    c.ones_f = c.cst[:, 256:384]
    c.ident_b = c.cstb[:, 0:128]
    c.tri_b = c.cstb[:, 128:256]
    c.ones_b = c.cstb[:, 256:384]
    c.normw = nc.alloc_sbuf_tensor("sb_normw", [128, 5, 8], F32)
    c.Bnormw = P.buf("normw")
    P.dma("sp", c.normw[:, :, :], dram["normw"], [], [c.Bnormw])
    c.sq = Ring(P, [nc.alloc_sbuf_tensor(f"sq{i}", [128, 512], F32) for i in range(2)], "sq")
    c.rstd = Ring(P, [nc.alloc_sbuf_tensor(f"rstd{i}", [128, 512], F32) for i in range(2)], "rstd")
    c.wn = Ring(P, [nc.alloc_sbuf_tensor(f"wn{i}", [128, 8, 128], BF16) for i in range(4)], "wn")
    c.ww = Ring(P, [nc.alloc_sbuf_tensor(f"ww{i}", [128, 8, 256], BF16) for i in range(2)], "ww")
    c.wo = Ring(P, [nc.alloc_sbuf_tensor(f"wo{i}", [128, 2, 1024], BF16) for i in range(2)], "wo")
    c.evac_rr = 0
    return c


def rms_stats(c, src, Bsrc_fn, nk, tt, scale_inv_n):
    P = c.P
    ps, Bps = c.G.next()
    for k in range(nk):
        sq, Bsq = c.sq.next()
        P.act(sq[:, :], src[:, k, tt * 512:(tt + 1) * 512], AF.Square, [Bsrc_fn(k, tt)], [Bsq])
        P.mm(ps[:, :], c.ones_f, sq[:, :], k == 0, k == nk - 1, [Bsq, c.Bcst], [Bps])
    r, Br = c.rstd.next()
    P.act(r[:, :], ps[:, :], AF.Sqrt, [Bps], [Br], scale=scale_inv_n, bias=EPS)
    P.recip(r[:, :], r[:, :], [Br], [Br])
    return r, Br


def emit_rmsnorm_u(c, layer):
    P = c.P
    for tt in range(4):
        r, Br = rms_stats(c, c.hT, lambda k, t: c.BhT[k][t], 8, tt, 1.0 / D)
        for k in range(8):
            eng = "dve"
            P.stt(eng, c.uT[:, k, tt * 512:(tt + 1) * 512], c.hT[:, k, tt * 512:(tt + 1) * 512],
                  c.normw[:, layer, k:k + 1], r[:, :], ALU.mult, ALU.mult,
                  [c.BhT[k][tt], Br, c.Bnormw], [c.BuT[tt]])


def load_wn(c, src):
    w, Bw = c.wn.next()
    kc = src.shape[1]
    c.P.dma("pool", w[:, 0:kc, :], src, [], [Bw])
    return w, Bw


def proj_multi(c, wsrcs, evac_multi, rhs=None, Brhs=None, kc=8):
    P = c.P
    ws = []
    for src, m in wsrcs:
        w, Bw = c.wn.next()
        P.dma("pool", w[:, 0:kc, 0:m], src, [], [Bw])
        ws.append((w, Bw, m))
    if rhs is None:
        rhs, Brhs = c.uT, c.BuT
    for tt in range(4):
        pss = []
        for (w, Bw, m) in ws:
            ps, Bps = c.G.next()
            for k in range(kc):
                P.mm(ps[0:m, :], w[:, k, 0:m], rhs[:, k, tt * 512:(tt + 1) * 512], k == 0, k == kc - 1,
                     [Bw, Brhs[tt]], [Bps])
            pss.append((ps, Bps))
        evac_multi(pss, tt)


def proj_fm(c, wsrc, evac, rhs=None, Brhs=None, kc=8):
    proj_multi(c, [(wsrc, 128)], lambda pss, tt: evac(pss[0][0], pss[0][1], tt), rhs, Brhs, kc)


def evac_copy(c, dst, Bdst):
    def f(ps, Bps, tt):
        c.evac_rr += 1
        eng = getattr(c, "evac_force", None) or ("act" if c.evac_rr % 2 == 0 else "dve")
        c.P.copy(eng, dst[:, tt * 512:(tt + 1) * 512], ps[:, :], [Bps], [Bdst[tt] if isinstance(Bdst, list) else Bdst])
    return f


def evac_silu(c, dst, Bdst):
    def f(ps, Bps, tt):
        c.P.act(dst[:, tt * 512:(tt + 1) * 512], ps[:, :], AF.Silu, [Bps], [Bdst[tt] if isinstance(Bdst, list) else Bdst])
    return f


def proj_tm(c, wsrc, ncols, evac, lhs=None, Blhs=None, kc=8):
    P = c.P
    w, Bw = c.ww.next()
    P.dma("pool", w[:, 0:kc, 0:ncols], wsrc, [], [Bw])
    if lhs is None:
        lhs, Blhs = c.uT, c.BuT
    for tb in range(16):
        ps, Bps = c.G.next()
        for k in range(kc):
            P.mm(ps[:, 0:ncols], lhs[:, k, tb * 128:(tb + 1) * 128], w[:, k, 0:ncols], k == 0, k == kc - 1,
                 [Bw, Blhs[tb // 4]], [Bps])
        evac(ps, Bps, tb)


def outproj_acc(c, wsrc, gT, BgT, nk):
    P = c.P
    w, Bw = c.wo.next()
    P.dma("pool", w[:, 0:nk, :], wsrc, [], [Bw])
    for oc in range(8):
        for tt in range(4):
            ps, Bps = c.G.next()
            for j in range(nk):
                P.mm(ps[:, :], w[:, j, oc * 128:(oc + 1) * 128], gT[:, j, tt * 512:(tt + 1) * 512],
                     j == 0, j == nk - 1, [Bw, BgT[j][tt] if isinstance(BgT, list) else BgT], [Bps])
            P.tt("dve", c.hT[:, oc, tt * 512:(tt + 1) * 512], ps[:, :], c.hT[:, oc, tt * 512:(tt + 1) * 512],
                 ALU.add, [Bps, c.BhT[oc][tt]], [c.BhT[oc][tt]])


ATTN_DEPTH = 3


def attn_head(c, parts, Bparts, v_fn, Bv, scale, bias_fn, Bbias, gz, Bgz, gT, BgT, L):
    P = c.P
    dq = []

    def pump(limit):
        while dq and (dq[0][0] == "fin" or sum(1 for t in dq if t[0] == "pv") > limit):
            dq.pop(0)[1]()

    for qt in range(4):
        oacc, Bo = c.A.next()
        dacc, Bd = c.Dn.next()
        nkb = 4 * qt + 4
        for kb in range(nkb):
            d = kb - 4 * qt
            c0 = max(d, 0) * 128
            ps, Bps = c.G.next()
            for i, (kT, qT) in enumerate(parts):
                P.mm(ps[:, c0:512], kT[:, kb * 128:(kb + 1) * 128], qT[:, qt * 512 + c0:(qt + 1) * 512],
                     i == 0, i == len(parts) - 1, Bparts, [Bps])
            pt, Bpt0 = L.pt.next()
            if not hasattr(L, "_ptsub"):
                L._ptsub = {}
            if id(Bpt0) not in L._ptsub:
                L._ptsub[id(Bpt0)] = P.bufs(4, "ptsub")
            Bsub = L._ptsub[id(Bpt0)]
            j0 = c0 // 128
            if bias_fn is None:
                P.act(pt[:, c0:512], ps[:, c0:512], AF.Exp, [Bps], Bsub[j0:], scale=scale)
            else:
                for jj in range(j0, 4):
                    P.act(pt[:, jj * 128:(jj + 1) * 128], ps[:, jj * 128:(jj + 1) * 128], AF.Exp,
                          [Bps, Bbias], [Bsub[jj]], scale=scale, bias=bias_fn(kb, 4 * qt + jj))
            if d >= 0:
                P.tt("pool", pt[:, c0:c0 + 128], pt[:, c0:c0 + 128], c.tri_b, ALU.mult, [Bsub[j0], c.Bcstb], [Bsub[j0]])

            def pv(kb=kb, c0=c0, pt=pt, Bpt=Bsub[j0:], oacc=oacc, Bo=Bo, dacc=dacc, Bd=Bd, nkb=nkb):
                P.mm(oacc[:, c0:512], v_fn(kb), pt[:, c0:512], kb == 0, kb == nkb - 1,
                     [Bv[kb] if isinstance(Bv, list) else Bv] + Bpt, [Bo])
                P.mm(dacc[:, c0:512], c.ones_b, pt[:, c0:512], kb == 0, kb == nkb - 1, [c.Bcstb] + Bpt, [Bd])
            dq.append(("pv", pv))
            pump(ATTN_DEPTH)

        def fin(qt=qt, oacc=oacc, Bo=Bo, dacc=dacc, Bd=Bd):
            rd, Brd = L.rden.next()
            P.recip(rd[:, :], dacc[:, :], [Bd], [Brd])
            P.tt("dve", rd[:, :], oacc[:, :], rd[:, :], ALU.mult, [Bo, Brd], [Brd])
            P.tt("pool", gT[:, qt * 512:(qt + 1) * 512], rd[:, :], gz[:, qt * 512:(qt + 1) * 512], ALU.mult,
                 [Brd, Bgz[qt] if isinstance(Bgz, list) else Bgz], [BgT[qt] if isinstance(BgT, list) else BgT])
        dq.append(("fin", fin))
    pump(-1)


def emit_fox(c, layer):
    c.evac_force = "dve"
    try:
        _emit_fox(c, layer)
    finally:
        c.evac_force = None


def _emit_fox(c, layer):
    nc, P, dram = c.nc, c.P, c.dram
    L = Ctx()
    L.pt = Ring(P, [nc.alloc_sbuf_tensor(f"fx_pt{i}", [128, 512], BF16) for i in range(6)], "pt")
    L.rden = Ring(P, [nc.alloc_sbuf_tensor(f"fx_rd{i}", [128, 512], F32) for i in range(2)], "rden")
    qT = nc.alloc_sbuf_tensor("fx_qT", [128, 2, S], BF16); BqT = [P.bufs(4, f"qT{j}_") for j in range(2)]
    kT = nc.alloc_sbuf_tensor("fx_kT", [128, 2, S], BF16); BkT = [P.bufs(4, f"kT{j}_") for j in range(2)]
    gz = nc.alloc_sbuf_tensor("fx_gz", [128, 2, S], BF16); Bgz = [P.bufs(4, f"gz{j}_") for j in range(2)]
    gT = nc.alloc_sbuf_tensor("fx_gT", [128, 2, S], BF16); BgT = [P.bufs(4, f"gT{j}_") for j in range(2)]
    vt = nc.alloc_sbuf_tensor("fx_vt", [128, 16, 256], BF16); Bvt = P.bufs(16, "vt")
    lsp = nc.alloc_sbuf_tensor("fx_lsp", [128, 16, 16], F32); Blsp = P.buf("lsp")
    cT = nc.alloc_sbuf_tensor("fx_cT", [128, 16, 16], F32); BcT = P.buf("cT")
    cref = nc.alloc_sbuf_tensor("fx_cref", [128, 16, 16], F32); Bcref = P.buf("cref")
    fb = nc.alloc_sbuf_tensor("fx_fb", [128, 16], F32); Bfb = P.buf("fb")
    tmp = nc.alloc_sbuf_tensor("fx_tmp", [128, 16], F32); Btmp = P.buf("tmp")
    btab = nc.alloc_sbuf_tensor("fx_btab", [128, 2, 16, 16], F32); Bbt = P.buf("btab")
    negtri_f = c.cst[:, 384:512]
    P.dma("sp", fb[:, :], dram["fox_fb"], [], [Bfb])
    wf_src = dram["fox_wf"]
    scale = 128 ** -0.5

    def evac_f(ps, Bps, tb):
        P.tt("dve", tmp[:, :], ps[:, 0:16], fb[:, :], ALU.add, [Bps, Bfb], [Btmp])
        P.act(tmp[:, :], tmp[:, :], AF.Exp, [Btmp], [Btmp], scale=-1.0)
        P.act(lsp[:, tb, :], tmp[:, :], AF.Ln, [Btmp], [Blsp], bias=1.0)
    proj_tm(c, wf_src, 16, evac_f)
    negones_f = c.cst[:, 512:640]
    for tb in range(16):
        ps, Bps = c.G.next()
        for t2 in range(tb + 1):
            lhs = negtri_f if t2 == tb else negones_f
            P.mm(ps[:, 0:16], lhs, lsp[:, t2, :], t2 == 0, t2 == tb, [c.Bcst, Blsp], [Bps])
        P.copy("dve", cT[:, tb, :], ps[:, 0:16], [Bps], [BcT])
    ps, Bps = c.G.next()
    for tb in range(16):
        rhs = lsp[:, tb:tb + 1, :].to_broadcast([128, 16 - tb, 16])
        P.mm(ps[:, tb * 16:256].rearrange("p (j h) -> p j h", h=16), negones_f, rhs, tb == 0, tb == 15,
             [c.Bcst, Blsp], [Bps])
    P.copy("dve", cref[:, :, :].rearrange("p j h -> p (j h)"), ps[:, 0:256], [Bps], [Bcref])

    for hp in range(8):
        for j in range(2):
            h = 2 * hp + j
            proj_fm(c, dram["fox_wq"][h], evac_copy(c, qT[:, j, :], BqT[j]))
            proj_fm(c, dram["fox_wk"][h], evac_copy(c, kT[:, j, :], BkT[j]))
            proj_fm(c, dram["fox_wz"][h], evac_silu(c, gz[:, j, :], Bgz[j]))
            P.tt("dve", btab[:, j, :, :],
                 cref[:, :, h:h + 1].rearrange("p j o -> p o j").to_broadcast([128, 16, 16]),
                 cT[:, :, h:h + 1].to_broadcast([128, 16, 16]),
                 ALU.subtract, [Bcref, BcT], [Bbt])

        def evac_v(ps, Bps, tb):
            c.evac_rr += 1
            eng = "act" if c.evac_rr % 2 == 0 else "dve"
            P.copy("dve", vt[:, tb, :], ps[:, 0:256], [Bps], [Bvt[tb]])
        proj_tm(c, dram["fox_wv"][hp], 256, evac_v)
        if hp > 0:
            outproj_acc(c, dram["fox_wo"][hp - 1], gT, BgT, 2)
        for j in range(2):
            attn_head(c, [(kT[:, j, :], qT[:, j, :])], BkT[j] + BqT[j],
                      lambda kb, j=j: vt[:, kb, j * 128:(j + 1) * 128], Bvt, scale,
                      lambda kb, jq, j=j: btab[:, j, kb, jq:jq + 1], Bbt,
                      gz[:, j, :], Bgz[j], gT[:, j, :], BgT[j], L)
    outproj_acc(c, dram["fox_wo"][7], gT, BgT, 2)


def rope_combine(c, L, psX, BpsX, psXr, BpsXr, rows, cos_ap, sin_ap, Btab, out_ap, Bout, d=1):
    P = c.P
    t1, Bt1 = L.rt.next()
    t2, Bt2 = L.rt.next()
    P.tt("dve", t1[0:rows, :], psX[0:rows, :], cos_ap, ALU.mult, [BpsX, Btab], [Bt1])
    P.tt("dve", t2[0:rows, :], psXr[0:rows, :], sin_ap, ALU.mult, [BpsXr, Btab], [Bt2])
    a1, a2 = t1[0:rows, :], t2[0:rows, :]
    if d > 1:
        a1 = a1.rearrange("p (n r) -> p n r", r=d)
        a2 = a2.rearrange("p (n r) -> p n r", r=d)
    P.tt("pool", out_ap, a1, a2, ALU.add, [Bt1, Bt2], [Bout])


def normed_proj(c, L, wsrcs, nw_ap, Bnw, dst, Bdst, inv_n):
    P = c.P
    nk = len(wsrcs)

    def ev(pss, tt):
        psS, BpS = c.A.next()
        for k, (ps, Bps) in enumerate(pss):
            sq, Bsq = c.sq.next()
            P.act(sq[:, :], ps[:, :], AF.Square, [Bps], [Bsq])
            P.mm(psS[:, :], c.ones_f, sq[:, :], k == 0, k == nk - 1, [Bsq, c.Bcst], [BpS])
        r, Br = c.rstd.next()
        P.act(r[:, :], psS[:, :], AF.Sqrt, [BpS], [Br], scale=inv_n, bias=EPS)
        P.recip(r[:, :], r[:, :], [Br], [Br])
        for k, (ps, Bps) in enumerate(pss):
            P.stt("dve", dst[:, k, tt * 512:(tt + 1) * 512], ps[:, :], nw_ap[:, k:k + 1], r[:, :], ALU.mult, ALU.mult,
                  [Bps, Br, Bnw], [Bdst[tt]])
    proj_multi(c, [(w, 128) for w in wsrcs], ev)


def emit_mla(c, layer):
    nc, P, dram = c.nc, c.P, c.dram
    L = Ctx()
    L.pt = Ring(P, [nc.alloc_sbuf_tensor(f"ml_pt{i}", [128, 512], BF16) for i in range(5)], "pt")
    L.rt = Ring(P, [nc.alloc_sbuf_tensor(f"ml_rt{i}", [128, 512], F32) for i in range(3)], "rt")
    L.rden = L.rt
    cqn = nc.alloc_sbuf_tensor("ml_cqn", [128, 3, S], BF16); Bcqn = P.bufs(4, "cqn")
    ckvn = nc.alloc_sbuf_tensor("ml_ckvn", [128, 2, S], BF16); Bckvn = P.bufs(4, "ckvn")
    kr = nc.alloc_sbuf_tensor("ml_kr", [128, S], BF16); Bkr = P.buf("kr")
    qn = nc.alloc_sbuf_tensor("ml_qn", [128, S], BF16); Bqn = P.bufs(4, "qn")
    qr = nc.alloc_sbuf_tensor("ml_qr", [128, S], BF16); Bqr = P.bufs(4, "qr")
    kn = nc.alloc_sbuf_tensor("ml_kn", [128, S], BF16); Bkn = P.bufs(4, "kn")
    gz = nc.alloc_sbuf_tensor("ml_gz", [128, 1, S], BF16); Bgz = P.bufs(4, "gz")
    gT = nc.alloc_sbuf_tensor("ml_gT", [128, 1, S], BF16); BgT = P.bufs(4, "gT")
    vt = nc.alloc_sbuf_tensor("ml_vt", [128, 16, 128], BF16); Bvt = P.bufs(16, "vt")
    tab = nc.alloc_sbuf_tensor("ml_tab", [128, 2, S], F32); Btab = P.buf("tab")
    nws = nc.alloc_sbuf_tensor("ml_nws", [128, 5], F32); Bnws = P.buf("nws")
    P.dma("sp", tab[:, 0, :], dram["mla_rope"][0], [], [Btab])
    P.dma("sp", tab[:, 1, :], dram["mla_rope"][1], [], [Btab])
    P.dma("sp", nws[:, :], dram["mla_nw"], [], [Bnws])
    scale = 192 ** -0.5

    normed_proj(c, L, [dram["mla_wcq"][i] for i in range(3)], nws[:, 0:3], Bnws, cqn, Bcqn, 1.0 / 384)
    normed_proj(c, L, [dram["mla_wckv"][i] for i in range(2)], nws[:, 3:5], Bnws, ckvn, Bckvn, 1.0 / 256)

    def ev_kr(pss, tt):
        (pX, BX), (pXr, BXr) = pss
        rope_combine(c, L, pX, BX, pXr, BXr, 64, tab[0:64, 0, tt * 512:(tt + 1) * 512],
                     tab[0:64, 1, tt * 512:(tt + 1) * 512], Btab, kr[0:64, tt * 512:(tt + 1) * 512], Bkr)
    proj_multi(c, [(dram["mla_wkr"], 64), (dram["mla_wkrr"], 64)], ev_kr)

    for h in range(16):
        proj_fm(c, dram["mla_wqn"][h], evac_copy(c, qn, Bqn), cqn, Bcqn, 3)

        def ev_qr(pss, tt):
            (pX, BX), (pXr, BXr) = pss
            rope_combine(c, L, pX, BX, pXr, BXr, 64, tab[0:64, 0, tt * 512:(tt + 1) * 512],
                         tab[0:64, 1, tt * 512:(tt + 1) * 512], Btab, qr[0:64, tt * 512:(tt + 1) * 512], Bqr[tt])
        proj_multi(c, [(dram["mla_wqr"][h], 64), (dram["mla_wqrr"][h], 64)], ev_qr, cqn, Bcqn, 3)
        proj_fm(c, dram["mla_wkn"][h], evac_copy(c, kn, Bkn), ckvn, Bckvn, 2)
        proj_fm(c, dram["mla_wz"][h], evac_silu(c, gz[:, 0, :], Bgz))

        def evac_v(ps, Bps, tb):
            c.evac_rr += 1
            eng = "act" if c.evac_rr % 2 == 0 else "dve"
            P.copy(eng, vt[:, tb, :], ps[:, 0:128], [Bps], [Bvt[tb]])
        proj_tm(c, dram["mla_wv"][h], 128, evac_v, ckvn, Bckvn, 2)
        if h > 0:
            outproj_acc(c, dram["mla_wo"][h - 1], gT, [BgT], 1)
        attn_head(c, [(kn, qn), (kr[0:64, :], qr[0:64, :])], Bkn + Bqn + [Bkr] + Bqr,
                  lambda kb: vt[:, kb, :], Bvt, scale, None, None,
                  gz[:, 0, :], Bgz, gT[:, 0, :], BgT, L)
    outproj_acc(c, dram["mla_wo"][15], gT, [BgT], 1)


DIL_CFG = ((128, 1), (512, 4), (2048, 16))
import os as _os
DIL_GROUPS = [int(x) for x in _os.environ.get('DIL_GROUPS', '0,1,2').split(',')]


def emit_dil(c, layer):
    nc, P, dram = c.nc, c.P, c.dram
    L = Ctx()
    L.pt = Ring(P, [nc.alloc_sbuf_tensor(f"dl_pt{i}", [128, 256], BF16) for i in range(6)], "pt")
    L.xs = Ring(P, [nc.alloc_sbuf_tensor(f"dl_xs{i}", [32, 512], F32) for i in range(3)], "xs")
    L.xr = Ring(P, [nc.alloc_sbuf_tensor(f"dl_xr{i}", [32, 512], F32) for i in range(1)], "xr")
    qTs = [nc.alloc_sbuf_tensor(f"dl_qT{i}", [128, S], BF16) for i in range(2)]
    kTs = [nc.alloc_sbuf_tensor(f"dl_kT{i}", [128, S], BF16) for i in range(2)]
    vts = [nc.alloc_sbuf_tensor(f"dl_vt{i}", [128, 16, 128], BF16) for i in range(2)]
    BqTs = [P.bufs(4, f"qT{i}_") for i in range(2)]
    BkTs = [P.bufs(4, f"kT{i}_") for i in range(2)]
    Bvts = [P.bufs(16, f"vt{i}_") for i in range(2)]
    gz = nc.alloc_sbuf_tensor("dl_gz", [128, 1, S], BF16); Bgz = P.buf("gz")
    gT = nc.alloc_sbuf_tensor("dl_gT", [128, 1, S], BF16); BgT = P.buf("gT")
    oN = nc.alloc_sbuf_tensor("dl_oN", [128, S], F32); BoNg = [P.bufs(4, f"oN{g}_") for g in range(3)]
    dN = nc.alloc_sbuf_tensor("dl_dN", [128, S], F32); BdNg = [P.bufs(4, f"dN{g}_") for g in range(3)]
    tab = nc.alloc_sbuf_tensor("dl_tab", [128, 2, S], F32); Btab = P.buf("tab")
    msk = nc.alloc_sbuf_tensor("dl_msk", [128, 256], BF16); Bmsk = P.buf("msk")
    P.dma("sp", tab[:, 0, :], dram["dil_rope"][0], [], [Btab])
    P.dma("sp", tab[:, 1, :], dram["dil_rope"][1], [], [Btab])
    P.dma("pool", msk[:, :], dram["dil_mask"], [], [Bmsk])
    scale = 128 ** -0.5

    perm_f = c.cst[0:32, 640:672]

    def ev_rope(dst, Bdst):
        pend = []

        def run(task):
            tt, xs, Bxs = task
            sl = slice(tt * 512, (tt + 1) * 512)
            pp, Bpp = c.G.next()
            P.mm(pp[0:32, :], perm_f, xs[0:32, :], True, True, [Bxs, c.Bcst], [Bpp])
            xr, Bxr = L.xr.next()
            P.tt("dve", xr[0:32, :], pp[0:32, :], tab[0:32, 1, sl], ALU.mult, [Bpp, Btab], [Bxr])
            P.tt("dve", xs[0:32, :], xs[0:32, :], tab[0:32, 0, sl], ALU.mult, [Bxs, Btab], [Bxs])
            P.tt("pool", dst[0:32, sl], xs[0:32, :], xr[0:32, :], ALU.add, [Bxs, Bxr], [Bdst[tt]])

        def f(ps, Bps, tt):
            sl = slice(tt * 512, (tt + 1) * 512)
            P.copy("act", dst[:, sl], ps[:, :], [Bps], [Bdst[tt]])
            xs, Bxs = L.xs.next()
            P.copy("act", xs[0:32, :], ps[0:32, :], [Bps], [Bxs])
            pend.append((tt, xs, Bxs))
            if len(pend) > 1:
                run(pend.pop(0))

        def flush():
            while pend:
                run(pend.pop(0))
        return f, flush

    def proj_unit(h, g, bi):
        window, d = DIL_CFG[g]
        nb = (S // d) // 128
        qT, kT, vt = qTs[bi], kTs[bi], vts[bi]
        fq, flq = ev_rope(qT, BqTs[bi])
        proj_fm(c, dram["dil_wq"][g * 8 + h], fq)
        flq()
        fk, flk = ev_rope(kT, BkTs[bi])
        proj_fm(c, dram["dil_wk"][g * 8 + h], fk)
        flk()
        if g == 1:
            proj_fm(c, dram["dil_wz"][h], evac_silu(c, gz[:, 0, :], Bgz))
        wv, Bwv = c.wn.next()
        P.dma("pool", wv[:, :, :], dram["dil_wv"][g * 8 + h], [], [Bwv])
        for r in range(d):
            for kb in range(nb):
                blk = r * nb + kb
                t0 = kb * 128 * d + r
                ps, Bps = c.G.next()
                for k in range(8):
                    lhs = c.uT[:, k, t0:t0 + 127 * d + 1:d]
                    P.mm(ps[:, 0:128], lhs, wv[:, k, :], k == 0, k == 7, [Bwv] + c.BuT, [Bps])
                c.evac_rr += 1
                P.copy("act" if c.evac_rr % 2 == 0 else "dve", vt[:, blk, :], ps[:, 0:128], [Bps], [Bvts[bi][blk]])

    def attn_unit(h, g, bi):
        window, d = DIL_CFG[g]
        nb = (S // d) // 128
        qT, kT, vt = qTs[bi], kTs[bi], vts[bi]
        BqT, BkT, Bvt = BqTs[bi], BkTs[bi], Bvts[bi]
        dq = []

        def pump(limit):
            while dq and (dq[0][0] == "fin" or sum(1 for t in dq if t[0] == "pv") > limit):
                dq.pop(0)[1]()

        for bank in range(4):
            oacc, Bo = c.A.next()
            dacc, Bd = c.Dn.next()
            for qi in range(4):
                blk = bank * 4 + qi
                b = blk % nb
                ps, Bps = c.G.next()
                nk = 2 if b > 0 else 1
                r = blk // nb
                t0 = b * 128 * d + r
                qv = qT[:, t0:t0 + 127 * d + 1:d]
                P.mm(ps[:, 0:128], kT[:, t0:t0 + 127 * d + 1:d], qv, True, True, BkT + BqT, [Bps])
                if b > 0:
                    tp = t0 - 128 * d
                    P.mm(ps[:, 128:256], kT[:, tp:tp + 127 * d + 1:d], qv, True, True, BkT + BqT, [Bps])
                pt, Bpt = L.pt.next()
                P.act(pt[:, 0:128 * nk], ps[:, 0:128 * nk], AF.Exp, [Bps], [Bpt], scale=scale)
                P.tt("dve", pt[:, 0:128 * nk], pt[:, 0:128 * nk], msk[:, 0:128 * nk], ALU.mult, [Bpt, Bmsk], [Bpt])

                def pv(qi=qi, blk=blk, b=b, pt=pt, Bpt=Bpt, oacc=oacc, Bo=Bo, dacc=dacc, Bd=Bd):
                    oc = oacc[:, qi * 128:(qi + 1) * 128]
                    dc = dacc[:, qi * 128:(qi + 1) * 128]
                    P.mm(oc, vt[:, blk, :], pt[:, 0:128], True, b == 0, [Bvt[blk], Bpt], [Bo])
                    if b > 0:
                        P.mm(oc, vt[:, blk - 1, :], pt[:, 128:256], False, True, [Bvt[blk - 1], Bpt], [Bo])
                    P.mm(dc, c.ones_b, pt[:, 0:128], True, b == 0, [c.Bcstb, Bpt], [Bd])
                    if b > 0:
                        P.mm(dc, c.ones_b, pt[:, 128:256], False, True, [c.Bcstb, Bpt], [Bd])
                dq.append(("pv", pv))
                pump(ATTN_DEPTH)

            def fin(bank=bank, oacc=oacc, Bo=Bo, dacc=dacc, Bd=Bd):
                pieces = []
                if d == 1:
                    pieces.append((oN[:, bank * 512:(bank + 1) * 512], dN[:, bank * 512:(bank + 1) * 512], oacc[:, :], dacc[:, :]))
                elif d == 4:
                    pieces.append((oN[:, bank:S:4], dN[:, bank:S:4], oacc[:, :], dacc[:, :]))
                else:
                    for q4 in range(4):
                        r = bank * 4 + q4
                        pieces.append((oN[:, r:S:16], dN[:, r:S:16], oacc[:, q4 * 128:(q4 + 1) * 128], dacc[:, q4 * 128:(q4 + 1) * 128]))
                for (on, dn, oa, da) in pieces:
                    if g == 0:
                        P.copy("act", on, oa, [Bo], [BoNg[0][bank]])
                        P.copy("dve", dn, da, [Bd], [BdNg[0][bank]])
                    else:
                        P.tt("dve", on, oa, on, ALU.add, [Bo] + BoNg[g - 1], [BoNg[g][bank]])
                        P.tt("dve", dn, da, dn, ALU.add, [Bd] + BdNg[g - 1], [BdNg[g][bank]])
            dq.append(("fin", fin))
        pump(-1)

    def fin_head(h):
        for tt in range(4):
            sl = slice(tt * 512, (tt + 1) * 512)
            allo = BoNg[0] + BoNg[1] + BoNg[2]
            alld = BdNg[0] + BdNg[1] + BdNg[2]
            P.recip(dN[:, sl], dN[:, sl], alld, alld)
            P.tt("dve", oN[:, sl], oN[:, sl], dN[:, sl], ALU.mult, allo + alld, allo)
            P.tt("pool", gT[:, 0, sl], oN[:, sl], gz[:, 0, sl], ALU.mult, allo + [Bgz], [BgT])

    units = [(h, g) for h in range(8) for g in range(3)]
    proj_unit(units[0][0], units[0][1], 0)
    pending_out = None
    for i, (h, g) in enumerate(units):
        if i + 1 < len(units):
            proj_unit(units[i + 1][0], units[i + 1][1], (i + 1) % 2)
        if pending_out is not None:
            outproj_acc(c, dram["dil_wo"][pending_out], gT, BgT, 1)
            pending_out = None
        attn_unit(h, g, i % 2)
        if g == 2:
            fin_head(h)
            pending_out = h
    outproj_acc(c, dram["dil_wo"][pending_out], gT, BgT, 1)


def emit_ssd(c, layer):
    nc, P, dram = c.nc, c.P, c.dram
    tri_f = c.cst[:, 128:256]
    negtri_f = c.cst[:, 384:512]
    negones_f = c.cst[:, 512:640]
    dt = nc.alloc_sbuf_tensor("sd_dt", [128, 16, 32], F32); Bdt = P.buf("dt")
    absa = nc.alloc_sbuf_tensor("sd_absa", [128, 16, 32], F32); Babsa = P.buf("absa")
    acum = nc.alloc_sbuf_tensor("sd_acum", [128, 16, 32], F32); Bacum = P.buf("acum")
    wts = nc.alloc_sbuf_tensor("sd_w", [128, 16, 32], F32); Bw = P.buf("w")
    dlast = nc.alloc_sbuf_tensor("sd_dlast", [128, 16, 32], F32); Bdl = P.buf("dlast")
    vecs = nc.alloc_sbuf_tensor("sd_vecs", [128, 3, 32], F32); Bvecs = P.buf("vecs")
    cw = nc.alloc_sbuf_tensor("sd_cw", [128, 32, 5], F32); Bcw = P.buf("cw")
    dsk = nc.alloc_sbuf_tensor("sd_dsk", [128, 16], F32); Bdsk = P.buf("dsk")
    nrm = nc.alloc_sbuf_tensor("sd_nrm", [128, 16], F32); Bnrm = P.buf("nrm")
    tmp32 = nc.alloc_sbuf_tensor("sd_tmp32", [128, 32], F32); Bt32 = P.buf("t32")
    pre = nc.alloc_sbuf_tensor("sd_pre", [128, S + 3], F32); Bpre = P.bufs(4, "pre"); Bpad = P.buf("prepad")
    xTf = nc.alloc_sbuf_tensor("sd_xTf", [128, 2, S], F32); BxTf = P.bufs(2, "xTf")
    xTb = nc.alloc_sbuf_tensor("sd_xTb", [128, 2, S], BF16); BxTb = P.bufs(2, "xTb")
    BT = nc.alloc_sbuf_tensor("sd_BT", [128, S], BF16); BBT = P.buf("BT")
    CT = nc.alloc_sbuf_tensor("sd_CT", [128, S], BF16); BCT = P.buf("CT")
    gz = nc.alloc_sbuf_tensor("sd_gz", [128, 2, S], BF16); Bgz = P.buf("gz")
    dmR = Ring(P, [nc.alloc_sbuf_tensor(f"sd_dm{i}", [128, 512], F32) for i in range(2)], "dm4", 4)
    cacc = dmR
    eeR = Ring(P, [nc.alloc_sbuf_tensor(f"sd_ee{i}", [128, 512], F32) for i in range(1)], "ee4")
    mpR = Ring(P, [nc.alloc_sbuf_tensor(f"sd_mp{i}", [128, 512], BF16) for i in range(2)], "mp4", 4)
    csdR = Ring(P, [nc.alloc_sbuf_tensor(f"sd_csd{i}", [128, 512], BF16) for i in range(2)], "csd4", 4)
    xtokR = Ring(P, [nc.alloc_sbuf_tensor(f"sd_xtok{i}", [128, 256], BF16) for i in range(3)], "xtok")
    xwR = Ring(P, [nc.alloc_sbuf_tensor(f"sd_xw{i}", [128, 256], BF16) for i in range(3)], "xw", 4)
    cbmR = Ring(P, [nc.alloc_sbuf_tensor(f"sd_cbm{i}", [128, 128], F32) for i in range(2)], "cbm")
    btokR = Ring(P, [nc.alloc_sbuf_tensor(f"sd_btok{i}", [128, 128], BF16) for i in range(2)], "btok")
    Sf = nc.alloc_sbuf_tensor("sd_Sf", [128, 256], F32); BSf = P.bufs(4, "Sf")
    Sb = nc.alloc_sbuf_tensor("sd_Sb", [128, 256], BF16); BSb = P.buf("Sb")
    P.dma("sp", vecs[:, :, :], dram["ssm_vecs"], [], [Bvecs])
    P.dma("sp", cw[:, :, :], dram["ssm_cw"], [], [Bcw])
    P.dma("sp", dsk[:, :], dram["ssm_dsk"], [], [Bdsk])
    P.dma("sp", nrm[:, :], dram["ssm_nrm"], [], [Bnrm])
    P.memset("pool", pre[:, 0:3], 0.0, [Bpad])
    P.act(vecs[:, 1, :], vecs[:, 1, :], AF.Exp, [Bvecs], [Bvecs])

    def evac_dt(ps, Bps, tb):
        P.tt("dve", tmp32[:, :], ps[:, 0:32], vecs[:, 0, :], ALU.add, [Bps, Bvecs], [Bt32])
        P.act(tmp32[:, :], tmp32[:, :], AF.Exp, [Bt32], [Bt32])
        P.act(dt[:, tb, :], tmp32[:, :], AF.Ln, [Bt32], [Bdt], bias=1.0)
        P.tt("dve", absa[:, tb, :], dt[:, tb, :], vecs[:, 1, :], ALU.mult, [Bdt, Bvecs], [Babsa])
    proj_tm(c, dram["ssm_wdt"], 32, evac_dt)
    for tb in range(16):
        ps, Bps = c.G.next()
        P.mm(ps[:, 0:32], negtri_f, absa[:, tb, :], True, True, [c.Bcst, Babsa], [Bps])
        P.mm(ps[:, 32:64], negones_f, absa[:, tb, :], True, True, [c.Bcst, Babsa], [Bps])
        P.copy("dve", acum[:, tb, :], ps[:, 0:32], [Bps], [Bacum])
        P.copy("dve", dlast[:, tb, :], ps[:, 32:64], [Bps], [Bdl])
        P.tt("dve", wts[:, tb, :], dlast[:, tb, :], acum[:, tb, :], ALU.subtract, [Bdl, Bacum], [Bw])
        P.act(wts[:, tb, :], wts[:, tb, :], AF.Exp, [Bw], [Bw])
        P.tt("dve", wts[:, tb, :], wts[:, tb, :], dt[:, tb, :], ALU.mult, [Bw, Bdt], [Bw])
        P.act(dlast[:, tb, :], dlast[:, tb, :], AF.Exp, [Bdl], [Bdl])

    def conv_silu(ch, outs):
        for tt in range(4):
            a, Ba4 = cacc.next()
            Ba = Ba4[0]
            o = tt * 512
            Bp = [Bpre[tt], Bpre[tt - 1] if tt > 0 else Bpad]
            P.ts("dve", a[:, :], pre[:, o:o + 512], cw[:, ch, 0:1], cw[:, ch, 4:5], ALU.mult, ALU.add, Bp + [Bcw], [Ba])
            for k in range(1, 4):
                P.stt("dve", a[:, :], pre[:, o + k:o + k + 512], cw[:, ch, k:k + 1], a[:, :], ALU.mult, ALU.add,
                      Bp + [Bcw, Ba], [Ba])
            for (dst, Bdst) in outs:
                P.act(dst[:, o:o + 512], a[:, :], AF.Silu, [Ba], [Bdst])

    def evac_pre(ps, Bps, tt):
        P.copy("act", pre[:, 3 + tt * 512:3 + (tt + 1) * 512], ps[:, :], [Bps], [Bpre[tt]])

    for g in range(8):
        proj_fm(c, dram["ssm_wB"][g], evac_pre)
        if g > 0:
            outproj_acc(c, dram["ssm_wo"][g - 1], gz, Bgz, 2)
        conv_silu(16 + g, [(BT, BBT)])
        proj_fm(c, dram["ssm_wC"][g], evac_pre)
        conv_silu(24 + g, [(CT, BCT)])
        for i in range(2):
            proj_fm(c, dram["ssm_wx"][2 * g + i], evac_pre)
            proj_fm(c, dram["ssm_wz"][2 * g + i], evac_silu(c, gz[:, i, :], Bgz))
            conv_silu(2 * g + i, [(xTf[:, i, :], BxTf[i]), (xTb[:, i, :], BxTb[i])])
        dq = []
        for ck in range(16):
            cs = slice(ck * 128, (ck + 1) * 128)
            psT, BpsT = c.G.next()
            for i in range(2):
                P.mm(psT[:, i * 128:(i + 1) * 128], xTb[:, i, cs], c.ident_b, True, True, [BxTb[i], c.Bcstb], [BpsT])
            P.mm(psT[:, 256:384], BT[:, cs], c.ident_b, True, True, [BBT, c.Bcstb], [BpsT])
            pst, Bpst = c.A.next()
            P.mm(pst[:, 256:384], BT[:, cs], CT[:, cs], True, True, [BBT, BCT], [Bpst])
            pb4, Bpb4 = c.G.next()
            for j in range(4):
                h = 4 * g + j
                P.mm(pb4[:, j * 128:(j + 1) * 128], absa[:, ck, h:h + 1].to_broadcast([128, 128]), negtri_f, True, True,
                     [Babsa, c.Bcst], [Bpb4])
            xtok, Bxtok = xtokR.next()
            P.copy("act", xtok[:, :], psT[:, 0:256], [BpsT], [Bxtok])
            btok, Bbtok = btokR.next()
            P.copy("act", btok[:, :], psT[:, 256:384], [BpsT], [Bbtok])
            cbm, Bcbm = cbmR.next()
            P.tt("dve", cbm[:, :], pst[:, 256:384], tri_f, ALU.mult, [Bpst, c.Bcst], [Bcbm])
            dm4, Bdm4 = dmR.next()
            extra = []
            if ck > 0:
                ee4, Bee4 = eeR.next()
                P.act(ee4[:, :], pb4[:, :], AF.Exp, [Bpb4], [Bee4])
                extra = [Bee4]
            for j in range(4):
                h = 4 * g + j
                P.ts("dve", dm4[:, j * 128:(j + 1) * 128], pb4[:, j * 128:(j + 1) * 128], acum[:, ck, h:h + 1], 0.0,
                     ALU.subtract, ALU.min, [Bpb4, Bacum] + extra, [Bdm4[j]])
            P.act(dm4[:, :], dm4[:, :], AF.Exp, Bdm4, Bdm4)
            mp4, Bmp4 = mpR.next()
            for j in range(4):
                h = 4 * g + j
                P.stt("dve", mp4[:, j * 128:(j + 1) * 128], dm4[:, j * 128:(j + 1) * 128], dt[:, ck, h:h + 1], cbm[:, :],
                      ALU.mult, ALU.mult, [Bdm4[j], Bdt, Bcbm], [Bmp4[j]])
            csd4, Bcsd4 = csdR.next()
            if ck > 0:
                for j in range(4):
                    P.tt("pool", csd4[:, j * 128:(j + 1) * 128], ee4[:, j * 128:(j + 1) * 128], CT[:, cs], ALU.mult,
                         [BCT, Bee4], [Bcsd4[j]])
            xw, Bxw = xwR.next()
            for j in range(4):
                P.ts("pool", xw[:, j * 64:(j + 1) * 64], xtok[:, j * 64:(j + 1) * 64],
                     wts[:, ck, 4 * g + j:4 * g + j + 1], None, ALU.mult, None, [Bxtok, Bw], [Bxw[j]])

            def cd(ck=ck, cs=cs, pst=pst, Bpst=Bpst, xtok=xtok, Bxtok=Bxtok, btok=btok, Bbtok=Bbtok, xw=xw, Bxw=Bxw,
                   mp4=mp4, Bmp4=Bmp4, csd4=csd4, Bcsd4=Bcsd4):
                P.mm(pst[:, 0:256], btok[:, :], xw[:, :], True, True, [Bbtok] + Bxw, [Bpst])
                yps, Byps = c.Dn.next()
                for i in range(2):
                    for jj in range(2):
                        j = 2 * i + jj
                        yo = yps[64 * jj:64 * jj + 64, i * 128:(i + 1) * 128]
                        P.mm(yo, xtok[:, j * 64:(j + 1) * 64], mp4[:, j * 128:(j + 1) * 128], True, ck == 0,
                             [Bxtok, Bmp4[j]], [Byps])
                        if ck > 0:
                            P.mm(yo, Sb[:, j * 64:(j + 1) * 64], csd4[:, j * 128:(j + 1) * 128], False, True,
                                 [BSb, Bcsd4[j]], [Byps])
                for i in range(2):
                    P.stt("dve", xTf[:, i, cs], xTf[:, i, cs], dsk[:, 2 * g + i:2 * g + i + 1], yps[:, i * 128:(i + 1) * 128],
                          ALU.mult, ALU.add, [BxTf[i], Bdsk, Byps], [BxTf[i]])
                if ck == 0:
                    P.copy("dve", Sf[:, :], pst[:, 0:256], [Bpst], BSf)
                else:
                    for j in range(4):
                        P.stt("dve", Sf[:, j * 64:(j + 1) * 64], Sf[:, j * 64:(j + 1) * 64],
                              dlast[:, ck, 4 * g + j:4 * g + j + 1], pst[:, j * 64:(j + 1) * 64], ALU.mult, ALU.add,
                              [BSf[j], Bdl, Bpst], [BSf[j]])
                if ck < 15:
                    P.copy("act", Sb[:, :], Sf[:, :], BSf, [BSb])
            dq.append(cd)
            while len(dq) > 1:
                dq.pop(0)()
        while dq:
            dq.pop(0)()
        for tt in range(4):
            sl = slice(tt * 512, (tt + 1) * 512)
            for i in range(2):
                P.tt("pool", xTf[:, i, sl], xTf[:, i, sl], gz[:, i, sl], ALU.mult, [BxTf[i], Bgz], [BxTf[i]])
            r, Br = rms_stats(c, xTf, lambda k, t: BxTf[k], 2, tt, 1.0 / 256)
            for i in range(2):
                P.stt("dve", gz[:, i, sl], xTf[:, i, sl], nrm[:, 2 * g + i:2 * g + i + 1], r[:, :], ALU.mult, ALU.mult,
                      [BxTf[i], Br, Bnrm], [Bgz])
    outproj_acc(c, dram["ssm_wo"][7], gz, Bgz, 2)


def tile_cols(W, width=128):
    K, N = W.shape
    return np.ascontiguousarray(W.reshape(K // 128, 128, N // width, width).transpose(2, 1, 0, 3))


def tile_rows(W, nk):
    R, N = W.shape
    return np.ascontiguousarray(W.reshape(R // (128 * nk), nk, 128, N).transpose(0, 2, 1, 3))


def rope_tables(half, reps):
    inv_freq = (np.float32(ROPE_THETA) ** (-np.arange(half, dtype=np.float32) / np.float32(half))).astype(np.float32)
    ang = np.arange(S, dtype=np.float32)[None, :] * inv_freq[:, None]
    cos = np.cos(ang).astype(np.float32)
    sin = np.sin(ang).astype(np.float32)
    t = np.zeros((2, 128, S), np.float32)
    t[0] = 1.0
    for r in range(reps):
        b = r * 2 * half
        t[0, b:b + half] = cos
        t[0, b + half:b + 2 * half] = cos
        t[1, b:b + half] = -sin
        t[1, b + half:b + 2 * half] = sin
    return t


def make_consts():
    cst = np.zeros((128, 5 * 128 + 32), np.float32)
    for m in range(32):
        cst[(m + 16) % 32, 640 + m] = 1.0
    i = np.arange(128)
    cst[:, 0:128] = np.eye(128)
    cst[:, 128:256] = (i[:, None] <= i[None, :])
    cst[:, 256:384] = 1.0
    cst[:, 384:512] = -(i[:, None] <= i[None, :]).astype(np.float32)
    cst[:, 512:640] = -1.0
    return cst


def host_prep(inputs, layers):
    shared = {}
    shared["consts"] = make_consts()
    nw = np.concatenate([inputs["norm_w"], inputs["final_norm_w"][None]], axis=0)
    shared["normw"] = np.ascontiguousarray(nw.reshape(5, 8, 128).transpose(2, 0, 1))
    if 0 in layers:
        W = inputs["ssm_in_w"][0]
        shared["ssm_wz"] = tile_cols(W[:, 0:2048])
        shared["ssm_wx"] = tile_cols(W[:, 2048:4096])
        shared["ssm_wB"] = tile_cols(W[:, 4096:5120])
        shared["ssm_wC"] = tile_cols(W[:, 5120:6144])
        shared["ssm_wdt"] = tile_cols(W[:, 6144:6176], 32)[0]
        vec = np.stack([inputs["ssm_dt_bias"][0], inputs["ssm_A_log"][0], inputs["ssm_D"][0]], 0)
        shared["ssm_vecs"] = np.ascontiguousarray(np.broadcast_to(vec[None], (128, 3, 32)))
        cwb = np.concatenate([inputs["ssm_conv_w"][0], inputs["ssm_conv_b"][0][None]], 0)
        shared["ssm_cw"] = np.ascontiguousarray(cwb.reshape(5, 32, 128).transpose(2, 1, 0))
        shared["ssm_dsk"] = np.ascontiguousarray(np.repeat(inputs["ssm_D"][0], 64).reshape(16, 128).T)
        shared["ssm_nrm"] = np.ascontiguousarray(inputs["ssm_norm_w"][0].reshape(16, 128).T)
        shared["ssm_wo"] = tile_rows(inputs["ssm_out_w"][0], 2)
    if 1 in layers:
        W = inputs["mla_in_w"][0]
        perm = np.concatenate([np.arange(32, 64), np.arange(0, 32)])
        shared["mla_wcq"] = tile_cols(W[:, 0:384])
        shared["mla_wckv"] = tile_cols(W[:, 384:640])
        shared["mla_wkr"] = tile_cols(W[:, 640:704], 64)[0]
        shared["mla_wkrr"] = tile_cols(W[:, 640:704][:, perm], 64)[0]
        shared["mla_wz"] = tile_cols(W[:, 704:2752])
        UQ = inputs["mla_uq_w"][0].reshape(384, 16, 192)
        shared["mla_wqn"] = tile_cols(np.ascontiguousarray(UQ[:, :, 0:128]).reshape(384, 2048))
        shared["mla_wqr"] = tile_cols(np.ascontiguousarray(UQ[:, :, 128:192]).reshape(384, 1024), 64)
        shared["mla_wqrr"] = tile_cols(np.ascontiguousarray(UQ[:, :, 128:192][:, :, perm]).reshape(384, 1024), 64)
        UKV = inputs["mla_ukv_w"][0].reshape(256, 16, 256)
        shared["mla_wkn"] = tile_cols(np.ascontiguousarray(UKV[:, :, 0:128]).reshape(256, 2048))
        shared["mla_wv"] = tile_cols(np.ascontiguousarray(UKV[:, :, 128:256]).reshape(256, 2048))
        shared["mla_wo"] = tile_rows(inputs["mla_out_w"][0], 1)
        nwq = inputs["mla_q_norm_w"][0].reshape(3, 128).T
        nwkv = inputs["mla_kv_norm_w"][0].reshape(2, 128).T
        shared["mla_nw"] = np.ascontiguousarray(np.concatenate([nwq, nwkv], axis=1))
        shared["mla_rope"] = rope_tables(32, 2)
    if 3 in layers:
        W = inputs["dil_in_w"][0]
        perm = np.arange(128)
        perm[0:16] = np.arange(16, 32)
        perm[16:32] = np.arange(0, 16)
        wq, wk, wv, wqr, wkr = [], [], [], [], []
        for g in range(3):
            base = 3072 * g
            Q = W[:, base:base + 1024].reshape(1024, 8, 128)
            Kw = W[:, base + 1024:base + 2048].reshape(1024, 8, 128)
            wq.append(tile_cols(Q.reshape(1024, 1024)))
            wk.append(tile_cols(Kw.reshape(1024, 1024)))
            wv.append(tile_cols(W[:, base + 2048:base + 3072]))
        shared["dil_wq"] = np.concatenate(wq, 0)
        shared["dil_wk"] = np.concatenate(wk, 0)
        shared["dil_wv"] = np.concatenate(wv, 0)
        shared["dil_wz"] = tile_cols(W[:, 9216:10240])
        shared["dil_wo"] = tile_rows(inputs["dil_out_w"][0], 1)
        shared["dil_rope"] = rope_tables(16, 1)
        i = np.arange(128)
        m = np.zeros((128, 256), np.float32)
        m[:, 0:128] = (i[:, None] <= i[None, :])
        m[:, 128:256] = (i[:, None] >= i[None, :])
        shared["dil_mask"] = m
    if 2 in layers:
        W = inputs["fox_in_w"][0]
        shared["fox_wq"] = tile_cols(W[:, 0:2048])
        shared["fox_wk"] = tile_cols(W[:, 2048:4096])
        shared["fox_wv"] = tile_cols(W[:, 4096:6144], 256)
        shared["fox_wf"] = tile_cols(W[:, 6144:6160], 16)[0]
        shared["fox_wz"] = tile_cols(W[:, 6160:8208])
        shared["fox_fb"] = np.ascontiguousarray(np.broadcast_to(inputs["fox_f_bias"][0][None, :], (128, 16)))
        shared["fox_wo"] = tile_rows(inputs["fox_out_w"][0], 2)
    return shared


def build_program(layers, shared_shapes, final_norm):
    nc = bass.Bass("TRN2", target_bir_lowering=False)
    dram = {}
    for k, shp in shared_shapes.items():
        dram[k] = nc.dram_tensor(k, list(shp), F32, kind="ExternalInput").ap()
    hin = nc.dram_tensor("hin", [NSEQ, 8, 128, S], F32, kind="ExternalInput").ap()
    hout = nc.dram_tensor("hout", [NSEQ, 8, 128, S], F32, kind="ExternalOutput").ap()
    P = Prog(nc)
    c = setup_common(nc, P, dram)
    Bout = P.buf("hout")
    for s in range(NSEQ):
        for k in range(8):
            for t in range(4):
                P.dma("sp", c.hT[:, k, t * 512:(t + 1) * 512], hin[s, k, :, t * 512:(t + 1) * 512], [], [c.BhT[k][t]])
        for layer in layers:
            emit_rmsnorm_u(c, layer)
            emit_layer_cached(c, layer, LAYER_EMITTERS[layer])
        if final_norm:
            for tt in range(4):
                r, Br = rms_stats(c, c.hT, lambda k, t: c.BhT[k][t], 8, tt, 1.0 / D)
                for k in range(8):
                    eng = "dve"
                    P.stt(eng, c.hT[:, k, tt * 512:(tt + 1) * 512], c.hT[:, k, tt * 512:(tt + 1) * 512],
                          c.normw[:, 4, k:k + 1], r[:, :], ALU.mult, ALU.mult,
                          [c.BhT[k][tt], Br, c.Bnormw], [c.BhT[k][tt]])
        for k in range(8):
            for t in range(4):
                P.dma("sp", hout[s, k, :, t * 512:(t + 1) * 512], c.hT[:, k, t * 512:(t + 1) * 512], [c.BhT[k][t]], [Bout])
    P.barrier()
    stats = P.emit()
    return nc, stats


def emit_layer_cached(c, layer, fn):
    from contextlib import ExitStack
    nc = c.nc
    c._scope_id = getattr(c, "_scope_id", 0) + 1
    sid = c._scope_id
    with ExitStack() as st:
        class NCProxy:
            def __getattr__(self, a):
                if a == "alloc_sbuf_tensor":
                    return lambda name, shape, dtype: st.enter_context(nc.sbuf_tensor(f"{name}_s{sid}", shape, dtype))
                return getattr(nc, a)
        c.nc = NCProxy()
        try:
            fn(c, layer)
        finally:
            c.nc = nc
        c.P.barrier()


_CACHE = {}
LAYER_EMITTERS = {0: emit_ssd, 1: emit_mla, 2: emit_fox, 3: emit_dil}


def run_layers(hT_all, inputs, layers, final_norm):
    shared = host_prep(inputs, layers)
    key = (tuple(layers), final_norm)
    if key not in _CACHE:
        _CACHE[key] = build_program(layers, {k: v.shape for k, v in shared.items()}, final_norm)
    nc, stats = _CACHE[key]
    in_maps = []
    for core in range(8):
        m = dict(shared)
        m["hin"] = np.ascontiguousarray(hT_all[core * NSEQ:(core + 1) * NSEQ])
        in_maps.append(m)
    res = run_bass_kernel_spmd(nc, in_maps, core_ids=list(range(8)))
    return np.concatenate([r["hout"] for r in res.results], axis=0)


def to_fm(x):
    B = x.shape[0]
    return np.ascontiguousarray(x.transpose(0, 2, 1).reshape(B, 8, 128, S))


def from_fm(hT):
    B = hT.shape[0]
    return np.ascontiguousarray(hT.reshape(B, D, S).transpose(0, 2, 1))


def kernel(**inputs):
    inputs = {k: np.asarray(v, dtype=np.float32) for k, v in inputs.items()}
    hT = to_fm(inputs["x"])
    hT = run_layers(hT, inputs, [0, 1, 2, 3], True)
    return from_fm(hT)
```
